# Optimizing a Trainium2 kernel written in Bass

```python
import math
import jax, jax.numpy as jnp
from jax import lax
import numpy as np

D_MODEL = 1024
BATCH = 4
SEQ = 4096
DEPTH = 4

N_MIXERS = 4
RMS_EPS = 1e-6
NEG = -1e30
D_FF = -(-(8 * D_MODEL) // (3 * 256)) * 256

HEAD_DIM = 64
ATT_HEADS = D_MODEL // HEAD_DIM
ROPE_DIM = HEAD_DIM // 4
ROPE_THETA = 500000.0

S5_GROUP = 16
S5_GROUPS = D_MODEL // S5_GROUP
S5_STATE = 64
S5_DT_MIN = 1e-3
S5_DT_MAX = 1e-1

DIL_PATTERNS = ((128, 1), (512, 4), (2048, 16))
DIL_BLOCK = 128

HGRN_HEAD_DIM = 128
HGRN_HEADS = D_MODEL // HGRN_HEAD_DIM
HGRN_CHUNK = 64

MOBA_BLOCK = 256
MOBA_TOPK = 3
MOBA_Q_CHUNK = 32

kernel_name = "hybrid_s5_dilated_hgrn2_moba_trunk"


def n_layers_of(mixer):
    return len(range(mixer, DEPTH, N_MIXERS))


def rmsnorm(x, g):
    xf = x.astype(jnp.float32)
    y = xf * lax.rsqrt(jnp.mean(xf * xf, axis=-1, keepdims=True) + RMS_EPS)
    return (y * g.astype(jnp.float32)).astype(x.dtype)


def rope_tables(positions):
    inv = ROPE_THETA ** (-jnp.arange(0, ROPE_DIM, 2, dtype=jnp.float32) / ROPE_DIM)
    ang = positions.astype(jnp.float32)[..., None] * inv
    return jnp.cos(ang)[:, :, None, :], jnp.sin(ang)[:, :, None, :]


def apply_partial_rope(x, cos, sin):
    half = ROPE_DIM // 2
    x1, x2, rest = x[..., :half], x[..., half:ROPE_DIM], x[..., ROPE_DIM:]
    return jnp.concatenate([x1 * cos - x2 * sin, x2 * cos + x1 * sin, rest], axis=-1)


def swiglu(xn, w_gate_up, w_down):
    gate, up = jnp.split(xn @ w_gate_up, 2, axis=-1)
    return (jax.nn.silu(gate) * up) @ w_down


def _complex_affine_combine(left, right):
    a1r, a1i, b1r, b1i = left
    a2r, a2i, b2r, b2i = right
    return (a2r * a1r - a2i * a1i, a2r * a1i + a2i * a1r,
            a2r * b1r - a2i * b1i + b2r, a2r * b1i + a2i * b1r + b2i)


def s5_mixer(u, a_re, a_im, log_dt, b_re, b_im, c_re, c_im, d_skip, w_glu, b_glu):
    bsz, seq, _ = u.shape
    f32 = jnp.float32
    uf = u.astype(f32)
    ug = uf.reshape(bsz, seq, S5_GROUPS, S5_GROUP)
    lr, li = a_re.astype(f32), a_im.astype(f32)
    dt = jnp.exp(log_dt.astype(f32))[:, None]
    mag = jnp.exp(lr * dt)
    ab_re, ab_im = mag * jnp.cos(li * dt), mag * jnp.sin(li * dt)
    den = lr * lr + li * li
    m_re = ab_re - 1.0
    f_re = (m_re * lr + ab_im * li) / den
    f_im = (ab_im * lr - m_re * li) / den
    bu_re = jnp.einsum('bsgc,gpc->bsgp', ug, b_re.astype(f32))
    bu_im = jnp.einsum('bsgc,gpc->bsgp', ug, b_im.astype(f32))
    e_re = f_re * bu_re - f_im * bu_im
    e_im = f_re * bu_im + f_im * bu_re
    a_t_re = jnp.broadcast_to(ab_re, e_re.shape)
    a_t_im = jnp.broadcast_to(ab_im, e_im.shape)
    _, _, h_re, h_im = lax.associative_scan(_complex_affine_combine, (a_t_re, a_t_im, e_re, e_im), axis=1)
    y = (jnp.einsum('bsgp,gcp->bsgc', h_re, c_re.astype(f32))
         - jnp.einsum('bsgp,gcp->bsgc', h_im, c_im.astype(f32)))
    y = y.reshape(bsz, seq, D_MODEL) + d_skip.astype(f32) * uf
    z = jax.nn.gelu(y)
    val, gate = jnp.split(z @ w_glu.astype(f32) + b_glu.astype(f32), 2, axis=-1)
    return (val * jax.nn.sigmoid(gate)).astype(u.dtype)


def dilated_branch(q, k, v, window, dilation):
    bsz, seq, heads, hd = q.shape
    n_back = window // dilation
    blk = DIL_BLOCK
    padded = -(-seq // (dilation * blk)) * (dilation * blk)
    sub_len = padded // dilation
    n_blk = sub_len // blk

    def by_stride(t):
        t = jnp.pad(t, ((0, 0), (0, padded - seq), (0, 0), (0, 0)))
        t = t.reshape(bsz, sub_len, dilation, heads, hd).transpose(0, 2, 1, 3, 4)
        return t.reshape(bsz, dilation, n_blk, blk, heads, hd)

    def with_prev(t):
        prev = jnp.pad(t[:, :, :-1], ((0, 0), (0, 0), (1, 0), (0, 0), (0, 0), (0, 0)))
        return jnp.concatenate([prev, t], axis=3)

    qb = by_stride(q)
    kk, vv = with_prev(by_stride(k)), with_prev(by_stride(v))
    qi = jnp.arange(blk)[:, None]
    kj = jnp.arange(2 * blk)[None, :]
    rel = qi + blk - kj
    band = (rel >= 0) & (rel <= n_back)
    mask = band[None] & ((jnp.arange(n_blk) > 0)[:, None, None] | (kj >= blk)[None])
    s = jnp.einsum('brnqhd,brnkhd->brnhqk', qb, kk) * (hd ** -0.5)
    s = jnp.where(mask[None, None, :, None], s, NEG)
    lse = jax.nn.logsumexp(s, axis=-1)
    p = jnp.exp(s - lse[..., None])
    o = jnp.einsum('brnhqk,brnkhd->brnqhd', p, vv)

    def unstride(t):
        tail = t.shape[4:]
        t = t.reshape((bsz, dilation, sub_len) + tail)
        t = jnp.moveaxis(t, 1, 2).reshape((bsz, padded) + tail)
        return t[:, :seq]

    return unstride(o), unstride(jnp.moveaxis(lse, 3, 4))


def dilated_mixer(xn, w_qkv, w_o, cos, sin):
    bsz, seq, _ = xn.shape
    qkv = (xn @ w_qkv).astype(jnp.float32).reshape(bsz, seq, 3, ATT_HEADS, HEAD_DIM)
    q = apply_partial_rope(qkv[:, :, 0], cos, sin)
    k = apply_partial_rope(qkv[:, :, 1], cos, sin)
    v = qkv[:, :, 2]
    outs, lses = [], []
    for window, dilation in DIL_PATTERNS:
        o, l = dilated_branch(q, k, v, window, dilation)
        outs.append(o)
        lses.append(l)
    w = jax.nn.softmax(jnp.stack(lses, axis=0), axis=0)
    o = jnp.einsum('gbsh,gbshd->bshd', w, jnp.stack(outs, axis=0))
    return o.reshape(bsz, seq, D_MODEL).astype(xn.dtype) @ w_o


def hgrn2_mixer(xn, w_in, lower_bound, norm_g, w_o):
    bsz, seq, _ = xn.shape
    f32 = jnp.float32
    q, f, i, g = jnp.split((xn @ w_in).astype(f32), 4, axis=-1)
    q = jax.nn.silu(q)
    lb = lower_bound.astype(f32)
    fg = lb + (1.0 - lb) * jax.nn.sigmoid(f)
    k = 1.0 - fg
    log_f = jnp.log(fg)
    n_chunks = seq // HGRN_CHUNK

    def to_chunks(t):
        t = t.reshape(bsz, n_chunks, HGRN_CHUNK, HGRN_HEADS, HGRN_HEAD_DIM)
        return t.transpose(1, 0, 3, 2, 4)

    causal = jnp.tril(jnp.ones((HGRN_CHUNK, HGRN_CHUNK), dtype=bool))

    def chunk_step(state, inp):
        qc, kc, ic, lc = inp
        b = jnp.cumsum(lc, axis=2)
        o_inter = jnp.einsum('bhtk,bhkv->bhtv', qc * jnp.exp(b), state)
        diff = jnp.where(causal[:, :, None], b[:, :, :, None, :] - b[:, :, None, :, :], NEG)
        attn = jnp.einsum('bhtk,bhsk,bhtsk->bhts', qc, kc, jnp.exp(diff))
        o_intra = jnp.einsum('bhts,bhsv->bhtv', attn, ic)
        b_last = b[:, :, -1:, :]
        state = (jnp.exp(b_last[:, :, 0, :, None]) * state
                 + jnp.einsum('bhsk,bhsv->bhkv', kc * jnp.exp(b_last - b), ic))
        return state, o_inter + o_intra

    state0 = jnp.zeros((bsz, HGRN_HEADS, HGRN_HEAD_DIM, HGRN_HEAD_DIM), f32)
    _, o = lax.scan(chunk_step, state0, (to_chunks(q), to_chunks(k), to_chunks(i), to_chunks(log_f)))
    o = o.transpose(1, 0, 3, 2, 4).reshape(bsz, seq, HGRN_HEADS, HGRN_HEAD_DIM)
    o = o * lax.rsqrt(jnp.mean(o * o, axis=-1, keepdims=True) + RMS_EPS) * norm_g.astype(f32)
    o = o.reshape(bsz, seq, D_MODEL) * jax.nn.silu(g)
    return o.astype(xn.dtype) @ w_o


def moba_mixer(xn, w_qkv, w_o, cos, sin):
    bsz, seq, _ = xn.shape
    qkv = (xn @ w_qkv).astype(jnp.float32).reshape(bsz, seq, 3, ATT_HEADS, HEAD_DIM)
    padded = -(-seq // MOBA_BLOCK) * MOBA_BLOCK
    n_blk = padded // MOBA_BLOCK

    def heads_first(t):
        return jnp.pad(t, ((0, 0), (0, padded - seq), (0, 0), (0, 0))).transpose(0, 2, 1, 3)

    q = heads_first(apply_partial_rope(qkv[:, :, 0], cos, sin))
    k = heads_first(apply_partial_rope(qkv[:, :, 1], cos, sin))
    v = heads_first(qkv[:, :, 2])
    scale = HEAD_DIM ** -0.5
    kb = k.reshape(bsz, ATT_HEADS, n_blk, MOBA_BLOCK, HEAD_DIM)
    vb = v.reshape(bsz, ATT_HEADS, n_blk, MOBA_BLOCK, HEAD_DIM)
    n_sel = min(MOBA_TOPK, n_blk - 1)
    q_blk = jnp.arange(padded) // MOBA_BLOCK
    if n_sel > 0:
        gate = jnp.einsum('bhsd,bhnd->bhsn', q, kb.mean(axis=3))
        past = jnp.arange(n_blk)[None, :] < q_blk[:, None]
        _, sel = lax.top_k(jnp.where(past, gate, NEG), n_sel)
    else:
        sel = jnp.zeros((bsz, ATT_HEADS, padded, 0), jnp.int32)
    sel_ok = sel < q_blk[:, None]
    n_chunks = padded // MOBA_Q_CHUNK

    def chunkify(t):
        return jnp.moveaxis(t.reshape(t.shape[:2] + (n_chunks, MOBA_Q_CHUNK) + t.shape[3:]), 2, 0)

    bi = jnp.arange(bsz)[:, None, None, None]
    hi = jnp.arange(ATT_HEADS)[None, :, None, None]
    q_local = jnp.arange(MOBA_Q_CHUNK)
    k_local = jnp.arange(MOBA_BLOCK)
    n_past_keys = n_sel * MOBA_BLOCK

    def attend_chunk(args):
        c, qc, sc, okc = args
        kg, vg = kb[bi, hi, sc], vb[bi, hi, sc]
        s_past = jnp.einsum('bhqd,bhqjkd->bhqjk', qc, kg) * scale
        s_past = jnp.where(okc[..., None], s_past, NEG).reshape(bsz, ATT_HEADS, MOBA_Q_CHUNK, n_past_keys)
        start = (c * MOBA_Q_CHUNK // MOBA_BLOCK) * MOBA_BLOCK
        ko = lax.dynamic_slice_in_dim(k, start, MOBA_BLOCK, axis=2)
        vo = lax.dynamic_slice_in_dim(v, start, MOBA_BLOCK, axis=2)
        s_own = jnp.einsum('bhqd,bhkd->bhqk', qc, ko) * scale
        s_own = jnp.where((start + k_local)[None, :] <= (c * MOBA_Q_CHUNK + q_local)[:, None], s_own, NEG)
        p = jax.nn.softmax(jnp.concatenate([s_past, s_own], axis=-1), axis=-1)
        p_past = p[..., :n_past_keys].reshape(bsz, ATT_HEADS, MOBA_Q_CHUNK, n_sel, MOBA_BLOCK)
        return (jnp.einsum('bhqjk,bhqjkd->bhqd', p_past, vg)
                + jnp.einsum('bhqk,bhkd->bhqd', p[..., n_past_keys:], vo))

    o = lax.map(attend_chunk, (jnp.arange(n_chunks), chunkify(q), chunkify(sel), chunkify(sel_ok)))
    o = jnp.moveaxis(o, 0, 2).reshape(bsz, ATT_HEADS, padded, HEAD_DIM)[:, :, :seq]
    o = o.transpose(0, 2, 1, 3).reshape(bsz, seq, D_MODEL)
    return o.astype(xn.dtype) @ w_o


def setup_inputs(seed: int = 0) -> dict:
    key = jax.random.key(seed)
    ks = iter(jax.random.split(key, 40))

    def nrm(shape, scale):
        return jax.random.normal(next(ks), shape, jnp.float32) * scale

    na, nb, nc, nd = (n_layers_of(m) for m in range(N_MIXERS))
    D, G, P, C = D_MODEL, S5_GROUPS, S5_STATE, S5_GROUP
    x = nrm((BATCH, SEQ, D), 1.0)
    positions = (jnp.arange(SEQ, dtype=jnp.int32)[None, :]
                 + jax.random.randint(next(ks), (BATCH, 1), 0, 1024, dtype=jnp.int32))
    norm_mix = 1.0 + nrm((DEPTH, D), 0.02)
    norm_ffn = 1.0 + nrm((DEPTH, D), 0.02)
    norm_final = 1.0 + nrm((D,), 0.02)
    s5_a_re = -0.5 + nrm((na, G, P), 0.01)
    s5_a_im = jnp.pi * jnp.arange(P, dtype=jnp.float32) + nrm((na, G, P), 0.01)
    s5_log_dt = jax.random.uniform(next(ks), (na, G), jnp.float32, math.log(S5_DT_MIN), math.log(S5_DT_MAX))
    s5_b_re = nrm((na, G, P, C), (2 * C) ** -0.5)
    s5_b_im = nrm((na, G, P, C), (2 * C) ** -0.5)
    s5_c_re = nrm((na, G, C, P), (2 * P) ** -0.5)
    s5_c_im = nrm((na, G, C, P), (2 * P) ** -0.5)
    s5_d = nrm((na, D), 1.0)
    s5_w_glu = nrm((na, D, 2 * D), D ** -0.5)
    s5_b_glu = nrm((na, 2 * D), 0.01)
    dil_w_qkv = nrm((nb, D, 3 * D), D ** -0.5)
    dil_w_o = nrm((nb, D, D), D ** -0.5)
    hgrn_w_in = nrm((nc, D, 4 * D), D ** -0.5)
    hgrn_lower_bound = nrm((DEPTH, D), 0.1)
    hgrn_norm = 1.0 + nrm((nc, HGRN_HEAD_DIM), 0.02)
    hgrn_w_o = nrm((nc, D, D), D ** -0.5)
    moba_w_qkv = nrm((nd, D, 3 * D), D ** -0.5)
    moba_w_o = nrm((nd, D, D), D ** -0.5)
    ffn_w_gate_up = nrm((DEPTH, D, 2 * D_FF), D ** -0.5)
    ffn_w_down = nrm((DEPTH, D_FF, D), D_FF ** -0.5)
    return {"x": x, "positions": positions, "norm_mix": norm_mix, "norm_ffn": norm_ffn,
            "norm_final": norm_final, "s5_a_re": s5_a_re, "s5_a_im": s5_a_im, "s5_log_dt": s5_log_dt,
            "s5_b_re": s5_b_re, "s5_b_im": s5_b_im, "s5_c_re": s5_c_re, "s5_c_im": s5_c_im,
            "s5_d": s5_d, "s5_w_glu": s5_w_glu, "s5_b_glu": s5_b_glu, "dil_w_qkv": dil_w_qkv,
            "dil_w_o": dil_w_o, "hgrn_w_in": hgrn_w_in, "hgrn_lower_bound": hgrn_lower_bound,
            "hgrn_norm": hgrn_norm, "hgrn_w_o": hgrn_w_o, "moba_w_qkv": moba_w_qkv, "moba_w_o": moba_w_o,
            "ffn_w_gate_up": ffn_w_gate_up, "ffn_w_down": ffn_w_down}


def reference(x, positions, norm_mix, norm_ffn, norm_final, s5_a_re, s5_a_im, s5_log_dt, s5_b_re, s5_b_im,
              s5_c_re, s5_c_im, s5_d, s5_w_glu, s5_b_glu, dil_w_qkv, dil_w_o, hgrn_w_in, hgrn_lower_bound,
              hgrn_norm, hgrn_w_o, moba_w_qkv, moba_w_o, ffn_w_gate_up, ffn_w_down):
    cos, sin = rope_tables(positions)
    lb_w = jax.nn.softmax(hgrn_lower_bound.astype(jnp.float32), axis=0)
    lower_bounds = jnp.cumsum(lb_w, axis=0) - lb_w[0]
    h = x
    for layer in range(DEPTH):
        mixer = layer % N_MIXERS
        j = layer // N_MIXERS
        xn = rmsnorm(h, norm_mix[layer])
        if mixer == 0:
            y = s5_mixer(xn, s5_a_re[j], s5_a_im[j], s5_log_dt[j], s5_b_re[j], s5_b_im[j],
                         s5_c_re[j], s5_c_im[j], s5_d[j], s5_w_glu[j], s5_b_glu[j])
        elif mixer == 1:
            y = dilated_mixer(xn, dil_w_qkv[j], dil_w_o[j], cos, sin)
        elif mixer == 2:
            y = hgrn2_mixer(xn, hgrn_w_in[j], lower_bounds[layer], hgrn_norm[j], hgrn_w_o[j])
        else:
            y = moba_mixer(xn, moba_w_qkv[j], moba_w_o[j], cos, sin)
        h = h + y.astype(h.dtype)
        h = h + swiglu(rmsnorm(h, norm_ffn[layer]), ffn_w_gate_up[layer], ffn_w_down[layer]).astype(h.dtype)
    return rmsnorm(h, norm_final)
```

```python
import contextlib
import numpy as np
import concourse.bass as bass
import concourse.mybir as mybir
from concourse.bass_utils import run_bass_kernel_spmd

F32 = mybir.dt.float32
BF16 = mybir.dt.bfloat16
I32 = mybir.dt.int32
AF = mybir.ActivationFunctionType
ALU = mybir.AluOpType

S = 4096
D = 1024
DFF = 2816
KC = D // 128
EPS = 1e-6
SELF_SYNC = True
PI_LO = 3.1415925


class Buf:

    def __init__(self, name=""):
        self.w = None
        self.r = {}
        self.name = name


class Eng:
    def __init__(self, ctx, name, handle, self_sync):
        self.name = name
        self.h = handle
        self.sem = ctx.es.enter_context(ctx.nc.semaphore("s_" + name))
        self.cnt = 0
        self.seen = {}
        self.self_sync = self_sync


class DQ:
    def __init__(self, ctx, name, eng, k):
        self.name = name
        self.eng = eng
        self.k = k
        self.sems = [ctx.es.enter_context(ctx.nc.semaphore("q_%s%d" % (name, i))) for i in range(k)]
        self.cnts = [0] * k
        self.n = 0


class Ctx:
    def __init__(self, nc):
        self.nc = nc
        self.es = contextlib.ExitStack()
        self.eng = {}
        for name, h, ss in (("pe", nc.tensor, False), ("act", nc.scalar, SELF_SYNC), ("dve", nc.vector, SELF_SYNC),
                            ("pool", nc.gpsimd, SELF_SYNC), ("sp", nc.sync, False)):
            self.eng[name] = Eng(self, name, h, ss)
        self.dq = {"sp": DQ(self, "sp", self.eng["sp"], 8), "pool": DQ(self, "pool", self.eng["pool"], 4)}
        self.semtab = {}
        for e in self.eng.values():
            self.semtab[e.name] = e.sem
        for q in self.dq.values():
            for i, s in enumerate(q.sems):
                self.semtab[(q.name, i)] = s
        self.nwait = 0
        self.nins = 0

    def _need(self, reads, writes):
        need = {}
        for b in reads:
            if b.w is not None:
                k, v = b.w
                if need.get(k, 0) < v:
                    need[k] = v
        for b in writes:
            if b.w is not None:
                k, v = b.w
                if need.get(k, 0) < v:
                    need[k] = v
            for k, v in b.r.items():
                if need.get(k, 0) < v:
                    need[k] = v
        return need

    def _waits(self, E, need):
        for k, v in need.items():
            if k == E.name and not E.self_sync:
                continue
            if E.seen.get(k, 0) < v:
                E.h.wait_ge(self.semtab[k], v)
                E.seen[k] = v
                self.nwait += 1

    def op(self, eng, emit, reads=(), writes=()):
        E = self.eng[eng]
        self._waits(E, self._need(reads, writes))
        ins = emit(E.h)
        E.cnt += 1
        ins.then_inc(E.sem, 1)
        self.nins += 1
        for b in reads:
            b.r[E.name] = E.cnt
        for b in writes:
            b.w = (E.name, E.cnt)
            b.r = {}

    def dma(self, q, out, in_, reads=(), writes=()):
        Q = self.dq[q]
        E = Q.eng
        i = Q.n % Q.k
        need = self._need(reads, writes)
        key = (Q.name, i)
        if Q.cnts[i] > 0:
            need[key] = max(need.get(key, 0), 16 * Q.cnts[i])
        self._waits(E, need)
        E.h.dma_start(out=out, in_=in_).then_inc(Q.sems[i], 16)
        Q.cnts[i] += 1
        Q.n += 1
        self.nins += 1
        for b in reads:
            b.r[key] = 16 * Q.cnts[i]
        for b in writes:
            b.w = (key, 16 * Q.cnts[i])
            b.r = {}

    def barrier(self):
        tgt = {}
        for e in self.eng.values():
            if e.cnt:
                tgt[e.name] = e.cnt
        for q in self.dq.values():
            for i in range(q.k):
                if q.cnts[i]:
                    tgt[(q.name, i)] = 16 * q.cnts[i]
        for e in self.eng.values():
            if e.name == "pool":
                continue
            for k, v in tgt.items():
                if k == "pool" or (isinstance(k, tuple) and k[0] == "pool"):
                    continue
                if k == e.name:
                    continue
                if e.seen.get(k, 0) < v:
                    e.h.wait_ge(self.semtab[k], v)
                    e.seen[k] = v

    def final_wait(self):
        E = self.eng["sp"]
        Q = self.dq["sp"]
        for i in range(Q.k):
            if Q.cnts[i]:
                E.h.wait_ge(Q.sems[i], 16 * Q.cnts[i])


class Prog:
    def __init__(self, cfg):
        self.cfg = cfg
        nc = bass.Bass("TRN2", target_bir_lowering=False)
        self.nc = nc
        self.c = Ctx(nc)
        self.es = self.c.es
        self.ins = {}

    def din(self, name, shape, dt=F32):
        t = self.nc.dram_tensor(name, list(shape), dt, kind="ExternalInput").ap()
        self.ins[name] = t
        return t

    def dscr(self, name, shape, dt):
        return self.nc.dram_tensor(name, list(shape), dt, kind="Internal").ap()

    def sb(self, stack, name, shape, dt):
        self.sb_n = getattr(self, "sb_n", 0) + 1
        return stack.enter_context(self.nc.sbuf_tensor("%s_%d" % (name, self.sb_n), list(shape), dt))

    def cast_weight(self, src, dst, K, N, buf):
        c = self.c
        if not hasattr(buf, "cw_key"):
            sem = self.es.enter_context(self.nc.semaphore("cw%d" % len(c.semtab)))
            buf.cw_key = ("cw", len(c.semtab))
            c.semtab[buf.cw_key] = sem
            buf.cw_n = 0
        sem = c.semtab[buf.cw_key]
        step = 512
        for r0 in range(0, K, step):
            r1 = min(K, r0 + step)
            self.nc.gpsimd.dma_start(out=dst[r0:r1, :], in_=src[r0:r1, :]).then_inc(sem, 16)
            buf.cw_n += 1
        buf.w = (buf.cw_key, 16 * buf.cw_n)
        buf.r = {}

    def cast_weight_old(self, src, dst, K, N, buf):
        c = self.c
        CW = 2048
        for kc in range(K // 128):
            for c0 in range(0, N, CW):
                w = min(CW, N - c0)
                i = self.cast_i % 2
                self.cast_i += 1
                st, stb, bst, bstb = self.cast_st[i]
                c.dma("pool", st[:, 0:w], src[kc * 128:(kc + 1) * 128, c0:c0 + w], writes=[bst])
                c.op("pool", lambda h: h.tensor_copy(out=stb[:, 0:w], in_=st[:, 0:w]), reads=[bst], writes=[bstb])
                c.dma("pool", dst[kc * 128:(kc + 1) * 128, c0:c0 + w], stb[:, 0:w], reads=[bstb], writes=[buf])

    def rmsnorm_tile(self, hx, bhx, gcol, xn, bxn, ncols, tmp):
        c = self.c
        sq, bsq, rt, brt = tmp
        KH = sq.shape[1]
        for c0 in range(0, ncols, 512):
            ps, bps = self.ps_next()
            for k0 in range(0, KC, KH):
                c.op("act", lambda h: h.activation(out=sq[:, :, :], in_=hx[:, k0:k0 + KH, c0:c0 + 512], func=AF.Square),
                     reads=[bhx], writes=[bsq])
                for kk in range(KH):
                    kc = k0 + kk
                    c.op("pe", lambda h: h.matmul(ps[:, :], lhsT=self.ones_f[:, :], rhs=sq[:, kk, :],
                                                  start=(kc == 0), stop=(kc == KC - 1)),
                         reads=[bsq, self.bconst], writes=[bps])
            c.op("act", lambda h: h.activation(out=rt[:, :], in_=ps[:, :], func=AF.Sqrt, scale=1.0 / D,
                                               bias=self.eps_t[:, 0:1]),
                 reads=[bps, self.bconst], writes=[brt])
            c.op("dve", lambda h: h.reciprocal(out=rt[:, :], in_=rt[:, :]), reads=[brt], writes=[brt])
            for kc in range(KC):
                c.op("dve", lambda h: h.scalar_tensor_tensor(out=xn[:, kc, c0:c0 + 512], in0=hx[:, kc, c0:c0 + 512],
                                                             scalar=self.pvec[:, gcol + kc:gcol + kc + 1],
                                                             in1=rt[:, :], op0=ALU.mult, op1=ALU.mult),
                     reads=[bhx, brt, self.bconst], writes=[bxn])

    def run_hook(self):
        h = getattr(self, "hook", None)
        if h is not None:
            self.hook = None
            h()

    def ps_next(self, pool=None):
        if pool is None:
            i = self.ps_i % len(self.ps)
            self.ps_i += 1
            return self.ps[i], self.bps[i]
        lst = self.ps_pools[pool]
        i = lst[self.ps_pi.get(pool, 0) % len(lst)]
        self.ps_pi[pool] = self.ps_pi.get(pool, 0) + 1
        return self.ps[i], self.bps[i]

    def ffn_phase(self, layer):
        c = self.c
        nc = self.nc
        TS = 1024
        hT3 = self.hT.rearrange("(kc p) t -> p kc t", p=128)
        wgu = self.w_gu[layer]
        wd = self.w_d[layer]
        bw = self.bw_ffn[layer]
        with contextlib.ExitStack() as st:
            hx2 = [self.sb(st, "f_hx%d" % i, [128, KC, TS], F32) for i in range(2)]
            sq = self.sb(st, "f_sq", [128, 1, 512], F32)
            rt = self.sb(st, "f_rt", [128, 512], F32)
            xn2 = [self.sb(st, "f_xn%d" % i, [128, KC, TS], BF16) for i in range(2)]
            hf = self.sb(st, "f_hf", [128, DFF // 128, TS], BF16)
            wg = [self.sb(st, "f_wg%d" % i, [128, KC, 512], BF16) for i in range(2)]
            wu = [self.sb(st, "f_wu%d" % i, [128, KC, 512], BF16) for i in range(2)]
            wdn = [self.sb(st, "f_wd%d" % i, [128, DFF // 128, 256], BF16) for i in range(2)]
            sg = [self.sb(st, "f_sg%d" % i, [128, 512], BF16) for i in range(2)]
            bsq, brt = Buf(), Buf()
            bhx2, bxn2 = [Buf(), Buf()], [Buf(), Buf()]
            bhf = [Buf() for _ in range(DFF // 128)]
            bwg = [Buf(), Buf()]
            bwdn = [Buf(), Buf()]
            bsg = [Buf(), Buf()]
            wgu3 = wgu.rearrange("(kc p) n -> p kc n", p=128)
            wd3 = wd.rearrange("(kc p) n -> p kc n", p=128)
            NG = DFF // 512
            groups = [(g0, min(512, DFF - g0)) for g0 in range(0, DFF, 512)]
            def prep(ti_):
                t0_ = ti_ * TS
                u_ = ti_ % 2
                c.dma("sp", hx2[u_][:, :, :], hT3[:, :, t0_:t0_ + TS], reads=[self.bh[ti_]], writes=[bhx2[u_]])
                self.rmsnorm_tile(hx2[u_], bhx2[u_], self.col_nffn + layer * KC, xn2[u_], bxn2[u_], TS, (sq, bsq, rt, brt))

            prep(0)
            for t0 in range(0, S, TS):
                bh = self.bh[t0 // TS]
                u = (t0 // TS) % 2
                hx, bhx, xn, bxn = hx2[u], bhx2[u], xn2[u], bxn2[u]
                for gi, (g0, gw) in enumerate(groups):
                    b = gi % 2
                    c.dma("sp", wg[b][:, :, 0:gw], wgu3[:, :, g0:g0 + gw], reads=[bw], writes=[bwg[b]])
                    c.dma("sp", wu[b][:, :, 0:gw], wgu3[:, :, DFF + g0:DFF + g0 + gw], reads=[bw], writes=[bwg[b]])
                    for j in range(gw // 128):
                        fc = (g0 // 128) + j
                        for ct in range(TS // 512):
                            cs = slice(ct * 512, (ct + 1) * 512)
                            pg, bpg = self.ps_next()
                            pu, bpu = self.ps_next()
                            for kc in range(KC):
                                c.op("pe", lambda h: h.matmul(pg[:, :], lhsT=wg[b][:, kc, j * 128:(j + 1) * 128],
                                                              rhs=xn[:, kc, cs], start=(kc == 0), stop=(kc == KC - 1)),
                                     reads=[bwg[b], bxn], writes=[bpg])
                            for kc in range(KC):
                                c.op("pe", lambda h: h.matmul(pu[:, :], lhsT=wu[b][:, kc, j * 128:(j + 1) * 128],
                                                              rhs=xn[:, kc, cs], start=(kc == 0), stop=(kc == KC - 1)),
                                     reads=[bwg[b], bxn], writes=[bpu])
                            si = self.sg_i % 2
                            self.sg_i += 1
                            c.op("act", lambda h: h.activation(out=sg[si][:, :], in_=pg[:, :], func=AF.Silu),
                                 reads=[bpg], writes=[bsg[si]])
                            c.op("dve", lambda h: h.tensor_tensor(out=hf[:, fc, cs], in0=pu[:, :], in1=sg[si][:, :],
                                                                  op=ALU.mult),
                                 reads=[bpu, bsg[si]], writes=[bhf[fc]])
                if t0 + TS < S:
                    prep(t0 // TS + 1)
                self.run_hook()
                for dg in range(D // 256):
                    b = dg % 2
                    c.dma("sp", wdn[b][:, :, :], wd3[:, :, dg * 256:(dg + 1) * 256], reads=[bw], writes=[bwdn[b]])
                    for j in range(2):
                        dc = dg * 2 + j
                        for ct in range(TS // 512):
                            cs = slice(ct * 512, (ct + 1) * 512)
                            py, bpy = self.ps_next()
                            nk = DFF // 128
                            for kc in range(nk):
                                c.op("pe", lambda h: h.matmul(py[:, :], lhsT=wdn[b][:, kc, j * 128:(j + 1) * 128],
                                                              rhs=hf[:, kc, cs], start=(kc == 0), stop=(kc == nk - 1)),
                                     reads=[bwdn[b], bhf[kc]], writes=[bpy])
                            c.op("dve", lambda h: h.tensor_tensor(out=hx[:, dc, cs], in0=py[:, :], in1=hx[:, dc, cs],
                                                                  op=ALU.add),
                                 reads=[bpy, bhx], writes=[bhx])
                c.dma("sp", hT3[:, :, t0:t0 + TS], hx[:, :, :], reads=[bhx], writes=[bh])
            c.barrier()

    def load_cast(self, st, dst, bdst, src, shape_cols, tag):
        c = self.c
        P = dst.shape[0]
        stg = self.sb(st, "lc_" + tag, [P, 2048], F32)
        bs = Buf()
        for c0 in range(0, shape_cols, 2048):
            w = min(2048, shape_cols - c0)
            c.dma("sp", stg[:, 0:w], src[:, c0:c0 + w], writes=[bs])
            c.op("dve", lambda h: h.tensor_copy(out=dst[:, c0:c0 + w], in_=stg[:, 0:w]), reads=[bs], writes=[bdst])

    def range_reduce_sincos(self, st, x, bx, n, tag, want_cos=True):
        c = self.c
        TWO_PI = 2.0 * np.pi
        ki = self.sb(st, tag + "_ki", [128, n], I32)
        kf = self.sb(st, tag + "_kf", [128, n], F32)
        sn = self.sb(st, tag + "_sn", [128, n], F32)
        cs = self.sb(st, tag + "_cs", [128, n], F32) if want_cos else None
        bki, bkf, bsn, bcs = Buf(), Buf(), Buf(), Buf()
        self._rr(x, bx, n, ki, bki, kf, bkf, sn, bsn, cs, bcs)
        return sn, bsn, cs, bcs

    def _rr(self, x, bx, n, ki, bki, kf, bkf, sn, bsn, cs, bcs):
        c = self.c
        TWO_PI = 2.0 * np.pi
        c.op("act", lambda h: h.activation(out=ki[:, 0:n], in_=x[:, 0:n], func=AF.Identity, scale=1.0 / TWO_PI),
             reads=[bx], writes=[bki])
        c.op("dve", lambda h: h.scalar_tensor_tensor(out=sn[:, 0:n], in0=ki[:, 0:n], scalar=-TWO_PI, in1=x[:, 0:n],
                                                     op0=ALU.mult, op1=ALU.add), reads=[bki, bx], writes=[bsn])
        c.op("dve", lambda h: h.tensor_scalar(out=sn[:, 0:n], in0=sn[:, 0:n], scalar1=-PI_LO, scalar2=PI_LO,
                                              op0=ALU.max, op1=ALU.min), reads=[bsn], writes=[bsn])
        if cs is not None:
            c.op("dve", lambda h: h.scalar_tensor_tensor(out=cs[:, 0:n], in0=sn[:, 0:n], scalar=-1.0, in1=sn[:, 0:n],
                                                         op0=ALU.mult, op1=ALU.max), reads=[bsn], writes=[bcs])
        c.op("act", lambda h: h.activation(out=sn[:, 0:n], in_=sn[:, 0:n], func=AF.Sin), reads=[bsn], writes=[bsn])
        if cs is not None:
            c.op("act", lambda h: h.activation(out=cs[:, 0:n], in_=cs[:, 0:n], func=AF.Sin, scale=-1.0,
                                               bias=self.halfpi_t[:, 0:1]), reads=[bcs, self.bconst], writes=[bcs])

    def qkv_phase(self, layer, w_s, wp_s, bw, bwp):
        c = self.c
        hT3 = self.hT.rearrange("(kc p) t -> p kc t", p=128)
        QT3 = self.QT.rearrange("(kc p) t -> p kc t", p=128)
        KT3 = self.KT.rearrange("(kc p) t -> p kc t", p=128)
        w3 = w_s.rearrange("(kc p) n -> p kc n", p=128)
        self.ps_pools = {"a": [0, 1, 2, 3, 4, 5, 6, 7]}
        self.ps_pi = {}
        with contextlib.ExitStack() as st:
            SF = self.sb(st, "q_SF", [128, S], F32)
            CF = self.sb(st, "q_CF", [128, S], F32)
            bSF, bCF = Buf(), Buf()
            with contextlib.ExitStack() as st2:
                posi = self.sb(st2, "q_posi", [128, S], I32)
                x = self.sb(st2, "q_x", [128, S], F32)
                bposi, bx = Buf(), Buf()
                c.dma("sp", posi[:, :], self.posb[:, :], writes=[bposi])
                c.op("dve", lambda h: h.tensor_copy(out=x[:, :], in_=posi[:, :]), reads=[bposi], writes=[bx])
                c.op("dve", lambda h: h.tensor_scalar(out=x[:, :], in0=x[:, :], scalar1=self.pvec[:, self.col_invf:self.col_invf + 1],
                                                      scalar2=None, op0=ALU.mult), reads=[bx, self.bconst], writes=[bx])
                ki = posi
                kf = self.sb(st2, "q_kf", [128, S], F32)
                self._rr(x, bx, S, ki, bposi, kf, Buf(), SF, bSF, CF, bCF)
                c.op("dve", lambda h: h.tensor_scalar(out=SF[:, :], in0=SF[:, :],
                                                      scalar1=self.pvec[:, self.col_sgnrow:self.col_sgnrow + 1],
                                                      scalar2=None, op0=ALU.mult), reads=[bSF, self.bconst], writes=[bSF])
                c.barrier()
            wq = self.sb(st, "q_w", [128, KC, 3 * D], BF16)
            pm = self.sb(st, "q_pm", [128, 128], BF16)
            ab = [self.sb(st, "q_ab%d" % i, [128, 512], BF16) for i in range(2)]
            bab = [Buf(), Buf()]
            bpmc = Buf()
            with contextlib.ExitStack() as st3:
                self.load_cast(st3, pm, bpmc, self.ropeperm, 128, "pm")
                c.barrier()
            bwq = Buf()
            for i in range(3):
                c.dma("sp", wq[:, :, i * D:(i + 1) * D], w3[:, :, i * D:(i + 1) * D], reads=[bw], writes=[bwq])
            hx = self.sb(st, "q_hx", [128, KC, 512], F32)
            sq = self.sb(st, "q_sq", [128, 4, 512], F32)
            rt = self.sb(st, "q_rt", [128, 512], F32)
            xn = self.sb(st, "q_xn", [128, KC, 512], BF16)
            qo = [self.sb(st, "q_qo%d" % i, [128, KC, 512], BF16) for i in range(2)]
            tu = [self.sb(st, "q_tu%d" % i, [128, 512], F32) for i in range(2)]
            tt_ = [self.sb(st, "q_tt%d" % i, [128, 512], F32) for i in range(2)]
            vt = [self.sb(st, "q_vt%d" % i, [128, 16, 128], BF16) for i in range(2)]
            bhx, bsq, brt, bxn = Buf(), Buf(), Buf(), Buf()
            bqo = [Buf(), Buf()]
            btu = [Buf(), Buf()]
            btt = [Buf(), Buf()]
            bvt = [Buf(), Buf()]
            for i in range(2):
                c.op("dve", lambda h: h.memset(vt[i][:, :, 64:128], 1.0), writes=[bvt[i]])
            ei = 0
            vi = 0
            for tile in range(S // 512):
                cs = slice(tile * 512, (tile + 1) * 512)
                bh = self.bh[tile // 2]
                if tile == 2:
                    self.run_hook()
                c.dma("sp", hx[:, :, :], hT3[:, :, cs], reads=[bh], writes=[bhx])
                self.rmsnorm_tile(hx, bhx, self.col_nmix + layer * KC, xn, bxn, 512, (sq, bsq, rt, brt))
                for which in range(2):
                    for oc in range(KC):
                        pa, bpa = self.ps_next("a")
                        pb, bpb = self.ps_next("a")
                        col = which * D + oc * 128
                        for kc in range(KC):
                            c.op("pe", lambda h: h.matmul(pa[:, :], lhsT=wq[:, kc, col:col + 128], rhs=xn[:, kc, :],
                                                          start=(kc == 0), stop=(kc == KC - 1)),
                                 reads=[bwq, bxn], writes=[bpa])
                        e = ei % 2
                        ei += 1
                        c.op("act", lambda h: h.activation(out=ab[e][:, :], in_=pa[:, :], func=AF.Identity),
                             reads=[bpa], writes=[bab[e]])
                        c.op("pe", lambda h: h.matmul(pb[:, :], lhsT=pm[:, :], rhs=ab[e][:, :], start=True, stop=True),
                             reads=[bab[e], bpmc], writes=[bpb])
                        c.op("dve", lambda h: h.tensor_tensor(out=tu[e][:, :], in0=pa[:, :], in1=CF[:, cs], op=ALU.mult),
                             reads=[bpa, bCF, bab[e]], writes=[btu[e]])
                        c.op("dve", lambda h: h.tensor_tensor(out=tt_[e][:, :], in0=pb[:, :], in1=SF[:, cs], op=ALU.mult),
                             reads=[bpb, bSF], writes=[btt[e]])
                        c.op("dve", lambda h: h.tensor_tensor(out=qo[which][:, oc, :], in0=tu[e][:, :], in1=tt_[e][:, :],
                                                               op=ALU.add),
                             reads=[btu[e], btt[e]], writes=[bqo[which]])
                    dst = QT3 if which == 0 else KT3
                    c.dma("sp", dst[:, :, cs], qo[which][:, :, :], reads=[bqo[which]], writes=[self.bqkv])
                for sub in range(4):
                    v = vi % 2
                    vi += 1
                    for half in range(2):
                        pv, bpv = self.ps_next("a")
                        for kc in range(KC):
                            c.op("pe", lambda h: h.matmul(pv[:, :], lhsT=xn[:, kc, sub * 128:(sub + 1) * 128],
                                                          rhs=wq[:, kc, 2 * D + half * 512:2 * D + (half + 1) * 512],
                                                          start=(kc == 0), stop=(kc == KC - 1)),
                                 reads=[bwq, bxn], writes=[bpv])
                        c.op("act", lambda h: h.activation(out=vt[v][:, half * 8:(half + 1) * 8, 0:64],
                                                           in_=pv[:, :].rearrange("p (a b) -> p a b", b=64),
                                                           func=AF.Identity),
                             reads=[bpv], writes=[bvt[v]])
                    c.dma("sp", self.V1s[:, :, tile * 4 + sub, :].rearrange("h p e -> p h e"), vt[v][:, :, :],
                          reads=[bvt[v]], writes=[self.bqkv])
            c.barrier()

    def attn_phase(self, kind):
        c = self.c
        moba = kind == "moba"
        KR = 80 if moba else 64
        self.ps_pools = {"s": [0, 1, 2, 3], "o": [4, 5], "g": [6, 7]}
        self.ps_pi = {}
        with contextlib.ExitStack() as st:
            nm = 4 if moba else 20
            mk = self.sb(st, "a_mk", [128, nm * 512], BF16)
            bmk = Buf()
            with contextlib.ExitStack() as st2:
                self.load_cast(st2, mk, bmk, (self.mmask if moba else self.dmask), nm * 512, "mk")
                c.barrier()
            kaug = [self.sb(st, "a_k%d" % i, [KR, S], BF16) for i in range(2)]
            qaug = [self.sb(st, "a_q%d" % i, [KR, S], BF16) for i in range(2)]
            v1 = [self.sb(st, "a_v%d" % i, [128, 32, 128], BF16) for i in range(2)]
            ot = [self.sb(st, "a_o%d" % i, [64, S], BF16) for i in range(2)]
            pt = [self.sb(st, "a_p%d" % i, [128, 512], BF16) for i in range(8)]
            rd = [self.sb(st, "a_rd%d" % i, [64, 512], F32) for i in range(2)]
            bk = [Buf(), Buf()]
            bq = [Buf(), Buf()]
            bv = [Buf(), Buf()]
            bot = [Buf(), Buf()]
            bpt = [Buf() for _ in range(8)]
            brd = [Buf(), Buf()]
            if moba:
                with contextlib.ExitStack() as st2:
                    oh = self.sb(st2, "a_oh", [16, S], F32)
                    boh = Buf()
                    c.dma("sp", oh[:, :], self.onehot[:, :], writes=[boh])
                    for i in range(2):
                        c.op("dve", lambda h: h.tensor_copy(out=kaug[i][64:80, :], in_=oh[:, :]), reads=[boh], writes=[bk[i]])
                    c.barrier()
                pastneg = self.sb(st, "a_pn", [128, 512], F32)
                own = self.sb(st, "a_own", [128, 512], F32)
                ident = self.sb(st, "a_id", [128, 128], F32)
                bcn = Buf()
                c.dma("sp", pastneg[:, :], self.pastneg[:, :], writes=[bcn])
                c.dma("sp", own[:, :], self.own[:, :], writes=[bcn])
                c.dma("sp", ident[:, :], self.ident[:, :], writes=[bcn])
                km = self.sb(st, "a_km", [64, 16], F32)
                kmb = self.sb(st, "a_kmb", [64, 16], BF16)
                gm = self.sb(st, "a_gm", [128, 512], F32)
                m8 = self.sb(st, "a_m8", [128, 32, 8], F32)
                thr = self.sb(st, "a_thr", [128, 32], F32)
                sel = self.sb(st, "a_sel", [128, 512], F32)
                bkm, bkmb, bgm, bm8, bthr, bsel = Buf(), Buf(), Buf(), Buf(), Buf(), Buf()
            ri = 0

            def head_prep(hd):
                b = hd % 2
                c.dma("sp", kaug[b][0:64, :], self.KT[hd * 64:(hd + 1) * 64, :], reads=[self.bqkv], writes=[bk[b]])
                c.dma("sp", qaug[b][0:64, :], self.QT[hd * 64:(hd + 1) * 64, :], reads=[self.bqkv], writes=[bq[b]])
                c.dma("sp", v1[b][:, :, :], self.V1s[hd, :, :, :], reads=[self.bqkv], writes=[bv[b]])
                if moba:
                    c.op("dve", lambda h: h.tensor_reduce(out=km[:, :], in_=kaug[b][0:64, :].rearrange("p (n k) -> p n k", k=256),
                                                          axis=mybir.AxisListType.X, op=ALU.add),
                         reads=[bk[b]], writes=[bkm])
                    c.op("dve", lambda h: h.tensor_scalar(out=kmb[:, :], in0=km[:, :], scalar1=1.0 / 256, scalar2=None,
                                                          op0=ALU.mult), reads=[bkm], writes=[bkmb])
                    pg, bpg = self.ps_next("g")
                    for qt_ in range(32):
                        c.op("pe", lambda h: h.matmul(pg[:, qt_ * 16:(qt_ + 1) * 16], lhsT=qaug[b][0:64, qt_ * 128:(qt_ + 1) * 128],
                                                      rhs=kmb[:, :], start=True, stop=True),
                             reads=[bq[b], bkmb], writes=[bpg])
                    c.op("dve", lambda h: h.tensor_tensor(out=gm[:, :], in0=pg[:, :], in1=pastneg[:, :], op=ALU.add),
                         reads=[bpg, bcn], writes=[bgm])
                    for qt_ in range(32):
                        c.op("dve", lambda h: h.max(out=m8[:, qt_, :], in_=gm[:, qt_ * 16:(qt_ + 1) * 16]),
                             reads=[bgm], writes=[bm8])
                    c.op("dve", lambda h: h.tensor_scalar(out=thr[:, :], in0=m8[:, :, 2], scalar1=-1e29, scalar2=None,
                                                          op0=ALU.max), reads=[bm8], writes=[bthr])
                    c.op("dve", lambda h: h.tensor_tensor(out=sel[:, :].rearrange("p (a b) -> p a b", b=16),
                                                          in0=gm[:, :].rearrange("p (a b) -> p a b", b=16),
                                                          in1=thr[:, :].unsqueeze(2).to_broadcast([128, 32, 16]),
                                                          op=ALU.is_ge), reads=[bgm, bthr], writes=[bsel])
                    c.op("dve", lambda h: h.tensor_tensor(out=sel[:, :], in0=sel[:, :], in1=own[:, :], op=ALU.add),
                         reads=[bsel, bcn], writes=[bsel])
                    c.op("dve", lambda h: h.tensor_scalar(out=sel[:, :], in0=sel[:, :], scalar1=-1.0, scalar2=32768.0,
                                                          op0=ALU.add, op1=ALU.mult), reads=[bsel], writes=[bsel])
                    for g4 in range(8):
                        pt_, bpt_ = self.ps_next("g")
                        for j in range(4):
                            qt_ = g4 * 4 + j
                            c.op("pe", lambda h: h.transpose(pt_[0:16, j * 128:(j + 1) * 128], sel[:, qt_ * 16:(qt_ + 1) * 16],
                                                             ident[:, :]),
                                 reads=[bsel, bcn], writes=[bpt_])
                        c.op("act", lambda h: h.activation(out=qaug[b][64:80, g4 * 512:(g4 + 1) * 512], in_=pt_[0:16, :],
                                                           func=AF.Identity), reads=[bpt_], writes=[bq[b]])

            items = []
            for hd in range(16):
                for qt in range(8):
                    k0 = 0 if moba else max(0, 4 * qt - 16)
                    kts = list(range(k0, 4 * qt + 4))
                    for idx, kt in enumerate(kts):
                        items.append((hd, qt, idx, kt, len(kts)))
            LA = 5
            state = {}

            def front(i, it):
                hd, qt, idx, kt, n = it
                b = hd % 2
                pss, bpss = self.ps_next("s")
                c.op("pe", lambda h: h.matmul(pss[:, :], lhsT=kaug[b][0:KR, kt * 128:(kt + 1) * 128],
                                              rhs=qaug[b][0:KR, qt * 512:(qt + 1) * 512], start=True, stop=True),
                     reads=[bk[b], bq[b]], writes=[bpss])
                p = i % 8
                c.op("act", lambda h: h.activation(out=pt[p][:, :], in_=pss[:, :], func=AF.Exp, scale=0.125),
                     reads=[bpss], writes=[bpt[p]])
                off = kt - 4 * qt
                m = None
                if moba:
                    if off >= 0:
                        m = off
                else:
                    m = off + 16
                if m is not None:
                    eng = "dve"
                    c.op(eng, lambda h: h.tensor_tensor(out=pt[p][:, :], in0=pt[p][:, :], in1=mk[:, m * 512:(m + 1) * 512],
                                                        op=ALU.mult), reads=[bpt[p], bmk], writes=[bpt[p]])

            def back(i, it):
                nonlocal ri
                hd, qt, idx, kt, n = it
                b = hd % 2
                p = i % 8
                if qt == 0 and idx == 0 and hd + 1 < 16:
                    head_prep(hd + 1)
                if idx == 0:
                    state["po"] = self.ps_next("o")
                po, bpo = state["po"]
                c.op("pe", lambda h: h.matmul(po[:, :], lhsT=v1[b][:, kt, :], rhs=pt[p][:, :],
                                              start=(idx == 0), stop=(idx == n - 1)),
                     reads=[bv[b], bpt[p]], writes=[bpo])
                if idx == n - 1:
                    r = ri % 2
                    ri += 1
                    c.op("dve", lambda h: h.reciprocal(out=rd[r][:, :], in_=po[64:128, :]), reads=[bpo], writes=[brd[r]])
                    c.op("dve", lambda h: h.tensor_tensor(out=ot[b][:, qt * 512:(qt + 1) * 512], in0=po[0:64, :], in1=rd[r][:, :],
                                                          op=ALU.mult), reads=[bpo, brd[r]], writes=[bot[b]])
                    if qt == 7:
                        c.dma("sp", self.OT[hd * 64:(hd + 1) * 64, :], ot[b][:, :], reads=[bot[b]], writes=[self.bot])

            head_prep(0)
            for i in range(len(items) + LA):
                if i < len(items):
                    front(i, items[i])
                if i >= LA:
                    back(i - LA, items[i - LA])
            c.barrier()

    def linres_phase(self, inT, w_s, bw, bin_):
        c = self.c
        hT3 = self.hT.rearrange("(kc p) t -> p kc t", p=128)
        in3 = inT.rearrange("(kc p) t -> p kc t", p=128)
        w3 = w_s.rearrange("(kc p) n -> p kc n", p=128)
        with contextlib.ExitStack() as st:
            w = self.sb(st, "l_w", [128, KC, D], BF16)
            bwl = Buf()
            c.dma("sp", w[:, :, :], w3[:, :, :], reads=[bw], writes=[bwl])
            a = [self.sb(st, "l_a%d" % i, [128, KC, 512], BF16) for i in range(2)]
            hx = [self.sb(st, "l_h%d" % i, [128, KC, 512], F32) for i in range(2)]
            ba = [Buf(), Buf()]
            bhx = [Buf(), Buf()]
            for tile in range(S // 512):
                cs = slice(tile * 512, (tile + 1) * 512)
                b = tile % 2
                bh = self.bh[tile // 2]
                c.dma("sp", a[b][:, :, :], in3[:, :, cs], reads=[bin_], writes=[ba[b]])
                c.dma("sp", hx[b][:, :, :], hT3[:, :, cs], reads=[bh], writes=[bhx[b]])
                for dc in range(KC):
                    ps, bps = self.ps_next()
                    for kc in range(KC):
                        c.op("pe", lambda h: h.matmul(ps[:, :], lhsT=w[:, kc, dc * 128:(dc + 1) * 128], rhs=a[b][:, kc, :],
                                                      start=(kc == 0), stop=(kc == KC - 1)),
                             reads=[bwl, ba[b]], writes=[bps])
                    c.op("dve", lambda h: h.tensor_tensor(out=hx[b][:, dc, :], in0=ps[:, :], in1=hx[b][:, dc, :], op=ALU.add),
                         reads=[bps, bhx[b]], writes=[bhx[b]])
                c.dma("sp", hT3[:, :, cs], hx[b][:, :, :], reads=[bhx[b]], writes=[bh])
            c.barrier()

    def cast_weight_qk_perm(self, src, dst, buf):
        c = self.c
        for kc in range(KC):
            for c0 in range(0, 2 * D, 512):
                i = self.cast_i % 2
                self.cast_i += 1
                st, stb, bst, bstb = self.cast_st[i]
                c.dma("pool", st[:, :], src[kc * 128:(kc + 1) * 128, c0:c0 + 512], writes=[bst])
                st3 = st[:, :].rearrange("p (a b) -> p a b", b=64)
                sb3 = stb[:, :].rearrange("p (a b) -> p a b", b=64)
                c.op("pool", lambda h: h.tensor_copy(out=stb[:, :], in_=st[:, :]), reads=[bst], writes=[bstb])
                c.op("pool", lambda h: h.tensor_copy(out=sb3[:, :, 0:8], in_=st3[:, :, 8:16]), reads=[bst], writes=[bstb])
                c.op("pool", lambda h: h.tensor_copy(out=sb3[:, :, 8:16], in_=st3[:, :, 0:8]), reads=[bst], writes=[bstb])
                c.dma("pool", dst[kc * 128:(kc + 1) * 128, c0:c0 + 512], stb[:, :], reads=[bstb], writes=[buf])

    def hgrn_prep(self, layer, w_s, bw, ebl, bebl):
        c = self.c
        hT3 = self.hT.rearrange("(kc p) t -> p kc t", p=128)
        QT3 = self.QT.rearrange("(kc p) t -> p kc t", p=128)
        KT3 = self.KT.rearrange("(kc p) t -> p kc t", p=128)
        GT3 = self.GT.rearrange("(kc p) t -> p kc t", p=128)
        w3 = w_s.rearrange("(kc p) n -> p kc n", p=128)
        self.ps_pools = {"a": [0, 1, 2, 3, 4, 5], "t": [6, 7]}
        self.ps_pi = {}
        with contextlib.ExitStack() as st:
            w = self.sb(st, "h_w", [128, KC, 4 * D], BF16)
            bww = Buf()
            for i in range(4):
                c.dma("sp", w[:, :, i * D:(i + 1) * D], w3[:, :, i * D:(i + 1) * D], reads=[bw], writes=[bww])
            ex = self.sb(st, "h_ex", [128, 4 * KC], F32)
            ssum = self.sb(st, "h_ss", [128, KC], F32)
            lb = self.sb(st, "h_lb", [128, KC], F32)
            oml = self.sb(st, "h_oml", [128, KC], F32)
            blb = Buf()
            cl = self.col_lb
            c.op("act", lambda h: h.activation(out=ex[:, :], in_=self.pvec[:, cl:cl + 4 * KC], func=AF.Exp),
                 reads=[self.bconst], writes=[blb])
            c.op("dve", lambda h: h.tensor_tensor(out=ssum[:, :], in0=ex[:, 0:KC], in1=ex[:, KC:2 * KC], op=ALU.add),
                 reads=[blb], writes=[blb])
            c.op("dve", lambda h: h.tensor_tensor(out=ssum[:, :], in0=ssum[:, :], in1=ex[:, 2 * KC:3 * KC], op=ALU.add),
                 reads=[blb], writes=[blb])
            c.op("dve", lambda h: h.tensor_tensor(out=ssum[:, :], in0=ssum[:, :], in1=ex[:, 3 * KC:4 * KC], op=ALU.add),
                 reads=[blb], writes=[blb])
            c.op("dve", lambda h: h.reciprocal(out=ssum[:, :], in_=ssum[:, :]), reads=[blb], writes=[blb])
            c.op("dve", lambda h: h.tensor_copy(out=lb[:, :], in_=ex[:, KC:2 * KC]), reads=[blb], writes=[blb])
            for l in range(2, layer + 1):
                c.op("dve", lambda h: h.tensor_tensor(out=lb[:, :], in0=lb[:, :], in1=ex[:, l * KC:(l + 1) * KC], op=ALU.add),
                     reads=[blb], writes=[blb])
            c.op("dve", lambda h: h.tensor_tensor(out=lb[:, :], in0=lb[:, :], in1=ssum[:, :], op=ALU.mult),
                 reads=[blb], writes=[blb])
            c.op("dve", lambda h: h.tensor_scalar(out=oml[:, :], in0=lb[:, :], scalar1=-1.0, scalar2=1.0, op0=ALU.mult, op1=ALU.add),
                 reads=[blb], writes=[blb])
            rmask = self.sb(st, "h_rm", [128, 512], F32)
            identb = self.sb(st, "h_idb", [128, 128], BF16)
            identf = self.sb(st, "h_idf", [128, 128], F32)
            bcn = Buf()
            c.dma("sp", rmask[:, :], self.rmask[:, :], writes=[bcn])
            c.dma("sp", identf[:, :], self.ident[:, :], writes=[bcn])
            c.op("dve", lambda h: h.tensor_copy(out=identb[:, :], in_=identf[:, :]), reads=[bcn], writes=[bcn])
            hx = self.sb(st, "h_hx", [128, KC, 512], F32)
            sq = self.sb(st, "h_sq", [128, 2, 512], F32)
            rt = self.sb(st, "h_rt", [128, 512], F32)
            xn = self.sb(st, "h_xn", [128, KC, 512], BF16)
            bhx, bsq, brt, bxn = Buf(), Buf(), Buf(), Buf()
            names = ["qs", "th", "fg", "b", "eb", "ebn", "kk"]
            TT = [{n: self.sb(st, "h_t_%s%d" % (n, i), [128, 512], F32) for n in names} for i in range(2)]
            BB_ = [{n: Buf() for n in names} for i in range(2)]
            A_ = self.sb(st, "h_A", [128, KC], F32)
            nA_ = self.sb(st, "h_nA", [128, KC], F32)
            B_ = self.sb(st, "h_B", [128, KC], F32)
            c.op("dve", lambda h: h.tensor_scalar(out=A_[:, :], in0=oml[:, :], scalar1=0.5, scalar2=None, op0=ALU.mult), reads=[blb], writes=[blb])
            c.op("dve", lambda h: h.tensor_scalar(out=nA_[:, :], in0=oml[:, :], scalar1=-0.5, scalar2=None, op0=ALU.mult), reads=[blb], writes=[blb])
            c.op("dve", lambda h: h.tensor_tensor(out=B_[:, :], in0=lb[:, :], in1=A_[:, :], op=ALU.add), reads=[blb], writes=[blb])
            qo = self.sb(st, "h_qo", [128, KC, 512], BF16)
            ko = self.sb(st, "h_ko", [128, KC, 512], BF16)
            go = self.sb(st, "h_go", [128, KC, 512], BF16)
            ktm = self.sb(st, "h_ktm", [128, 8, 4, 128], BF16)
            itm = [self.sb(st, "h_itm%d" % i, [128, 8, 128], BF16) for i in range(2)]
            bqo, bko, bgo, bktm = Buf(), Buf(), Buf(), Buf()
            bitm = [Buf(), Buf()]
            ii = 0
            for tile in range(S // 512):
                cs = slice(tile * 512, (tile + 1) * 512)
                bh = self.bh[tile // 2]
                if tile == 2:
                    self.run_hook()
                c.dma("sp", hx[:, :, :], hT3[:, :, cs], reads=[bh], writes=[bhx])
                self.rmsnorm_tile(hx, bhx, self.col_nmix + layer * KC, xn, bxn, 512, (sq, bsq, rt, brt))
                def proj(hp_):
                    PS_ = {}
                    for i, hh in enumerate((2 * hp_, 2 * hp_ + 1)):
                        PS_[i] = [self.ps_next("a") for _ in range(3)]
                        for (pp, bpp), off in zip(PS_[i], (0, D, 3 * D)):
                            col = off + hh * 128
                            for kc in range(KC):
                                c.op("pe", lambda h: h.matmul(pp[:, :], lhsT=w[:, kc, col:col + 128], rhs=xn[:, kc, :],
                                                              start=(kc == 0), stop=(kc == KC - 1)),
                                     reads=[bww, bxn], writes=[bpp])
                    return PS_

                PSn = proj(0)
                for hp in range(4):
                    hs = (2 * hp, 2 * hp + 1)
                    PS = PSn
                    for i, hh in enumerate(hs):
                        T, B = TT[i], BB_[i]
                        (pq, bpq), (pf, bpf), (pg, bpg) = PS[i]
                        c.op("act", lambda h: h.activation(out=T["qs"][:, :], in_=pq[:, :], func=AF.Silu), reads=[bpq], writes=[B["qs"]])
                        c.op("act", lambda h: h.activation(out=go[:, hh, :], in_=pg[:, :], func=AF.Silu), reads=[bpg], writes=[bgo])
                        c.op("act", lambda h: h.activation(out=T["th"][:, :], in_=pf[:, :], func=AF.Tanh, scale=0.5), reads=[bpf], writes=[B["th"]])
                    if hp + 1 < 4:
                        PSn = proj(hp + 1)
                    for i, hh in enumerate(hs):
                        T, B = TT[i], BB_[i]
                        c.op("dve", lambda h: h.tensor_scalar(out=T["fg"][:, :], in0=T["th"][:, :], scalar1=A_[:, hh:hh + 1],
                                                              scalar2=B_[:, hh:hh + 1], op0=ALU.mult, op1=ALU.add),
                             reads=[B["th"], blb], writes=[B["fg"]])
                        c.op("dve", lambda h: h.tensor_scalar(out=T["kk"][:, :], in0=T["th"][:, :], scalar1=nA_[:, hh:hh + 1],
                                                              scalar2=A_[:, hh:hh + 1], op0=ALU.mult, op1=ALU.add),
                             reads=[B["th"], blb], writes=[B["kk"]])
                    for i, hh in enumerate(hs):
                        T, B = TT[i], BB_[i]
                        c.op("act", lambda h: h.activation(out=T["fg"][:, :], in_=T["fg"][:, :], func=AF.Ln), reads=[B["fg"]], writes=[B["fg"]])
                    for i, hh in enumerate(hs):
                        T, B = TT[i], BB_[i]
                        c.op("dve", lambda h: h.tensor_tensor_scan(out=T["b"][:, :], data0=rmask[:, :], data1=T["fg"][:, :], initial=0.0,
                                                                   op0=ALU.mult, op1=ALU.add), reads=[B["fg"], bcn], writes=[B["b"]])
                    for i, hh in enumerate(hs):
                        T, B = TT[i], BB_[i]
                        c.op("act", lambda h: h.activation(out=T["eb"][:, :], in_=T["b"][:, :], func=AF.Exp), reads=[B["b"]], writes=[B["eb"]])
                        c.op("act", lambda h: h.activation(out=T["ebn"][:, :], in_=T["b"][:, :], func=AF.Exp, scale=-1.0),
                             reads=[B["b"]], writes=[B["ebn"]])
                    for i, hh in enumerate(hs):
                        T, B = TT[i], BB_[i]
                        c.op("dve", lambda h: h.tensor_tensor(out=qo[:, hh, :], in0=T["qs"][:, :], in1=T["eb"][:, :], op=ALU.mult),
                             reads=[B["qs"], B["eb"]], writes=[bqo])
                        c.op("dve", lambda h: h.tensor_tensor(out=ko[:, hh, :], in0=T["kk"][:, :], in1=T["ebn"][:, :], op=ALU.mult),
                             reads=[B["kk"], B["ebn"]], writes=[bko])
                        c.op("dve", lambda h: h.tensor_copy(out=ebl[:, hh, tile * 8:(tile + 1) * 8],
                                                            in_=T["eb"][:, :].rearrange("p (a b) -> p a b", b=64)[:, :, 63]),
                             reads=[B["eb"]], writes=[bebl])
                        ptt, bptt = self.ps_next("t")
                        ptb = ptt[:, 0:256].bitcast(BF16)
                        for sub in range(4):
                            c.op("pe", lambda h: h.transpose(ptb[:, sub * 128:(sub + 1) * 128], ko[:, hh, sub * 128:(sub + 1) * 128],
                                                             identb[:, :]), reads=[bko, bcn], writes=[bptt])
                        c.op("act", lambda h: h.activation(out=ktm[:, hh, :, :], in_=ptb.rearrange("p (a b) -> p a b", b=128),
                                                           func=AF.Identity), reads=[bptt], writes=[bktm])
                c.dma("sp", QT3[:, :, cs], qo[:, :, :], reads=[bqo], writes=[self.bqkv])
                c.dma("sp", KT3[:, :, cs], ko[:, :, :], reads=[bko], writes=[self.bqkv])
                c.dma("sp", GT3[:, :, cs], go[:, :, :], reads=[bgo], writes=[self.bqkv])
                c.dma("sp", self.V1s[0:8, :, tile * 4:(tile + 1) * 4, :].rearrange("h p s e -> p h s e"), ktm[:, :, :, :],
                      reads=[bktm], writes=[self.bqkv])
                for sub in range(4):
                    v = ii % 2
                    ii += 1
                    for half in range(2):
                        pv, bpv = self.ps_next("a")
                        for kc in range(KC):
                            c.op("pe", lambda h: h.matmul(pv[:, :], lhsT=xn[:, kc, sub * 128:(sub + 1) * 128],
                                                          rhs=w[:, kc, 2 * D + half * 512:2 * D + (half + 1) * 512],
                                                          start=(kc == 0), stop=(kc == KC - 1)),
                                 reads=[bww, bxn], writes=[bpv])
                        c.op("act", lambda h: h.activation(out=itm[v][:, half * 4:(half + 1) * 4, :],
                                                           in_=pv[:, :].rearrange("p (a b) -> p a b", b=128), func=AF.Identity),
                             reads=[bpv], writes=[bitm[v]])
                    c.dma("sp", self.V1s[8:16, :, tile * 4 + sub, :].rearrange("h p e -> p h e"), itm[v][:, :, :],
                          reads=[bitm[v]], writes=[self.bqkv])
            c.barrier()

    def hgrn_rec(self, ebl, bebl):
        c = self.c
        QT3 = self.QT.rearrange("(kc p) t -> p kc t", p=128)
        KT3 = self.KT.rearrange("(kc p) t -> p kc t", p=128)
        GT3 = self.GT.rearrange("(kc p) t -> p kc t", p=128)
        OT3 = self.OT.rearrange("(kc p) t -> p kc t", p=128)
        self.ps_pools = {"at": [0, 1], "o": [2, 3], "d": [4, 5], "n": [6, 7]}
        self.ps_pi = {}
        with contextlib.ExitStack() as st:
            tri = self.sb(st, "r_tri", [64, 64], F32)
            bcn = Buf()
            c.dma("sp", tri[:, :], self.tri[:, :], writes=[bcn])
            S32 = self.sb(st, "r_S32", [128, 8, 128], F32)
            Sbf = self.sb(st, "r_Sbf", [128, 8, 128], BF16)
            Sbf2 = [Sbf, self.sb(st, "r_Sbfb", [128, 8, 128], BF16)]
            bSbf2 = [Buf(), Buf()]
            t32 = self.sb(st, "r_t32", [128, 8, 128], F32)
            bS32, bSbf, bt32 = Buf(), Buf(), Buf()
            c.op("dve", lambda h: h.memset(S32[:, :, :], 0.0), writes=[bS32])
            c.op("dve", lambda h: h.memset(Sbf2[0][:, :, :], 0.0), writes=[bSbf2[0]])
            c.op("dve", lambda h: h.memset(Sbf2[1][:, :, :], 0.0), writes=[bSbf2[1]])
            qT = [self.sb(st, "r_q%d" % i, [128, 8, 512], BF16) for i in range(2)]
            kT = [self.sb(st, "r_k%d" % i, [128, 8, 512], BF16) for i in range(2)]
            gT = [self.sb(st, "r_g%d" % i, [128, 8, 512], BF16) for i in range(2)]
            ktm = [self.sb(st, "r_ktm%d" % i, [128, 8, 4, 128], BF16) for i in range(2)]
            itm = [self.sb(st, "r_itm%d" % i, [128, 8, 4, 128], BF16) for i in range(2)]
            bin_ = [Buf(), Buf()]
            o32 = self.sb(st, "r_o32", [128, 8, 512], F32)
            bo32 = Buf()
            at = [self.sb(st, "r_at%d" % i, [128, 8, 64], BF16) for i in range(2)]
            bat = [Buf(), Buf()]
            sq = self.sb(st, "r_sq", [128, 512], F32)
            rt = self.sb(st, "r_rt", [128, 512], F32)
            tmp = self.sb(st, "r_tmp", [128, 512], F32)
            bsq, brt, btmp = Buf(), Buf(), Buf()
            oo = [self.sb(st, "r_oo%d" % i, [128, 8, 512], BF16) for i in range(2)]
            boo = [Buf(), Buf()]
            ai = 0
            for tile in range(S // 512):
                cs = slice(tile * 512, (tile + 1) * 512)
                b = tile % 2
                c.dma("sp", qT[b][:, :, :], QT3[:, :, cs], reads=[self.bqkv], writes=[bin_[b]])
                c.dma("sp", kT[b][:, :, :], KT3[:, :, cs], reads=[self.bqkv], writes=[bin_[b]])
                c.dma("sp", gT[b][:, :, :], GT3[:, :, cs], reads=[self.bqkv], writes=[bin_[b]])
                c.dma("sp", ktm[b][:, :, :, :], self.V1s[0:8, :, tile * 4:(tile + 1) * 4, :].rearrange("h p s e -> p h s e"),
                      reads=[self.bqkv], writes=[bin_[b]])
                c.dma("sp", itm[b][:, :, :, :], self.V1s[8:16, :, tile * 4:(tile + 1) * 4, :].rearrange("h p s e -> p h s e"),
                      reads=[self.bqkv], writes=[bin_[b]])
                for ch in range(8):
                    cc = slice(ch * 64, (ch + 1) * 64)
                    prt = 64 * (ch % 2)
                    sub = ch // 2
                    gch = tile * 8 + ch
                    pdA, bpdA = self.ps[4], self.bps[4]
                    pdB, bpdB = self.ps[5], self.bps[5]
                    for hh in range(8):
                        pd, bpd = (pdA, bpdA) if hh < 4 else (pdB, bpdB)
                        hc = (hh % 4) * 128
                        c.op("pe", lambda h: h.matmul(pd[:, hc:hc + 128], lhsT=ktm[b][prt:prt + 64, hh, sub, :],
                                                      rhs=itm[b][prt:prt + 64, hh, sub, :], start=True, stop=True),
                             reads=[bin_[b]], writes=[bpd])
                    c.op("dve", lambda h: h.tensor_tensor(out=t32[:, 0:4, :], in0=pdA[:, :].rearrange("p (a b) -> p a b", b=128),
                                                          in1=S32[:, 0:4, :], op=ALU.add), reads=[bpdA, bS32], writes=[bt32])
                    c.op("dve", lambda h: h.tensor_tensor(out=t32[:, 4:8, :], in0=pdB[:, :].rearrange("p (a b) -> p a b", b=128),
                                                          in1=S32[:, 4:8, :], op=ALU.add), reads=[bpdB, bS32], writes=[bt32])
                    c.op("dve", lambda h: h.tensor_tensor(out=S32[:, :, :], in0=t32[:, :, :],
                                                          in1=ebl[:, :, gch].unsqueeze(2).to_broadcast([128, 8, 128]), op=ALU.mult),
                         reads=[bt32, bebl], writes=[bS32])
                    c.op("act", lambda h: h.activation(out=Sbf2[gch % 2][:, :, :], in_=S32[:, :, :], func=AF.Identity),
                         reads=[bS32], writes=[bSbf2[gch % 2]])
                    pat, bpat = self.ps_next("at")
                    for hh in range(8):
                        c.op("pe", lambda h: h.matmul(pat[0:64, hh * 64:(hh + 1) * 64], lhsT=kT[b][:, hh, cc], rhs=qT[b][:, hh, cc],
                                                      start=True, stop=True), reads=[bin_[b]], writes=[bpat])
                    a = ai % 2
                    ai += 1
                    c.op("dve", lambda h: h.tensor_tensor(out=at[a][prt:prt + 64, :, :],
                                                          in0=pat[0:64, :].rearrange("p (a b) -> p a b", b=64),
                                                          in1=tri[:, :].unsqueeze(1).to_broadcast([64, 8, 64]), op=ALU.mult),
                         reads=[bpat, bcn], writes=[bat[a]])
                    po, bpo = self.ps_next("o")
                    for hh in range(8):
                        c.op("pe", lambda h: h.matmul(po[:, hh * 64:(hh + 1) * 64], lhsT=Sbf2[(gch - 1) % 2][:, hh, :], rhs=qT[b][:, hh, cc],
                                                      start=True, stop=False), reads=[bSbf2[(gch - 1) % 2], bin_[b]], writes=[bpo])
                        c.op("pe", lambda h: h.matmul(po[:, hh * 64:(hh + 1) * 64], lhsT=itm[b][prt:prt + 64, hh, sub, :],
                                                      rhs=at[a][prt:prt + 64, hh, :], start=False, stop=True),
                             reads=[bin_[b], bat[a]], writes=[bpo])
                    c.op("act", lambda h: h.activation(out=o32[:, :, cc], in_=po[:, :].rearrange("p (a b) -> p a b", b=64),
                                                       func=AF.Identity), reads=[bpo], writes=[bo32])
                for hh in range(8):
                    pn, bpn = self.ps_next("n")
                    c.op("act", lambda h: h.activation(out=sq[:, :], in_=o32[:, hh, :], func=AF.Square), reads=[bo32], writes=[bsq])
                    c.op("pe", lambda h: h.matmul(pn[:, :], lhsT=self.ones_f[:, :], rhs=sq[:, :], start=True, stop=True),
                         reads=[bsq, self.bconst], writes=[bpn])
                    c.op("act", lambda h: h.activation(out=rt[:, :], in_=pn[:, :], func=AF.Sqrt, scale=1.0 / 128,
                                                       bias=self.eps_t[:, 0:1]), reads=[bpn, self.bconst], writes=[brt])
                    c.op("dve", lambda h: h.reciprocal(out=rt[:, :], in_=rt[:, :]), reads=[brt], writes=[brt])
                    c.op("dve", lambda h: h.scalar_tensor_tensor(out=tmp[:, :], in0=o32[:, hh, :],
                                                                 scalar=self.pvec[:, self.col_hnorm:self.col_hnorm + 1],
                                                                 in1=rt[:, :], op0=ALU.mult, op1=ALU.mult),
                         reads=[bo32, brt, self.bconst], writes=[btmp])
                    c.op("dve", lambda h: h.tensor_tensor(out=oo[b][:, hh, :], in0=tmp[:, :], in1=gT[b][:, hh, :], op=ALU.mult),
                         reads=[btmp, bin_[b]], writes=[boo[b]])
                c.dma("sp", OT3[:, :, cs], oo[b][:, :, :], reads=[boo[b]], writes=[self.bot])
            c.barrier()

    def s5_phase(self, layer, wglu_s, bw):
        c = self.c
        hT3 = self.s5_src.rearrange("(kc p) t -> p kc t", p=128)
        w3 = wglu_s.rearrange("(kc p) n -> p kc n", p=128)
        self.ps_pools = {"y": [0, 1, 2, 3], "e": [4, 5, 6, 7]}
        self.ps_pi = {}
        TS = 1024
        sgnA = self.pvec[:, self.col_sgnA:self.col_sgnA + 1]
        sgnB = self.pvec[:, self.col_sgnA + 1:self.col_sgnA + 2]
        neg1 = self.pvec[:, self.col_sgnA + 2:self.col_sgnA + 3]
        with contextlib.ExitStack() as st:
            Bp = self.sb(st, "s_Bp", [128, 8, 4, 128], BF16)
            Bq = self.sb(st, "s_Bq", [128, 8, 4, 128], BF16)
            Cc = self.sb(st, "s_Cc", [128, 64, 64], BF16)
            Cs = self.sb(st, "s_Cs", [128, 64, 64], BF16)
            th = self.sb(st, "s_th", [128, 64], F32)
            rho = self.sb(st, "s_rho", [128, 64], F32)
            carry = self.sb(st, "s_carry", [128, 64], F32)
            bpar, bcarry = Buf(), Buf()
            with contextlib.ExitStack() as s2:
                def t64(n):
                    return self.sb(s2, "s_" + n, [128, 64], F32)
                are, aim, ldt, dt, lr, x0, kf0, sn0, cs0, abr, abi, den, mre, fre, fim, tA, tB = [t64(n) for n in (
                    "are", "aim", "ldt", "dt", "lr", "x0", "kf0", "sn0", "cs0", "abr", "abi", "den", "mre", "fre", "fim", "tA", "tB")]
                ki0 = self.sb(s2, "s_ki0", [128, 64], I32)
                bl = Buf()
                c.dma("sp", are[:, :], self.s5p[:, 0:64], writes=[bl])
                c.dma("sp", aim[:, :], self.s5p[:, 64:128], writes=[bl])
                c.dma("sp", ldt[:, :], self.s5p[:, 128:192], writes=[bl])
                b0 = Buf()
                R, W = [bl, b0], [b0]
                c.op("act", lambda h: h.activation(out=dt[:, :], in_=ldt[:, :], func=AF.Exp), reads=R, writes=W)
                c.op("dve", lambda h: h.tensor_tensor(out=lr[:, :], in0=are[:, :], in1=dt[:, :], op=ALU.mult), reads=R, writes=W)
                c.op("dve", lambda h: h.tensor_tensor(out=th[:, :], in0=aim[:, :], in1=dt[:, :], op=ALU.mult), reads=R, writes=[b0, bpar])
                c.op("act", lambda h: h.activation(out=rho[:, :], in_=lr[:, :], func=AF.Exp), reads=R, writes=[b0, bpar])
                self._rr(th, b0, 64, ki0, b0, kf0, b0, sn0, b0, cs0, b0)
                c.op("dve", lambda h: h.tensor_tensor(out=abr[:, :], in0=rho[:, :], in1=cs0[:, :], op=ALU.mult), reads=R, writes=W)
                c.op("dve", lambda h: h.tensor_tensor(out=abi[:, :], in0=rho[:, :], in1=sn0[:, :], op=ALU.mult), reads=R, writes=W)
                c.op("dve", lambda h: h.tensor_tensor(out=den[:, :], in0=are[:, :], in1=are[:, :], op=ALU.mult), reads=R, writes=W)
                c.op("dve", lambda h: h.tensor_tensor(out=tA[:, :], in0=aim[:, :], in1=aim[:, :], op=ALU.mult), reads=R, writes=W)
                c.op("dve", lambda h: h.tensor_tensor(out=den[:, :], in0=den[:, :], in1=tA[:, :], op=ALU.add), reads=R, writes=W)
                c.op("dve", lambda h: h.reciprocal(out=den[:, :], in_=den[:, :]), reads=R, writes=W)
                c.op("dve", lambda h: h.tensor_scalar(out=mre[:, :], in0=abr[:, :], scalar1=-1.0, scalar2=None, op0=ALU.add), reads=R, writes=W)
                c.op("dve", lambda h: h.tensor_tensor(out=tA[:, :], in0=mre[:, :], in1=are[:, :], op=ALU.mult), reads=R, writes=W)
                c.op("dve", lambda h: h.tensor_tensor(out=tB[:, :], in0=abi[:, :], in1=aim[:, :], op=ALU.mult), reads=R, writes=W)
                c.op("dve", lambda h: h.tensor_tensor(out=fre[:, :], in0=tA[:, :], in1=tB[:, :], op=ALU.add), reads=R, writes=W)
                c.op("dve", lambda h: h.tensor_tensor(out=fre[:, :], in0=fre[:, :], in1=den[:, :], op=ALU.mult), reads=R, writes=W)
                c.op("dve", lambda h: h.tensor_tensor(out=tA[:, :], in0=abi[:, :], in1=are[:, :], op=ALU.mult), reads=R, writes=W)
                c.op("dve", lambda h: h.tensor_tensor(out=tB[:, :], in0=mre[:, :], in1=aim[:, :], op=ALU.mult), reads=R, writes=W)
                c.op("dve", lambda h: h.tensor_tensor(out=fim[:, :], in0=tA[:, :], in1=tB[:, :], op=ALU.subtract), reads=R, writes=W)
                c.op("dve", lambda h: h.tensor_tensor(out=fim[:, :], in0=fim[:, :], in1=den[:, :], op=ALU.mult), reads=R, writes=W)
                c.op("dve", lambda h: h.tensor_scalar(out=tA[:, :], in0=fim[:, :], scalar1=sgnA, scalar2=None, op0=ALU.mult),
                     reads=[b0, self.bconst], writes=W)
                c.op("dve", lambda h: h.tensor_scalar(out=tB[:, :], in0=fre[:, :], scalar1=sgnB, scalar2=None, op0=ALU.mult),
                     reads=[b0, self.bconst], writes=W)
                BB = self.sb(s2, "s_BB", [128, 1024], F32)
                BS = self.sb(s2, "s_BS", [128, 1024], F32)
                u1 = self.sb(s2, "s_u1", [128, 1024], F32)
                u2 = self.sb(s2, "s_u2", [128, 1024], F32)
                Z = self.sb(s2, "s_Z", [128, 8, 4, 128], F32)
                idf = self.sb(s2, "s_idf", [128, 128], F32)
                c.dma("sp", BB[:, :], self.s5B[:, 0:1024], writes=[bl])
                c.dma("sp", BS[:, :], self.s5B[:, 1024:2048], writes=[bl])
                c.dma("sp", idf[:, :], self.ident[:, :], writes=[bl])

                def bc(t):
                    return t[:, :].unsqueeze(2).to_broadcast([128, 64, 16])

                def v3(t):
                    return t[:, :].rearrange("p (g c) -> p g c", c=16)
                for (dst, ca, cb) in ((Bp, (fre, BB, tA, BS), None), (Bq, (tB, BS, fim, BB), None)):
                    fa, Xa, fb, Xb = ca
                    c.op("dve", lambda h: h.memset(Z[:, :, :, :], 0.0), reads=R, writes=W)
                    c.op("dve", lambda h: h.tensor_tensor(out=v3(u1), in0=v3(Xa), in1=bc(fa), op=ALU.mult), reads=R, writes=W)
                    c.op("dve", lambda h: h.tensor_tensor(out=v3(u2), in0=v3(Xb), in1=bc(fb), op=ALU.mult), reads=R, writes=W)
                    for par in range(4):
                        zv = Z[:, :, par, :].rearrange("p k (m q) -> p k m q", q=64)[:, :, :, 16 * par:16 * par + 16]
                        a1 = u1[:, :].rearrange("p (k m r c) -> p k m r c", k=8, m=2, r=4, c=16)[:, :, :, par, :]
                        a2 = u2[:, :].rearrange("p (k m r c) -> p k m r c", k=8, m=2, r=4, c=16)[:, :, :, par, :]
                        c.op("dve", lambda h: h.tensor_tensor(out=zv, in0=a1, in1=a2, op=ALU.add), reads=R, writes=W)
                    for kc in range(8):
                        for par in range(4):
                            pz, bpz = self.ps_next("e")
                            c.op("pe", lambda h: h.transpose(pz[:, 0:128], Z[:, kc, par, :], idf[:, :]), reads=R, writes=[bpz])
                            c.op("act", lambda h: h.activation(out=dst[:, kc, par, :], in_=pz[:, 0:128], func=AF.Identity),
                                 reads=[bpz], writes=[bpar])
                CC = self.sb(s2, "s_CC", [128, 2048], F32)
                c.dma("sp", CC[:, :], self.s5C[:, :], writes=[bl])
                c.op("dve", lambda h: h.memset(Cc[:, :, :], 0.0), writes=[bpar])
                c.op("dve", lambda h: h.memset(Cs[:, :, :], 0.0), writes=[bpar])
                for (dst, off, sg) in ((Cc, 0, sgnB), (Cs, 1024, neg1)):
                    for par in range(4):
                        dv = dst[:, :, :].rearrange("p (gm r) (q c) -> p gm r q c", r=4, q=4)[:, :, par, par, :]
                        sv = CC[:, off:off + 1024].rearrange("p (gm r c) -> p gm r c", r=4, c=16)[:, :, par, :]
                        c.op("dve", lambda h: h.tensor_scalar(out=dv, in0=sv, scalar1=sg, scalar2=None, op0=ALU.mult),
                             reads=[bl, self.bconst], writes=[bpar])
                c.op("dve", lambda h: h.memset(carry[:, :], 0.0), writes=[bcarry])
                c.barrier()
            tt = self.sb(st, "s_tt", [128, TS], F32)
            thq = self.sb(st, "s_thq", [128, 64], F32)
            btt, bthq = Buf(), Buf()
            c.dma("sp", tt[:, :], self.ttc[:, 0:TS], writes=[btt])
            wg = self.sb(st, "s_wg", [128, KC, 2 * D], BF16)
            bwg = Buf()
            for i in range(2):
                c.dma("sp", wg[:, :, i * D:(i + 1) * D], w3[:, :, i * D:(i + 1) * D], reads=[bw], writes=[bwg])
            hx = self.sb(st, "s_hx", [128, KC, 512], F32)
            sq = self.sb(st, "s_sq", [128, 2, 512], F32)
            rt = self.sb(st, "s_rt", [128, 512], F32)
            xn = self.sb(st, "s_xn", [128, KC, TS], BF16)
            z = self.sb(st, "s_z", [128, KC, TS], BF16)
            bhx, bsq, brt, bxn, bz = Buf(), Buf(), Buf(), Buf(), Buf()
            x = self.sb(st, "s_x", [128, TS], F32)
            ki = self.sb(st, "s_ki", [128, TS], I32)
            sn = [self.sb(st, "s_sn%d" % i, [128, TS], F32) for i in range(3)]
            cs_ = [self.sb(st, "s_cs%d" % i, [128, TS], F32) for i in range(3)]
            eh = [self.sb(st, "s_eh%d" % i, [128, TS], F32) for i in range(2)]
            G = [self.sb(st, "s_G%d" % i, [128, TS], F32) for i in range(2)]
            Gc = [self.sb(st, "s_Gc%d" % i, [128, TS], BF16) for i in range(2)]
            Gs = [self.sb(st, "s_Gs%d" % i, [128, TS], BF16) for i in range(2)]
            bsn, bcs = [Buf() for _ in range(3)], [Buf() for _ in range(3)]
            beh, bG, bGc, bGs = [[Buf() for _ in range(2)] for _ in range(4)]
            bx, bki = Buf(), Buf()
            t1 = [self.sb(st, "s_t1%d" % i, [128, 512], F32) for i in range(2)]
            t2 = [self.sb(st, "s_t2%d" % i, [128, 512], F32) for i in range(2)]
            bt1 = [Buf(), Buf()]
            bt2 = [Buf(), Buf()]
            yv = self.sb(st, "s_yv", [128, 512], F32)
            y2 = self.sb(st, "s_y2", [128, 512], F32)
            y3 = self.sb(st, "s_y3", [128, 512], F32)
            byv, by2, by3 = Buf(), Buf(), Buf()
            hr = [self.sb(st, "s_hr", [128, TS], F32)] * 2
            bhr = [Buf()] * 2
            ti = 0
            for q in range(S // TS):
                t0 = q * TS
                bh = self.bh[q]
                c.op("dve", lambda h: h.tensor_scalar(out=thq[:, :], in0=th[:, :], scalar1=float(t0), scalar2=None, op0=ALU.mult),
                     reads=[bpar], writes=[bthq])
                for half in range(2):
                    c.dma("sp", hx[:, :, :], hT3[:, :, t0 + half * 512:t0 + (half + 1) * 512], reads=[bh], writes=[bhx])
                    self.rmsnorm_tile(hx, bhx, self.col_nmix + layer * KC, xn[:, :, half * 512:(half + 1) * 512], bxn, 512,
                                      (sq, bsq, rt, brt))
                pys = {}

                TWO_PI = 2.0 * np.pi

                def S1(g):
                    kc, gi = g // 8, g % 8
                    u3 = g % 3
                    if gi == 0:
                        pys[kc] = [self.ps_next("y") for _ in range(2)]
                    c.op("act", lambda h: h.activation(out=x[:, :], in_=tt[:, :], func=AF.Identity, scale=th[:, g:g + 1],
                                                       bias=thq[:, g:g + 1]),
                         reads=[btt, bpar, bthq], writes=[bx])
                    c.op("act", lambda h: h.activation(out=ki[:, :], in_=x[:, :], func=AF.Identity, scale=1.0 / TWO_PI),
                         reads=[bx], writes=[bki])

                def S1b(g):
                    u3 = g % 3
                    c.op("dve", lambda h: h.scalar_tensor_tensor(out=sn[u3][:, :], in0=ki[:, :], scalar=-TWO_PI, in1=x[:, :],
                                                                 op0=ALU.mult, op1=ALU.add), reads=[bki, bx], writes=[bsn[u3]])
                    c.op("dve", lambda h: h.tensor_scalar(out=sn[u3][:, :], in0=sn[u3][:, :], scalar1=-PI_LO, scalar2=PI_LO,
                                                          op0=ALU.max, op1=ALU.min), reads=[bsn[u3]], writes=[bsn[u3]])
                    c.op("act", lambda h: h.activation(out=cs_[u3][:, :], in_=sn[u3][:, :], func=AF.Abs),
                         reads=[bsn[u3]], writes=[bcs[u3]])

                def S2a(g):
                    u3 = g % 3
                    c.op("act", lambda h: h.activation(out=sn[u3][:, :], in_=sn[u3][:, :], func=AF.Sin), reads=[bsn[u3]], writes=[bsn[u3]])
                    c.op("act", lambda h: h.activation(out=cs_[u3][:, :], in_=cs_[u3][:, :], func=AF.Sin, scale=-1.0,
                                                       bias=self.halfpi_t[:, 0:1]), reads=[bcs[u3], self.bconst], writes=[bcs[u3]])

                def S2(g):
                    nonlocal ti
                    kc, gi = g // 8, g % 8
                    u3, u = g % 3, g % 2
                    m, par = gi // 4, gi % 4
                    rows = slice(64 * m, 64 * m + 64)
                    for ct in range(2):
                        cc = slice(ct * 512, (ct + 1) * 512)
                        pe_, bpe = self.ps_next("e")
                        pq_, bpq = self.ps_next("e")
                        c.op("pe", lambda h: h.matmul(pe_[:, :], lhsT=Bp[rows, kc, par, :], rhs=xn[rows, kc, cc], start=True, stop=True),
                             reads=[bpar, bxn], writes=[bpe])
                        c.op("pe", lambda h: h.matmul(pq_[:, :], lhsT=Bq[rows, kc, par, :], rhs=xn[rows, kc, cc], start=True, stop=True),
                             reads=[bpar, bxn], writes=[bpq])
                        e = ti % 2
                        ti += 1
                        c.op("dve", lambda h: h.tensor_tensor(out=t1[e][:, :], in0=pe_[:, :], in1=cs_[u3][:, cc], op=ALU.mult),
                             reads=[bpe, bcs[u3]], writes=[bt1[e]])
                        c.op("dve", lambda h: h.tensor_tensor(out=t2[e][:, :], in0=pq_[:, :], in1=sn[u3][:, cc], op=ALU.mult),
                             reads=[bpq, bsn[u3]], writes=[bt2[e]])
                        c.op("pool", lambda h: h.tensor_tensor(out=eh[u][:, cc], in0=t1[e][:, :], in1=t2[e][:, :], op=ALU.add),
                             reads=[bt1[e], bt2[e]], writes=[beh[u]])

                def S3(g):
                    u3, u = g % 3, g % 2
                    c.op("dve", lambda h: h.tensor_tensor_scan(out=G[u][:, :], data0=rho[:, g:g + 1].to_broadcast([128, TS]),
                                                               data1=eh[u][:, :], initial=carry[:, g:g + 1],
                                                               op0=ALU.mult, op1=ALU.add),
                         reads=[beh[u], bpar, bcarry], writes=[bG[u]])
                    c.op("act", lambda h: h.activation(out=carry[:, g:g + 1], in_=G[u][:, TS - 1:TS], func=AF.Identity),
                         reads=[bG[u]], writes=[bcarry])
                    c.op("dve", lambda h: h.tensor_tensor(out=Gc[u][:, :], in0=G[u][:, :], in1=cs_[u3][:, :], op=ALU.mult),
                         reads=[bG[u], bcs[u3]], writes=[bGc[u]])
                    c.op("pool", lambda h: h.tensor_tensor(out=Gs[u][:, :], in0=G[u][:, :], in1=sn[u3][:, :], op=ALU.mult),
                         reads=[bG[u], bsn[u3]], writes=[bGs[u]])

                def S4(g):
                    kc, gi = g // 8, g % 8
                    u = g % 2
                    m, par = gi // 4, gi % 4
                    rows = slice(64 * m, 64 * m + 64)
                    py = pys[kc]
                    for ct in range(2):
                        cc = slice(ct * 512, (ct + 1) * 512)
                        pyy, bpy = py[ct]
                        c.op("pe", lambda h: h.matmul(pyy[rows, :], lhsT=Cc[:, g, :], rhs=Gc[u][:, cc], start=(par == 0), stop=False),
                             reads=[bpar, bGc[u]], writes=[bpy])
                        c.op("pe", lambda h: h.matmul(pyy[rows, :], lhsT=Cs[:, g, :], rhs=Gs[u][:, cc], start=False, stop=(par == 3)),
                             reads=[bpar, bGs[u]], writes=[bpy])
                    if gi == 7:
                        for ct in range(2):
                            cc = slice(ct * 512, (ct + 1) * 512)
                            pyy, bpy = py[ct]
                            dcol = self.pvec[:, self.col_s5d + kc:self.col_s5d + kc + 1]
                            c.op("dve", lambda h: h.scalar_tensor_tensor(out=yv[:, :], in0=xn[:, kc, cc], scalar=dcol, in1=pyy[:, :],
                                                                         op0=ALU.mult, op1=ALU.add),
                                 reads=[bxn, bpy, self.bconst], writes=[byv])
                            c.op("act", lambda h: h.activation(out=y2[:, :], in_=yv[:, :], func=AF.Square), reads=[byv], writes=[by2])
                            c.op("pool", lambda h: h.tensor_scalar(out=y2[:, :], in0=y2[:, :], scalar1=0.044715, scalar2=1.0,
                                                                   op0=ALU.mult, op1=ALU.add), reads=[by2], writes=[by2])
                            c.op("pool", lambda h: h.tensor_tensor(out=y2[:, :], in0=y2[:, :], in1=yv[:, :], op=ALU.mult),
                                 reads=[by2, byv], writes=[by2])
                            c.op("act", lambda h: h.activation(out=y3[:, :], in_=y2[:, :], func=AF.Sigmoid, scale=1.5957691216),
                                 reads=[by2], writes=[by3])
                            c.op("pool", lambda h: h.tensor_tensor(out=z[:, kc, cc], in0=y3[:, :], in1=yv[:, :], op=ALU.mult),
                                 reads=[by3, byv], writes=[bz])

                NG = 64
                for i in range(NG + 3):
                    if 0 <= i - 1 < NG:
                        S2a(i - 1)
                    if 0 <= i - 2 < NG:
                        S3(i - 2)
                    if i < NG:
                        S1(i)
                    if 0 <= i - 1 < NG:
                        S2(i - 1)
                    if i < NG:
                        S1b(i)
                    if 0 <= i - 3 < NG:
                        S4(i - 3)
                for oc in range(KC):
                    r = oc % 2
                    c.dma("sp", hr[r][:, :], self.s5_src[oc * 128:(oc + 1) * 128, t0:t0 + TS], reads=[bh], writes=[bhr[r]])
                    for ct in range(2):
                        cc = slice(ct * 512, (ct + 1) * 512)
                        pv, bpv = self.ps_next("e")
                        pg, bpg = self.ps_next("e")
                        for kc in range(KC):
                            c.op("pe", lambda h: h.matmul(pv[:, :], lhsT=wg[:, kc, oc * 128:(oc + 1) * 128], rhs=z[:, kc, cc],
                                                          start=(kc == 0), stop=(kc == KC - 1)), reads=[bwg, bz], writes=[bpv])
                        for kc in range(KC):
                            c.op("pe", lambda h: h.matmul(pg[:, :], lhsT=wg[:, kc, D + oc * 128:D + (oc + 1) * 128], rhs=z[:, kc, cc],
                                                          start=(kc == 0), stop=(kc == KC - 1)), reads=[bwg, bz], writes=[bpg])
                        bv = self.pvec[:, self.col_bglu + oc:self.col_bglu + oc + 1]
                        bg = self.pvec[:, self.col_bglu + 8 + oc:self.col_bglu + 8 + oc + 1]
                        c.op("act", lambda h: h.activation(out=y2[:, :], in_=pv[:, :], func=AF.Identity, bias=bv),
                             reads=[bpv, self.bconst], writes=[by2])
                        c.op("act", lambda h: h.activation(out=y3[:, :], in_=pg[:, :], func=AF.Sigmoid, bias=bg),
                             reads=[bpg, self.bconst], writes=[by3])
                        c.op("dve", lambda h: h.tensor_tensor(out=y2[:, :], in0=y2[:, :], in1=y3[:, :], op=ALU.mult),
                             reads=[by2, by3], writes=[by2])
                        c.op("dve", lambda h: h.tensor_tensor(out=hr[r][:, cc], in0=hr[r][:, cc], in1=y2[:, :], op=ALU.add),
                             reads=[by2, bhr[r]], writes=[bhr[r]])
                    c.dma("sp", self.hT[oc * 128:(oc + 1) * 128, t0:t0 + TS], hr[r][:, :], reads=[bhr[r]], writes=[bh])
            c.barrier()

    def final_phase(self, outT, do_norm):
        c = self.c
        hT3 = self.hT.rearrange("(kc p) t -> p kc t", p=128)
        oT3 = outT.rearrange("(kc p) t -> p kc t", p=128)
        with contextlib.ExitStack() as st:
            hx = [self.sb(st, "o_hx%d" % i, [128, KC, 512], F32) for i in range(2)]
            ox = [self.sb(st, "o_ox%d" % i, [128, KC, 512], F32) for i in range(2)]
            sq = self.sb(st, "o_sq", [128, KC, 512], F32)
            rt = self.sb(st, "o_rt", [128, 512], F32)
            bhx = [Buf(), Buf()]
            box = [Buf(), Buf()]
            bsq, brt, bo = Buf(), Buf(), Buf()
            for tile in range(S // 512):
                cs = slice(tile * 512, (tile + 1) * 512)
                b = tile % 2
                c.dma("sp", hx[b][:, :, :], hT3[:, :, cs], reads=[self.bh[tile // 2]], writes=[bhx[b]])
                if do_norm:
                    self.rmsnorm_tile(hx[b], bhx[b], self.col_nfin, ox[b], box[b], 512, (sq, bsq, rt, brt))
                    c.dma("sp", oT3[:, :, cs], ox[b][:, :, :], reads=[box[b]], writes=[bo])
                else:
                    c.dma("sp", oT3[:, :, cs], hx[b][:, :, :], reads=[bhx[b]], writes=[bo])
            c.barrier()

    def build(self):
        cfg = self.cfg
        nc = self.nc
        c = self.c
        es = self.es
        stages = cfg["stages"]
        mixl = [l for (k, l) in stages if k == "mix"]
        ffnl = [l for (k, l) in stages if k == "ffn"]
        xT = self.din("xT", [D, S])
        pvec = self.din("pvec", [128, cfg["npvec"]])
        self.col_nmix, self.col_nffn, self.col_nfin = 0, 4 * KC, 8 * KC
        self.col_invf, self.col_sgnrow = 9 * KC, 9 * KC + 1
        self.posb = self.din("posb", [128, S], I32)
        self.dmask = self.din("dmask", [128, 20 * 512])
        self.mmask = self.din("mmask", [128, 4 * 512])
        self.onehot = self.din("onehot", [16, S])
        self.pastneg = self.din("pastneg", [128, 512])
        self.own = self.din("own", [128, 512])
        self.ident = self.din("ident", [128, 128])
        self.ropeperm = self.din("ropeperm", [128, 128])
        self.rmask = self.din("rmask", [128, 512])
        self.tri = self.din("tri", [64, 64])
        self.col_lb, self.col_hnorm = 9 * KC + 2, 9 * KC + 2 + 4 * KC
        self.col_sgnA = self.col_hnorm + 1
        self.col_s5d = self.col_sgnA + 3
        self.col_bglu = self.col_s5d + KC
        self.s5p = self.din("s5p", [128, 192])
        self.s5B = self.din("s5B", [128, 2048])
        self.s5C = self.din("s5C", [128, 2048])
        self.ttc = self.din("ttc", [128, S])
        w_gu_in = {l: self.din("w_gu%d" % l, [D, 2 * DFF]) for l in ffnl}
        w_d_in = {l: self.din("w_d%d" % l, [DFF, D]) for l in ffnl}
        self.w_gu = {l: self.dscr("s_wgu%d" % l, [D, 2 * DFF], BF16) for l in ffnl}
        self.w_d = {l: self.dscr("s_wd%d" % l, [DFF, D], BF16) for l in ffnl}
        self.bw_ffn = {l: Buf() for l in ffnl}
        win, wsc, bwm = {}, {}, {}
        for l in mixl:
            if l in (1, 3):
                nm = "dil" if l == 1 else "moba"
                win[l] = (self.din(nm + "_qkv", [D, 3 * D]), self.din(nm + "_o", [D, D]))
                wsc[l] = (self.dscr("s_%s_qkv" % nm, [D, 3 * D], BF16), None,
                          self.dscr("s_%s_o" % nm, [D, D], BF16))
                bwm[l] = Buf()
            elif l == 0:
                win[l] = (self.din("s5_wglu", [D, 2 * D]),)
                wsc[l] = (self.dscr("s_s5_wglu", [D, 2 * D], BF16),)
                bwm[l] = Buf()
            elif l == 2:
                win[l] = (self.din("hgrn_in", [D, 4 * D]), self.din("hgrn_o", [D, D]))
                wsc[l] = (self.dscr("s_hgrn_in", [D, 4 * D], BF16), self.dscr("s_hgrn_o", [D, D], BF16))
                bwm[l] = Buf()
        outT = nc.dram_tensor("outT", [D, S], F32, kind="ExternalOutput").ap()
        self.GT = self.dscr("GT", [D, S], BF16)
        self.hT = self.dscr("hT", [D, S], F32)
        self.QT = self.dscr("QT", [D, S], BF16)
        self.KT = self.dscr("KT", [D, S], BF16)
        self.OT = self.dscr("OT", [D, S], BF16)
        self.V1s = self.dscr("V1s", [16, 128, 32, 128], BF16)
        self.bqkv, self.bot = Buf(), Buf()
        self.bh = [Buf() for _ in range(S // 1024)]
        self.pvec = self.sb(es, "pvec_sb", [128, cfg["npvec"]], F32)
        self.ones_f = self.sb(es, "ones_f", [128, 128], F32)
        self.eps_t = self.sb(es, "eps_t", [128, 1], F32)
        self.bconst = Buf()
        c.dma("sp", self.pvec[:, :], pvec[:, :], writes=[self.bconst])
        c.op("dve", lambda h: h.memset(self.ones_f[:, :], 1.0), writes=[self.bconst])
        c.op("dve", lambda h: h.memset(self.eps_t[:, :], EPS), writes=[self.bconst])
        self.halfpi_t = self.sb(es, "halfpi_t", [128, 1], F32)
        c.op("dve", lambda h: h.memset(self.halfpi_t[:, :], float(np.pi / 2)), writes=[self.bconst])
        self.ps = [es.enter_context(nc.psum_tensor("ps%d" % i, [128, 512], F32)) for i in range(8)]
        self.bps = [Buf() for _ in range(8)]
        self.ps_i = 0
        self.sg_i = 0
        self.ps_pools = {}
        self.ps_pi = {}
        self.cast_i = 0
        self.cast_st = []
        for i in range(2):
            self.cast_st.append((self.sb(es, "cst%d" % i, [128, 512], F32), self.sb(es, "cstb%d" % i, [128, 512], BF16),
                                 Buf(), Buf()))
        xT3 = xT.rearrange("(kc p) t -> p kc t", p=128)
        hT3 = self.hT.rearrange("(kc p) t -> p kc t", p=128)
        self.s5_src = self.hT
        if stages[0] == ("mix", 0):
            self.s5_src = xT
        else:
            with contextlib.ExitStack() as st:
                tmp = [self.sb(st, "cp%d" % i, [128, KC, 1024], F32) for i in range(2)]
                bt = [Buf(), Buf()]
                for i, t0 in enumerate(range(0, S, 1024)):
                    c.dma("sp", tmp[i % 2][:, :, :], xT3[:, :, t0:t0 + 1024], writes=[bt[i % 2]])
                    c.dma("sp", hT3[:, :, t0:t0 + 1024], tmp[i % 2][:, :, :], reads=[bt[i % 2]], writes=[self.bh[i]])
                c.barrier()
        cast_done = set()

        def emit_cast(si_):
            if si_ >= len(stages) or si_ in cast_done:
                return
            cast_done.add(si_)
            k, l = stages[si_]
            if k == "ffn":
                self.cast_weight(w_gu_in[l], self.w_gu[l], D, 2 * DFF, self.bw_ffn[l])
                self.cast_weight(w_d_in[l], self.w_d[l], DFF, D, self.bw_ffn[l])
            elif k == "mix" and l in (1, 3):
                self.cast_weight(win[l][0], wsc[l][0], D, 3 * D, bwm[l])
                self.cast_weight(win[l][1], wsc[l][2], D, D, bwm[l])
            elif k == "mix" and l == 0:
                self.cast_weight(win[l][0], wsc[l][0], D, 2 * D, bwm[l])
            elif k == "mix" and l == 2:
                self.cast_weight(win[l][0], wsc[l][0], D, 4 * D, bwm[l])
                self.cast_weight(win[l][1], wsc[l][1], D, D, bwm[l])
        self.hook = None
        bwp = {l: Buf() for l in mixl}
        for si, (k, l) in enumerate(stages):
            for (k2, l2) in stages[si + 1:si + 2] + (stages[0:1] if si == 0 else []):
                if False:
                    self.cast_weight_qk_perm(win[l2][0], wsc[l2][1], bwp[l2])
                    bwp[l2].done = True
            emit_cast(si)
            if si == 0:
                emit_cast(1)
            self.hook = lambda si=si: emit_cast(si + 1)
            if k == "ffn":
                self.ffn_phase(l)
            elif k == "mix" and l in (1, 3):
                self.qkv_phase(l, wsc[l][0], wsc[l][1], bwm[l], bwp[l])
                self.attn_phase("dil" if l == 1 else "moba")
                self.linres_phase(self.OT, wsc[l][2], bwm[l], self.bot)
            elif k == "mix" and l == 0:
                self.s5_phase(l, wsc[l][0], bwm[l])
            elif k == "mix" and l == 2:
                with contextlib.ExitStack() as st:
                    ebl = self.sb(st, "ebl", [128, 8, 64], F32)
                    bebl = Buf()
                    self.hgrn_prep(l, wsc[l][0], bwm[l], ebl, bebl)
                    self.hgrn_rec(ebl, bebl)
                self.linres_phase(self.OT, wsc[l][1], bwm[l], self.bot)
            self.run_hook()
        self.final_phase(outT, cfg.get("final_norm", True))
        c.final_wait()
        es.close()
        return nc


ROPE_THETA = 500000.0


def consts_build():
    cst = {}
    kl = np.arange(128)[:, None]
    ql = np.arange(512)[None, :]
    dm = np.zeros((128, 20, 512), np.float32)
    for mi in range(20):
        off = mi - 16
        dl = ql - kl - off * 128
        m = ((dl >= 0) & (dl <= 128)).astype(np.float32)
        m += ((dl >= 0) & (dl <= 512) & (dl % 4 == 0)).astype(np.float32)
        m += ((dl >= 0) & (dl <= 2048) & (dl % 16 == 0)).astype(np.float32)
        dm[:, mi, :] = m
    cst["dmask"] = dm.reshape(128, 20 * 512)
    mm = np.zeros((128, 4, 512), np.float32)
    for j in range(4):
        mm[:, j, :] = (j * 128 + kl <= ql).astype(np.float32)
    cst["mmask"] = mm.reshape(128, 4 * 512)
    oh = np.zeros((16, S), np.float32)
    for n in range(16):
        oh[n, n * 256:(n + 1) * 256] = 1.0
    cst["onehot"] = oh
    pn = np.zeros((128, 32, 16), np.float32)
    ow = np.zeros((128, 32, 16), np.float32)
    for qt in range(32):
        qb = qt // 2
        pn[:, qt, qb:] = -1e30
        ow[:, qt, qb] = 1.0
    cst["pastneg"] = pn.reshape(128, 512)
    cst["own"] = ow.reshape(128, 512)
    cst["ident"] = np.eye(128, dtype=np.float32)
    pmx = np.zeros((128, 128), np.float32)
    for f in range(128):
        j = f % 64
        if j < 8:
            pmx[f + 8, f] = 1.0
        elif j < 16:
            pmx[f - 8, f] = 1.0
    cst["ropeperm"] = pmx
    rm = np.ones((128, 512), np.float32)
    rm[:, 0::64] = 0.0
    cst["rmask"] = rm
    cst["ttc"] = np.ascontiguousarray(np.broadcast_to(np.arange(S, dtype=np.float32)[None, :], (128, S)))
    cst["tri"] = (np.arange(64)[:, None] <= np.arange(64)[None, :]).astype(np.float32)
    return cst


def pvec_build(inp):
    cols = []
    for l in range(4):
        cols.append(inp["norm_mix"][l].reshape(KC, 128).T)
    for l in range(4):
        cols.append(inp["norm_ffn"][l].reshape(KC, 128).T)
    cols.append(inp["norm_final"].reshape(KC, 128).T)
    f = np.arange(128) % 64
    inv = ROPE_THETA ** (-np.arange(0, 16, 2, dtype=np.float32) / 16.0)
    invf = np.where(f < 16, inv[f % 8], 0.0).astype(np.float32)
    sgn = np.where(f < 8, -1.0, np.where(f < 16, 1.0, 0.0)).astype(np.float32)
    cols.append(invf[:, None])
    cols.append(sgn[:, None])
    for l in range(4):
        cols.append(inp["hgrn_lower_bound"][l].reshape(KC, 128).T)
    cols.append(inp["hgrn_norm"][0].reshape(128, 1))
    p = np.arange(128)
    sa = np.where(p < 64, -1.0, 1.0).astype(np.float32)
    cols.append(sa[:, None])
    cols.append(-sa[:, None])
    cols.append(-np.ones((128, 1), np.float32))
    cols.append(inp["s5_d"][0].reshape(KC, 128).T)
    cols.append(inp["s5_b_glu"][0].reshape(2 * KC, 128).T)
    return np.ascontiguousarray(np.concatenate(cols, axis=1).astype(np.float32))


def make_inmaps(inp, cfg, cores, xs=None):
    pv = pvec_build(inp)
    cfg["npvec"] = pv.shape[1]
    cst = consts_build()
    stages = cfg["stages"]
    maps = []
    for b in cores:
        x = inp["x"][b] if xs is None else xs[b]
        m = {"xT": np.ascontiguousarray(x.T), "pvec": pv,
             "posb": np.ascontiguousarray(np.broadcast_to(inp["positions"][b][None, :], (128, S)).astype(np.int32))}
        m.update(cst)
        are, aim, ldt = inp["s5_a_re"][0], inp["s5_a_im"][0], inp["s5_log_dt"][0]
        m["s5p"] = np.ascontiguousarray(np.concatenate([
            np.concatenate([are.T, are.T], 0), np.concatenate([aim.T, aim.T], 0),
            np.broadcast_to(ldt[None, :], (128, 64))], axis=1).astype(np.float32))
        bre = inp["s5_b_re"][0].transpose(1, 0, 2).reshape(64, 1024)
        bim = inp["s5_b_im"][0].transpose(1, 0, 2).reshape(64, 1024)
        m["s5B"] = np.ascontiguousarray(np.concatenate([np.concatenate([bre, bim], 0), np.concatenate([bim, bre], 0)], axis=1))
        cre = inp["s5_c_re"][0].transpose(2, 0, 1).reshape(64, 1024)
        cim = inp["s5_c_im"][0].transpose(2, 0, 1).reshape(64, 1024)
        m["s5C"] = np.ascontiguousarray(np.concatenate([np.concatenate([cre, cim], 0), np.concatenate([cim, cre], 0)], axis=1))
        for (k, l) in stages:
            if k == "ffn":
                m["w_gu%d" % l] = np.ascontiguousarray(inp["ffn_w_gate_up"][l])
                m["w_d%d" % l] = np.ascontiguousarray(inp["ffn_w_down"][l])
            elif k == "mix" and l == 1:
                m["dil_qkv"] = np.ascontiguousarray(inp["dil_w_qkv"][0])
                m["dil_o"] = np.ascontiguousarray(inp["dil_w_o"][0])
            elif k == "mix" and l == 0:
                m["s5_wglu"] = np.ascontiguousarray(inp["s5_w_glu"][0])
            elif k == "mix" and l == 2:
                m["hgrn_in"] = np.ascontiguousarray(inp["hgrn_w_in"][0])
                m["hgrn_o"] = np.ascontiguousarray(inp["hgrn_w_o"][0])
            elif k == "mix" and l == 3:
                m["moba_qkv"] = np.ascontiguousarray(inp["moba_w_qkv"][0])
                m["moba_o"] = np.ascontiguousarray(inp["moba_w_o"][0])
        maps.append(m)
    return maps


FULL_STAGES = [("mix", 0), ("ffn", 0), ("mix", 1), ("ffn", 1), ("mix", 2), ("ffn", 2), ("mix", 3), ("ffn", 3)]


def kernel(**inp):
    inp = {k: np.asarray(v) for k, v in inp.items()}
    cfg = {"stages": FULL_STAGES, "final_norm": True}
    maps = make_inmaps(inp, cfg, range(4))
    prog = Prog(cfg)
    nc = prog.build()
    res = run_bass_kernel_spmd(nc, maps, core_ids=list(range(4)))
    out = np.stack([res.results[b]["outT"].T for b in range(4)], axis=0)
    return np.ascontiguousarray(out.astype(np.float32))
```

```python
import contextlib
import numpy as np
import concourse.bass as bass
import concourse.mybir as mybir
from concourse.bass_utils import run_bass_kernel_spmd

F32 = mybir.dt.float32
BF16 = mybir.dt.bfloat16
I32 = mybir.dt.int32
AF = mybir.ActivationFunctionType
ALU = mybir.AluOpType

S = 4096
D = 1024
DFF = 2816
KC = D // 128
EPS = 1e-6
SELF_SYNC = True
PI_LO = 3.1415925


class Buf:

    def __init__(self, name=""):
        self.w = None
        self.r = {}
        self.name = name


class Eng:
    def __init__(self, ctx, name, handle, self_sync):
        self.name = name
        self.h = handle
        self.sem = ctx.es.enter_context(ctx.nc.semaphore("s_" + name))
        self.cnt = 0
        self.seen = {}
        self.self_sync = self_sync


class DQ:
    def __init__(self, ctx, name, eng, k):
        self.name = name
        self.eng = eng
        self.k = k
        self.sems = [ctx.es.enter_context(ctx.nc.semaphore("q_%s%d" % (name, i))) for i in range(k)]
        self.cnts = [0] * k
        self.n = 0


class Ctx:
    def __init__(self, nc):
        self.nc = nc
        self.es = contextlib.ExitStack()
        self.eng = {}
        for name, h, ss in (("pe", nc.tensor, False), ("act", nc.scalar, SELF_SYNC), ("dve", nc.vector, SELF_SYNC),
                            ("pool", nc.gpsimd, SELF_SYNC), ("sp", nc.sync, False)):
            self.eng[name] = Eng(self, name, h, ss)
        self.dq = {"sp": DQ(self, "sp", self.eng["sp"], 8), "pool": DQ(self, "pool", self.eng["pool"], 4)}
        self.semtab = {}
        for e in self.eng.values():
            self.semtab[e.name] = e.sem
        for q in self.dq.values():
            for i, s in enumerate(q.sems):
                self.semtab[(q.name, i)] = s
        self.nwait = 0
        self.nins = 0

    def _need(self, reads, writes):
        need = {}
        for b in reads:
            if b.w is not None:
                k, v = b.w
                if need.get(k, 0) < v:
                    need[k] = v
        for b in writes:
            if b.w is not None:
                k, v = b.w
                if need.get(k, 0) < v:
                    need[k] = v
            for k, v in b.r.items():
                if need.get(k, 0) < v:
                    need[k] = v
        return need

    def _waits(self, E, need):
        for k, v in need.items():
            if k == E.name and not E.self_sync:
                continue
            if E.seen.get(k, 0) < v:
                E.h.wait_ge(self.semtab[k], v)
                E.seen[k] = v
                self.nwait += 1

    def op(self, eng, emit, reads=(), writes=()):
        E = self.eng[eng]
        self._waits(E, self._need(reads, writes))
        ins = emit(E.h)
        E.cnt += 1
        ins.then_inc(E.sem, 1)
        self.nins += 1
        for b in reads:
            b.r[E.name] = E.cnt
        for b in writes:
            b.w = (E.name, E.cnt)
            b.r = {}

    def dma(self, q, out, in_, reads=(), writes=()):
        Q = self.dq[q]
        E = Q.eng
        i = Q.n % Q.k
        need = self._need(reads, writes)
        key = (Q.name, i)
        if Q.cnts[i] > 0:
            need[key] = max(need.get(key, 0), 16 * Q.cnts[i])
        self._waits(E, need)
        E.h.dma_start(out=out, in_=in_).then_inc(Q.sems[i], 16)
        Q.cnts[i] += 1
        Q.n += 1
        self.nins += 1
        for b in reads:
            b.r[key] = 16 * Q.cnts[i]
        for b in writes:
            b.w = (key, 16 * Q.cnts[i])
            b.r = {}

    def barrier(self):
        tgt = {}
        for e in self.eng.values():
            if e.cnt:
                tgt[e.name] = e.cnt
        for q in self.dq.values():
            for i in range(q.k):
                if q.cnts[i]:
                    tgt[(q.name, i)] = 16 * q.cnts[i]
        for e in self.eng.values():
            if e.name == "pool":
                continue
            for k, v in tgt.items():
                if k == "pool" or (isinstance(k, tuple) and k[0] == "pool"):
                    continue
                if k == e.name:
                    continue
                if e.seen.get(k, 0) < v:
                    e.h.wait_ge(self.semtab[k], v)
                    e.seen[k] = v

    def final_wait(self):
        E = self.eng["sp"]
        Q = self.dq["sp"]
        for i in range(Q.k):
            if Q.cnts[i]:
                E.h.wait_ge(Q.sems[i], 16 * Q.cnts[i])


class Prog:
    def __init__(self, cfg):
        self.cfg = cfg
        nc = bass.Bass("TRN2", target_bir_lowering=False)
        self.nc = nc
        self.c = Ctx(nc)
        self.es = self.c.es
        self.ins = {}

    def din(self, name, shape, dt=F32):
        t = self.nc.dram_tensor(name, list(shape), dt, kind="ExternalInput").ap()
        self.ins[name] = t
        return t

    def dscr(self, name, shape, dt):
        return self.nc.dram_tensor(name, list(shape), dt, kind="Internal").ap()

    def sb(self, stack, name, shape, dt):
        self.sb_n = getattr(self, "sb_n", 0) + 1
        return stack.enter_context(self.nc.sbuf_tensor("%s_%d" % (name, self.sb_n), list(shape), dt))

    def cast_weight(self, src, dst, K, N, buf):
        c = self.c
        if not hasattr(buf, "cw_key"):
            sem = self.es.enter_context(self.nc.semaphore("cw%d" % len(c.semtab)))
            buf.cw_key = ("cw", len(c.semtab))
            c.semtab[buf.cw_key] = sem
            buf.cw_n = 0
        sem = c.semtab[buf.cw_key]
        step = 512
        for r0 in range(0, K, step):
            r1 = min(K, r0 + step)
            self.nc.gpsimd.dma_start(out=dst[r0:r1, :], in_=src[r0:r1, :]).then_inc(sem, 16)
            buf.cw_n += 1
        buf.w = (buf.cw_key, 16 * buf.cw_n)
        buf.r = {}

    def cast_weight_old(self, src, dst, K, N, buf):
        c = self.c
        CW = 2048
        for kc in range(K // 128):
            for c0 in range(0, N, CW):
                w = min(CW, N - c0)
                i = self.cast_i % 2
                self.cast_i += 1
                st, stb, bst, bstb = self.cast_st[i]
                c.dma("pool", st[:, 0:w], src[kc * 128:(kc + 1) * 128, c0:c0 + w], writes=[bst])
                c.op("pool", lambda h: h.tensor_copy(out=stb[:, 0:w], in_=st[:, 0:w]), reads=[bst], writes=[bstb])
                c.dma("pool", dst[kc * 128:(kc + 1) * 128, c0:c0 + w], stb[:, 0:w], reads=[bstb], writes=[buf])

    def rmsnorm_tile(self, hx, bhx, gcol, xn, bxn, ncols, tmp):
        c = self.c
        sq, bsq, rt, brt = tmp
        KH = sq.shape[1]
        for c0 in range(0, ncols, 512):
            ps, bps = self.ps_next()
            for k0 in range(0, KC, KH):
                c.op("act", lambda h: h.activation(out=sq[:, :, :], in_=hx[:, k0:k0 + KH, c0:c0 + 512], func=AF.Square),
                     reads=[bhx], writes=[bsq])
                for kk in range(KH):
                    kc = k0 + kk
                    c.op("pe", lambda h: h.matmul(ps[:, :], lhsT=self.ones_f[:, :], rhs=sq[:, kk, :],
                                                  start=(kc == 0), stop=(kc == KC - 1)),
                         reads=[bsq, self.bconst], writes=[bps])
            c.op("act", lambda h: h.activation(out=rt[:, :], in_=ps[:, :], func=AF.Sqrt, scale=1.0 / D,
                                               bias=self.eps_t[:, 0:1]),
                 reads=[bps, self.bconst], writes=[brt])
            c.op("dve", lambda h: h.reciprocal(out=rt[:, :], in_=rt[:, :]), reads=[brt], writes=[brt])
            for kc in range(KC):
                c.op("dve", lambda h: h.scalar_tensor_tensor(out=xn[:, kc, c0:c0 + 512], in0=hx[:, kc, c0:c0 + 512],
                                                             scalar=self.pvec[:, gcol + kc:gcol + kc + 1],
                                                             in1=rt[:, :], op0=ALU.mult, op1=ALU.mult),
                     reads=[bhx, brt, self.bconst], writes=[bxn])

    def run_hook(self):
        h = getattr(self, "hook", None)
        if h is not None:
            self.hook = None
            h()

    def ps_next(self, pool=None):
        if pool is None:
            i = self.ps_i % len(self.ps)
            self.ps_i += 1
            return self.ps[i], self.bps[i]
        lst = self.ps_pools[pool]
        i = lst[self.ps_pi.get(pool, 0) % len(lst)]
        self.ps_pi[pool] = self.ps_pi.get(pool, 0) + 1
        return self.ps[i], self.bps[i]

    def ffn_phase(self, layer):
        c = self.c
        nc = self.nc
        TS = 1024
        hT3 = self.hT.rearrange("(kc p) t -> p kc t", p=128)
        wgu = self.w_gu[layer]
        wd = self.w_d[layer]
        bw = self.bw_ffn[layer]
        with contextlib.ExitStack() as st:
            hx2 = [self.sb(st, "f_hx%d" % i, [128, KC, TS], F32) for i in range(2)]
            sq = self.sb(st, "f_sq", [128, 1, 512], F32)
            rt = self.sb(st, "f_rt", [128, 512], F32)
            xn2 = [self.sb(st, "f_xn%d" % i, [128, KC, TS], BF16) for i in range(2)]
            hf = self.sb(st, "f_hf", [128, DFF // 128, TS], BF16)
            wg = [self.sb(st, "f_wg%d" % i, [128, KC, 512], BF16) for i in range(2)]
            wu = [self.sb(st, "f_wu%d" % i, [128, KC, 512], BF16) for i in range(2)]
            wdn = [self.sb(st, "f_wd%d" % i, [128, DFF // 128, 256], BF16) for i in range(2)]
            sg = [self.sb(st, "f_sg%d" % i, [128, 512], BF16) for i in range(2)]
            bsq, brt = Buf(), Buf()
            bhx2, bxn2 = [Buf(), Buf()], [Buf(), Buf()]
            bhf = [Buf() for _ in range(DFF // 128)]
            bwg = [Buf(), Buf()]
            bwdn = [Buf(), Buf()]
            bsg = [Buf(), Buf()]
            wgu3 = wgu.rearrange("(kc p) n -> p kc n", p=128)
            wd3 = wd.rearrange("(kc p) n -> p kc n", p=128)
            NG = DFF // 512
            groups = [(g0, min(512, DFF - g0)) for g0 in range(0, DFF, 512)]
            def prep(ti_):
                t0_ = ti_ * TS
                u_ = ti_ % 2
                c.dma("sp", hx2[u_][:, :, :], hT3[:, :, t0_:t0_ + TS], reads=[self.bh[ti_]], writes=[bhx2[u_]])
                self.rmsnorm_tile(hx2[u_], bhx2[u_], self.col_nffn + layer * KC, xn2[u_], bxn2[u_], TS, (sq, bsq, rt, brt))

            prep(0)
            for t0 in range(0, S, TS):
                bh = self.bh[t0 // TS]
                u = (t0 // TS) % 2
                hx, bhx, xn, bxn = hx2[u], bhx2[u], xn2[u], bxn2[u]
                for gi, (g0, gw) in enumerate(groups):
                    b = gi % 2
                    c.dma("sp", wg[b][:, :, 0:gw], wgu3[:, :, g0:g0 + gw], reads=[bw], writes=[bwg[b]])
                    c.dma("sp", wu[b][:, :, 0:gw], wgu3[:, :, DFF + g0:DFF + g0 + gw], reads=[bw], writes=[bwg[b]])
                    for j in range(gw // 128):
                        fc = (g0 // 128) + j
                        for ct in range(TS // 512):
                            cs = slice(ct * 512, (ct + 1) * 512)
                            pg, bpg = self.ps_next()
                            pu, bpu = self.ps_next()
                            for kc in range(KC):
                                c.op("pe", lambda h: h.matmul(pg[:, :], lhsT=wg[b][:, kc, j * 128:(j + 1) * 128],
                                                              rhs=xn[:, kc, cs], start=(kc == 0), stop=(kc == KC - 1)),
                                     reads=[bwg[b], bxn], writes=[bpg])
                            for kc in range(KC):
                                c.op("pe", lambda h: h.matmul(pu[:, :], lhsT=wu[b][:, kc, j * 128:(j + 1) * 128],
                                                              rhs=xn[:, kc, cs], start=(kc == 0), stop=(kc == KC - 1)),
                                     reads=[bwg[b], bxn], writes=[bpu])
                            si = self.sg_i % 2
                            self.sg_i += 1
                            c.op("act", lambda h: h.activation(out=sg[si][:, :], in_=pg[:, :], func=AF.Silu),
                                 reads=[bpg], writes=[bsg[si]])
                            c.op("dve", lambda h: h.tensor_tensor(out=hf[:, fc, cs], in0=pu[:, :], in1=sg[si][:, :],
                                                                  op=ALU.mult),
                                 reads=[bpu, bsg[si]], writes=[bhf[fc]])
                if t0 + TS < S:
                    prep(t0 // TS + 1)
                self.run_hook()
                for dg in range(D // 256):
                    b = dg % 2
                    c.dma("sp", wdn[b][:, :, :], wd3[:, :, dg * 256:(dg + 1) * 256], reads=[bw], writes=[bwdn[b]])
                    for j in range(2):
                        dc = dg * 2 + j
                        for ct in range(TS // 512):
                            cs = slice(ct * 512, (ct + 1) * 512)
                            py, bpy = self.ps_next()
                            nk = DFF // 128
                            for kc in range(nk):
                                c.op("pe", lambda h: h.matmul(py[:, :], lhsT=wdn[b][:, kc, j * 128:(j + 1) * 128],
                                                              rhs=hf[:, kc, cs], start=(kc == 0), stop=(kc == nk - 1)),
                                     reads=[bwdn[b], bhf[kc]], writes=[bpy])
                            c.op("dve", lambda h: h.tensor_tensor(out=hx[:, dc, cs], in0=py[:, :], in1=hx[:, dc, cs],
                                                                  op=ALU.add),
                                 reads=[bpy, bhx], writes=[bhx])
                c.dma("sp", hT3[:, :, t0:t0 + TS], hx[:, :, :], reads=[bhx], writes=[bh])
            c.barrier()

    def load_cast(self, st, dst, bdst, src, shape_cols, tag):
        c = self.c
        P = dst.shape[0]
        stg = self.sb(st, "lc_" + tag, [P, 2048], F32)
        bs = Buf()
        for c0 in range(0, shape_cols, 2048):
            w = min(2048, shape_cols - c0)
            c.dma("sp", stg[:, 0:w], src[:, c0:c0 + w], writes=[bs])
            c.op("dve", lambda h: h.tensor_copy(out=dst[:, c0:c0 + w], in_=stg[:, 0:w]), reads=[bs], writes=[bdst])

    def range_reduce_sincos(self, st, x, bx, n, tag, want_cos=True):
        c = self.c
        TWO_PI = 2.0 * np.pi
        ki = self.sb(st, tag + "_ki", [128, n], I32)
        kf = self.sb(st, tag + "_kf", [128, n], F32)
        sn = self.sb(st, tag + "_sn", [128, n], F32)
        cs = self.sb(st, tag + "_cs", [128, n], F32) if want_cos else None
        bki, bkf, bsn, bcs = Buf(), Buf(), Buf(), Buf()
        self._rr(x, bx, n, ki, bki, kf, bkf, sn, bsn, cs, bcs)
        return sn, bsn, cs, bcs

    def _rr(self, x, bx, n, ki, bki, kf, bkf, sn, bsn, cs, bcs):
        c = self.c
        TWO_PI = 2.0 * np.pi
        c.op("act", lambda h: h.activation(out=ki[:, 0:n], in_=x[:, 0:n], func=AF.Identity, scale=1.0 / TWO_PI),
             reads=[bx], writes=[bki])
        c.op("dve", lambda h: h.scalar_tensor_tensor(out=sn[:, 0:n], in0=ki[:, 0:n], scalar=-TWO_PI, in1=x[:, 0:n],
                                                     op0=ALU.mult, op1=ALU.add), reads=[bki, bx], writes=[bsn])
        c.op("dve", lambda h: h.tensor_scalar(out=sn[:, 0:n], in0=sn[:, 0:n], scalar1=-PI_LO, scalar2=PI_LO,
                                              op0=ALU.max, op1=ALU.min), reads=[bsn], writes=[bsn])
        if cs is not None:
            c.op("dve", lambda h: h.scalar_tensor_tensor(out=cs[:, 0:n], in0=sn[:, 0:n], scalar=-1.0, in1=sn[:, 0:n],
                                                         op0=ALU.mult, op1=ALU.max), reads=[bsn], writes=[bcs])
        c.op("act", lambda h: h.activation(out=sn[:, 0:n], in_=sn[:, 0:n], func=AF.Sin), reads=[bsn], writes=[bsn])
        if cs is not None:
            c.op("act", lambda h: h.activation(out=cs[:, 0:n], in_=cs[:, 0:n], func=AF.Sin, scale=-1.0,
                                               bias=self.halfpi_t[:, 0:1]), reads=[bcs, self.bconst], writes=[bcs])

    def qkv_phase(self, layer, w_s, wp_s, bw, bwp):
        c = self.c
        hT3 = self.hT.rearrange("(kc p) t -> p kc t", p=128)
        QT3 = self.QT.rearrange("(kc p) t -> p kc t", p=128)
        KT3 = self.KT.rearrange("(kc p) t -> p kc t", p=128)
        w3 = w_s.rearrange("(kc p) n -> p kc n", p=128)
        self.ps_pools = {"a": [0, 1, 2, 3, 4, 5, 6, 7]}
        self.ps_pi = {}
        with contextlib.ExitStack() as st:
            SF = self.sb(st, "q_SF", [128, S], F32)
            CF = self.sb(st, "q_CF", [128, S], F32)
            bSF, bCF = Buf(), Buf()
            with contextlib.ExitStack() as st2:
                posi = self.sb(st2, "q_posi", [128, S], I32)
                x = self.sb(st2, "q_x", [128, S], F32)
                bposi, bx = Buf(), Buf()
                c.dma("sp", posi[:, :], self.posb[:, :], writes=[bposi])
                c.op("dve", lambda h: h.tensor_copy(out=x[:, :], in_=posi[:, :]), reads=[bposi], writes=[bx])
                c.op("dve", lambda h: h.tensor_scalar(out=x[:, :], in0=x[:, :], scalar1=self.pvec[:, self.col_invf:self.col_invf + 1],
                                                      scalar2=None, op0=ALU.mult), reads=[bx, self.bconst], writes=[bx])
                ki = posi
                kf = self.sb(st2, "q_kf", [128, S], F32)
                self._rr(x, bx, S, ki, bposi, kf, Buf(), SF, bSF, CF, bCF)
                c.op("dve", lambda h: h.tensor_scalar(out=SF[:, :], in0=SF[:, :],
                                                      scalar1=self.pvec[:, self.col_sgnrow:self.col_sgnrow + 1],
                                                      scalar2=None, op0=ALU.mult), reads=[bSF, self.bconst], writes=[bSF])
                c.barrier()
            wq = self.sb(st, "q_w", [128, KC, 3 * D], BF16)
            pm = self.sb(st, "q_pm", [128, 128], BF16)
            ab = [self.sb(st, "q_ab%d" % i, [128, 512], BF16) for i in range(2)]
            bab = [Buf(), Buf()]
            bpmc = Buf()
            with contextlib.ExitStack() as st3:
                self.load_cast(st3, pm, bpmc, self.ropeperm, 128, "pm")
                c.barrier()
            bwq = Buf()
            for i in range(3):
                c.dma("sp", wq[:, :, i * D:(i + 1) * D], w3[:, :, i * D:(i + 1) * D], reads=[bw], writes=[bwq])
            hx = self.sb(st, "q_hx", [128, KC, 512], F32)
            sq = self.sb(st, "q_sq", [128, 4, 512], F32)
            rt = self.sb(st, "q_rt", [128, 512], F32)
            xn = self.sb(st, "q_xn", [128, KC, 512], BF16)
            qo = [self.sb(st, "q_qo%d" % i, [128, KC, 512], BF16) for i in range(2)]
            tu = [self.sb(st, "q_tu%d" % i, [128, 512], F32) for i in range(2)]
            tt_ = [self.sb(st, "q_tt%d" % i, [128, 512], F32) for i in range(2)]
            vt = [self.sb(st, "q_vt%d" % i, [128, 16, 128], BF16) for i in range(2)]
            bhx, bsq, brt, bxn = Buf(), Buf(), Buf(), Buf()
            bqo = [Buf(), Buf()]
            btu = [Buf(), Buf()]
            btt = [Buf(), Buf()]
            bvt = [Buf(), Buf()]
            for i in range(2):
                c.op("dve", lambda h: h.memset(vt[i][:, :, 64:128], 1.0), writes=[bvt[i]])
            ei = 0
            vi = 0
            for tile in range(S // 512):
                cs = slice(tile * 512, (tile + 1) * 512)
                bh = self.bh[tile // 2]
                if tile == 2:
                    self.run_hook()
                c.dma("sp", hx[:, :, :], hT3[:, :, cs], reads=[bh], writes=[bhx])
                self.rmsnorm_tile(hx, bhx, self.col_nmix + layer * KC, xn, bxn, 512, (sq, bsq, rt, brt))
                for which in range(2):
                    for oc in range(KC):
                        pa, bpa = self.ps_next("a")
                        pb, bpb = self.ps_next("a")
                        col = which * D + oc * 128
                        for kc in range(KC):
                            c.op("pe", lambda h: h.matmul(pa[:, :], lhsT=wq[:, kc, col:col + 128], rhs=xn[:, kc, :],
                                                          start=(kc == 0), stop=(kc == KC - 1)),
                                 reads=[bwq, bxn], writes=[bpa])
                        e = ei % 2
                        ei += 1
                        c.op("act", lambda h: h.activation(out=ab[e][:, :], in_=pa[:, :], func=AF.Identity),
                             reads=[bpa], writes=[bab[e]])
                        c.op("pe", lambda h: h.matmul(pb[:, :], lhsT=pm[:, :], rhs=ab[e][:, :], start=True, stop=True),
                             reads=[bab[e], bpmc], writes=[bpb])
                        c.op("dve", lambda h: h.tensor_tensor(out=tu[e][:, :], in0=pa[:, :], in1=CF[:, cs], op=ALU.mult),
                             reads=[bpa, bCF, bab[e]], writes=[btu[e]])
                        c.op("dve", lambda h: h.tensor_tensor(out=tt_[e][:, :], in0=pb[:, :], in1=SF[:, cs], op=ALU.mult),
                             reads=[bpb, bSF], writes=[btt[e]])
                        c.op("dve", lambda h: h.tensor_tensor(out=qo[which][:, oc, :], in0=tu[e][:, :], in1=tt_[e][:, :],
                                                               op=ALU.add),
                             reads=[btu[e], btt[e]], writes=[bqo[which]])
                    dst = QT3 if which == 0 else KT3
                    c.dma("sp", dst[:, :, cs], qo[which][:, :, :], reads=[bqo[which]], writes=[self.bqkv])
                for sub in range(4):
                    v = vi % 2
                    vi += 1
                    for half in range(2):
                        pv, bpv = self.ps_next("a")
                        for kc in range(KC):
                            c.op("pe", lambda h: h.matmul(pv[:, :], lhsT=xn[:, kc, sub * 128:(sub + 1) * 128],
                                                          rhs=wq[:, kc, 2 * D + half * 512:2 * D + (half + 1) * 512],
                                                          start=(kc == 0), stop=(kc == KC - 1)),
                                 reads=[bwq, bxn], writes=[bpv])
                        c.op("act", lambda h: h.activation(out=vt[v][:, half * 8:(half + 1) * 8, 0:64],
                                                           in_=pv[:, :].rearrange("p (a b) -> p a b", b=64),
                                                           func=AF.Identity),
                             reads=[bpv], writes=[bvt[v]])
                    c.dma("sp", self.V1s[:, :, tile * 4 + sub, :].rearrange("h p e -> p h e"), vt[v][:, :, :],
                          reads=[bvt[v]], writes=[self.bqkv])
            c.barrier()

    def attn_phase(self, kind):
        c = self.c
        moba = kind == "moba"
        KR = 80 if moba else 64
        self.ps_pools = {"s": [0, 1, 2, 3], "o": [4, 5], "g": [6, 7]}
        self.ps_pi = {}
        with contextlib.ExitStack() as st:
            nm = 4 if moba else 20
            mk = self.sb(st, "a_mk", [128, nm * 512], BF16)
            bmk = Buf()
            with contextlib.ExitStack() as st2:
                self.load_cast(st2, mk, bmk, (self.mmask if moba else self.dmask), nm * 512, "mk")
                c.barrier()
            kaug = [self.sb(st, "a_k%d" % i, [KR, S], BF16) for i in range(2)]
            qaug = [self.sb(st, "a_q%d" % i, [KR, S], BF16) for i in range(2)]
            v1 = [self.sb(st, "a_v%d" % i, [128, 32, 128], BF16) for i in range(2)]
            ot = [self.sb(st, "a_o%d" % i, [64, S], BF16) for i in range(2)]
            pt = [self.sb(st, "a_p%d" % i, [128, 512], BF16) for i in range(8)]
            rd = [self.sb(st, "a_rd%d" % i, [64, 512], F32) for i in range(2)]
            bk = [Buf(), Buf()]
            bq = [Buf(), Buf()]
            bv = [Buf(), Buf()]
            bot = [Buf(), Buf()]
            bpt = [Buf() for _ in range(8)]
            brd = [Buf(), Buf()]
            if moba:
                with contextlib.ExitStack() as st2:
                    oh = self.sb(st2, "a_oh", [16, S], F32)
                    boh = Buf()
                    c.dma("sp", oh[:, :], self.onehot[:, :], writes=[boh])
                    for i in range(2):
                        c.op("dve", lambda h: h.tensor_copy(out=kaug[i][64:80, :], in_=oh[:, :]), reads=[boh], writes=[bk[i]])
                    c.barrier()
                pastneg = self.sb(st, "a_pn", [128, 512], F32)
                own = self.sb(st, "a_own", [128, 512], F32)
                ident = self.sb(st, "a_id", [128, 128], F32)
                bcn = Buf()
                c.dma("sp", pastneg[:, :], self.pastneg[:, :], writes=[bcn])
                c.dma("sp", own[:, :], self.own[:, :], writes=[bcn])
                c.dma("sp", ident[:, :], self.ident[:, :], writes=[bcn])
                km = self.sb(st, "a_km", [64, 16], F32)
                kmb = self.sb(st, "a_kmb", [64, 16], BF16)
                gm = self.sb(st, "a_gm", [128, 512], F32)
                m8 = self.sb(st, "a_m8", [128, 32, 8], F32)
                thr = self.sb(st, "a_thr", [128, 32], F32)
                sel = self.sb(st, "a_sel", [128, 512], F32)
                bkm, bkmb, bgm, bm8, bthr, bsel = Buf(), Buf(), Buf(), Buf(), Buf(), Buf()
            ri = 0

            def head_prep(hd):
                b = hd % 2
                c.dma("sp", kaug[b][0:64, :], self.KT[hd * 64:(hd + 1) * 64, :], reads=[self.bqkv], writes=[bk[b]])
                c.dma("sp", qaug[b][0:64, :], self.QT[hd * 64:(hd + 1) * 64, :], reads=[self.bqkv], writes=[bq[b]])
                c.dma("sp", v1[b][:, :, :], self.V1s[hd, :, :, :], reads=[self.bqkv], writes=[bv[b]])
                if moba:
                    c.op("dve", lambda h: h.tensor_reduce(out=km[:, :], in_=kaug[b][0:64, :].rearrange("p (n k) -> p n k", k=256),
                                                          axis=mybir.AxisListType.X, op=ALU.add),
                         reads=[bk[b]], writes=[bkm])
                    c.op("dve", lambda h: h.tensor_scalar(out=kmb[:, :], in0=km[:, :], scalar1=1.0 / 256, scalar2=None,
                                                          op0=ALU.mult), reads=[bkm], writes=[bkmb])
                    pg, bpg = self.ps_next("g")
                    for qt_ in range(32):
                        c.op("pe", lambda h: h.matmul(pg[:, qt_ * 16:(qt_ + 1) * 16], lhsT=qaug[b][0:64, qt_ * 128:(qt_ + 1) * 128],
                                                      rhs=kmb[:, :], start=True, stop=True),
                             reads=[bq[b], bkmb], writes=[bpg])
                    c.op("dve", lambda h: h.tensor_tensor(out=gm[:, :], in0=pg[:, :], in1=pastneg[:, :], op=ALU.add),
                         reads=[bpg, bcn], writes=[bgm])
                    for qt_ in range(32):
                        c.op("dve", lambda h: h.max(out=m8[:, qt_, :], in_=gm[:, qt_ * 16:(qt_ + 1) * 16]),
                             reads=[bgm], writes=[bm8])
                    c.op("dve", lambda h: h.tensor_scalar(out=thr[:, :], in0=m8[:, :, 2], scalar1=-1e29, scalar2=None,
                                                          op0=ALU.max), reads=[bm8], writes=[bthr])
                    c.op("dve", lambda h: h.tensor_tensor(out=sel[:, :].rearrange("p (a b) -> p a b", b=16),
                                                          in0=gm[:, :].rearrange("p (a b) -> p a b", b=16),
                                                          in1=thr[:, :].unsqueeze(2).to_broadcast([128, 32, 16]),
                                                          op=ALU.is_ge), reads=[bgm, bthr], writes=[bsel])
                    c.op("dve", lambda h: h.tensor_tensor(out=sel[:, :], in0=sel[:, :], in1=own[:, :], op=ALU.add),
                         reads=[bsel, bcn], writes=[bsel])
                    c.op("dve", lambda h: h.tensor_scalar(out=sel[:, :], in0=sel[:, :], scalar1=-1.0, scalar2=32768.0,
                                                          op0=ALU.add, op1=ALU.mult), reads=[bsel], writes=[bsel])
                    for g4 in range(8):
                        pt_, bpt_ = self.ps_next("g")
                        for j in range(4):
                            qt_ = g4 * 4 + j
                            c.op("pe", lambda h: h.transpose(pt_[0:16, j * 128:(j + 1) * 128], sel[:, qt_ * 16:(qt_ + 1) * 16],
                                                             ident[:, :]),
                                 reads=[bsel, bcn], writes=[bpt_])
                        c.op("act", lambda h: h.activation(out=qaug[b][64:80, g4 * 512:(g4 + 1) * 512], in_=pt_[0:16, :],
                                                           func=AF.Identity), reads=[bpt_], writes=[bq[b]])

            items = []
            for hd in range(16):
                for qt in range(8):
                    k0 = 0 if moba else max(0, 4 * qt - 16)
                    kts = list(range(k0, 4 * qt + 4))
                    for idx, kt in enumerate(kts):
                        items.append((hd, qt, idx, kt, len(kts)))
            LA = 5
            state = {}

            def front(i, it):
                hd, qt, idx, kt, n = it
                b = hd % 2
                pss, bpss = self.ps_next("s")
                c.op("pe", lambda h: h.matmul(pss[:, :], lhsT=kaug[b][0:KR, kt * 128:(kt + 1) * 128],
                                              rhs=qaug[b][0:KR, qt * 512:(qt + 1) * 512], start=True, stop=True),
                     reads=[bk[b], bq[b]], writes=[bpss])
                p = i % 8
                c.op("act", lambda h: h.activation(out=pt[p][:, :], in_=pss[:, :], func=AF.Exp, scale=0.125),
                     reads=[bpss], writes=[bpt[p]])
                off = kt - 4 * qt
                m = None
                if moba:
                    if off >= 0:
                        m = off
                else:
                    m = off + 16
                if m is not None:
                    eng = "dve"
                    c.op(eng, lambda h: h.tensor_tensor(out=pt[p][:, :], in0=pt[p][:, :], in1=mk[:, m * 512:(m + 1) * 512],
                                                        op=ALU.mult), reads=[bpt[p], bmk], writes=[bpt[p]])

            def back(i, it):
                nonlocal ri
                hd, qt, idx, kt, n = it
                b = hd % 2
                p = i % 8
                if qt == 0 and idx == 0 and hd + 1 < 16:
                    head_prep(hd + 1)
                if idx == 0:
                    state["po"] = self.ps_next("o")
                po, bpo = state["po"]
                c.op("pe", lambda h: h.matmul(po[:, :], lhsT=v1[b][:, kt, :], rhs=pt[p][:, :],
                                              start=(idx == 0), stop=(idx == n - 1)),
                     reads=[bv[b], bpt[p]], writes=[bpo])
                if idx == n - 1:
                    r = ri % 2
                    ri += 1
                    c.op("dve", lambda h: h.reciprocal(out=rd[r][:, :], in_=po[64:128, :]), reads=[bpo], writes=[brd[r]])
                    c.op("dve", lambda h: h.tensor_tensor(out=ot[b][:, qt * 512:(qt + 1) * 512], in0=po[0:64, :], in1=rd[r][:, :],
                                                          op=ALU.mult), reads=[bpo, brd[r]], writes=[bot[b]])
                    if qt == 7:
                        c.dma("sp", self.OT[hd * 64:(hd + 1) * 64, :], ot[b][:, :], reads=[bot[b]], writes=[self.bot])

            head_prep(0)
            for i in range(len(items) + LA):
                if i < len(items):
                    front(i, items[i])
                if i >= LA:
                    back(i - LA, items[i - LA])
            c.barrier()

    def linres_phase(self, inT, w_s, bw, bin_):
        c = self.c
        hT3 = self.hT.rearrange("(kc p) t -> p kc t", p=128)
        in3 = inT.rearrange("(kc p) t -> p kc t", p=128)
        w3 = w_s.rearrange("(kc p) n -> p kc n", p=128)
        with contextlib.ExitStack() as st:
            w = self.sb(st, "l_w", [128, KC, D], BF16)
            bwl = Buf()
            c.dma("sp", w[:, :, :], w3[:, :, :], reads=[bw], writes=[bwl])
            a = [self.sb(st, "l_a%d" % i, [128, KC, 512], BF16) for i in range(2)]
            hx = [self.sb(st, "l_h%d" % i, [128, KC, 512], F32) for i in range(2)]
            ba = [Buf(), Buf()]
            bhx = [Buf(), Buf()]
            for tile in range(S // 512):
                cs = slice(tile * 512, (tile + 1) * 512)
                b = tile % 2
                bh = self.bh[tile // 2]
                c.dma("sp", a[b][:, :, :], in3[:, :, cs], reads=[bin_], writes=[ba[b]])
                c.dma("sp", hx[b][:, :, :], hT3[:, :, cs], reads=[bh], writes=[bhx[b]])
                for dc in range(KC):
                    ps, bps = self.ps_next()
                    for kc in range(KC):
                        c.op("pe", lambda h: h.matmul(ps[:, :], lhsT=w[:, kc, dc * 128:(dc + 1) * 128], rhs=a[b][:, kc, :],
                                                      start=(kc == 0), stop=(kc == KC - 1)),
                             reads=[bwl, ba[b]], writes=[bps])
                    c.op("dve", lambda h: h.tensor_tensor(out=hx[b][:, dc, :], in0=ps[:, :], in1=hx[b][:, dc, :], op=ALU.add),
                         reads=[bps, bhx[b]], writes=[bhx[b]])
                c.dma("sp", hT3[:, :, cs], hx[b][:, :, :], reads=[bhx[b]], writes=[bh])
            c.barrier()

    def cast_weight_qk_perm(self, src, dst, buf):
        c = self.c
        for kc in range(KC):
            for c0 in range(0, 2 * D, 512):
                i = self.cast_i % 2
                self.cast_i += 1
                st, stb, bst, bstb = self.cast_st[i]
                c.dma("pool", st[:, :], src[kc * 128:(kc + 1) * 128, c0:c0 + 512], writes=[bst])
                st3 = st[:, :].rearrange("p (a b) -> p a b", b=64)
                sb3 = stb[:, :].rearrange("p (a b) -> p a b", b=64)
                c.op("pool", lambda h: h.tensor_copy(out=stb[:, :], in_=st[:, :]), reads=[bst], writes=[bstb])
                c.op("pool", lambda h: h.tensor_copy(out=sb3[:, :, 0:8], in_=st3[:, :, 8:16]), reads=[bst], writes=[bstb])
                c.op("pool", lambda h: h.tensor_copy(out=sb3[:, :, 8:16], in_=st3[:, :, 0:8]), reads=[bst], writes=[bstb])
                c.dma("pool", dst[kc * 128:(kc + 1) * 128, c0:c0 + 512], stb[:, :], reads=[bstb], writes=[buf])

    def hgrn_prep(self, layer, w_s, bw, ebl, bebl):
        c = self.c
        hT3 = self.hT.rearrange("(kc p) t -> p kc t", p=128)
        QT3 = self.QT.rearrange("(kc p) t -> p kc t", p=128)
        KT3 = self.KT.rearrange("(kc p) t -> p kc t", p=128)
        GT3 = self.GT.rearrange("(kc p) t -> p kc t", p=128)
        w3 = w_s.rearrange("(kc p) n -> p kc n", p=128)
        self.ps_pools = {"a": [0, 1, 2, 3, 4, 5], "t": [6, 7]}
        self.ps_pi = {}
        with contextlib.ExitStack() as st:
            w = self.sb(st, "h_w", [128, KC, 4 * D], BF16)
            bww = Buf()
            for i in range(4):
                c.dma("sp", w[:, :, i * D:(i + 1) * D], w3[:, :, i * D:(i + 1) * D], reads=[bw], writes=[bww])
            ex = self.sb(st, "h_ex", [128, 4 * KC], F32)
            ssum = self.sb(st, "h_ss", [128, KC], F32)
            lb = self.sb(st, "h_lb", [128, KC], F32)
            oml = self.sb(st, "h_oml", [128, KC], F32)
            blb = Buf()
            cl = self.col_lb
            c.op("act", lambda h: h.activation(out=ex[:, :], in_=self.pvec[:, cl:cl + 4 * KC], func=AF.Exp),
                 reads=[self.bconst], writes=[blb])
            c.op("dve", lambda h: h.tensor_tensor(out=ssum[:, :], in0=ex[:, 0:KC], in1=ex[:, KC:2 * KC], op=ALU.add),
                 reads=[blb], writes=[blb])
            c.op("dve", lambda h: h.tensor_tensor(out=ssum[:, :], in0=ssum[:, :], in1=ex[:, 2 * KC:3 * KC], op=ALU.add),
                 reads=[blb], writes=[blb])
            c.op("dve", lambda h: h.tensor_tensor(out=ssum[:, :], in0=ssum[:, :], in1=ex[:, 3 * KC:4 * KC], op=ALU.add),
                 reads=[blb], writes=[blb])
            c.op("dve", lambda h: h.reciprocal(out=ssum[:, :], in_=ssum[:, :]), reads=[blb], writes=[blb])
            c.op("dve", lambda h: h.tensor_copy(out=lb[:, :], in_=ex[:, KC:2 * KC]), reads=[blb], writes=[blb])
            for l in range(2, layer + 1):
                c.op("dve", lambda h: h.tensor_tensor(out=lb[:, :], in0=lb[:, :], in1=ex[:, l * KC:(l + 1) * KC], op=ALU.add),
                     reads=[blb], writes=[blb])
            c.op("dve", lambda h: h.tensor_tensor(out=lb[:, :], in0=lb[:, :], in1=ssum[:, :], op=ALU.mult),
                 reads=[blb], writes=[blb])
            c.op("dve", lambda h: h.tensor_scalar(out=oml[:, :], in0=lb[:, :], scalar1=-1.0, scalar2=1.0, op0=ALU.mult, op1=ALU.add),
                 reads=[blb], writes=[blb])
            rmask = self.sb(st, "h_rm", [128, 512], F32)
            identb = self.sb(st, "h_idb", [128, 128], BF16)
            identf = self.sb(st, "h_idf", [128, 128], F32)
            bcn = Buf()
            c.dma("sp", rmask[:, :], self.rmask[:, :], writes=[bcn])
            c.dma("sp", identf[:, :], self.ident[:, :], writes=[bcn])
            c.op("dve", lambda h: h.tensor_copy(out=identb[:, :], in_=identf[:, :]), reads=[bcn], writes=[bcn])
            hx = self.sb(st, "h_hx", [128, KC, 512], F32)
            sq = self.sb(st, "h_sq", [128, 2, 512], F32)
            rt = self.sb(st, "h_rt", [128, 512], F32)
            xn = self.sb(st, "h_xn", [128, KC, 512], BF16)
            bhx, bsq, brt, bxn = Buf(), Buf(), Buf(), Buf()
            names = ["qs", "th", "fg", "b", "eb", "ebn", "kk"]
            TT = [{n: self.sb(st, "h_t_%s%d" % (n, i), [128, 512], F32) for n in names} for i in range(2)]
            BB_ = [{n: Buf() for n in names} for i in range(2)]
            A_ = self.sb(st, "h_A", [128, KC], F32)
            nA_ = self.sb(st, "h_nA", [128, KC], F32)
            B_ = self.sb(st, "h_B", [128, KC], F32)
            c.op("dve", lambda h: h.tensor_scalar(out=A_[:, :], in0=oml[:, :], scalar1=0.5, scalar2=None, op0=ALU.mult), reads=[blb], writes=[blb])
            c.op("dve", lambda h: h.tensor_scalar(out=nA_[:, :], in0=oml[:, :], scalar1=-0.5, scalar2=None, op0=ALU.mult), reads=[blb], writes=[blb])
            c.op("dve", lambda h: h.tensor_tensor(out=B_[:, :], in0=lb[:, :], in1=A_[:, :], op=ALU.add), reads=[blb], writes=[blb])
            qo = self.sb(st, "h_qo", [128, KC, 512], BF16)
            ko = self.sb(st, "h_ko", [128, KC, 512], BF16)
            go = self.sb(st, "h_go", [128, KC, 512], BF16)
            ktm = self.sb(st, "h_ktm", [128, 8, 4, 128], BF16)
            itm = [self.sb(st, "h_itm%d" % i, [128, 8, 128], BF16) for i in range(2)]
            bqo, bko, bgo, bktm = Buf(), Buf(), Buf(), Buf()
            bitm = [Buf(), Buf()]
            ii = 0
            for tile in range(S // 512):
                cs = slice(tile * 512, (tile + 1) * 512)
                bh = self.bh[tile // 2]
                if tile == 2:
                    self.run_hook()
                c.dma("sp", hx[:, :, :], hT3[:, :, cs], reads=[bh], writes=[bhx])
                self.rmsnorm_tile(hx, bhx, self.col_nmix + layer * KC, xn, bxn, 512, (sq, bsq, rt, brt))
                def proj(hp_):
                    PS_ = {}
                    for i, hh in enumerate((2 * hp_, 2 * hp_ + 1)):
                        PS_[i] = [self.ps_next("a") for _ in range(3)]
                        for (pp, bpp), off in zip(PS_[i], (0, D, 3 * D)):
                            col = off + hh * 128
                            for kc in range(KC):
                                c.op("pe", lambda h: h.matmul(pp[:, :], lhsT=w[:, kc, col:col + 128], rhs=xn[:, kc, :],
                                                              start=(kc == 0), stop=(kc == KC - 1)),
                                     reads=[bww, bxn], writes=[bpp])
                    return PS_

                PSn = proj(0)
                for hp in range(4):
                    hs = (2 * hp, 2 * hp + 1)
                    PS = PSn
                    for i, hh in enumerate(hs):
                        T, B = TT[i], BB_[i]
                        (pq, bpq), (pf, bpf), (pg, bpg) = PS[i]
                        c.op("act", lambda h: h.activation(out=T["qs"][:, :], in_=pq[:, :], func=AF.Silu), reads=[bpq], writes=[B["qs"]])
                        c.op("act", lambda h: h.activation(out=go[:, hh, :], in_=pg[:, :], func=AF.Silu), reads=[bpg], writes=[bgo])
                        c.op("act", lambda h: h.activation(out=T["th"][:, :], in_=pf[:, :], func=AF.Tanh, scale=0.5), reads=[bpf], writes=[B["th"]])
                    if hp + 1 < 4:
                        PSn = proj(hp + 1)
                    for i, hh in enumerate(hs):
                        T, B = TT[i], BB_[i]
                        c.op("dve", lambda h: h.tensor_scalar(out=T["fg"][:, :], in0=T["th"][:, :], scalar1=A_[:, hh:hh + 1],
                                                              scalar2=B_[:, hh:hh + 1], op0=ALU.mult, op1=ALU.add),
                             reads=[B["th"], blb], writes=[B["fg"]])
                        c.op("dve", lambda h: h.tensor_scalar(out=T["kk"][:, :], in0=T["th"][:, :], scalar1=nA_[:, hh:hh + 1],
                                                              scalar2=A_[:, hh:hh + 1], op0=ALU.mult, op1=ALU.add),
                             reads=[B["th"], blb], writes=[B["kk"]])
                    for i, hh in enumerate(hs):
                        T, B = TT[i], BB_[i]
                        c.op("act", lambda h: h.activation(out=T["fg"][:, :], in_=T["fg"][:, :], func=AF.Ln), reads=[B["fg"]], writes=[B["fg"]])
                    for i, hh in enumerate(hs):
                        T, B = TT[i], BB_[i]
                        c.op("dve", lambda h: h.tensor_tensor_scan(out=T["b"][:, :], data0=rmask[:, :], data1=T["fg"][:, :], initial=0.0,
                                                                   op0=ALU.mult, op1=ALU.add), reads=[B["fg"], bcn], writes=[B["b"]])
                    for i, hh in enumerate(hs):
                        T, B = TT[i], BB_[i]
                        c.op("act", lambda h: h.activation(out=T["eb"][:, :], in_=T["b"][:, :], func=AF.Exp), reads=[B["b"]], writes=[B["eb"]])
                        c.op("act", lambda h: h.activation(out=T["ebn"][:, :], in_=T["b"][:, :], func=AF.Exp, scale=-1.0),
                             reads=[B["b"]], writes=[B["ebn"]])
                    for i, hh in enumerate(hs):
                        T, B = TT[i], BB_[i]
                        c.op("dve", lambda h: h.tensor_tensor(out=qo[:, hh, :], in0=T["qs"][:, :], in1=T["eb"][:, :], op=ALU.mult),
                             reads=[B["qs"], B["eb"]], writes=[bqo])
                        c.op("dve", lambda h: h.tensor_tensor(out=ko[:, hh, :], in0=T["kk"][:, :], in1=T["ebn"][:, :], op=ALU.mult),
                             reads=[B["kk"], B["ebn"]], writes=[bko])
                        c.op("dve", lambda h: h.tensor_copy(out=ebl[:, hh, tile * 8:(tile + 1) * 8],
                                                            in_=T["eb"][:, :].rearrange("p (a b) -> p a b", b=64)[:, :, 63]),
                             reads=[B["eb"]], writes=[bebl])
                        ptt, bptt = self.ps_next("t")
                        ptb = ptt[:, 0:256].bitcast(BF16)
                        for sub in range(4):
                            c.op("pe", lambda h: h.transpose(ptb[:, sub * 128:(sub + 1) * 128], ko[:, hh, sub * 128:(sub + 1) * 128],
                                                             identb[:, :]), reads=[bko, bcn], writes=[bptt])
                        c.op("act", lambda h: h.activation(out=ktm[:, hh, :, :], in_=ptb.rearrange("p (a b) -> p a b", b=128),
                                                           func=AF.Identity), reads=[bptt], writes=[bktm])
                c.dma("sp", QT3[:, :, cs], qo[:, :, :], reads=[bqo], writes=[self.bqkv])
                c.dma("sp", KT3[:, :, cs], ko[:, :, :], reads=[bko], writes=[self.bqkv])
                c.dma("sp", GT3[:, :, cs], go[:, :, :], reads=[bgo], writes=[self.bqkv])
                c.dma("sp", self.V1s[0:8, :, tile * 4:(tile + 1) * 4, :].rearrange("h p s e -> p h s e"), ktm[:, :, :, :],
                      reads=[bktm], writes=[self.bqkv])
                for sub in range(4):
                    v = ii % 2
                    ii += 1
                    for half in range(2):
                        pv, bpv = self.ps_next("a")
                        for kc in range(KC):
                            c.op("pe", lambda h: h.matmul(pv[:, :], lhsT=xn[:, kc, sub * 128:(sub + 1) * 128],
                                                          rhs=w[:, kc, 2 * D + half * 512:2 * D + (half + 1) * 512],
                                                          start=(kc == 0), stop=(kc == KC - 1)),
                                 reads=[bww, bxn], writes=[bpv])
                        c.op("act", lambda h: h.activation(out=itm[v][:, half * 4:(half + 1) * 4, :],
                                                           in_=pv[:, :].rearrange("p (a b) -> p a b", b=128), func=AF.Identity),
                             reads=[bpv], writes=[bitm[v]])
                    c.dma("sp", self.V1s[8:16, :, tile * 4 + sub, :].rearrange("h p e -> p h e"), itm[v][:, :, :],
                          reads=[bitm[v]], writes=[self.bqkv])
            c.barrier()

    def hgrn_rec(self, ebl, bebl):
        c = self.c
        QT3 = self.QT.rearrange("(kc p) t -> p kc t", p=128)
        KT3 = self.KT.rearrange("(kc p) t -> p kc t", p=128)
        GT3 = self.GT.rearrange("(kc p) t -> p kc t", p=128)
        OT3 = self.OT.rearrange("(kc p) t -> p kc t", p=128)
        self.ps_pools = {"at": [0, 1], "o": [2, 3], "d": [4, 5], "n": [6, 7]}
        self.ps_pi = {}
        with contextlib.ExitStack() as st:
            tri = self.sb(st, "r_tri", [64, 64], F32)
            bcn = Buf()
            c.dma("sp", tri[:, :], self.tri[:, :], writes=[bcn])
            S32 = self.sb(st, "r_S32", [128, 8, 128], F32)
            Sbf = self.sb(st, "r_Sbf", [128, 8, 128], BF16)
            Sbf2 = [Sbf, self.sb(st, "r_Sbfb", [128, 8, 128], BF16)]
            bSbf2 = [Buf(), Buf()]
            t32 = self.sb(st, "r_t32", [128, 8, 128], F32)
            bS32, bSbf, bt32 = Buf(), Buf(), Buf()
            c.op("dve", lambda h: h.memset(S32[:, :, :], 0.0), writes=[bS32])
            c.op("dve", lambda h: h.memset(Sbf2[0][:, :, :], 0.0), writes=[bSbf2[0]])
            c.op("dve", lambda h: h.memset(Sbf2[1][:, :, :], 0.0), writes=[bSbf2[1]])
            qT = [self.sb(st, "r_q%d" % i, [128, 8, 512], BF16) for i in range(2)]
            kT = [self.sb(st, "r_k%d" % i, [128, 8, 512], BF16) for i in range(2)]
            gT = [self.sb(st, "r_g%d" % i, [128, 8, 512], BF16) for i in range(2)]
            ktm = [self.sb(st, "r_ktm%d" % i, [128, 8, 4, 128], BF16) for i in range(2)]
            itm = [self.sb(st, "r_itm%d" % i, [128, 8, 4, 128], BF16) for i in range(2)]
            bin_ = [Buf(), Buf()]
            o32 = self.sb(st, "r_o32", [128, 8, 512], F32)
            bo32 = Buf()
            at = [self.sb(st, "r_at%d" % i, [128, 8, 64], BF16) for i in range(2)]
            bat = [Buf(), Buf()]
            sq = self.sb(st, "r_sq", [128, 512], F32)
            rt = self.sb(st, "r_rt", [128, 512], F32)
            tmp = self.sb(st, "r_tmp", [128, 512], F32)
            bsq, brt, btmp = Buf(), Buf(), Buf()
            oo = [self.sb(st, "r_oo%d" % i, [128, 8, 512], BF16) for i in range(2)]
            boo = [Buf(), Buf()]
            ai = 0
            for tile in range(S // 512):
                cs = slice(tile * 512, (tile + 1) * 512)
                b = tile % 2
                c.dma("sp", qT[b][:, :, :], QT3[:, :, cs], reads=[self.bqkv], writes=[bin_[b]])
                c.dma("sp", kT[b][:, :, :], KT3[:, :, cs], reads=[self.bqkv], writes=[bin_[b]])
                c.dma("sp", gT[b][:, :, :], GT3[:, :, cs], reads=[self.bqkv], writes=[bin_[b]])
                c.dma("sp", ktm[b][:, :, :, :], self.V1s[0:8, :, tile * 4:(tile + 1) * 4, :].rearrange("h p s e -> p h s e"),
                      reads=[self.bqkv], writes=[bin_[b]])
                c.dma("sp", itm[b][:, :, :, :], self.V1s[8:16, :, tile * 4:(tile + 1) * 4, :].rearrange("h p s e -> p h s e"),
                      reads=[self.bqkv], writes=[bin_[b]])
                for ch in range(8):
                    cc = slice(ch * 64, (ch + 1) * 64)
                    prt = 64 * (ch % 2)
                    sub = ch // 2
                    gch = tile * 8 + ch
                    pdA, bpdA = self.ps[4], self.bps[4]
                    pdB, bpdB = self.ps[5], self.bps[5]
                    for hh in range(8):
                        pd, bpd = (pdA, bpdA) if hh < 4 else (pdB, bpdB)
                        hc = (hh % 4) * 128
                        c.op("pe", lambda h: h.matmul(pd[:, hc:hc + 128], lhsT=ktm[b][prt:prt + 64, hh, sub, :],
                                                      rhs=itm[b][prt:prt + 64, hh, sub, :], start=True, stop=True),
                             reads=[bin_[b]], writes=[bpd])
                    c.op("dve", lambda h: h.tensor_tensor(out=t32[:, 0:4, :], in0=pdA[:, :].rearrange("p (a b) -> p a b", b=128),
                                                          in1=S32[:, 0:4, :], op=ALU.add), reads=[bpdA, bS32], writes=[bt32])
                    c.op("dve", lambda h: h.tensor_tensor(out=t32[:, 4:8, :], in0=pdB[:, :].rearrange("p (a b) -> p a b", b=128),
                                                          in1=S32[:, 4:8, :], op=ALU.add), reads=[bpdB, bS32], writes=[bt32])
                    c.op("dve", lambda h: h.tensor_tensor(out=S32[:, :, :], in0=t32[:, :, :],
                                                          in1=ebl[:, :, gch].unsqueeze(2).to_broadcast([128, 8, 128]), op=ALU.mult),
                         reads=[bt32, bebl], writes=[bS32])
                    c.op("act", lambda h: h.activation(out=Sbf2[gch % 2][:, :, :], in_=S32[:, :, :], func=AF.Identity),
                         reads=[bS32], writes=[bSbf2[gch % 2]])
                    pat, bpat = self.ps_next("at")
                    for hh in range(8):
                        c.op("pe", lambda h: h.matmul(pat[0:64, hh * 64:(hh + 1) * 64], lhsT=kT[b][:, hh, cc], rhs=qT[b][:, hh, cc],
                                                      start=True, stop=True), reads=[bin_[b]], writes=[bpat])
                    a = ai % 2
                    ai += 1
                    c.op("dve", lambda h: h.tensor_tensor(out=at[a][prt:prt + 64, :, :],
                                                          in0=pat[0:64, :].rearrange("p (a b) -> p a b", b=64),
                                                          in1=tri[:, :].unsqueeze(1).to_broadcast([64, 8, 64]), op=ALU.mult),
                         reads=[bpat, bcn], writes=[bat[a]])
                    po, bpo = self.ps_next("o")
                    for hh in range(8):
                        c.op("pe", lambda h: h.matmul(po[:, hh * 64:(hh + 1) * 64], lhsT=Sbf2[(gch - 1) % 2][:, hh, :], rhs=qT[b][:, hh, cc],
                                                      start=True, stop=False), reads=[bSbf2[(gch - 1) % 2], bin_[b]], writes=[bpo])
                        c.op("pe", lambda h: h.matmul(po[:, hh * 64:(hh + 1) * 64], lhsT=itm[b][prt:prt + 64, hh, sub, :],
                                                      rhs=at[a][prt:prt + 64, hh, :], start=False, stop=True),
                             reads=[bin_[b], bat[a]], writes=[bpo])
                    c.op("act", lambda h: h.activation(out=o32[:, :, cc], in_=po[:, :].rearrange("p (a b) -> p a b", b=64),
                                                       func=AF.Identity), reads=[bpo], writes=[bo32])
                for hh in range(8):
                    pn, bpn = self.ps_next("n")
                    c.op("act", lambda h: h.activation(out=sq[:, :], in_=o32[:, hh, :], func=AF.Square), reads=[bo32], writes=[bsq])
                    c.op("pe", lambda h: h.matmul(pn[:, :], lhsT=self.ones_f[:, :], rhs=sq[:, :], start=True, stop=True),
                         reads=[bsq, self.bconst], writes=[bpn])
                    c.op("act", lambda h: h.activation(out=rt[:, :], in_=pn[:, :], func=AF.Sqrt, scale=1.0 / 128,
                                                       bias=self.eps_t[:, 0:1]), reads=[bpn, self.bconst], writes=[brt])
                    c.op("dve", lambda h: h.reciprocal(out=rt[:, :], in_=rt[:, :]), reads=[brt], writes=[brt])
                    c.op("dve", lambda h: h.scalar_tensor_tensor(out=tmp[:, :], in0=o32[:, hh, :],
                                                                 scalar=self.pvec[:, self.col_hnorm:self.col_hnorm + 1],
                                                                 in1=rt[:, :], op0=ALU.mult, op1=ALU.mult),
                         reads=[bo32, brt, self.bconst], writes=[btmp])
                    c.op("dve", lambda h: h.tensor_tensor(out=oo[b][:, hh, :], in0=tmp[:, :], in1=gT[b][:, hh, :], op=ALU.mult),
                         reads=[btmp, bin_[b]], writes=[boo[b]])
                c.dma("sp", OT3[:, :, cs], oo[b][:, :, :], reads=[boo[b]], writes=[self.bot])
            c.barrier()

    def s5_phase(self, layer, wglu_s, bw):
        c = self.c
        hT3 = self.s5_src.rearrange("(kc p) t -> p kc t", p=128)
        w3 = wglu_s.rearrange("(kc p) n -> p kc n", p=128)
        self.ps_pools = {"y": [0, 1, 2, 3], "e": [4, 5, 6, 7]}
        self.ps_pi = {}
        TS = 1024
        sgnA = self.pvec[:, self.col_sgnA:self.col_sgnA + 1]
        sgnB = self.pvec[:, self.col_sgnA + 1:self.col_sgnA + 2]
        neg1 = self.pvec[:, self.col_sgnA + 2:self.col_sgnA + 3]
        with contextlib.ExitStack() as st:
            Bp = self.sb(st, "s_Bp", [128, 8, 4, 128], BF16)
            Bq = self.sb(st, "s_Bq", [128, 8, 4, 128], BF16)
            Cc = self.sb(st, "s_Cc", [128, 64, 64], BF16)
            Cs = self.sb(st, "s_Cs", [128, 64, 64], BF16)
            th = self.sb(st, "s_th", [128, 64], F32)
            rho = self.sb(st, "s_rho", [128, 64], F32)
            carry = self.sb(st, "s_carry", [128, 64], F32)
            bpar, bcarry = Buf(), Buf()
            with contextlib.ExitStack() as s2:
                def t64(n):
                    return self.sb(s2, "s_" + n, [128, 64], F32)
                are, aim, ldt, dt, lr, x0, kf0, sn0, cs0, abr, abi, den, mre, fre, fim, tA, tB = [t64(n) for n in (
                    "are", "aim", "ldt", "dt", "lr", "x0", "kf0", "sn0", "cs0", "abr", "abi", "den", "mre", "fre", "fim", "tA", "tB")]
                ki0 = self.sb(s2, "s_ki0", [128, 64], I32)
                bl = Buf()
                c.dma("sp", are[:, :], self.s5p[:, 0:64], writes=[bl])
                c.dma("sp", aim[:, :], self.s5p[:, 64:128], writes=[bl])
                c.dma("sp", ldt[:, :], self.s5p[:, 128:192], writes=[bl])
                b0 = Buf()
                R, W = [bl, b0], [b0]
                c.op("act", lambda h: h.activation(out=dt[:, :], in_=ldt[:, :], func=AF.Exp), reads=R, writes=W)
                c.op("dve", lambda h: h.tensor_tensor(out=lr[:, :], in0=are[:, :], in1=dt[:, :], op=ALU.mult), reads=R, writes=W)
                c.op("dve", lambda h: h.tensor_tensor(out=th[:, :], in0=aim[:, :], in1=dt[:, :], op=ALU.mult), reads=R, writes=[b0, bpar])
                c.op("act", lambda h: h.activation(out=rho[:, :], in_=lr[:, :], func=AF.Exp), reads=R, writes=[b0, bpar])
                self._rr(th, b0, 64, ki0, b0, kf0, b0, sn0, b0, cs0, b0)
                c.op("dve", lambda h: h.tensor_tensor(out=abr[:, :], in0=rho[:, :], in1=cs0[:, :], op=ALU.mult), reads=R, writes=W)
                c.op("dve", lambda h: h.tensor_tensor(out=abi[:, :], in0=rho[:, :], in1=sn0[:, :], op=ALU.mult), reads=R, writes=W)
                c.op("dve", lambda h: h.tensor_tensor(out=den[:, :], in0=are[:, :], in1=are[:, :], op=ALU.mult), reads=R, writes=W)
                c.op("dve", lambda h: h.tensor_tensor(out=tA[:, :], in0=aim[:, :], in1=aim[:, :], op=ALU.mult), reads=R, writes=W)
                c.op("dve", lambda h: h.tensor_tensor(out=den[:, :], in0=den[:, :], in1=tA[:, :], op=ALU.add), reads=R, writes=W)
                c.op("dve", lambda h: h.reciprocal(out=den[:, :], in_=den[:, :]), reads=R, writes=W)
                c.op("dve", lambda h: h.tensor_scalar(out=mre[:, :], in0=abr[:, :], scalar1=-1.0, scalar2=None, op0=ALU.add), reads=R, writes=W)
                c.op("dve", lambda h: h.tensor_tensor(out=tA[:, :], in0=mre[:, :], in1=are[:, :], op=ALU.mult), reads=R, writes=W)
                c.op("dve", lambda h: h.tensor_tensor(out=tB[:, :], in0=abi[:, :], in1=aim[:, :], op=ALU.mult), reads=R, writes=W)
                c.op("dve", lambda h: h.tensor_tensor(out=fre[:, :], in0=tA[:, :], in1=tB[:, :], op=ALU.add), reads=R, writes=W)
                c.op("dve", lambda h: h.tensor_tensor(out=fre[:, :], in0=fre[:, :], in1=den[:, :], op=ALU.mult), reads=R, writes=W)
                c.op("dve", lambda h: h.tensor_tensor(out=tA[:, :], in0=abi[:, :], in1=are[:, :], op=ALU.mult), reads=R, writes=W)
                c.op("dve", lambda h: h.tensor_tensor(out=tB[:, :], in0=mre[:, :], in1=aim[:, :], op=ALU.mult), reads=R, writes=W)
                c.op("dve", lambda h: h.tensor_tensor(out=fim[:, :], in0=tA[:, :], in1=tB[:, :], op=ALU.subtract), reads=R, writes=W)
                c.op("dve", lambda h: h.tensor_tensor(out=fim[:, :], in0=fim[:, :], in1=den[:, :], op=ALU.mult), reads=R, writes=W)
                c.op("dve", lambda h: h.tensor_scalar(out=tA[:, :], in0=fim[:, :], scalar1=sgnA, scalar2=None, op0=ALU.mult),
                     reads=[b0, self.bconst], writes=W)
                c.op("dve", lambda h: h.tensor_scalar(out=tB[:, :], in0=fre[:, :], scalar1=sgnB, scalar2=None, op0=ALU.mult),
                     reads=[b0, self.bconst], writes=W)
                BB = self.sb(s2, "s_BB", [128, 1024], F32)
                BS = self.sb(s2, "s_BS", [128, 1024], F32)
                u1 = self.sb(s2, "s_u1", [128, 1024], F32)
                u2 = self.sb(s2, "s_u2", [128, 1024], F32)
                Z = self.sb(s2, "s_Z", [128, 8, 4, 128], F32)
                idf = self.sb(s2, "s_idf", [128, 128], F32)
                c.dma("sp", BB[:, :], self.s5B[:, 0:1024], writes=[bl])
                c.dma("sp", BS[:, :], self.s5B[:, 1024:2048], writes=[bl])
                c.dma("sp", idf[:, :], self.ident[:, :], writes=[bl])

                def bc(t):
                    return t[:, :].unsqueeze(2).to_broadcast([128, 64, 16])

                def v3(t):
                    return t[:, :].rearrange("p (g c) -> p g c", c=16)
                for (dst, ca, cb) in ((Bp, (fre, BB, tA, BS), None), (Bq, (tB, BS, fim, BB), None)):
                    fa, Xa, fb, Xb = ca
                    c.op("dve", lambda h: h.memset(Z[:, :, :, :], 0.0), reads=R, writes=W)
                    c.op("dve", lambda h: h.tensor_tensor(out=v3(u1), in0=v3(Xa), in1=bc(fa), op=ALU.mult), reads=R, writes=W)
                    c.op("dve", lambda h: h.tensor_tensor(out=v3(u2), in0=v3(Xb), in1=bc(fb), op=ALU.mult), reads=R, writes=W)
                    for par in range(4):
                        zv = Z[:, :, par, :].rearrange("p k (m q) -> p k m q", q=64)[:, :, :, 16 * par:16 * par + 16]
                        a1 = u1[:, :].rearrange("p (k m r c) -> p k m r c", k=8, m=2, r=4, c=16)[:, :, :, par, :]
                        a2 = u2[:, :].rearrange("p (k m r c) -> p k m r c", k=8, m=2, r=4, c=16)[:, :, :, par, :]
                        c.op("dve", lambda h: h.tensor_tensor(out=zv, in0=a1, in1=a2, op=ALU.add), reads=R, writes=W)
                    for kc in range(8):
                        for par in range(4):
                            pz, bpz = self.ps_next("e")
                            c.op("pe", lambda h: h.transpose(pz[:, 0:128], Z[:, kc, par, :], idf[:, :]), reads=R, writes=[bpz])
                            c.op("act", lambda h: h.activation(out=dst[:, kc, par, :], in_=pz[:, 0:128], func=AF.Identity),
                                 reads=[bpz], writes=[bpar])
                CC = self.sb(s2, "s_CC", [128, 2048], F32)
                c.dma("sp", CC[:, :], self.s5C[:, :], writes=[bl])
                c.op("dve", lambda h: h.memset(Cc[:, :, :], 0.0), writes=[bpar])
                c.op("dve", lambda h: h.memset(Cs[:, :, :], 0.0), writes=[bpar])
                for (dst, off, sg) in ((Cc, 0, sgnB), (Cs, 1024, neg1)):
                    for par in range(4):
                        dv = dst[:, :, :].rearrange("p (gm r) (q c) -> p gm r q c", r=4, q=4)[:, :, par, par, :]
                        sv = CC[:, off:off + 1024].rearrange("p (gm r c) -> p gm r c", r=4, c=16)[:, :, par, :]
                        c.op("dve", lambda h: h.tensor_scalar(out=dv, in0=sv, scalar1=sg, scalar2=None, op0=ALU.mult),
                             reads=[bl, self.bconst], writes=[bpar])
                c.op("dve", lambda h: h.memset(carry[:, :], 0.0), writes=[bcarry])
                c.barrier()
            tt = self.sb(st, "s_tt", [128, TS], F32)
            thq = self.sb(st, "s_thq", [128, 64], F32)
            btt, bthq = Buf(), Buf()
            c.dma("sp", tt[:, :], self.ttc[:, 0:TS], writes=[btt])
            wg = self.sb(st, "s_wg", [128, KC, 2 * D], BF16)
            bwg = Buf()
            for i in range(2):
                c.dma("sp", wg[:, :, i * D:(i + 1) * D], w3[:, :, i * D:(i + 1) * D], reads=[bw], writes=[bwg])
            hx = self.sb(st, "s_hx", [128, KC, 512], F32)
            sq = self.sb(st, "s_sq", [128, 2, 512], F32)
            rt = self.sb(st, "s_rt", [128, 512], F32)
            xn = self.sb(st, "s_xn", [128, KC, TS], BF16)
            z = self.sb(st, "s_z", [128, KC, TS], BF16)
            bhx, bsq, brt, bxn, bz = Buf(), Buf(), Buf(), Buf(), Buf()
            x = self.sb(st, "s_x", [128, TS], F32)
            ki = self.sb(st, "s_ki", [128, TS], I32)
            sn = [self.sb(st, "s_sn%d" % i, [128, TS], F32) for i in range(3)]
            cs_ = [self.sb(st, "s_cs%d" % i, [128, TS], F32) for i in range(3)]
            eh = [self.sb(st, "s_eh%d" % i, [128, TS], F32) for i in range(2)]
            G = [self.sb(st, "s_G%d" % i, [128, TS], F32) for i in range(2)]
            Gc = [self.sb(st, "s_Gc%d" % i, [128, TS], BF16) for i in range(2)]
            Gs = [self.sb(st, "s_Gs%d" % i, [128, TS], BF16) for i in range(2)]
            bsn, bcs = [Buf() for _ in range(3)], [Buf() for _ in range(3)]
            beh, bG, bGc, bGs = [[Buf() for _ in range(2)] for _ in range(4)]
            bx, bki = Buf(), Buf()
            t1 = [self.sb(st, "s_t1%d" % i, [128, 512], F32) for i in range(2)]
            t2 = [self.sb(st, "s_t2%d" % i, [128, 512], F32) for i in range(2)]
            bt1 = [Buf(), Buf()]
            bt2 = [Buf(), Buf()]
            yv = self.sb(st, "s_yv", [128, 512], F32)
            y2 = self.sb(st, "s_y2", [128, 512], F32)
            y3 = self.sb(st, "s_y3", [128, 512], F32)
            byv, by2, by3 = Buf(), Buf(), Buf()
            hr = [self.sb(st, "s_hr", [128, TS], F32)] * 2
            bhr = [Buf()] * 2
            ti = 0
            for q in range(S // TS):
                t0 = q * TS
                bh = self.bh[q]
                c.op("dve", lambda h: h.tensor_scalar(out=thq[:, :], in0=th[:, :], scalar1=float(t0), scalar2=None, op0=ALU.mult),
                     reads=[bpar], writes=[bthq])
                for half in range(2):
                    c.dma("sp", hx[:, :, :], hT3[:, :, t0 + half * 512:t0 + (half + 1) * 512], reads=[bh], writes=[bhx])
                    self.rmsnorm_tile(hx, bhx, self.col_nmix + layer * KC, xn[:, :, half * 512:(half + 1) * 512], bxn, 512,
                                      (sq, bsq, rt, brt))
                pys = {}
                pool_eng = "dve" if (q == 0 and self.s5_spare_pool) else "pool"

                TWO_PI = 2.0 * np.pi

                def S1(g):
                    kc, gi = g // 8, g % 8
                    u3 = g % 3
                    if gi == 0:
                        pys[kc] = [self.ps_next("y") for _ in range(2)]
                    c.op("act", lambda h: h.activation(out=x[:, :], in_=tt[:, :], func=AF.Identity, scale=th[:, g:g + 1],
                                                       bias=thq[:, g:g + 1]),
                         reads=[btt, bpar, bthq], writes=[bx])
                    c.op("act", lambda h: h.activation(out=ki[:, :], in_=x[:, :], func=AF.Identity, scale=1.0 / TWO_PI),
                         reads=[bx], writes=[bki])

                def S1b(g):
                    u3 = g % 3
                    c.op("dve", lambda h: h.scalar_tensor_tensor(out=sn[u3][:, :], in0=ki[:, :], scalar=-TWO_PI, in1=x[:, :],
                                                                 op0=ALU.mult, op1=ALU.add), reads=[bki, bx], writes=[bsn[u3]])
                    c.op("dve", lambda h: h.tensor_scalar(out=sn[u3][:, :], in0=sn[u3][:, :], scalar1=-PI_LO, scalar2=PI_LO,
                                                          op0=ALU.max, op1=ALU.min), reads=[bsn[u3]], writes=[bsn[u3]])
                    c.op("act", lambda h: h.activation(out=cs_[u3][:, :], in_=sn[u3][:, :], func=AF.Abs),
                         reads=[bsn[u3]], writes=[bcs[u3]])

                def S2a(g):
                    u3 = g % 3
                    c.op("act", lambda h: h.activation(out=sn[u3][:, :], in_=sn[u3][:, :], func=AF.Sin), reads=[bsn[u3]], writes=[bsn[u3]])
                    c.op("act", lambda h: h.activation(out=cs_[u3][:, :], in_=cs_[u3][:, :], func=AF.Sin, scale=-1.0,
                                                       bias=self.halfpi_t[:, 0:1]), reads=[bcs[u3], self.bconst], writes=[bcs[u3]])

                def S2(g):
                    nonlocal ti
                    kc, gi = g // 8, g % 8
                    u3, u = g % 3, g % 2
                    m, par = gi // 4, gi % 4
                    rows = slice(64 * m, 64 * m + 64)
                    for ct in range(2):
                        cc = slice(ct * 512, (ct + 1) * 512)
                        pe_, bpe = self.ps_next("e")
                        pq_, bpq = self.ps_next("e")
                        c.op("pe", lambda h: h.matmul(pe_[:, :], lhsT=Bp[rows, kc, par, :], rhs=xn[rows, kc, cc], start=True, stop=True),
                             reads=[bpar, bxn], writes=[bpe])
                        c.op("pe", lambda h: h.matmul(pq_[:, :], lhsT=Bq[rows, kc, par, :], rhs=xn[rows, kc, cc], start=True, stop=True),
                             reads=[bpar, bxn], writes=[bpq])
                        e = ti % 2
                        ti += 1
                        c.op("dve", lambda h: h.tensor_tensor(out=t1[e][:, :], in0=pe_[:, :], in1=cs_[u3][:, cc], op=ALU.mult),
                             reads=[bpe, bcs[u3]], writes=[bt1[e]])
                        c.op("dve", lambda h: h.tensor_tensor(out=t2[e][:, :], in0=pq_[:, :], in1=sn[u3][:, cc], op=ALU.mult),
                             reads=[bpq, bsn[u3]], writes=[bt2[e]])
                        c.op(pool_eng, lambda h: h.tensor_tensor(out=eh[u][:, cc], in0=t1[e][:, :], in1=t2[e][:, :], op=ALU.add),
                             reads=[bt1[e], bt2[e]], writes=[beh[u]])

                def S3(g):
                    u3, u = g % 3, g % 2
                    c.op("dve", lambda h: h.tensor_tensor_scan(out=G[u][:, :], data0=rho[:, g:g + 1].to_broadcast([128, TS]),
                                                               data1=eh[u][:, :], initial=carry[:, g:g + 1],
                                                               op0=ALU.mult, op1=ALU.add),
                         reads=[beh[u], bpar, bcarry], writes=[bG[u]])
                    c.op("act", lambda h: h.activation(out=carry[:, g:g + 1], in_=G[u][:, TS - 1:TS], func=AF.Identity),
                         reads=[bG[u]], writes=[bcarry])
                    c.op("dve", lambda h: h.tensor_tensor(out=Gc[u][:, :], in0=G[u][:, :], in1=cs_[u3][:, :], op=ALU.mult),
                         reads=[bG[u], bcs[u3]], writes=[bGc[u]])
                    c.op(pool_eng, lambda h: h.tensor_tensor(out=Gs[u][:, :], in0=G[u][:, :], in1=sn[u3][:, :], op=ALU.mult),
                         reads=[bG[u], bsn[u3]], writes=[bGs[u]])

                def S4(g):
                    kc, gi = g // 8, g % 8
                    u = g % 2
                    m, par = gi // 4, gi % 4
                    rows = slice(64 * m, 64 * m + 64)
                    py = pys[kc]
                    for ct in range(2):
                        cc = slice(ct * 512, (ct + 1) * 512)
                        pyy, bpy = py[ct]
                        c.op("pe", lambda h: h.matmul(pyy[rows, :], lhsT=Cc[:, g, :], rhs=Gc[u][:, cc], start=(par == 0), stop=False),
                             reads=[bpar, bGc[u]], writes=[bpy])
                        c.op("pe", lambda h: h.matmul(pyy[rows, :], lhsT=Cs[:, g, :], rhs=Gs[u][:, cc], start=False, stop=(par == 3)),
                             reads=[bpar, bGs[u]], writes=[bpy])
                    if gi == 7:
                        for ct in range(2):
                            cc = slice(ct * 512, (ct + 1) * 512)
                            pyy, bpy = py[ct]
                            dcol = self.pvec[:, self.col_s5d + kc:self.col_s5d + kc + 1]
                            c.op("dve", lambda h: h.scalar_tensor_tensor(out=yv[:, :], in0=xn[:, kc, cc], scalar=dcol, in1=pyy[:, :],
                                                                         op0=ALU.mult, op1=ALU.add),
                                 reads=[bxn, bpy, self.bconst], writes=[byv])
                            c.op("act", lambda h: h.activation(out=y2[:, :], in_=yv[:, :], func=AF.Square), reads=[byv], writes=[by2])
                            c.op(pool_eng, lambda h: h.tensor_scalar(out=y2[:, :], in0=y2[:, :], scalar1=0.044715, scalar2=1.0,
                                                                   op0=ALU.mult, op1=ALU.add), reads=[by2], writes=[by2])
                            c.op(pool_eng, lambda h: h.tensor_tensor(out=y2[:, :], in0=y2[:, :], in1=yv[:, :], op=ALU.mult),
                                 reads=[by2, byv], writes=[by2])
                            c.op("act", lambda h: h.activation(out=y3[:, :], in_=y2[:, :], func=AF.Sigmoid, scale=1.5957691216),
                                 reads=[by2], writes=[by3])
                            c.op(pool_eng, lambda h: h.tensor_tensor(out=z[:, kc, cc], in0=y3[:, :], in1=yv[:, :], op=ALU.mult),
                                 reads=[by3, byv], writes=[bz])

                NG = 64
                for i in range(NG + 3):
                    if 0 <= i - 1 < NG:
                        S2a(i - 1)
                    if 0 <= i - 2 < NG:
                        S3(i - 2)
                    if i < NG:
                        S1(i)
                    if 0 <= i - 1 < NG:
                        S2(i - 1)
                    if i < NG:
                        S1b(i)
                    if 0 <= i - 3 < NG:
                        S4(i - 3)
                for oc in range(KC):
                    r = oc % 2
                    c.dma("sp", hr[r][:, :], self.s5_src[oc * 128:(oc + 1) * 128, t0:t0 + TS], reads=[bh], writes=[bhr[r]])
                    for ct in range(2):
                        cc = slice(ct * 512, (ct + 1) * 512)
                        pv, bpv = self.ps_next("e")
                        pg, bpg = self.ps_next("e")
                        for kc in range(KC):
                            c.op("pe", lambda h: h.matmul(pv[:, :], lhsT=wg[:, kc, oc * 128:(oc + 1) * 128], rhs=z[:, kc, cc],
                                                          start=(kc == 0), stop=(kc == KC - 1)), reads=[bwg, bz], writes=[bpv])
                        for kc in range(KC):
                            c.op("pe", lambda h: h.matmul(pg[:, :], lhsT=wg[:, kc, D + oc * 128:D + (oc + 1) * 128], rhs=z[:, kc, cc],
                                                          start=(kc == 0), stop=(kc == KC - 1)), reads=[bwg, bz], writes=[bpg])
                        bv = self.pvec[:, self.col_bglu + oc:self.col_bglu + oc + 1]
                        bg = self.pvec[:, self.col_bglu + 8 + oc:self.col_bglu + 8 + oc + 1]
                        c.op("act", lambda h: h.activation(out=y2[:, :], in_=pv[:, :], func=AF.Identity, bias=bv),
                             reads=[bpv, self.bconst], writes=[by2])
                        c.op("act", lambda h: h.activation(out=y3[:, :], in_=pg[:, :], func=AF.Sigmoid, bias=bg),
                             reads=[bpg, self.bconst], writes=[by3])
                        c.op("dve", lambda h: h.tensor_tensor(out=y2[:, :], in0=y2[:, :], in1=y3[:, :], op=ALU.mult),
                             reads=[by2, by3], writes=[by2])
                        c.op("dve", lambda h: h.tensor_tensor(out=hr[r][:, cc], in0=hr[r][:, cc], in1=y2[:, :], op=ALU.add),
                             reads=[by2, bhr[r]], writes=[bhr[r]])
                    c.dma("sp", self.hT[oc * 128:(oc + 1) * 128, t0:t0 + TS], hr[r][:, :], reads=[bhr[r]], writes=[bh])
            c.barrier()

    def final_phase(self, outT, do_norm):
        c = self.c
        hT3 = self.hT.rearrange("(kc p) t -> p kc t", p=128)
        oT3 = outT.rearrange("(kc p) t -> p kc t", p=128)
        with contextlib.ExitStack() as st:
            hx = [self.sb(st, "o_hx%d" % i, [128, KC, 512], F32) for i in range(2)]
            ox = [self.sb(st, "o_ox%d" % i, [128, KC, 512], F32) for i in range(2)]
            sq = self.sb(st, "o_sq", [128, KC, 512], F32)
            rt = self.sb(st, "o_rt", [128, 512], F32)
            bhx = [Buf(), Buf()]
            box = [Buf(), Buf()]
            bsq, brt, bo = Buf(), Buf(), Buf()
            for tile in range(S // 512):
                cs = slice(tile * 512, (tile + 1) * 512)
                b = tile % 2
                c.dma("sp", hx[b][:, :, :], hT3[:, :, cs], reads=[self.bh[tile // 2]], writes=[bhx[b]])
                if do_norm:
                    self.rmsnorm_tile(hx[b], bhx[b], self.col_nfin, ox[b], box[b], 512, (sq, bsq, rt, brt))
                    c.dma("sp", oT3[:, :, cs], ox[b][:, :, :], reads=[box[b]], writes=[bo])
                else:
                    c.dma("sp", oT3[:, :, cs], hx[b][:, :, :], reads=[bhx[b]], writes=[bo])
            c.barrier()

    def build(self):
        cfg = self.cfg
        nc = self.nc
        c = self.c
        es = self.es
        stages = cfg["stages"]
        mixl = [l for (k, l) in stages if k == "mix"]
        ffnl = [l for (k, l) in stages if k == "ffn"]
        xT = self.din("xT", [D, S])
        pvec = self.din("pvec", [128, cfg["npvec"]])
        self.col_nmix, self.col_nffn, self.col_nfin = 0, 4 * KC, 8 * KC
        self.col_invf, self.col_sgnrow = 9 * KC, 9 * KC + 1
        self.posb = self.din("posb", [128, S], I32)
        self.dmask = self.din("dmask", [128, 20 * 512])
        self.mmask = self.din("mmask", [128, 4 * 512])
        self.onehot = self.din("onehot", [16, S])
        self.pastneg = self.din("pastneg", [128, 512])
        self.own = self.din("own", [128, 512])
        self.ident = self.din("ident", [128, 128])
        self.ropeperm = self.din("ropeperm", [128, 128])
        self.rmask = self.din("rmask", [128, 512])
        self.tri = self.din("tri", [64, 64])
        self.col_lb, self.col_hnorm = 9 * KC + 2, 9 * KC + 2 + 4 * KC
        self.col_sgnA = self.col_hnorm + 1
        self.col_s5d = self.col_sgnA + 3
        self.col_bglu = self.col_s5d + KC
        self.s5p = self.din("s5p", [128, 192])
        self.s5B = self.din("s5B", [128, 2048])
        self.s5C = self.din("s5C", [128, 2048])
        self.ttc = self.din("ttc", [128, S])
        w_gu_in = {l: self.din("w_gu%d" % l, [D, 2 * DFF]) for l in ffnl}
        w_d_in = {l: self.din("w_d%d" % l, [DFF, D]) for l in ffnl}
        self.w_gu = {l: self.dscr("s_wgu%d" % l, [D, 2 * DFF], BF16) for l in ffnl}
        self.w_d = {l: self.dscr("s_wd%d" % l, [DFF, D], BF16) for l in ffnl}
        self.bw_ffn = {l: Buf() for l in ffnl}
        win, wsc, bwm = {}, {}, {}
        for l in mixl:
            if l in (1, 3):
                nm = "dil" if l == 1 else "moba"
                win[l] = (self.din(nm + "_qkv", [D, 3 * D]), self.din(nm + "_o", [D, D]))
                wsc[l] = (self.dscr("s_%s_qkv" % nm, [D, 3 * D], BF16), None,
                          self.dscr("s_%s_o" % nm, [D, D], BF16))
                bwm[l] = Buf()
            elif l == 0:
                win[l] = (self.din("s5_wglu", [D, 2 * D]),)
                wsc[l] = (self.dscr("s_s5_wglu", [D, 2 * D], BF16),)
                bwm[l] = Buf()
            elif l == 2:
                win[l] = (self.din("hgrn_in", [D, 4 * D]), self.din("hgrn_o", [D, D]))
                wsc[l] = (self.dscr("s_hgrn_in", [D, 4 * D], BF16), self.dscr("s_hgrn_o", [D, D], BF16))
                bwm[l] = Buf()
        outT = nc.dram_tensor("outT", [D, S], F32, kind="ExternalOutput").ap()
        self.GT = self.dscr("GT", [D, S], BF16)
        self.hT = self.dscr("hT", [D, S], F32)
        self.QT = self.dscr("QT", [D, S], BF16)
        self.KT = self.dscr("KT", [D, S], BF16)
        self.OT = self.dscr("OT", [D, S], BF16)
        self.V1s = self.dscr("V1s", [16, 128, 32, 128], BF16)
        self.bqkv, self.bot = Buf(), Buf()
        self.bh = [Buf() for _ in range(S // 1024)]
        self.pvec = self.sb(es, "pvec_sb", [128, cfg["npvec"]], F32)
        self.ones_f = self.sb(es, "ones_f", [128, 128], F32)
        self.eps_t = self.sb(es, "eps_t", [128, 1], F32)
        self.bconst = Buf()
        c.dma("sp", self.pvec[:, :], pvec[:, :], writes=[self.bconst])
        c.op("dve", lambda h: h.memset(self.ones_f[:, :], 1.0), writes=[self.bconst])
        c.op("dve", lambda h: h.memset(self.eps_t[:, :], EPS), writes=[self.bconst])
        self.halfpi_t = self.sb(es, "halfpi_t", [128, 1], F32)
        c.op("dve", lambda h: h.memset(self.halfpi_t[:, :], float(np.pi / 2)), writes=[self.bconst])
        self.ps = [es.enter_context(nc.psum_tensor("ps%d" % i, [128, 512], F32)) for i in range(8)]
        self.bps = [Buf() for _ in range(8)]
        self.ps_i = 0
        self.sg_i = 0
        self.ps_pools = {}
        self.ps_pi = {}
        self.cast_i = 0
        self.cast_st = []
        for i in range(2):
            self.cast_st.append((self.sb(es, "cst%d" % i, [128, 512], F32), self.sb(es, "cstb%d" % i, [128, 512], BF16),
                                 Buf(), Buf()))
        xT3 = xT.rearrange("(kc p) t -> p kc t", p=128)
        hT3 = self.hT.rearrange("(kc p) t -> p kc t", p=128)
        self.s5_src = self.hT
        if stages[0] == ("mix", 0):
            self.s5_src = xT
        else:
            with contextlib.ExitStack() as st:
                tmp = [self.sb(st, "cp%d" % i, [128, KC, 1024], F32) for i in range(2)]
                bt = [Buf(), Buf()]
                for i, t0 in enumerate(range(0, S, 1024)):
                    c.dma("sp", tmp[i % 2][:, :, :], xT3[:, :, t0:t0 + 1024], writes=[bt[i % 2]])
                    c.dma("sp", hT3[:, :, t0:t0 + 1024], tmp[i % 2][:, :, :], reads=[bt[i % 2]], writes=[self.bh[i]])
                c.barrier()
        cast_done = set()

        def emit_cast(si_):
            if si_ >= len(stages) or si_ in cast_done:
                return
            cast_done.add(si_)
            k, l = stages[si_]
            if k == "ffn":
                self.cast_weight(w_gu_in[l], self.w_gu[l], D, 2 * DFF, self.bw_ffn[l])
                self.cast_weight(w_d_in[l], self.w_d[l], DFF, D, self.bw_ffn[l])
            elif k == "mix" and l in (1, 3):
                self.cast_weight(win[l][0], wsc[l][0], D, 3 * D, bwm[l])
                self.cast_weight(win[l][1], wsc[l][2], D, D, bwm[l])
            elif k == "mix" and l == 0:
                self.cast_weight(win[l][0], wsc[l][0], D, 2 * D, bwm[l])
            elif k == "mix" and l == 2:
                self.cast_weight(win[l][0], wsc[l][0], D, 4 * D, bwm[l])
                self.cast_weight(win[l][1], wsc[l][1], D, D, bwm[l])
        self.hook = None
        bwp = {l: Buf() for l in mixl}
        for si, (k, l) in enumerate(stages):
            for (k2, l2) in stages[si + 1:si + 2] + (stages[0:1] if si == 0 else []):
                if False:
                    self.cast_weight_qk_perm(win[l2][0], wsc[l2][1], bwp[l2])
                    bwp[l2].done = True
            emit_cast(si)
            if si == 0:
                emit_cast(1)
            self.s5_spare_pool = (si == 0 and len(stages) > 1)
            self.hook = lambda si=si: emit_cast(si + 1)
            if k == "ffn":
                self.ffn_phase(l)
            elif k == "mix" and l in (1, 3):
                self.qkv_phase(l, wsc[l][0], wsc[l][1], bwm[l], bwp[l])
                self.attn_phase("dil" if l == 1 else "moba")
                self.linres_phase(self.OT, wsc[l][2], bwm[l], self.bot)
            elif k == "mix" and l == 0:
                self.s5_phase(l, wsc[l][0], bwm[l])
            elif k == "mix" and l == 2:
                with contextlib.ExitStack() as st:
                    ebl = self.sb(st, "ebl", [128, 8, 64], F32)
                    bebl = Buf()
                    self.hgrn_prep(l, wsc[l][0], bwm[l], ebl, bebl)
                    self.hgrn_rec(ebl, bebl)
                self.linres_phase(self.OT, wsc[l][1], bwm[l], self.bot)
            self.run_hook()
        self.final_phase(outT, cfg.get("final_norm", True))
        c.final_wait()
        es.close()
        return nc


ROPE_THETA = 500000.0


def consts_build():
    cst = {}
    kl = np.arange(128)[:, None]
    ql = np.arange(512)[None, :]
    dm = np.zeros((128, 20, 512), np.float32)
    for mi in range(20):
        off = mi - 16
        dl = ql - kl - off * 128
        m = ((dl >= 0) & (dl <= 128)).astype(np.float32)
        m += ((dl >= 0) & (dl <= 512) & (dl % 4 == 0)).astype(np.float32)
        m += ((dl >= 0) & (dl <= 2048) & (dl % 16 == 0)).astype(np.float32)
        dm[:, mi, :] = m
    cst["dmask"] = dm.reshape(128, 20 * 512)
    mm = np.zeros((128, 4, 512), np.float32)
    for j in range(4):
        mm[:, j, :] = (j * 128 + kl <= ql).astype(np.float32)
    cst["mmask"] = mm.reshape(128, 4 * 512)
    oh = np.zeros((16, S), np.float32)
    for n in range(16):
        oh[n, n * 256:(n + 1) * 256] = 1.0
    cst["onehot"] = oh
    pn = np.zeros((128, 32, 16), np.float32)
    ow = np.zeros((128, 32, 16), np.float32)
    for qt in range(32):
        qb = qt // 2
        pn[:, qt, qb:] = -1e30
        ow[:, qt, qb] = 1.0
    cst["pastneg"] = pn.reshape(128, 512)
    cst["own"] = ow.reshape(128, 512)
    cst["ident"] = np.eye(128, dtype=np.float32)
    pmx = np.zeros((128, 128), np.float32)
    for f in range(128):
        j = f % 64
        if j < 8:
            pmx[f + 8, f] = 1.0
        elif j < 16:
            pmx[f - 8, f] = 1.0
    cst["ropeperm"] = pmx
    rm = np.ones((128, 512), np.float32)
    rm[:, 0::64] = 0.0
    cst["rmask"] = rm
    cst["ttc"] = np.ascontiguousarray(np.broadcast_to(np.arange(S, dtype=np.float32)[None, :], (128, S)))
    cst["tri"] = (np.arange(64)[:, None] <= np.arange(64)[None, :]).astype(np.float32)
    return cst


def pvec_build(inp):
    cols = []
    for l in range(4):
        cols.append(inp["norm_mix"][l].reshape(KC, 128).T)
    for l in range(4):
        cols.append(inp["norm_ffn"][l].reshape(KC, 128).T)
    cols.append(inp["norm_final"].reshape(KC, 128).T)
    f = np.arange(128) % 64
    inv = ROPE_THETA ** (-np.arange(0, 16, 2, dtype=np.float32) / 16.0)
    invf = np.where(f < 16, inv[f % 8], 0.0).astype(np.float32)
    sgn = np.where(f < 8, -1.0, np.where(f < 16, 1.0, 0.0)).astype(np.float32)
    cols.append(invf[:, None])
    cols.append(sgn[:, None])
    for l in range(4):
        cols.append(inp["hgrn_lower_bound"][l].reshape(KC, 128).T)
    cols.append(inp["hgrn_norm"][0].reshape(128, 1))
    p = np.arange(128)
    sa = np.where(p < 64, -1.0, 1.0).astype(np.float32)
    cols.append(sa[:, None])
    cols.append(-sa[:, None])
    cols.append(-np.ones((128, 1), np.float32))
    cols.append(inp["s5_d"][0].reshape(KC, 128).T)
    cols.append(inp["s5_b_glu"][0].reshape(2 * KC, 128).T)
    return np.ascontiguousarray(np.concatenate(cols, axis=1).astype(np.float32))


def make_inmaps(inp, cfg, cores, xs=None):
    pv = pvec_build(inp)
    cfg["npvec"] = pv.shape[1]
    cst = consts_build()
    stages = cfg["stages"]
    maps = []
    for b in cores:
        x = inp["x"][b] if xs is None else xs[b]
        m = {"xT": np.ascontiguousarray(x.T), "pvec": pv,
             "posb": np.ascontiguousarray(np.broadcast_to(inp["positions"][b][None, :], (128, S)).astype(np.int32))}
        m.update(cst)
        are, aim, ldt = inp["s5_a_re"][0], inp["s5_a_im"][0], inp["s5_log_dt"][0]
        m["s5p"] = np.ascontiguousarray(np.concatenate([
            np.concatenate([are.T, are.T], 0), np.concatenate([aim.T, aim.T], 0),
            np.broadcast_to(ldt[None, :], (128, 64))], axis=1).astype(np.float32))
        bre = inp["s5_b_re"][0].transpose(1, 0, 2).reshape(64, 1024)
        bim = inp["s5_b_im"][0].transpose(1, 0, 2).reshape(64, 1024)
        m["s5B"] = np.ascontiguousarray(np.concatenate([np.concatenate([bre, bim], 0), np.concatenate([bim, bre], 0)], axis=1))
        cre = inp["s5_c_re"][0].transpose(2, 0, 1).reshape(64, 1024)
        cim = inp["s5_c_im"][0].transpose(2, 0, 1).reshape(64, 1024)
        m["s5C"] = np.ascontiguousarray(np.concatenate([np.concatenate([cre, cim], 0), np.concatenate([cim, cre], 0)], axis=1))
        for (k, l) in stages:
            if k == "ffn":
                m["w_gu%d" % l] = np.ascontiguousarray(inp["ffn_w_gate_up"][l])
                m["w_d%d" % l] = np.ascontiguousarray(inp["ffn_w_down"][l])
            elif k == "mix" and l == 1:
                m["dil_qkv"] = np.ascontiguousarray(inp["dil_w_qkv"][0])
                m["dil_o"] = np.ascontiguousarray(inp["dil_w_o"][0])
            elif k == "mix" and l == 0:
                m["s5_wglu"] = np.ascontiguousarray(inp["s5_w_glu"][0])
            elif k == "mix" and l == 2:
                m["hgrn_in"] = np.ascontiguousarray(inp["hgrn_w_in"][0])
                m["hgrn_o"] = np.ascontiguousarray(inp["hgrn_w_o"][0])
            elif k == "mix" and l == 3:
                m["moba_qkv"] = np.ascontiguousarray(inp["moba_w_qkv"][0])
                m["moba_o"] = np.ascontiguousarray(inp["moba_w_o"][0])
        maps.append(m)
    return maps


FULL_STAGES = [("mix", 0), ("ffn", 0), ("mix", 1), ("ffn", 1), ("mix", 2), ("ffn", 2), ("mix", 3), ("ffn", 3)]


def kernel(**inp):
    inp = {k: np.asarray(v) for k, v in inp.items()}
    cfg = {"stages": FULL_STAGES, "final_norm": True}
    maps = make_inmaps(inp, cfg, range(4))
    prog = Prog(cfg)
    nc = prog.build()
    res = run_bass_kernel_spmd(nc, maps, core_ids=list(range(4)))
    out = np.stack([res.results[b]["outT"].T for b in range(4)], axis=0)
    return np.ascontiguousarray(out.astype(np.float32))
```

```python
import contextlib
import numpy as np
import concourse.bass as bass
import concourse.mybir as mybir
from concourse.bass_utils import run_bass_kernel_spmd

F32 = mybir.dt.float32
BF16 = mybir.dt.bfloat16
I32 = mybir.dt.int32
AF = mybir.ActivationFunctionType
ALU = mybir.AluOpType

S = 4096
D = 1024
DFF = 2816
KC = D // 128
EPS = 1e-6
SELF_SYNC = True
PI_LO = 3.1415925


class Buf:

    def __init__(self, name=""):
        self.w = None
        self.r = {}
        self.name = name


class Eng:
    def __init__(self, ctx, name, handle, self_sync):
        self.name = name
        self.h = handle
        self.sem = ctx.es.enter_context(ctx.nc.semaphore("s_" + name))
        self.cnt = 0
        self.seen = {}
        self.self_sync = self_sync


class DQ:
    def __init__(self, ctx, name, eng, k):
        self.name = name
        self.eng = eng
        self.k = k
        self.sems = [ctx.es.enter_context(ctx.nc.semaphore("q_%s%d" % (name, i))) for i in range(k)]
        self.cnts = [0] * k
        self.n = 0


class Ctx:
    def __init__(self, nc):
        self.nc = nc
        self.es = contextlib.ExitStack()
        self.eng = {}
        for name, h, ss in (("pe", nc.tensor, False), ("act", nc.scalar, SELF_SYNC), ("dve", nc.vector, SELF_SYNC),
                            ("pool", nc.gpsimd, SELF_SYNC), ("sp", nc.sync, False)):
            self.eng[name] = Eng(self, name, h, ss)
        self.dq = {"sp": DQ(self, "sp", self.eng["sp"], 8), "pool": DQ(self, "pool", self.eng["pool"], 4)}
        self.semtab = {}
        for e in self.eng.values():
            self.semtab[e.name] = e.sem
        for q in self.dq.values():
            for i, s in enumerate(q.sems):
                self.semtab[(q.name, i)] = s
        self.nwait = 0
        self.nins = 0

    def _need(self, reads, writes):
        need = {}
        for b in reads:
            if b.w is not None:
                k, v = b.w
                if need.get(k, 0) < v:
                    need[k] = v
        for b in writes:
            if b.w is not None:
                k, v = b.w
                if need.get(k, 0) < v:
                    need[k] = v
            for k, v in b.r.items():
                if need.get(k, 0) < v:
                    need[k] = v
        return need

    def _waits(self, E, need):
        for k, v in need.items():
            if k == E.name and not E.self_sync:
                continue
            if E.seen.get(k, 0) < v:
                E.h.wait_ge(self.semtab[k], v)
                E.seen[k] = v
                self.nwait += 1

    def op(self, eng, emit, reads=(), writes=()):
        E = self.eng[eng]
        self._waits(E, self._need(reads, writes))
        ins = emit(E.h)
        E.cnt += 1
        ins.then_inc(E.sem, 1)
        self.nins += 1
        for b in reads:
            b.r[E.name] = E.cnt
        for b in writes:
            b.w = (E.name, E.cnt)
            b.r = {}

    def dma(self, q, out, in_, reads=(), writes=()):
        Q = self.dq[q]
        E = Q.eng
        i = Q.n % Q.k
        need = self._need(reads, writes)
        key = (Q.name, i)
        if Q.cnts[i] > 0:
            need[key] = max(need.get(key, 0), 16 * Q.cnts[i])
        self._waits(E, need)
        E.h.dma_start(out=out, in_=in_).then_inc(Q.sems[i], 16)
        Q.cnts[i] += 1
        Q.n += 1
        self.nins += 1
        for b in reads:
            b.r[key] = 16 * Q.cnts[i]
        for b in writes:
            b.w = (key, 16 * Q.cnts[i])
            b.r = {}

    def barrier(self):
        tgt = {}
        for e in self.eng.values():
            if e.cnt:
                tgt[e.name] = e.cnt
        for q in self.dq.values():
            for i in range(q.k):
                if q.cnts[i]:
                    tgt[(q.name, i)] = 16 * q.cnts[i]
        for e in self.eng.values():
            if e.name == "pool":
                continue
            for k, v in tgt.items():
                if k == "pool" or (isinstance(k, tuple) and k[0] == "pool"):
                    continue
                if k == e.name:
                    continue
                if e.seen.get(k, 0) < v:
                    e.h.wait_ge(self.semtab[k], v)
                    e.seen[k] = v

    def final_wait(self):
        E = self.eng["sp"]
        Q = self.dq["sp"]
        for i in range(Q.k):
            if Q.cnts[i]:
                E.h.wait_ge(Q.sems[i], 16 * Q.cnts[i])


class Prog:
    def __init__(self, cfg):
        self.cfg = cfg
        nc = bass.Bass("TRN2", target_bir_lowering=False)
        self.nc = nc
        self.c = Ctx(nc)
        self.es = self.c.es
        self.ins = {}

    def din(self, name, shape, dt=F32):
        t = self.nc.dram_tensor(name, list(shape), dt, kind="ExternalInput").ap()
        self.ins[name] = t
        return t

    def dscr(self, name, shape, dt):
        return self.nc.dram_tensor(name, list(shape), dt, kind="Internal").ap()

    def sb(self, stack, name, shape, dt):
        self.sb_n = getattr(self, "sb_n", 0) + 1
        return stack.enter_context(self.nc.sbuf_tensor("%s_%d" % (name, self.sb_n), list(shape), dt))

    def cast_weight(self, src, dst, K, N, buf):
        c = self.c
        if not hasattr(buf, "cw_key"):
            sem = self.es.enter_context(self.nc.semaphore("cw%d" % len(c.semtab)))
            buf.cw_key = ("cw", len(c.semtab))
            c.semtab[buf.cw_key] = sem
            buf.cw_n = 0
        sem = c.semtab[buf.cw_key]
        step = 512
        for r0 in range(0, K, step):
            r1 = min(K, r0 + step)
            self.nc.gpsimd.dma_start(out=dst[r0:r1, :], in_=src[r0:r1, :]).then_inc(sem, 16)
            buf.cw_n += 1
        buf.w = (buf.cw_key, 16 * buf.cw_n)
        buf.r = {}

    def cast_weight_old(self, src, dst, K, N, buf):
        c = self.c
        CW = 2048
        for kc in range(K // 128):
            for c0 in range(0, N, CW):
                w = min(CW, N - c0)
                i = self.cast_i % 2
                self.cast_i += 1
                st, stb, bst, bstb = self.cast_st[i]
                c.dma("pool", st[:, 0:w], src[kc * 128:(kc + 1) * 128, c0:c0 + w], writes=[bst])
                c.op("pool", lambda h: h.tensor_copy(out=stb[:, 0:w], in_=st[:, 0:w]), reads=[bst], writes=[bstb])
                c.dma("pool", dst[kc * 128:(kc + 1) * 128, c0:c0 + w], stb[:, 0:w], reads=[bstb], writes=[buf])

    def rmsnorm_tile(self, hx, bhx, gcol, xn, bxn, ncols, tmp):
        c = self.c
        sq, bsq, rt, brt = tmp
        KH = sq.shape[1]
        for c0 in range(0, ncols, 512):
            ps, bps = self.ps_next()
            for k0 in range(0, KC, KH):
                c.op("act", lambda h: h.activation(out=sq[:, :, :], in_=hx[:, k0:k0 + KH, c0:c0 + 512], func=AF.Square),
                     reads=[bhx], writes=[bsq])
                for kk in range(KH):
                    kc = k0 + kk
                    c.op("pe", lambda h: h.matmul(ps[:, :], lhsT=self.ones_f[:, :], rhs=sq[:, kk, :],
                                                  start=(kc == 0), stop=(kc == KC - 1)),
                         reads=[bsq, self.bconst], writes=[bps])
            c.op("act", lambda h: h.activation(out=rt[:, :], in_=ps[:, :], func=AF.Sqrt, scale=1.0 / D,
                                               bias=self.eps_t[:, 0:1]),
                 reads=[bps, self.bconst], writes=[brt])
            c.op("dve", lambda h: h.reciprocal(out=rt[:, :], in_=rt[:, :]), reads=[brt], writes=[brt])
            for kc in range(KC):
                c.op("dve", lambda h: h.scalar_tensor_tensor(out=xn[:, kc, c0:c0 + 512], in0=hx[:, kc, c0:c0 + 512],
                                                             scalar=self.pvec[:, gcol + kc:gcol + kc + 1],
                                                             in1=rt[:, :], op0=ALU.mult, op1=ALU.mult),
                     reads=[bhx, brt, self.bconst], writes=[bxn])

    def run_hook(self):
        h = getattr(self, "hook", None)
        if h is not None:
            self.hook = None
            h()

    def ps_next(self, pool=None):
        if pool is None:
            i = self.ps_i % len(self.ps)
            self.ps_i += 1
            return self.ps[i], self.bps[i]
        lst = self.ps_pools[pool]
        i = lst[self.ps_pi.get(pool, 0) % len(lst)]
        self.ps_pi[pool] = self.ps_pi.get(pool, 0) + 1
        return self.ps[i], self.bps[i]

    def ffn_phase(self, layer):
        c = self.c
        nc = self.nc
        TS = 1024
        hT3 = self.hT.rearrange("(kc p) t -> p kc t", p=128)
        wgu = self.w_gu[layer]
        wd = self.w_d[layer]
        bw = self.bw_ffn[layer]
        with contextlib.ExitStack() as st:
            hx2 = [self.sb(st, "f_hx%d" % i, [128, KC, TS], F32) for i in range(2)]
            sq = self.sb(st, "f_sq", [128, 1, 512], F32)
            rt = self.sb(st, "f_rt", [128, 512], F32)
            xn2 = [self.sb(st, "f_xn%d" % i, [128, KC, TS], BF16) for i in range(2)]
            hf = self.sb(st, "f_hf", [128, DFF // 128, TS], BF16)
            wg = [self.sb(st, "f_wg%d" % i, [128, KC, 512], BF16) for i in range(2)]
            wu = [self.sb(st, "f_wu%d" % i, [128, KC, 512], BF16) for i in range(2)]
            wdn = [self.sb(st, "f_wd%d" % i, [128, DFF // 128, 256], BF16) for i in range(2)]
            sg = [self.sb(st, "f_sg%d" % i, [128, 512], BF16) for i in range(2)]
            bsq, brt = Buf(), Buf()
            bhx2, bxn2 = [Buf(), Buf()], [Buf(), Buf()]
            bhf = [Buf() for _ in range(DFF // 128)]
            bwg = [Buf(), Buf()]
            bwdn = [Buf(), Buf()]
            bsg = [Buf(), Buf()]
            wgu3 = wgu.rearrange("(kc p) n -> p kc n", p=128)
            wd3 = wd.rearrange("(kc p) n -> p kc n", p=128)
            NG = DFF // 512
            groups = [(g0, min(512, DFF - g0)) for g0 in range(0, DFF, 512)]
            def prep(ti_):
                t0_ = ti_ * TS
                u_ = ti_ % 2
                c.dma("sp", hx2[u_][:, :, :], hT3[:, :, t0_:t0_ + TS], reads=[self.bh[ti_]], writes=[bhx2[u_]])
                self.rmsnorm_tile(hx2[u_], bhx2[u_], self.col_nffn + layer * KC, xn2[u_], bxn2[u_], TS, (sq, bsq, rt, brt))

            prep(0)
            for t0 in range(0, S, TS):
                bh = self.bh[t0 // TS]
                u = (t0 // TS) % 2
                hx, bhx, xn, bxn = hx2[u], bhx2[u], xn2[u], bxn2[u]
                for gi, (g0, gw) in enumerate(groups):
                    b = gi % 2
                    c.dma("sp", wg[b][:, :, 0:gw], wgu3[:, :, g0:g0 + gw], reads=[bw], writes=[bwg[b]])
                    c.dma("sp", wu[b][:, :, 0:gw], wgu3[:, :, DFF + g0:DFF + g0 + gw], reads=[bw], writes=[bwg[b]])
                    for j in range(gw // 128):
                        fc = (g0 // 128) + j
                        for ct in range(TS // 512):
                            cs = slice(ct * 512, (ct + 1) * 512)
                            pg, bpg = self.ps_next()
                            pu, bpu = self.ps_next()
                            for kc in range(KC):
                                c.op("pe", lambda h: h.matmul(pg[:, :], lhsT=wg[b][:, kc, j * 128:(j + 1) * 128],
                                                              rhs=xn[:, kc, cs], start=(kc == 0), stop=(kc == KC - 1)),
                                     reads=[bwg[b], bxn], writes=[bpg])
                            for kc in range(KC):
                                c.op("pe", lambda h: h.matmul(pu[:, :], lhsT=wu[b][:, kc, j * 128:(j + 1) * 128],
                                                              rhs=xn[:, kc, cs], start=(kc == 0), stop=(kc == KC - 1)),
                                     reads=[bwg[b], bxn], writes=[bpu])
                            si = self.sg_i % 2
                            self.sg_i += 1
                            c.op("act", lambda h: h.activation(out=sg[si][:, :], in_=pg[:, :], func=AF.Silu),
                                 reads=[bpg], writes=[bsg[si]])
                            c.op("dve", lambda h: h.tensor_tensor(out=hf[:, fc, cs], in0=pu[:, :], in1=sg[si][:, :],
                                                                  op=ALU.mult),
                                 reads=[bpu, bsg[si]], writes=[bhf[fc]])
                if t0 + TS < S:
                    prep(t0 // TS + 1)
                self.run_hook()
                for dg in range(D // 256):
                    b = dg % 2
                    c.dma("sp", wdn[b][:, :, :], wd3[:, :, dg * 256:(dg + 1) * 256], reads=[bw], writes=[bwdn[b]])
                    for j in range(2):
                        dc = dg * 2 + j
                        for ct in range(TS // 512):
                            cs = slice(ct * 512, (ct + 1) * 512)
                            py, bpy = self.ps_next()
                            nk = DFF // 128
                            for kc in range(nk):
                                c.op("pe", lambda h: h.matmul(py[:, :], lhsT=wdn[b][:, kc, j * 128:(j + 1) * 128],
                                                              rhs=hf[:, kc, cs], start=(kc == 0), stop=(kc == nk - 1)),
                                     reads=[bwdn[b], bhf[kc]], writes=[bpy])
                            c.op("dve", lambda h: h.tensor_tensor(out=hx[:, dc, cs], in0=py[:, :], in1=hx[:, dc, cs],
                                                                  op=ALU.add),
                                 reads=[bpy, bhx], writes=[bhx])
                c.dma("sp", hT3[:, :, t0:t0 + TS], hx[:, :, :], reads=[bhx], writes=[bh])
            c.barrier()

    def load_cast(self, st, dst, bdst, src, shape_cols, tag):
        c = self.c
        P = dst.shape[0]
        stg = self.sb(st, "lc_" + tag, [P, 2048], F32)
        bs = Buf()
        for c0 in range(0, shape_cols, 2048):
            w = min(2048, shape_cols - c0)
            c.dma("sp", stg[:, 0:w], src[:, c0:c0 + w], writes=[bs])
            c.op("dve", lambda h: h.tensor_copy(out=dst[:, c0:c0 + w], in_=stg[:, 0:w]), reads=[bs], writes=[bdst])

    def range_reduce_sincos(self, st, x, bx, n, tag, want_cos=True):
        c = self.c
        TWO_PI = 2.0 * np.pi
        ki = self.sb(st, tag + "_ki", [128, n], I32)
        kf = self.sb(st, tag + "_kf", [128, n], F32)
        sn = self.sb(st, tag + "_sn", [128, n], F32)
        cs = self.sb(st, tag + "_cs", [128, n], F32) if want_cos else None
        bki, bkf, bsn, bcs = Buf(), Buf(), Buf(), Buf()
        self._rr(x, bx, n, ki, bki, kf, bkf, sn, bsn, cs, bcs)
        return sn, bsn, cs, bcs

    def _rr(self, x, bx, n, ki, bki, kf, bkf, sn, bsn, cs, bcs):
        c = self.c
        TWO_PI = 2.0 * np.pi
        c.op("act", lambda h: h.activation(out=ki[:, 0:n], in_=x[:, 0:n], func=AF.Identity, scale=1.0 / TWO_PI),
             reads=[bx], writes=[bki])
        c.op("dve", lambda h: h.scalar_tensor_tensor(out=sn[:, 0:n], in0=ki[:, 0:n], scalar=-TWO_PI, in1=x[:, 0:n],
                                                     op0=ALU.mult, op1=ALU.add), reads=[bki, bx], writes=[bsn])
        c.op("dve", lambda h: h.tensor_scalar(out=sn[:, 0:n], in0=sn[:, 0:n], scalar1=-PI_LO, scalar2=PI_LO,
                                              op0=ALU.max, op1=ALU.min), reads=[bsn], writes=[bsn])
        if cs is not None:
            c.op("dve", lambda h: h.scalar_tensor_tensor(out=cs[:, 0:n], in0=sn[:, 0:n], scalar=-1.0, in1=sn[:, 0:n],
                                                         op0=ALU.mult, op1=ALU.max), reads=[bsn], writes=[bcs])
        c.op("act", lambda h: h.activation(out=sn[:, 0:n], in_=sn[:, 0:n], func=AF.Sin), reads=[bsn], writes=[bsn])
        if cs is not None:
            c.op("act", lambda h: h.activation(out=cs[:, 0:n], in_=cs[:, 0:n], func=AF.Sin, scale=-1.0,
                                               bias=self.halfpi_t[:, 0:1]), reads=[bcs, self.bconst], writes=[bcs])

    def qkv_phase(self, layer, w_s, wp_s, bw, bwp):
        c = self.c
        hT3 = self.hT.rearrange("(kc p) t -> p kc t", p=128)
        QT3 = self.QT.rearrange("(kc p) t -> p kc t", p=128)
        KT3 = self.KT.rearrange("(kc p) t -> p kc t", p=128)
        w3 = w_s.rearrange("(kc p) n -> p kc n", p=128)
        self.ps_pools = {"a": [0, 1, 2, 3, 4, 5, 6, 7]}
        self.ps_pi = {}
        with contextlib.ExitStack() as st:
            SF = self.sb(st, "q_SF", [128, S], F32)
            CF = self.sb(st, "q_CF", [128, S], F32)
            bSF, bCF = Buf(), Buf()
            with contextlib.ExitStack() as st2:
                posi = self.sb(st2, "q_posi", [128, S], I32)
                x = self.sb(st2, "q_x", [128, S], F32)
                bposi, bx = Buf(), Buf()
                c.dma("sp", posi[:, :], self.posb[:, :], writes=[bposi])
                c.op("dve", lambda h: h.tensor_copy(out=x[:, :], in_=posi[:, :]), reads=[bposi], writes=[bx])
                c.op("dve", lambda h: h.tensor_scalar(out=x[:, :], in0=x[:, :], scalar1=self.pvec[:, self.col_invf:self.col_invf + 1],
                                                      scalar2=None, op0=ALU.mult), reads=[bx, self.bconst], writes=[bx])
                ki = posi
                kf = self.sb(st2, "q_kf", [128, S], F32)
                self._rr(x, bx, S, ki, bposi, kf, Buf(), SF, bSF, CF, bCF)
                c.op("dve", lambda h: h.tensor_scalar(out=SF[:, :], in0=SF[:, :],
                                                      scalar1=self.pvec[:, self.col_sgnrow:self.col_sgnrow + 1],
                                                      scalar2=None, op0=ALU.mult), reads=[bSF, self.bconst], writes=[bSF])
                c.barrier()
            wq = self.sb(st, "q_w", [128, KC, 3 * D], BF16)
            pm = self.sb(st, "q_pm", [128, 128], BF16)
            ab = [self.sb(st, "q_ab%d" % i, [128, 512], BF16) for i in range(2)]
            bab = [Buf(), Buf()]
            bpmc = Buf()
            with contextlib.ExitStack() as st3:
                self.load_cast(st3, pm, bpmc, self.ropeperm, 128, "pm")
                c.barrier()
            bwq = Buf()
            for i in range(3):
                c.dma("sp", wq[:, :, i * D:(i + 1) * D], w3[:, :, i * D:(i + 1) * D], reads=[bw], writes=[bwq])
            hx = self.sb(st, "q_hx", [128, KC, 512], F32)
            sq = self.sb(st, "q_sq", [128, 4, 512], F32)
            rt = self.sb(st, "q_rt", [128, 512], F32)
            xn = self.sb(st, "q_xn", [128, KC, 512], BF16)
            qo = [self.sb(st, "q_qo%d" % i, [128, KC, 512], BF16) for i in range(2)]
            tu = [self.sb(st, "q_tu%d" % i, [128, 512], F32) for i in range(2)]
            tt_ = [self.sb(st, "q_tt%d" % i, [128, 512], F32) for i in range(2)]
            vt = [self.sb(st, "q_vt%d" % i, [128, 16, 128], BF16) for i in range(2)]
            bhx, bsq, brt, bxn = Buf(), Buf(), Buf(), Buf()
            bqo = [Buf(), Buf()]
            btu = [Buf(), Buf()]
            btt = [Buf(), Buf()]
            bvt = [Buf(), Buf()]
            for i in range(2):
                c.op("dve", lambda h: h.memset(vt[i][:, :, 64:128], 1.0), writes=[bvt[i]])
            ei = 0
            vi = 0
            for tile in range(S // 512):
                cs = slice(tile * 512, (tile + 1) * 512)
                bh = self.bh[tile // 2]
                if tile == 2:
                    self.run_hook()
                c.dma("sp", hx[:, :, :], hT3[:, :, cs], reads=[bh], writes=[bhx])
                self.rmsnorm_tile(hx, bhx, self.col_nmix + layer * KC, xn, bxn, 512, (sq, bsq, rt, brt))
                for which in range(2):
                    for oc in range(KC):
                        pa, bpa = self.ps_next("a")
                        pb, bpb = self.ps_next("a")
                        col = which * D + oc * 128
                        for kc in range(KC):
                            c.op("pe", lambda h: h.matmul(pa[:, :], lhsT=wq[:, kc, col:col + 128], rhs=xn[:, kc, :],
                                                          start=(kc == 0), stop=(kc == KC - 1)),
                                 reads=[bwq, bxn], writes=[bpa])
                        e = ei % 2
                        ei += 1
                        c.op("act", lambda h: h.activation(out=ab[e][:, :], in_=pa[:, :], func=AF.Identity),
                             reads=[bpa], writes=[bab[e]])
                        c.op("pe", lambda h: h.matmul(pb[:, :], lhsT=pm[:, :], rhs=ab[e][:, :], start=True, stop=True),
                             reads=[bab[e], bpmc], writes=[bpb])
                        c.op("dve", lambda h: h.tensor_tensor(out=tu[e][:, :], in0=pa[:, :], in1=CF[:, cs], op=ALU.mult),
                             reads=[bpa, bCF, bab[e]], writes=[btu[e]])
                        c.op("dve", lambda h: h.tensor_tensor(out=tt_[e][:, :], in0=pb[:, :], in1=SF[:, cs], op=ALU.mult),
                             reads=[bpb, bSF], writes=[btt[e]])
                        c.op("dve", lambda h: h.tensor_tensor(out=qo[which][:, oc, :], in0=tu[e][:, :], in1=tt_[e][:, :],
                                                               op=ALU.add),
                             reads=[btu[e], btt[e]], writes=[bqo[which]])
                    dst = QT3 if which == 0 else KT3
                    c.dma("sp", dst[:, :, cs], qo[which][:, :, :], reads=[bqo[which]], writes=[self.bqkv])
                for sub in range(4):
                    v = vi % 2
                    vi += 1
                    for half in range(2):
                        pv, bpv = self.ps_next("a")
                        for kc in range(KC):
                            c.op("pe", lambda h: h.matmul(pv[:, :], lhsT=xn[:, kc, sub * 128:(sub + 1) * 128],
                                                          rhs=wq[:, kc, 2 * D + half * 512:2 * D + (half + 1) * 512],
                                                          start=(kc == 0), stop=(kc == KC - 1)),
                                 reads=[bwq, bxn], writes=[bpv])
                        c.op("act", lambda h: h.activation(out=vt[v][:, half * 8:(half + 1) * 8, 0:64],
                                                           in_=pv[:, :].rearrange("p (a b) -> p a b", b=64),
                                                           func=AF.Identity),
                             reads=[bpv], writes=[bvt[v]])
                    c.dma("sp", self.V1s[:, :, tile * 4 + sub, :].rearrange("h p e -> p h e"), vt[v][:, :, :],
                          reads=[bvt[v]], writes=[self.bqkv])
            c.barrier()

    def attn_phase(self, kind):
        c = self.c
        moba = kind == "moba"
        KR = 80 if moba else 64
        self.ps_pools = {"s": [0, 1, 2, 3], "o": [4, 5], "g": [6, 7]}
        self.ps_pi = {}
        with contextlib.ExitStack() as st:
            nm = 4 if moba else 20
            mk = self.sb(st, "a_mk", [128, nm * 512], BF16)
            bmk = Buf()
            with contextlib.ExitStack() as st2:
                self.load_cast(st2, mk, bmk, (self.mmask if moba else self.dmask), nm * 512, "mk")
                c.barrier()
            kaug = [self.sb(st, "a_k%d" % i, [KR, S], BF16) for i in range(2)]
            qaug = [self.sb(st, "a_q%d" % i, [KR, S], BF16) for i in range(2)]
            v1 = [self.sb(st, "a_v%d" % i, [128, 32, 128], BF16) for i in range(2)]
            ot = [self.sb(st, "a_o%d" % i, [64, S], BF16) for i in range(2)]
            pt = [self.sb(st, "a_p%d" % i, [128, 512], BF16) for i in range(8)]
            rd = [self.sb(st, "a_rd%d" % i, [64, 512], F32) for i in range(2)]
            bk = [Buf(), Buf()]
            bq = [Buf(), Buf()]
            bv = [Buf(), Buf()]
            bot = [Buf(), Buf()]
            bpt = [Buf() for _ in range(8)]
            brd = [Buf(), Buf()]
            if moba:
                with contextlib.ExitStack() as st2:
                    oh = self.sb(st2, "a_oh", [16, S], F32)
                    boh = Buf()
                    c.dma("sp", oh[:, :], self.onehot[:, :], writes=[boh])
                    for i in range(2):
                        c.op("dve", lambda h: h.tensor_copy(out=kaug[i][64:80, :], in_=oh[:, :]), reads=[boh], writes=[bk[i]])
                    c.barrier()
                pastneg = self.sb(st, "a_pn", [128, 512], F32)
                own = self.sb(st, "a_own", [128, 512], F32)
                ident = self.sb(st, "a_id", [128, 128], F32)
                bcn = Buf()
                c.dma("sp", pastneg[:, :], self.pastneg[:, :], writes=[bcn])
                c.dma("sp", own[:, :], self.own[:, :], writes=[bcn])
                c.dma("sp", ident[:, :], self.ident[:, :], writes=[bcn])
                km = self.sb(st, "a_km", [64, 16], F32)
                kmb = self.sb(st, "a_kmb", [64, 16], BF16)
                gm = self.sb(st, "a_gm", [128, 512], F32)
                m8 = self.sb(st, "a_m8", [128, 32, 8], F32)
                thr = self.sb(st, "a_thr", [128, 32], F32)
                sel = self.sb(st, "a_sel", [128, 512], F32)
                bkm, bkmb, bgm, bm8, bthr, bsel = Buf(), Buf(), Buf(), Buf(), Buf(), Buf()
            ri = 0

            def head_prep(hd):
                b = hd % 2
                c.dma("sp", kaug[b][0:64, :], self.KT[hd * 64:(hd + 1) * 64, :], reads=[self.bqkv], writes=[bk[b]])
                c.dma("sp", qaug[b][0:64, :], self.QT[hd * 64:(hd + 1) * 64, :], reads=[self.bqkv], writes=[bq[b]])
                c.dma("sp", v1[b][:, :, :], self.V1s[hd, :, :, :], reads=[self.bqkv], writes=[bv[b]])
                if moba:
                    c.op("dve", lambda h: h.tensor_reduce(out=km[:, :], in_=kaug[b][0:64, :].rearrange("p (n k) -> p n k", k=256),
                                                          axis=mybir.AxisListType.X, op=ALU.add),
                         reads=[bk[b]], writes=[bkm])
                    c.op("dve", lambda h: h.tensor_scalar(out=kmb[:, :], in0=km[:, :], scalar1=1.0 / 256, scalar2=None,
                                                          op0=ALU.mult), reads=[bkm], writes=[bkmb])
                    pg, bpg = self.ps_next("g")
                    for qt_ in range(32):
                        c.op("pe", lambda h: h.matmul(pg[:, qt_ * 16:(qt_ + 1) * 16], lhsT=qaug[b][0:64, qt_ * 128:(qt_ + 1) * 128],
                                                      rhs=kmb[:, :], start=True, stop=True),
                             reads=[bq[b], bkmb], writes=[bpg])
                    c.op("dve", lambda h: h.tensor_tensor(out=gm[:, :], in0=pg[:, :], in1=pastneg[:, :], op=ALU.add),
                         reads=[bpg, bcn], writes=[bgm])
                    for qt_ in range(32):
                        c.op("dve", lambda h: h.max(out=m8[:, qt_, :], in_=gm[:, qt_ * 16:(qt_ + 1) * 16]),
                             reads=[bgm], writes=[bm8])
                    c.op("dve", lambda h: h.tensor_scalar(out=thr[:, :], in0=m8[:, :, 2], scalar1=-1e29, scalar2=None,
                                                          op0=ALU.max), reads=[bm8], writes=[bthr])
                    c.op("dve", lambda h: h.tensor_tensor(out=sel[:, :].rearrange("p (a b) -> p a b", b=16),
                                                          in0=gm[:, :].rearrange("p (a b) -> p a b", b=16),
                                                          in1=thr[:, :].unsqueeze(2).to_broadcast([128, 32, 16]),
                                                          op=ALU.is_ge), reads=[bgm, bthr], writes=[bsel])
                    c.op("dve", lambda h: h.tensor_tensor(out=sel[:, :], in0=sel[:, :], in1=own[:, :], op=ALU.add),
                         reads=[bsel, bcn], writes=[bsel])
                    c.op("dve", lambda h: h.tensor_scalar(out=sel[:, :], in0=sel[:, :], scalar1=-1.0, scalar2=32768.0,
                                                          op0=ALU.add, op1=ALU.mult), reads=[bsel], writes=[bsel])
                    for g4 in range(8):
                        pt_, bpt_ = self.ps_next("g")
                        for j in range(4):
                            qt_ = g4 * 4 + j
                            c.op("pe", lambda h: h.transpose(pt_[0:16, j * 128:(j + 1) * 128], sel[:, qt_ * 16:(qt_ + 1) * 16],
                                                             ident[:, :]),
                                 reads=[bsel, bcn], writes=[bpt_])
                        c.op("act", lambda h: h.activation(out=qaug[b][64:80, g4 * 512:(g4 + 1) * 512], in_=pt_[0:16, :],
                                                           func=AF.Identity), reads=[bpt_], writes=[bq[b]])

            items = []
            for hd in range(16):
                for qt in range(8):
                    k0 = 0 if moba else max(0, 4 * qt - 16)
                    kts = list(range(k0, 4 * qt + 4))
                    for idx, kt in enumerate(kts):
                        items.append((hd, qt, idx, kt, len(kts)))
            LA = 5
            state = {}

            def front(i, it):
                hd, qt, idx, kt, n = it
                b = hd % 2
                pss, bpss = self.ps_next("s")
                c.op("pe", lambda h: h.matmul(pss[:, :], lhsT=kaug[b][0:KR, kt * 128:(kt + 1) * 128],
                                              rhs=qaug[b][0:KR, qt * 512:(qt + 1) * 512], start=True, stop=True),
                     reads=[bk[b], bq[b]], writes=[bpss])
                p = i % 8
                c.op("act", lambda h: h.activation(out=pt[p][:, :], in_=pss[:, :], func=AF.Exp, scale=0.125),
                     reads=[bpss], writes=[bpt[p]])
                off = kt - 4 * qt
                m = None
                if moba:
                    if off >= 0:
                        m = off
                else:
                    m = off + 16
                if m is not None:
                    eng = "dve"
                    c.op(eng, lambda h: h.tensor_tensor(out=pt[p][:, :], in0=pt[p][:, :], in1=mk[:, m * 512:(m + 1) * 512],
                                                        op=ALU.mult), reads=[bpt[p], bmk], writes=[bpt[p]])

            def back(i, it):
                nonlocal ri
                hd, qt, idx, kt, n = it
                b = hd % 2
                p = i % 8
                if qt == 0 and idx == 0 and hd + 1 < 16:
                    head_prep(hd + 1)
                if idx == 0:
                    state["po"] = self.ps_next("o")
                po, bpo = state["po"]
                c.op("pe", lambda h: h.matmul(po[:, :], lhsT=v1[b][:, kt, :], rhs=pt[p][:, :],
                                              start=(idx == 0), stop=(idx == n - 1)),
                     reads=[bv[b], bpt[p]], writes=[bpo])
                if idx == n - 1:
                    r = ri % 2
                    ri += 1
                    c.op("dve", lambda h: h.reciprocal(out=rd[r][:, :], in_=po[64:128, :]), reads=[bpo], writes=[brd[r]])
                    c.op("dve", lambda h: h.tensor_tensor(out=ot[b][:, qt * 512:(qt + 1) * 512], in0=po[0:64, :], in1=rd[r][:, :],
                                                          op=ALU.mult), reads=[bpo, brd[r]], writes=[bot[b]])
                    if qt == 7:
                        c.dma("sp", self.OT[hd * 64:(hd + 1) * 64, :], ot[b][:, :], reads=[bot[b]], writes=[self.bot])

            head_prep(0)
            for i in range(len(items) + LA):
                if i < len(items):
                    front(i, items[i])
                if i >= LA:
                    back(i - LA, items[i - LA])
            c.barrier()

    def linres_phase(self, inT, w_s, bw, bin_):
        c = self.c
        hT3 = self.hT.rearrange("(kc p) t -> p kc t", p=128)
        in3 = inT.rearrange("(kc p) t -> p kc t", p=128)
        w3 = w_s.rearrange("(kc p) n -> p kc n", p=128)
        with contextlib.ExitStack() as st:
            w = self.sb(st, "l_w", [128, KC, D], BF16)
            bwl = Buf()
            c.dma("sp", w[:, :, :], w3[:, :, :], reads=[bw], writes=[bwl])
            a = [self.sb(st, "l_a%d" % i, [128, KC, 512], BF16) for i in range(2)]
            hx = [self.sb(st, "l_h%d" % i, [128, KC, 512], F32) for i in range(2)]
            ba = [Buf(), Buf()]
            bhx = [Buf(), Buf()]
            for tile in range(S // 512):
                cs = slice(tile * 512, (tile + 1) * 512)
                b = tile % 2
                bh = self.bh[tile // 2]
                c.dma("sp", a[b][:, :, :], in3[:, :, cs], reads=[bin_], writes=[ba[b]])
                c.dma("sp", hx[b][:, :, :], hT3[:, :, cs], reads=[bh], writes=[bhx[b]])
                for dc in range(KC):
                    ps, bps = self.ps_next()
                    for kc in range(KC):
                        c.op("pe", lambda h: h.matmul(ps[:, :], lhsT=w[:, kc, dc * 128:(dc + 1) * 128], rhs=a[b][:, kc, :],
                                                      start=(kc == 0), stop=(kc == KC - 1)),
                             reads=[bwl, ba[b]], writes=[bps])
                    c.op("dve", lambda h: h.tensor_tensor(out=hx[b][:, dc, :], in0=ps[:, :], in1=hx[b][:, dc, :], op=ALU.add),
                         reads=[bps, bhx[b]], writes=[bhx[b]])
                c.dma("sp", hT3[:, :, cs], hx[b][:, :, :], reads=[bhx[b]], writes=[bh])
            c.barrier()

    def cast_weight_qk_perm(self, src, dst, buf):
        c = self.c
        for kc in range(KC):
            for c0 in range(0, 2 * D, 512):
                i = self.cast_i % 2
                self.cast_i += 1
                st, stb, bst, bstb = self.cast_st[i]
                c.dma("pool", st[:, :], src[kc * 128:(kc + 1) * 128, c0:c0 + 512], writes=[bst])
                st3 = st[:, :].rearrange("p (a b) -> p a b", b=64)
                sb3 = stb[:, :].rearrange("p (a b) -> p a b", b=64)
                c.op("pool", lambda h: h.tensor_copy(out=stb[:, :], in_=st[:, :]), reads=[bst], writes=[bstb])
                c.op("pool", lambda h: h.tensor_copy(out=sb3[:, :, 0:8], in_=st3[:, :, 8:16]), reads=[bst], writes=[bstb])
                c.op("pool", lambda h: h.tensor_copy(out=sb3[:, :, 8:16], in_=st3[:, :, 0:8]), reads=[bst], writes=[bstb])
                c.dma("pool", dst[kc * 128:(kc + 1) * 128, c0:c0 + 512], stb[:, :], reads=[bstb], writes=[buf])

    def hgrn_prep(self, layer, w_s, bw, ebl, bebl):
        c = self.c
        hT3 = self.hT.rearrange("(kc p) t -> p kc t", p=128)
        QT3 = self.QT.rearrange("(kc p) t -> p kc t", p=128)
        KT3 = self.KT.rearrange("(kc p) t -> p kc t", p=128)
        GT3 = self.GT.rearrange("(kc p) t -> p kc t", p=128)
        w3 = w_s.rearrange("(kc p) n -> p kc n", p=128)
        self.ps_pools = {"a": [0, 1, 2, 3, 4, 5], "t": [6, 7]}
        self.ps_pi = {}
        with contextlib.ExitStack() as st:
            w = self.sb(st, "h_w", [128, KC, 4 * D], BF16)
            bww = Buf()
            for i in range(4):
                c.dma("sp", w[:, :, i * D:(i + 1) * D], w3[:, :, i * D:(i + 1) * D], reads=[bw], writes=[bww])
            ex = self.sb(st, "h_ex", [128, 4 * KC], F32)
            ssum = self.sb(st, "h_ss", [128, KC], F32)
            lb = self.sb(st, "h_lb", [128, KC], F32)
            oml = self.sb(st, "h_oml", [128, KC], F32)
            blb = Buf()
            cl = self.col_lb
            c.op("act", lambda h: h.activation(out=ex[:, :], in_=self.pvec[:, cl:cl + 4 * KC], func=AF.Exp),
                 reads=[self.bconst], writes=[blb])
            c.op("dve", lambda h: h.tensor_tensor(out=ssum[:, :], in0=ex[:, 0:KC], in1=ex[:, KC:2 * KC], op=ALU.add),
                 reads=[blb], writes=[blb])
            c.op("dve", lambda h: h.tensor_tensor(out=ssum[:, :], in0=ssum[:, :], in1=ex[:, 2 * KC:3 * KC], op=ALU.add),
                 reads=[blb], writes=[blb])
            c.op("dve", lambda h: h.tensor_tensor(out=ssum[:, :], in0=ssum[:, :], in1=ex[:, 3 * KC:4 * KC], op=ALU.add),
                 reads=[blb], writes=[blb])
            c.op("dve", lambda h: h.reciprocal(out=ssum[:, :], in_=ssum[:, :]), reads=[blb], writes=[blb])
            c.op("dve", lambda h: h.tensor_copy(out=lb[:, :], in_=ex[:, KC:2 * KC]), reads=[blb], writes=[blb])
            for l in range(2, layer + 1):
                c.op("dve", lambda h: h.tensor_tensor(out=lb[:, :], in0=lb[:, :], in1=ex[:, l * KC:(l + 1) * KC], op=ALU.add),
                     reads=[blb], writes=[blb])
            c.op("dve", lambda h: h.tensor_tensor(out=lb[:, :], in0=lb[:, :], in1=ssum[:, :], op=ALU.mult),
                 reads=[blb], writes=[blb])
            c.op("dve", lambda h: h.tensor_scalar(out=oml[:, :], in0=lb[:, :], scalar1=-1.0, scalar2=1.0, op0=ALU.mult, op1=ALU.add),
                 reads=[blb], writes=[blb])
            rmask = self.sb(st, "h_rm", [128, 512], F32)
            identb = self.sb(st, "h_idb", [128, 128], BF16)
            identf = self.sb(st, "h_idf", [128, 128], F32)
            bcn = Buf()
            c.dma("sp", rmask[:, :], self.rmask[:, :], writes=[bcn])
            c.dma("sp", identf[:, :], self.ident[:, :], writes=[bcn])
            c.op("dve", lambda h: h.tensor_copy(out=identb[:, :], in_=identf[:, :]), reads=[bcn], writes=[bcn])
            hx = self.sb(st, "h_hx", [128, KC, 512], F32)
            sq = self.sb(st, "h_sq", [128, 2, 512], F32)
            rt = self.sb(st, "h_rt", [128, 512], F32)
            xn = self.sb(st, "h_xn", [128, KC, 512], BF16)
            bhx, bsq, brt, bxn = Buf(), Buf(), Buf(), Buf()
            names = ["qs", "th", "fg", "b", "eb", "ebn", "kk"]
            TT = [{n: self.sb(st, "h_t_%s%d" % (n, i), [128, 512], F32) for n in names} for i in range(2)]
            BB_ = [{n: Buf() for n in names} for i in range(2)]
            A_ = self.sb(st, "h_A", [128, KC], F32)
            nA_ = self.sb(st, "h_nA", [128, KC], F32)
            B_ = self.sb(st, "h_B", [128, KC], F32)
            c.op("dve", lambda h: h.tensor_scalar(out=A_[:, :], in0=oml[:, :], scalar1=0.5, scalar2=None, op0=ALU.mult), reads=[blb], writes=[blb])
            c.op("dve", lambda h: h.tensor_scalar(out=nA_[:, :], in0=oml[:, :], scalar1=-0.5, scalar2=None, op0=ALU.mult), reads=[blb], writes=[blb])
            c.op("dve", lambda h: h.tensor_tensor(out=B_[:, :], in0=lb[:, :], in1=A_[:, :], op=ALU.add), reads=[blb], writes=[blb])
            qo = self.sb(st, "h_qo", [128, KC, 512], BF16)
            ko = self.sb(st, "h_ko", [128, KC, 512], BF16)
            go = self.sb(st, "h_go", [128, KC, 512], BF16)
            ktm = self.sb(st, "h_ktm", [128, 8, 4, 128], BF16)
            itm = [self.sb(st, "h_itm%d" % i, [128, 8, 128], BF16) for i in range(2)]
            bqo, bko, bgo, bktm = Buf(), Buf(), Buf(), Buf()
            bitm = [Buf(), Buf()]
            ii = 0
            for tile in range(S // 512):
                cs = slice(tile * 512, (tile + 1) * 512)
                bh = self.bh[tile // 2]
                if tile == 2:
                    self.run_hook()
                c.dma("sp", hx[:, :, :], hT3[:, :, cs], reads=[bh], writes=[bhx])
                self.rmsnorm_tile(hx, bhx, self.col_nmix + layer * KC, xn, bxn, 512, (sq, bsq, rt, brt))
                def proj(hp_):
                    PS_ = {}
                    for i, hh in enumerate((2 * hp_, 2 * hp_ + 1)):
                        PS_[i] = [self.ps_next("a") for _ in range(3)]
                        for (pp, bpp), off in zip(PS_[i], (0, D, 3 * D)):
                            col = off + hh * 128
                            for kc in range(KC):
                                c.op("pe", lambda h: h.matmul(pp[:, :], lhsT=w[:, kc, col:col + 128], rhs=xn[:, kc, :],
                                                              start=(kc == 0), stop=(kc == KC - 1)),
                                     reads=[bww, bxn], writes=[bpp])
                    return PS_

                PSn = proj(0)
                for hp in range(4):
                    hs = (2 * hp, 2 * hp + 1)
                    PS = PSn
                    for i, hh in enumerate(hs):
                        T, B = TT[i], BB_[i]
                        (pq, bpq), (pf, bpf), (pg, bpg) = PS[i]
                        c.op("act", lambda h: h.activation(out=T["qs"][:, :], in_=pq[:, :], func=AF.Silu), reads=[bpq], writes=[B["qs"]])
                        c.op("act", lambda h: h.activation(out=go[:, hh, :], in_=pg[:, :], func=AF.Silu), reads=[bpg], writes=[bgo])
                        c.op("act", lambda h: h.activation(out=T["th"][:, :], in_=pf[:, :], func=AF.Tanh, scale=0.5), reads=[bpf], writes=[B["th"]])
                    if hp + 1 < 4:
                        PSn = proj(hp + 1)
                    for i, hh in enumerate(hs):
                        T, B = TT[i], BB_[i]
                        c.op("dve", lambda h: h.tensor_scalar(out=T["fg"][:, :], in0=T["th"][:, :], scalar1=A_[:, hh:hh + 1],
                                                              scalar2=B_[:, hh:hh + 1], op0=ALU.mult, op1=ALU.add),
                             reads=[B["th"], blb], writes=[B["fg"]])
                        c.op("dve", lambda h: h.tensor_scalar(out=T["kk"][:, :], in0=T["th"][:, :], scalar1=nA_[:, hh:hh + 1],
                                                              scalar2=A_[:, hh:hh + 1], op0=ALU.mult, op1=ALU.add),
                             reads=[B["th"], blb], writes=[B["kk"]])
                    for i, hh in enumerate(hs):
                        T, B = TT[i], BB_[i]
                        c.op("act", lambda h: h.activation(out=T["fg"][:, :], in_=T["fg"][:, :], func=AF.Ln), reads=[B["fg"]], writes=[B["fg"]])
                    for i, hh in enumerate(hs):
                        T, B = TT[i], BB_[i]
                        c.op("dve", lambda h: h.tensor_tensor_scan(out=T["b"][:, :], data0=rmask[:, :], data1=T["fg"][:, :], initial=0.0,
                                                                   op0=ALU.mult, op1=ALU.add), reads=[B["fg"], bcn], writes=[B["b"]])
                    for i, hh in enumerate(hs):
                        T, B = TT[i], BB_[i]
                        c.op("act", lambda h: h.activation(out=T["eb"][:, :], in_=T["b"][:, :], func=AF.Exp), reads=[B["b"]], writes=[B["eb"]])
                        c.op("act", lambda h: h.activation(out=T["ebn"][:, :], in_=T["b"][:, :], func=AF.Exp, scale=-1.0),
                             reads=[B["b"]], writes=[B["ebn"]])
                    for i, hh in enumerate(hs):
                        T, B = TT[i], BB_[i]
                        c.op("dve", lambda h: h.tensor_tensor(out=qo[:, hh, :], in0=T["qs"][:, :], in1=T["eb"][:, :], op=ALU.mult),
                             reads=[B["qs"], B["eb"]], writes=[bqo])
                        c.op("dve", lambda h: h.tensor_tensor(out=ko[:, hh, :], in0=T["kk"][:, :], in1=T["ebn"][:, :], op=ALU.mult),
                             reads=[B["kk"], B["ebn"]], writes=[bko])
                        c.op("dve", lambda h: h.tensor_copy(out=ebl[:, hh, tile * 8:(tile + 1) * 8],
                                                            in_=T["eb"][:, :].rearrange("p (a b) -> p a b", b=64)[:, :, 63]),
                             reads=[B["eb"]], writes=[bebl])
                        ptt, bptt = self.ps_next("t")
                        ptb = ptt[:, 0:256].bitcast(BF16)
                        for sub in range(4):
                            c.op("pe", lambda h: h.transpose(ptb[:, sub * 128:(sub + 1) * 128], ko[:, hh, sub * 128:(sub + 1) * 128],
                                                             identb[:, :]), reads=[bko, bcn], writes=[bptt])
                        c.op("act", lambda h: h.activation(out=ktm[:, hh, :, :], in_=ptb.rearrange("p (a b) -> p a b", b=128),
                                                           func=AF.Identity), reads=[bptt], writes=[bktm])
                c.dma("sp", QT3[:, :, cs], qo[:, :, :], reads=[bqo], writes=[self.bqkv])
                c.dma("sp", KT3[:, :, cs], ko[:, :, :], reads=[bko], writes=[self.bqkv])
                c.dma("sp", GT3[:, :, cs], go[:, :, :], reads=[bgo], writes=[self.bqkv])
                c.dma("sp", self.V1s[0:8, :, tile * 4:(tile + 1) * 4, :].rearrange("h p s e -> p h s e"), ktm[:, :, :, :],
                      reads=[bktm], writes=[self.bqkv])
                for sub in range(4):
                    v = ii % 2
                    ii += 1
                    for half in range(2):
                        pv, bpv = self.ps_next("a")
                        for kc in range(KC):
                            c.op("pe", lambda h: h.matmul(pv[:, :], lhsT=xn[:, kc, sub * 128:(sub + 1) * 128],
                                                          rhs=w[:, kc, 2 * D + half * 512:2 * D + (half + 1) * 512],
                                                          start=(kc == 0), stop=(kc == KC - 1)),
                                 reads=[bww, bxn], writes=[bpv])
                        c.op("act", lambda h: h.activation(out=itm[v][:, half * 4:(half + 1) * 4, :],
                                                           in_=pv[:, :].rearrange("p (a b) -> p a b", b=128), func=AF.Identity),
                             reads=[bpv], writes=[bitm[v]])
                    c.dma("sp", self.V1s[8:16, :, tile * 4 + sub, :].rearrange("h p e -> p h e"), itm[v][:, :, :],
                          reads=[bitm[v]], writes=[self.bqkv])
            c.barrier()

    def hgrn_rec(self, ebl, bebl):
        c = self.c
        QT3 = self.QT.rearrange("(kc p) t -> p kc t", p=128)
        KT3 = self.KT.rearrange("(kc p) t -> p kc t", p=128)
        GT3 = self.GT.rearrange("(kc p) t -> p kc t", p=128)
        OT3 = self.OT.rearrange("(kc p) t -> p kc t", p=128)
        self.ps_pools = {"at": [0, 1], "o": [2, 3], "d": [4, 5], "n": [6, 7]}
        self.ps_pi = {}
        with contextlib.ExitStack() as st:
            tri = self.sb(st, "r_tri", [64, 64], F32)
            bcn = Buf()
            c.dma("sp", tri[:, :], self.tri[:, :], writes=[bcn])
            S32 = self.sb(st, "r_S32", [128, 8, 128], F32)
            Sbf = self.sb(st, "r_Sbf", [128, 8, 128], BF16)
            Sbf2 = [Sbf, self.sb(st, "r_Sbfb", [128, 8, 128], BF16)]
            bSbf2 = [Buf(), Buf()]
            t32 = self.sb(st, "r_t32", [128, 8, 128], F32)
            bS32, bSbf, bt32 = Buf(), Buf(), Buf()
            c.op("dve", lambda h: h.memset(S32[:, :, :], 0.0), writes=[bS32])
            c.op("dve", lambda h: h.memset(Sbf2[0][:, :, :], 0.0), writes=[bSbf2[0]])
            c.op("dve", lambda h: h.memset(Sbf2[1][:, :, :], 0.0), writes=[bSbf2[1]])
            qT = [self.sb(st, "r_q%d" % i, [128, 8, 512], BF16) for i in range(2)]
            kT = [self.sb(st, "r_k%d" % i, [128, 8, 512], BF16) for i in range(2)]
            gT = [self.sb(st, "r_g%d" % i, [128, 8, 512], BF16) for i in range(2)]
            ktm = [self.sb(st, "r_ktm%d" % i, [128, 8, 4, 128], BF16) for i in range(2)]
            itm = [self.sb(st, "r_itm%d" % i, [128, 8, 4, 128], BF16) for i in range(2)]
            bin_ = [Buf(), Buf()]
            o32 = self.sb(st, "r_o32", [128, 8, 512], F32)
            bo32 = Buf()
            at = [self.sb(st, "r_at%d" % i, [128, 8, 64], BF16) for i in range(2)]
            bat = [Buf(), Buf()]
            sq = self.sb(st, "r_sq", [128, 512], F32)
            rt = self.sb(st, "r_rt", [128, 512], F32)
            tmp = self.sb(st, "r_tmp", [128, 512], F32)
            bsq, brt, btmp = Buf(), Buf(), Buf()
            oo = [self.sb(st, "r_oo%d" % i, [128, 8, 512], BF16) for i in range(2)]
            boo = [Buf(), Buf()]
            ai = 0
            for tile in range(S // 512):
                cs = slice(tile * 512, (tile + 1) * 512)
                b = tile % 2
                c.dma("sp", qT[b][:, :, :], QT3[:, :, cs], reads=[self.bqkv], writes=[bin_[b]])
                c.dma("sp", kT[b][:, :, :], KT3[:, :, cs], reads=[self.bqkv], writes=[bin_[b]])
                c.dma("sp", gT[b][:, :, :], GT3[:, :, cs], reads=[self.bqkv], writes=[bin_[b]])
                c.dma("sp", ktm[b][:, :, :, :], self.V1s[0:8, :, tile * 4:(tile + 1) * 4, :].rearrange("h p s e -> p h s e"),
                      reads=[self.bqkv], writes=[bin_[b]])
                c.dma("sp", itm[b][:, :, :, :], self.V1s[8:16, :, tile * 4:(tile + 1) * 4, :].rearrange("h p s e -> p h s e"),
                      reads=[self.bqkv], writes=[bin_[b]])
                for ch in range(8):
                    cc = slice(ch * 64, (ch + 1) * 64)
                    prt = 64 * (ch % 2)
                    sub = ch // 2
                    gch = tile * 8 + ch
                    pdA, bpdA = self.ps[4], self.bps[4]
                    pdB, bpdB = self.ps[5], self.bps[5]
                    for hh in range(8):
                        pd, bpd = (pdA, bpdA) if hh < 4 else (pdB, bpdB)
                        hc = (hh % 4) * 128
                        c.op("pe", lambda h: h.matmul(pd[:, hc:hc + 128], lhsT=ktm[b][prt:prt + 64, hh, sub, :],
                                                      rhs=itm[b][prt:prt + 64, hh, sub, :], start=True, stop=True),
                             reads=[bin_[b]], writes=[bpd])
                    c.op("dve", lambda h: h.tensor_tensor(out=t32[:, 0:4, :], in0=pdA[:, :].rearrange("p (a b) -> p a b", b=128),
                                                          in1=S32[:, 0:4, :], op=ALU.add), reads=[bpdA, bS32], writes=[bt32])
                    c.op("dve", lambda h: h.tensor_tensor(out=t32[:, 4:8, :], in0=pdB[:, :].rearrange("p (a b) -> p a b", b=128),
                                                          in1=S32[:, 4:8, :], op=ALU.add), reads=[bpdB, bS32], writes=[bt32])
                    c.op("dve", lambda h: h.tensor_tensor(out=S32[:, :, :], in0=t32[:, :, :],
                                                          in1=ebl[:, :, gch].unsqueeze(2).to_broadcast([128, 8, 128]), op=ALU.mult),
                         reads=[bt32, bebl], writes=[bS32])
                    c.op("act", lambda h: h.activation(out=Sbf2[gch % 2][:, :, :], in_=S32[:, :, :], func=AF.Identity),
                         reads=[bS32], writes=[bSbf2[gch % 2]])
                    pat, bpat = self.ps_next("at")
                    for hh in range(8):
                        c.op("pe", lambda h: h.matmul(pat[0:64, hh * 64:(hh + 1) * 64], lhsT=kT[b][:, hh, cc], rhs=qT[b][:, hh, cc],
                                                      start=True, stop=True), reads=[bin_[b]], writes=[bpat])
                    a = ai % 2
                    ai += 1
                    c.op("dve", lambda h: h.tensor_tensor(out=at[a][prt:prt + 64, :, :],
                                                          in0=pat[0:64, :].rearrange("p (a b) -> p a b", b=64),
                                                          in1=tri[:, :].unsqueeze(1).to_broadcast([64, 8, 64]), op=ALU.mult),
                         reads=[bpat, bcn], writes=[bat[a]])
                    po, bpo = self.ps_next("o")
                    for hh in range(8):
                        c.op("pe", lambda h: h.matmul(po[:, hh * 64:(hh + 1) * 64], lhsT=Sbf2[(gch - 1) % 2][:, hh, :], rhs=qT[b][:, hh, cc],
                                                      start=True, stop=False), reads=[bSbf2[(gch - 1) % 2], bin_[b]], writes=[bpo])
                        c.op("pe", lambda h: h.matmul(po[:, hh * 64:(hh + 1) * 64], lhsT=itm[b][prt:prt + 64, hh, sub, :],
                                                      rhs=at[a][prt:prt + 64, hh, :], start=False, stop=True),
                             reads=[bin_[b], bat[a]], writes=[bpo])
                    c.op("act", lambda h: h.activation(out=o32[:, :, cc], in_=po[:, :].rearrange("p (a b) -> p a b", b=64),
                                                       func=AF.Identity), reads=[bpo], writes=[bo32])
                for hh in range(8):
                    pn, bpn = self.ps_next("n")
                    c.op("act", lambda h: h.activation(out=sq[:, :], in_=o32[:, hh, :], func=AF.Square), reads=[bo32], writes=[bsq])
                    c.op("pe", lambda h: h.matmul(pn[:, :], lhsT=self.ones_f[:, :], rhs=sq[:, :], start=True, stop=True),
                         reads=[bsq, self.bconst], writes=[bpn])
                    c.op("act", lambda h: h.activation(out=rt[:, :], in_=pn[:, :], func=AF.Sqrt, scale=1.0 / 128,
                                                       bias=self.eps_t[:, 0:1]), reads=[bpn, self.bconst], writes=[brt])
                    c.op("dve", lambda h: h.reciprocal(out=rt[:, :], in_=rt[:, :]), reads=[brt], writes=[brt])
                    c.op("dve", lambda h: h.scalar_tensor_tensor(out=tmp[:, :], in0=o32[:, hh, :],
                                                                 scalar=self.pvec[:, self.col_hnorm:self.col_hnorm + 1],
                                                                 in1=rt[:, :], op0=ALU.mult, op1=ALU.mult),
                         reads=[bo32, brt, self.bconst], writes=[btmp])
                    c.op("dve", lambda h: h.tensor_tensor(out=oo[b][:, hh, :], in0=tmp[:, :], in1=gT[b][:, hh, :], op=ALU.mult),
                         reads=[btmp, bin_[b]], writes=[boo[b]])
                c.dma("sp", OT3[:, :, cs], oo[b][:, :, :], reads=[boo[b]], writes=[self.bot])
            c.barrier()

    def s5_phase(self, layer, wglu_s, bw):
        c = self.c
        hT3 = self.s5_src.rearrange("(kc p) t -> p kc t", p=128)
        w3 = wglu_s.rearrange("(kc p) n -> p kc n", p=128)
        self.ps_pools = {"y": [0, 1, 2, 3], "e": [4, 5, 6, 7]}
        self.ps_pi = {}
        TS = 1024
        sgnA = self.pvec[:, self.col_sgnA:self.col_sgnA + 1]
        sgnB = self.pvec[:, self.col_sgnA + 1:self.col_sgnA + 2]
        neg1 = self.pvec[:, self.col_sgnA + 2:self.col_sgnA + 3]
        with contextlib.ExitStack() as st:
            Bp = self.sb(st, "s_Bp", [128, 8, 4, 128], BF16)
            Bq = self.sb(st, "s_Bq", [128, 8, 4, 128], BF16)
            Cc = self.sb(st, "s_Cc", [128, 64, 64], BF16)
            Cs = self.sb(st, "s_Cs", [128, 64, 64], BF16)
            th = self.sb(st, "s_th", [128, 64], F32)
            rho = self.sb(st, "s_rho", [128, 64], F32)
            carry = self.sb(st, "s_carry", [128, 64], F32)
            bpar, bcarry = Buf(), Buf()
            with contextlib.ExitStack() as s2:
                def t64(n):
                    return self.sb(s2, "s_" + n, [128, 64], F32)
                are, aim, ldt, dt, lr, x0, kf0, sn0, cs0, abr, abi, den, mre, fre, fim, tA, tB = [t64(n) for n in (
                    "are", "aim", "ldt", "dt", "lr", "x0", "kf0", "sn0", "cs0", "abr", "abi", "den", "mre", "fre", "fim", "tA", "tB")]
                ki0 = self.sb(s2, "s_ki0", [128, 64], I32)
                bl = Buf()
                c.dma("sp", are[:, :], self.s5p[:, 0:64], writes=[bl])
                c.dma("sp", aim[:, :], self.s5p[:, 64:128], writes=[bl])
                c.dma("sp", ldt[:, :], self.s5p[:, 128:192], writes=[bl])
                b0 = Buf()
                R, W = [bl, b0], [b0]
                c.op("act", lambda h: h.activation(out=dt[:, :], in_=ldt[:, :], func=AF.Exp), reads=R, writes=W)
                c.op("dve", lambda h: h.tensor_tensor(out=lr[:, :], in0=are[:, :], in1=dt[:, :], op=ALU.mult), reads=R, writes=W)
                c.op("dve", lambda h: h.tensor_tensor(out=th[:, :], in0=aim[:, :], in1=dt[:, :], op=ALU.mult), reads=R, writes=[b0, bpar])
                c.op("act", lambda h: h.activation(out=rho[:, :], in_=lr[:, :], func=AF.Exp), reads=R, writes=[b0, bpar])
                self._rr(th, b0, 64, ki0, b0, kf0, b0, sn0, b0, cs0, b0)
                c.op("dve", lambda h: h.tensor_tensor(out=abr[:, :], in0=rho[:, :], in1=cs0[:, :], op=ALU.mult), reads=R, writes=W)
                c.op("dve", lambda h: h.tensor_tensor(out=abi[:, :], in0=rho[:, :], in1=sn0[:, :], op=ALU.mult), reads=R, writes=W)
                c.op("dve", lambda h: h.tensor_tensor(out=den[:, :], in0=are[:, :], in1=are[:, :], op=ALU.mult), reads=R, writes=W)
                c.op("dve", lambda h: h.tensor_tensor(out=tA[:, :], in0=aim[:, :], in1=aim[:, :], op=ALU.mult), reads=R, writes=W)
                c.op("dve", lambda h: h.tensor_tensor(out=den[:, :], in0=den[:, :], in1=tA[:, :], op=ALU.add), reads=R, writes=W)
                c.op("dve", lambda h: h.reciprocal(out=den[:, :], in_=den[:, :]), reads=R, writes=W)
                c.op("dve", lambda h: h.tensor_scalar(out=mre[:, :], in0=abr[:, :], scalar1=-1.0, scalar2=None, op0=ALU.add), reads=R, writes=W)
                c.op("dve", lambda h: h.tensor_tensor(out=tA[:, :], in0=mre[:, :], in1=are[:, :], op=ALU.mult), reads=R, writes=W)
                c.op("dve", lambda h: h.tensor_tensor(out=tB[:, :], in0=abi[:, :], in1=aim[:, :], op=ALU.mult), reads=R, writes=W)
                c.op("dve", lambda h: h.tensor_tensor(out=fre[:, :], in0=tA[:, :], in1=tB[:, :], op=ALU.add), reads=R, writes=W)
                c.op("dve", lambda h: h.tensor_tensor(out=fre[:, :], in0=fre[:, :], in1=den[:, :], op=ALU.mult), reads=R, writes=W)
                c.op("dve", lambda h: h.tensor_tensor(out=tA[:, :], in0=abi[:, :], in1=are[:, :], op=ALU.mult), reads=R, writes=W)
                c.op("dve", lambda h: h.tensor_tensor(out=tB[:, :], in0=mre[:, :], in1=aim[:, :], op=ALU.mult), reads=R, writes=W)
                c.op("dve", lambda h: h.tensor_tensor(out=fim[:, :], in0=tA[:, :], in1=tB[:, :], op=ALU.subtract), reads=R, writes=W)
                c.op("dve", lambda h: h.tensor_tensor(out=fim[:, :], in0=fim[:, :], in1=den[:, :], op=ALU.mult), reads=R, writes=W)
                c.op("dve", lambda h: h.tensor_scalar(out=tA[:, :], in0=fim[:, :], scalar1=sgnA, scalar2=None, op0=ALU.mult),
                     reads=[b0, self.bconst], writes=W)
                c.op("dve", lambda h: h.tensor_scalar(out=tB[:, :], in0=fre[:, :], scalar1=sgnB, scalar2=None, op0=ALU.mult),
                     reads=[b0, self.bconst], writes=W)
                BB = self.sb(s2, "s_BB", [128, 1024], F32)
                BS = self.sb(s2, "s_BS", [128, 1024], F32)
                u1 = self.sb(s2, "s_u1", [128, 1024], F32)
                u2 = self.sb(s2, "s_u2", [128, 1024], F32)
                Z = self.sb(s2, "s_Z", [128, 8, 4, 128], F32)
                idf = self.sb(s2, "s_idf", [128, 128], F32)
                c.dma("sp", BB[:, :], self.s5B[:, 0:1024], writes=[bl])
                c.dma("sp", BS[:, :], self.s5B[:, 1024:2048], writes=[bl])
                c.dma("sp", idf[:, :], self.ident[:, :], writes=[bl])

                def bc(t):
                    return t[:, :].unsqueeze(2).to_broadcast([128, 64, 16])

                def v3(t):
                    return t[:, :].rearrange("p (g c) -> p g c", c=16)
                for (dst, ca, cb) in ((Bp, (fre, BB, tA, BS), None), (Bq, (tB, BS, fim, BB), None)):
                    fa, Xa, fb, Xb = ca
                    c.op("dve", lambda h: h.memset(Z[:, :, :, :], 0.0), reads=R, writes=W)
                    c.op("dve", lambda h: h.tensor_tensor(out=v3(u1), in0=v3(Xa), in1=bc(fa), op=ALU.mult), reads=R, writes=W)
                    c.op("dve", lambda h: h.tensor_tensor(out=v3(u2), in0=v3(Xb), in1=bc(fb), op=ALU.mult), reads=R, writes=W)
                    for par in range(4):
                        zv = Z[:, :, par, :].rearrange("p k (m q) -> p k m q", q=64)[:, :, :, 16 * par:16 * par + 16]
                        a1 = u1[:, :].rearrange("p (k m r c) -> p k m r c", k=8, m=2, r=4, c=16)[:, :, :, par, :]
                        a2 = u2[:, :].rearrange("p (k m r c) -> p k m r c", k=8, m=2, r=4, c=16)[:, :, :, par, :]
                        c.op("dve", lambda h: h.tensor_tensor(out=zv, in0=a1, in1=a2, op=ALU.add), reads=R, writes=W)
                    for kc in range(8):
                        for par in range(4):
                            pz, bpz = self.ps_next("e")
                            c.op("pe", lambda h: h.transpose(pz[:, 0:128], Z[:, kc, par, :], idf[:, :]), reads=R, writes=[bpz])
                            c.op("act", lambda h: h.activation(out=dst[:, kc, par, :], in_=pz[:, 0:128], func=AF.Identity),
                                 reads=[bpz], writes=[bpar])
                CC = self.sb(s2, "s_CC", [128, 2048], F32)
                c.dma("sp", CC[:, :], self.s5C[:, :], writes=[bl])
                c.op("dve", lambda h: h.memset(Cc[:, :, :], 0.0), writes=[bpar])
                c.op("dve", lambda h: h.memset(Cs[:, :, :], 0.0), writes=[bpar])
                for (dst, off, sg) in ((Cc, 0, sgnB), (Cs, 1024, neg1)):
                    for par in range(4):
                        dv = dst[:, :, :].rearrange("p (gm r) (q c) -> p gm r q c", r=4, q=4)[:, :, par, par, :]
                        sv = CC[:, off:off + 1024].rearrange("p (gm r c) -> p gm r c", r=4, c=16)[:, :, par, :]
                        c.op("dve", lambda h: h.tensor_scalar(out=dv, in0=sv, scalar1=sg, scalar2=None, op0=ALU.mult),
                             reads=[bl, self.bconst], writes=[bpar])
                c.op("dve", lambda h: h.memset(carry[:, :], 0.0), writes=[bcarry])
                c.barrier()
            tt = self.sb(st, "s_tt", [128, TS], F32)
            thq = self.sb(st, "s_thq", [128, 64], F32)
            btt, bthq = Buf(), Buf()
            c.dma("sp", tt[:, :], self.ttc[:, 0:TS], writes=[btt])
            wg = self.sb(st, "s_wg", [128, KC, 2 * D], BF16)
            bwg = Buf()
            for i in range(2):
                c.dma("sp", wg[:, :, i * D:(i + 1) * D], w3[:, :, i * D:(i + 1) * D], reads=[bw], writes=[bwg])
            hx = self.sb(st, "s_hx", [128, KC, 512], F32)
            sq = self.sb(st, "s_sq", [128, 2, 512], F32)
            rt = self.sb(st, "s_rt", [128, 512], F32)
            xn = self.sb(st, "s_xn", [128, KC, TS], BF16)
            z = self.sb(st, "s_z", [128, KC, TS], BF16)
            bhx, bsq, brt, bxn, bz = Buf(), Buf(), Buf(), Buf(), Buf()
            x = self.sb(st, "s_x", [128, TS], F32)
            ki = self.sb(st, "s_ki", [128, TS], I32)
            sn = [self.sb(st, "s_sn%d" % i, [128, TS], F32) for i in range(3)]
            cs_ = [self.sb(st, "s_cs%d" % i, [128, TS], F32) for i in range(3)]
            eh = [self.sb(st, "s_eh%d" % i, [128, TS], F32) for i in range(2)]
            G = [self.sb(st, "s_G%d" % i, [128, TS], F32) for i in range(2)]
            Gc = [self.sb(st, "s_Gc%d" % i, [128, TS], BF16) for i in range(2)]
            Gs = [self.sb(st, "s_Gs%d" % i, [128, TS], BF16) for i in range(2)]
            bsn, bcs = [Buf() for _ in range(3)], [Buf() for _ in range(3)]
            beh, bG, bGc, bGs = [[Buf() for _ in range(2)] for _ in range(4)]
            bx, bki = Buf(), Buf()
            t1 = [self.sb(st, "s_t1%d" % i, [128, 512], F32) for i in range(2)]
            t2 = [self.sb(st, "s_t2%d" % i, [128, 512], F32) for i in range(2)]
            bt1 = [Buf(), Buf()]
            bt2 = [Buf(), Buf()]
            yv = self.sb(st, "s_yv", [128, 512], F32)
            y2 = self.sb(st, "s_y2", [128, 512], F32)
            y3 = self.sb(st, "s_y3", [128, 512], F32)
            byv, by2, by3 = Buf(), Buf(), Buf()
            hr = [self.sb(st, "s_hr", [128, TS], F32)] * 2
            bhr = [Buf()] * 2
            ti = 0
            for q in range(S // TS):
                t0 = q * TS
                bh = self.bh[q]
                c.op("dve", lambda h: h.tensor_scalar(out=thq[:, :], in0=th[:, :], scalar1=float(t0), scalar2=None, op0=ALU.mult),
                     reads=[bpar], writes=[bthq])
                for half in range(2):
                    gate_ = self.s5_gate if (q == 0 and half == 0) else []
                    c.dma("sp", hx[:, :, :], hT3[:, :, t0 + half * 512:t0 + (half + 1) * 512], reads=[bh] + gate_, writes=[bhx])
                    self.rmsnorm_tile(hx, bhx, self.col_nmix + layer * KC, xn[:, :, half * 512:(half + 1) * 512], bxn, 512,
                                      (sq, bsq, rt, brt))
                pys = {}
                pool_eng = "dve" if (q == 0 and self.s5_spare_pool) else "pool"

                TWO_PI = 2.0 * np.pi

                def S1(g):
                    kc, gi = g // 8, g % 8
                    u3 = g % 3
                    if gi == 0:
                        pys[kc] = [self.ps_next("y") for _ in range(2)]
                    c.op("act", lambda h: h.activation(out=x[:, :], in_=tt[:, :], func=AF.Identity, scale=th[:, g:g + 1],
                                                       bias=thq[:, g:g + 1]),
                         reads=[btt, bpar, bthq], writes=[bx])
                    c.op("act", lambda h: h.activation(out=ki[:, :], in_=x[:, :], func=AF.Identity, scale=1.0 / TWO_PI),
                         reads=[bx], writes=[bki])

                def S1b(g):
                    u3 = g % 3
                    c.op("dve", lambda h: h.scalar_tensor_tensor(out=sn[u3][:, :], in0=ki[:, :], scalar=-TWO_PI, in1=x[:, :],
                                                                 op0=ALU.mult, op1=ALU.add), reads=[bki, bx], writes=[bsn[u3]])
                    c.op("dve", lambda h: h.tensor_scalar(out=sn[u3][:, :], in0=sn[u3][:, :], scalar1=-PI_LO, scalar2=PI_LO,
                                                          op0=ALU.max, op1=ALU.min), reads=[bsn[u3]], writes=[bsn[u3]])
                    c.op("act", lambda h: h.activation(out=cs_[u3][:, :], in_=sn[u3][:, :], func=AF.Abs),
                         reads=[bsn[u3]], writes=[bcs[u3]])

                def S2a(g):
                    u3 = g % 3
                    c.op("act", lambda h: h.activation(out=sn[u3][:, :], in_=sn[u3][:, :], func=AF.Sin), reads=[bsn[u3]], writes=[bsn[u3]])
                    c.op("act", lambda h: h.activation(out=cs_[u3][:, :], in_=cs_[u3][:, :], func=AF.Sin, scale=-1.0,
                                                       bias=self.halfpi_t[:, 0:1]), reads=[bcs[u3], self.bconst], writes=[bcs[u3]])

                def S2(g):
                    nonlocal ti
                    kc, gi = g // 8, g % 8
                    u3, u = g % 3, g % 2
                    m, par = gi // 4, gi % 4
                    rows = slice(64 * m, 64 * m + 64)
                    for ct in range(2):
                        cc = slice(ct * 512, (ct + 1) * 512)
                        pe_, bpe = self.ps_next("e")
                        pq_, bpq = self.ps_next("e")
                        c.op("pe", lambda h: h.matmul(pe_[:, :], lhsT=Bp[rows, kc, par, :], rhs=xn[rows, kc, cc], start=True, stop=True),
                             reads=[bpar, bxn], writes=[bpe])
                        c.op("pe", lambda h: h.matmul(pq_[:, :], lhsT=Bq[rows, kc, par, :], rhs=xn[rows, kc, cc], start=True, stop=True),
                             reads=[bpar, bxn], writes=[bpq])
                        e = ti % 2
                        ti += 1
                        c.op("dve", lambda h: h.tensor_tensor(out=t1[e][:, :], in0=pe_[:, :], in1=cs_[u3][:, cc], op=ALU.mult),
                             reads=[bpe, bcs[u3]], writes=[bt1[e]])
                        c.op("dve", lambda h: h.tensor_tensor(out=t2[e][:, :], in0=pq_[:, :], in1=sn[u3][:, cc], op=ALU.mult),
                             reads=[bpq, bsn[u3]], writes=[bt2[e]])
                        c.op(pool_eng, lambda h: h.tensor_tensor(out=eh[u][:, cc], in0=t1[e][:, :], in1=t2[e][:, :], op=ALU.add),
                             reads=[bt1[e], bt2[e]], writes=[beh[u]])

                def S3(g):
                    u3, u = g % 3, g % 2
                    c.op("dve", lambda h: h.tensor_tensor_scan(out=G[u][:, :], data0=rho[:, g:g + 1].to_broadcast([128, TS]),
                                                               data1=eh[u][:, :], initial=carry[:, g:g + 1],
                                                               op0=ALU.mult, op1=ALU.add),
                         reads=[beh[u], bpar, bcarry], writes=[bG[u]])
                    c.op("act", lambda h: h.activation(out=carry[:, g:g + 1], in_=G[u][:, TS - 1:TS], func=AF.Identity),
                         reads=[bG[u]], writes=[bcarry])
                    c.op("dve", lambda h: h.tensor_tensor(out=Gc[u][:, :], in0=G[u][:, :], in1=cs_[u3][:, :], op=ALU.mult),
                         reads=[bG[u], bcs[u3]], writes=[bGc[u]])
                    c.op(pool_eng, lambda h: h.tensor_tensor(out=Gs[u][:, :], in0=G[u][:, :], in1=sn[u3][:, :], op=ALU.mult),
                         reads=[bG[u], bsn[u3]], writes=[bGs[u]])

                def S4(g):
                    kc, gi = g // 8, g % 8
                    u = g % 2
                    m, par = gi // 4, gi % 4
                    rows = slice(64 * m, 64 * m + 64)
                    py = pys[kc]
                    for ct in range(2):
                        cc = slice(ct * 512, (ct + 1) * 512)
                        pyy, bpy = py[ct]
                        c.op("pe", lambda h: h.matmul(pyy[rows, :], lhsT=Cc[:, g, :], rhs=Gc[u][:, cc], start=(par == 0), stop=False),
                             reads=[bpar, bGc[u]], writes=[bpy])
                        c.op("pe", lambda h: h.matmul(pyy[rows, :], lhsT=Cs[:, g, :], rhs=Gs[u][:, cc], start=False, stop=(par == 3)),
                             reads=[bpar, bGs[u]], writes=[bpy])
                    if gi == 7:
                        for ct in range(2):
                            cc = slice(ct * 512, (ct + 1) * 512)
                            pyy, bpy = py[ct]
                            dcol = self.pvec[:, self.col_s5d + kc:self.col_s5d + kc + 1]
                            c.op("dve", lambda h: h.scalar_tensor_tensor(out=yv[:, :], in0=xn[:, kc, cc], scalar=dcol, in1=pyy[:, :],
                                                                         op0=ALU.mult, op1=ALU.add),
                                 reads=[bxn, bpy, self.bconst], writes=[byv])
                            c.op("act", lambda h: h.activation(out=y2[:, :], in_=yv[:, :], func=AF.Square), reads=[byv], writes=[by2])
                            c.op(pool_eng, lambda h: h.tensor_scalar(out=y2[:, :], in0=y2[:, :], scalar1=0.044715, scalar2=1.0,
                                                                   op0=ALU.mult, op1=ALU.add), reads=[by2], writes=[by2])
                            c.op(pool_eng, lambda h: h.tensor_tensor(out=y2[:, :], in0=y2[:, :], in1=yv[:, :], op=ALU.mult),
                                 reads=[by2, byv], writes=[by2])
                            c.op("act", lambda h: h.activation(out=y3[:, :], in_=y2[:, :], func=AF.Sigmoid, scale=1.5957691216),
                                 reads=[by2], writes=[by3])
                            c.op(pool_eng, lambda h: h.tensor_tensor(out=z[:, kc, cc], in0=y3[:, :], in1=yv[:, :], op=ALU.mult),
                                 reads=[by3, byv], writes=[bz])

                NG = 64
                for i in range(NG + 3):
                    if 0 <= i - 1 < NG:
                        S2a(i - 1)
                    if 0 <= i - 2 < NG:
                        S3(i - 2)
                    if i < NG:
                        S1(i)
                    if 0 <= i - 1 < NG:
                        S2(i - 1)
                    if i < NG:
                        S1b(i)
                    if 0 <= i - 3 < NG:
                        S4(i - 3)
                for oc in range(KC):
                    r = oc % 2
                    c.dma("sp", hr[r][:, :], self.s5_src[oc * 128:(oc + 1) * 128, t0:t0 + TS], reads=[bh], writes=[bhr[r]])
                    for ct in range(2):
                        cc = slice(ct * 512, (ct + 1) * 512)
                        pv, bpv = self.ps_next("e")
                        pg, bpg = self.ps_next("e")
                        for kc in range(KC):
                            c.op("pe", lambda h: h.matmul(pv[:, :], lhsT=wg[:, kc, oc * 128:(oc + 1) * 128], rhs=z[:, kc, cc],
                                                          start=(kc == 0), stop=(kc == KC - 1)), reads=[bwg, bz], writes=[bpv])
                        for kc in range(KC):
                            c.op("pe", lambda h: h.matmul(pg[:, :], lhsT=wg[:, kc, D + oc * 128:D + (oc + 1) * 128], rhs=z[:, kc, cc],
                                                          start=(kc == 0), stop=(kc == KC - 1)), reads=[bwg, bz], writes=[bpg])
                        bv = self.pvec[:, self.col_bglu + oc:self.col_bglu + oc + 1]
                        bg = self.pvec[:, self.col_bglu + 8 + oc:self.col_bglu + 8 + oc + 1]
                        c.op("act", lambda h: h.activation(out=y2[:, :], in_=pv[:, :], func=AF.Identity, bias=bv),
                             reads=[bpv, self.bconst], writes=[by2])
                        c.op("act", lambda h: h.activation(out=y3[:, :], in_=pg[:, :], func=AF.Sigmoid, bias=bg),
                             reads=[bpg, self.bconst], writes=[by3])
                        c.op("dve", lambda h: h.tensor_tensor(out=y2[:, :], in0=y2[:, :], in1=y3[:, :], op=ALU.mult),
                             reads=[by2, by3], writes=[by2])
                        c.op("dve", lambda h: h.tensor_tensor(out=hr[r][:, cc], in0=hr[r][:, cc], in1=y2[:, :], op=ALU.add),
                             reads=[by2, bhr[r]], writes=[bhr[r]])
                    c.dma("sp", self.hT[oc * 128:(oc + 1) * 128, t0:t0 + TS], hr[r][:, :], reads=[bhr[r]], writes=[bh])
            c.barrier()

    def final_phase(self, outT, do_norm):
        c = self.c
        hT3 = self.hT.rearrange("(kc p) t -> p kc t", p=128)
        oT3 = outT.rearrange("(kc p) t -> p kc t", p=128)
        with contextlib.ExitStack() as st:
            hx = [self.sb(st, "o_hx%d" % i, [128, KC, 512], F32) for i in range(2)]
            ox = [self.sb(st, "o_ox%d" % i, [128, KC, 512], F32) for i in range(2)]
            sq = self.sb(st, "o_sq", [128, KC, 512], F32)
            rt = self.sb(st, "o_rt", [128, 512], F32)
            bhx = [Buf(), Buf()]
            box = [Buf(), Buf()]
            bsq, brt, bo = Buf(), Buf(), Buf()
            for tile in range(S // 512):
                cs = slice(tile * 512, (tile + 1) * 512)
                b = tile % 2
                c.dma("sp", hx[b][:, :, :], hT3[:, :, cs], reads=[self.bh[tile // 2]], writes=[bhx[b]])
                if do_norm:
                    self.rmsnorm_tile(hx[b], bhx[b], self.col_nfin, ox[b], box[b], 512, (sq, bsq, rt, brt))
                    c.dma("sp", oT3[:, :, cs], ox[b][:, :, :], reads=[box[b]], writes=[bo])
                else:
                    c.dma("sp", oT3[:, :, cs], hx[b][:, :, :], reads=[bhx[b]], writes=[bo])
            c.barrier()

    def build(self):
        cfg = self.cfg
        nc = self.nc
        c = self.c
        es = self.es
        stages = cfg["stages"]
        mixl = [l for (k, l) in stages if k == "mix"]
        ffnl = [l for (k, l) in stages if k == "ffn"]
        xT = self.din("xT", [D, S])
        pvec = self.din("pvec", [128, cfg["npvec"]])
        self.col_nmix, self.col_nffn, self.col_nfin = 0, 4 * KC, 8 * KC
        self.col_invf, self.col_sgnrow = 9 * KC, 9 * KC + 1
        self.posb = self.din("posb", [128, S], I32)
        self.dmask = self.din("dmask", [128, 20 * 512])
        self.mmask = self.din("mmask", [128, 4 * 512])
        self.onehot = self.din("onehot", [16, S])
        self.pastneg = self.din("pastneg", [128, 512])
        self.own = self.din("own", [128, 512])
        self.ident = self.din("ident", [128, 128])
        self.ropeperm = self.din("ropeperm", [128, 128])
        self.rmask = self.din("rmask", [128, 512])
        self.tri = self.din("tri", [64, 64])
        self.col_lb, self.col_hnorm = 9 * KC + 2, 9 * KC + 2 + 4 * KC
        self.col_sgnA = self.col_hnorm + 1
        self.col_s5d = self.col_sgnA + 3
        self.col_bglu = self.col_s5d + KC
        self.s5p = self.din("s5p", [128, 192])
        self.s5B = self.din("s5B", [128, 2048])
        self.s5C = self.din("s5C", [128, 2048])
        self.ttc = self.din("ttc", [128, S])
        w_gu_in = {l: self.din("w_gu%d" % l, [D, 2 * DFF]) for l in ffnl}
        w_d_in = {l: self.din("w_d%d" % l, [DFF, D]) for l in ffnl}
        self.w_gu = {l: self.dscr("s_wgu%d" % l, [D, 2 * DFF], BF16) for l in ffnl}
        self.w_d = {l: self.dscr("s_wd%d" % l, [DFF, D], BF16) for l in ffnl}
        self.bw_ffn = {l: Buf() for l in ffnl}
        win, wsc, bwm = {}, {}, {}
        for l in mixl:
            if l in (1, 3):
                nm = "dil" if l == 1 else "moba"
                win[l] = (self.din(nm + "_qkv", [D, 3 * D]), self.din(nm + "_o", [D, D]))
                wsc[l] = (self.dscr("s_%s_qkv" % nm, [D, 3 * D], BF16), None,
                          self.dscr("s_%s_o" % nm, [D, D], BF16))
                bwm[l] = Buf()
            elif l == 0:
                win[l] = (self.din("s5_wglu", [D, 2 * D]),)
                wsc[l] = (self.dscr("s_s5_wglu", [D, 2 * D], BF16),)
                bwm[l] = Buf()
            elif l == 2:
                win[l] = (self.din("hgrn_in", [D, 4 * D]), self.din("hgrn_o", [D, D]))
                wsc[l] = (self.dscr("s_hgrn_in", [D, 4 * D], BF16), self.dscr("s_hgrn_o", [D, D], BF16))
                bwm[l] = Buf()
        outT = nc.dram_tensor("outT", [D, S], F32, kind="ExternalOutput").ap()
        self.GT = self.dscr("GT", [D, S], BF16)
        self.hT = self.dscr("hT", [D, S], F32)
        self.QT = self.dscr("QT", [D, S], BF16)
        self.KT = self.dscr("KT", [D, S], BF16)
        self.OT = self.dscr("OT", [D, S], BF16)
        self.V1s = self.dscr("V1s", [16, 128, 32, 128], BF16)
        self.bqkv, self.bot = Buf(), Buf()
        self.bh = [Buf() for _ in range(S // 1024)]
        self.pvec = self.sb(es, "pvec_sb", [128, cfg["npvec"]], F32)
        self.ones_f = self.sb(es, "ones_f", [128, 128], F32)
        self.eps_t = self.sb(es, "eps_t", [128, 1], F32)
        self.bconst = Buf()
        c.dma("sp", self.pvec[:, :], pvec[:, :], writes=[self.bconst])
        c.op("dve", lambda h: h.memset(self.ones_f[:, :], 1.0), writes=[self.bconst])
        c.op("dve", lambda h: h.memset(self.eps_t[:, :], EPS), writes=[self.bconst])
        self.halfpi_t = self.sb(es, "halfpi_t", [128, 1], F32)
        c.op("dve", lambda h: h.memset(self.halfpi_t[:, :], float(np.pi / 2)), writes=[self.bconst])
        self.ps = [es.enter_context(nc.psum_tensor("ps%d" % i, [128, 512], F32)) for i in range(8)]
        self.bps = [Buf() for _ in range(8)]
        self.ps_i = 0
        self.sg_i = 0
        self.ps_pools = {}
        self.ps_pi = {}
        self.cast_i = 0
        self.cast_st = []
        for i in range(2):
            self.cast_st.append((self.sb(es, "cst%d" % i, [128, 512], F32), self.sb(es, "cstb%d" % i, [128, 512], BF16),
                                 Buf(), Buf()))
        xT3 = xT.rearrange("(kc p) t -> p kc t", p=128)
        hT3 = self.hT.rearrange("(kc p) t -> p kc t", p=128)
        self.s5_src = self.hT
        if stages[0] == ("mix", 0):
            self.s5_src = xT
        else:
            with contextlib.ExitStack() as st:
                tmp = [self.sb(st, "cp%d" % i, [128, KC, 1024], F32) for i in range(2)]
                bt = [Buf(), Buf()]
                for i, t0 in enumerate(range(0, S, 1024)):
                    c.dma("sp", tmp[i % 2][:, :, :], xT3[:, :, t0:t0 + 1024], writes=[bt[i % 2]])
                    c.dma("sp", hT3[:, :, t0:t0 + 1024], tmp[i % 2][:, :, :], reads=[bt[i % 2]], writes=[self.bh[i]])
                c.barrier()
        cast_done = set()

        def emit_cast(si_):
            if si_ >= len(stages) or si_ in cast_done:
                return
            cast_done.add(si_)
            k, l = stages[si_]
            if k == "ffn":
                self.cast_weight(w_gu_in[l], self.w_gu[l], D, 2 * DFF, self.bw_ffn[l])
                self.cast_weight(w_d_in[l], self.w_d[l], DFF, D, self.bw_ffn[l])
            elif k == "mix" and l in (1, 3):
                self.cast_weight(win[l][0], wsc[l][0], D, 3 * D, bwm[l])
                self.cast_weight(win[l][1], wsc[l][2], D, D, bwm[l])
            elif k == "mix" and l == 0:
                self.cast_weight(win[l][0], wsc[l][0], D, 2 * D, bwm[l])
            elif k == "mix" and l == 2:
                self.cast_weight(win[l][0], wsc[l][0], D, 4 * D, bwm[l])
                self.cast_weight(win[l][1], wsc[l][1], D, D, bwm[l])
        self.hook = None
        bwp = {l: Buf() for l in mixl}
        for si, (k, l) in enumerate(stages):
            for (k2, l2) in stages[si + 1:si + 2] + (stages[0:1] if si == 0 else []):
                if False:
                    self.cast_weight_qk_perm(win[l2][0], wsc[l2][1], bwp[l2])
                    bwp[l2].done = True
            emit_cast(si)
            if si == 0:
                emit_cast(1)
            self.s5_spare_pool = False
            self.s5_gate = []
            if si == 0 and len(stages) > 1 and stages[1][0] == "ffn":
                self.s5_gate = [self.bw_ffn[stages[1][1]]]
            self.hook = lambda si=si: emit_cast(si + 1)
            if k == "ffn":
                self.ffn_phase(l)
            elif k == "mix" and l in (1, 3):
                self.qkv_phase(l, wsc[l][0], wsc[l][1], bwm[l], bwp[l])
                self.attn_phase("dil" if l == 1 else "moba")
                self.linres_phase(self.OT, wsc[l][2], bwm[l], self.bot)
            elif k == "mix" and l == 0:
                self.s5_phase(l, wsc[l][0], bwm[l])
            elif k == "mix" and l == 2:
                with contextlib.ExitStack() as st:
                    ebl = self.sb(st, "ebl", [128, 8, 64], F32)
                    bebl = Buf()
                    self.hgrn_prep(l, wsc[l][0], bwm[l], ebl, bebl)
                    self.hgrn_rec(ebl, bebl)
                self.linres_phase(self.OT, wsc[l][1], bwm[l], self.bot)
            self.run_hook()
        self.final_phase(outT, cfg.get("final_norm", True))
        c.final_wait()
        es.close()
        return nc


ROPE_THETA = 500000.0


def consts_build():
    cst = {}
    kl = np.arange(128)[:, None]
    ql = np.arange(512)[None, :]
    dm = np.zeros((128, 20, 512), np.float32)
    for mi in range(20):
        off = mi - 16
        dl = ql - kl - off * 128
        m = ((dl >= 0) & (dl <= 128)).astype(np.float32)
        m += ((dl >= 0) & (dl <= 512) & (dl % 4 == 0)).astype(np.float32)
        m += ((dl >= 0) & (dl <= 2048) & (dl % 16 == 0)).astype(np.float32)
        dm[:, mi, :] = m
    cst["dmask"] = dm.reshape(128, 20 * 512)
    mm = np.zeros((128, 4, 512), np.float32)
    for j in range(4):
        mm[:, j, :] = (j * 128 + kl <= ql).astype(np.float32)
    cst["mmask"] = mm.reshape(128, 4 * 512)
    oh = np.zeros((16, S), np.float32)
    for n in range(16):
        oh[n, n * 256:(n + 1) * 256] = 1.0
    cst["onehot"] = oh
    pn = np.zeros((128, 32, 16), np.float32)
    ow = np.zeros((128, 32, 16), np.float32)
    for qt in range(32):
        qb = qt // 2
        pn[:, qt, qb:] = -1e30
        ow[:, qt, qb] = 1.0
    cst["pastneg"] = pn.reshape(128, 512)
    cst["own"] = ow.reshape(128, 512)
    cst["ident"] = np.eye(128, dtype=np.float32)
    pmx = np.zeros((128, 128), np.float32)
    for f in range(128):
        j = f % 64
        if j < 8:
            pmx[f + 8, f] = 1.0
        elif j < 16:
            pmx[f - 8, f] = 1.0
    cst["ropeperm"] = pmx
    rm = np.ones((128, 512), np.float32)
    rm[:, 0::64] = 0.0
    cst["rmask"] = rm
    cst["ttc"] = np.ascontiguousarray(np.broadcast_to(np.arange(S, dtype=np.float32)[None, :], (128, S)))
    cst["tri"] = (np.arange(64)[:, None] <= np.arange(64)[None, :]).astype(np.float32)
    return cst


def pvec_build(inp):
    cols = []
    for l in range(4):
        cols.append(inp["norm_mix"][l].reshape(KC, 128).T)
    for l in range(4):
        cols.append(inp["norm_ffn"][l].reshape(KC, 128).T)
    cols.append(inp["norm_final"].reshape(KC, 128).T)
    f = np.arange(128) % 64
    inv = ROPE_THETA ** (-np.arange(0, 16, 2, dtype=np.float32) / 16.0)
    invf = np.where(f < 16, inv[f % 8], 0.0).astype(np.float32)
    sgn = np.where(f < 8, -1.0, np.where(f < 16, 1.0, 0.0)).astype(np.float32)
    cols.append(invf[:, None])
    cols.append(sgn[:, None])
    for l in range(4):
        cols.append(inp["hgrn_lower_bound"][l].reshape(KC, 128).T)
    cols.append(inp["hgrn_norm"][0].reshape(128, 1))
    p = np.arange(128)
    sa = np.where(p < 64, -1.0, 1.0).astype(np.float32)
    cols.append(sa[:, None])
    cols.append(-sa[:, None])
    cols.append(-np.ones((128, 1), np.float32))
    cols.append(inp["s5_d"][0].reshape(KC, 128).T)
    cols.append(inp["s5_b_glu"][0].reshape(2 * KC, 128).T)
    return np.ascontiguousarray(np.concatenate(cols, axis=1).astype(np.float32))


def make_inmaps(inp, cfg, cores, xs=None):
    pv = pvec_build(inp)
    cfg["npvec"] = pv.shape[1]
    cst = consts_build()
    stages = cfg["stages"]
    maps = []
    for b in cores:
        x = inp["x"][b] if xs is None else xs[b]
        m = {"xT": np.ascontiguousarray(x.T), "pvec": pv,
             "posb": np.ascontiguousarray(np.broadcast_to(inp["positions"][b][None, :], (128, S)).astype(np.int32))}
        m.update(cst)
        are, aim, ldt = inp["s5_a_re"][0], inp["s5_a_im"][0], inp["s5_log_dt"][0]
        m["s5p"] = np.ascontiguousarray(np.concatenate([
            np.concatenate([are.T, are.T], 0), np.concatenate([aim.T, aim.T], 0),
            np.broadcast_to(ldt[None, :], (128, 64))], axis=1).astype(np.float32))
        bre = inp["s5_b_re"][0].transpose(1, 0, 2).reshape(64, 1024)
        bim = inp["s5_b_im"][0].transpose(1, 0, 2).reshape(64, 1024)
        m["s5B"] = np.ascontiguousarray(np.concatenate([np.concatenate([bre, bim], 0), np.concatenate([bim, bre], 0)], axis=1))
        cre = inp["s5_c_re"][0].transpose(2, 0, 1).reshape(64, 1024)
        cim = inp["s5_c_im"][0].transpose(2, 0, 1).reshape(64, 1024)
        m["s5C"] = np.ascontiguousarray(np.concatenate([np.concatenate([cre, cim], 0), np.concatenate([cim, cre], 0)], axis=1))
        for (k, l) in stages:
            if k == "ffn":
                m["w_gu%d" % l] = np.ascontiguousarray(inp["ffn_w_gate_up"][l])
                m["w_d%d" % l] = np.ascontiguousarray(inp["ffn_w_down"][l])
            elif k == "mix" and l == 1:
                m["dil_qkv"] = np.ascontiguousarray(inp["dil_w_qkv"][0])
                m["dil_o"] = np.ascontiguousarray(inp["dil_w_o"][0])
            elif k == "mix" and l == 0:
                m["s5_wglu"] = np.ascontiguousarray(inp["s5_w_glu"][0])
            elif k == "mix" and l == 2:
                m["hgrn_in"] = np.ascontiguousarray(inp["hgrn_w_in"][0])
                m["hgrn_o"] = np.ascontiguousarray(inp["hgrn_w_o"][0])
            elif k == "mix" and l == 3:
                m["moba_qkv"] = np.ascontiguousarray(inp["moba_w_qkv"][0])
                m["moba_o"] = np.ascontiguousarray(inp["moba_w_o"][0])
        maps.append(m)
    return maps


FULL_STAGES = [("mix", 0), ("ffn", 0), ("mix", 1), ("ffn", 1), ("mix", 2), ("ffn", 2), ("mix", 3), ("ffn", 3)]


def kernel(**inp):
    inp = {k: np.asarray(v) for k, v in inp.items()}
    cfg = {"stages": FULL_STAGES, "final_norm": True}
    maps = make_inmaps(inp, cfg, range(4))
    prog = Prog(cfg)
    nc = prog.build()
    res = run_bass_kernel_spmd(nc, maps, core_ids=list(range(4)))
    out = np.stack([res.results[b]["outT"].T for b in range(4)], axis=0)
    return np.ascontiguousarray(out.astype(np.float32))
```

```python
import contextlib
import numpy as np
import concourse.bass as bass
import concourse.mybir as mybir
from concourse.bass_utils import run_bass_kernel_spmd

F32 = mybir.dt.float32
BF16 = mybir.dt.bfloat16
I32 = mybir.dt.int32
AF = mybir.ActivationFunctionType
ALU = mybir.AluOpType

S = 4096
D = 1024
DFF = 2816
KC = D // 128
EPS = 1e-6
SELF_SYNC = True
PI_LO = 3.1415925


class Buf:

    def __init__(self, name=""):
        self.w = None
        self.r = {}
        self.name = name


class Eng:
    def __init__(self, ctx, name, handle, self_sync):
        self.name = name
        self.h = handle
        self.sem = ctx.es.enter_context(ctx.nc.semaphore("s_" + name))
        self.cnt = 0
        self.seen = {}
        self.self_sync = self_sync


class DQ:
    def __init__(self, ctx, name, eng, k):
        self.name = name
        self.eng = eng
        self.k = k
        self.sems = [ctx.es.enter_context(ctx.nc.semaphore("q_%s%d" % (name, i))) for i in range(k)]
        self.cnts = [0] * k
        self.n = 0


class Ctx:
    def __init__(self, nc):
        self.nc = nc
        self.es = contextlib.ExitStack()
        self.eng = {}
        for name, h, ss in (("pe", nc.tensor, False), ("act", nc.scalar, SELF_SYNC), ("dve", nc.vector, SELF_SYNC),
                            ("pool", nc.gpsimd, SELF_SYNC), ("sp", nc.sync, False)):
            self.eng[name] = Eng(self, name, h, ss)
        self.dq = {"sp": DQ(self, "sp", self.eng["sp"], 8), "pool": DQ(self, "pool", self.eng["pool"], 4)}
        self.semtab = {}
        for e in self.eng.values():
            self.semtab[e.name] = e.sem
        for q in self.dq.values():
            for i, s in enumerate(q.sems):
                self.semtab[(q.name, i)] = s
        self.nwait = 0
        self.nins = 0

    def _need(self, reads, writes):
        need = {}
        for b in reads:
            if b.w is not None:
                k, v = b.w
                if need.get(k, 0) < v:
                    need[k] = v
        for b in writes:
            if b.w is not None:
                k, v = b.w
                if need.get(k, 0) < v:
                    need[k] = v
            for k, v in b.r.items():
                if need.get(k, 0) < v:
                    need[k] = v
        return need

    def _waits(self, E, need):
        for k, v in need.items():
            if k == E.name and not E.self_sync:
                continue
            if E.seen.get(k, 0) < v:
                E.h.wait_ge(self.semtab[k], v)
                E.seen[k] = v
                self.nwait += 1

    def op(self, eng, emit, reads=(), writes=()):
        E = self.eng[eng]
        self._waits(E, self._need(reads, writes))
        ins = emit(E.h)
        E.cnt += 1
        ins.then_inc(E.sem, 1)
        self.nins += 1
        for b in reads:
            b.r[E.name] = E.cnt
        for b in writes:
            b.w = (E.name, E.cnt)
            b.r = {}

    def dma(self, q, out, in_, reads=(), writes=()):
        Q = self.dq[q]
        E = Q.eng
        i = Q.n % Q.k
        need = self._need(reads, writes)
        key = (Q.name, i)
        if Q.cnts[i] > 0:
            need[key] = max(need.get(key, 0), 16 * Q.cnts[i])
        self._waits(E, need)
        E.h.dma_start(out=out, in_=in_).then_inc(Q.sems[i], 16)
        Q.cnts[i] += 1
        Q.n += 1
        self.nins += 1
        for b in reads:
            b.r[key] = 16 * Q.cnts[i]
        for b in writes:
            b.w = (key, 16 * Q.cnts[i])
            b.r = {}

    def barrier(self):
        tgt = {}
        for e in self.eng.values():
            if e.cnt:
                tgt[e.name] = e.cnt
        for q in self.dq.values():
            for i in range(q.k):
                if q.cnts[i]:
                    tgt[(q.name, i)] = 16 * q.cnts[i]
        for e in self.eng.values():
            if e.name == "pool":
                continue
            for k, v in tgt.items():
                if k == "pool" or (isinstance(k, tuple) and k[0] == "pool"):
                    continue
                if k == e.name:
                    continue
                if e.seen.get(k, 0) < v:
                    e.h.wait_ge(self.semtab[k], v)
                    e.seen[k] = v

    def final_wait(self):
        E = self.eng["sp"]
        Q = self.dq["sp"]
        for i in range(Q.k):
            if Q.cnts[i]:
                E.h.wait_ge(Q.sems[i], 16 * Q.cnts[i])


class Prog:
    def __init__(self, cfg):
        self.cfg = cfg
        nc = bass.Bass("TRN2", target_bir_lowering=False)
        self.nc = nc
        self.c = Ctx(nc)
        self.es = self.c.es
        self.ins = {}

    def din(self, name, shape, dt=F32):
        t = self.nc.dram_tensor(name, list(shape), dt, kind="ExternalInput").ap()
        self.ins[name] = t
        return t

    def dscr(self, name, shape, dt):
        return self.nc.dram_tensor(name, list(shape), dt, kind="Internal").ap()

    def sb(self, stack, name, shape, dt):
        self.sb_n = getattr(self, "sb_n", 0) + 1
        return stack.enter_context(self.nc.sbuf_tensor("%s_%d" % (name, self.sb_n), list(shape), dt))

    def cast_weight(self, src, dst, K, N, buf):
        c = self.c
        if not hasattr(buf, "cw_key"):
            sem = self.es.enter_context(self.nc.semaphore("cw%d" % len(c.semtab)))
            buf.cw_key = ("cw", len(c.semtab))
            c.semtab[buf.cw_key] = sem
            buf.cw_n = 0
        sem = c.semtab[buf.cw_key]
        step = 512
        for r0 in range(0, K, step):
            r1 = min(K, r0 + step)
            self.nc.gpsimd.dma_start(out=dst[r0:r1, :], in_=src[r0:r1, :]).then_inc(sem, 16)
            buf.cw_n += 1
        buf.w = (buf.cw_key, 16 * buf.cw_n)
        buf.r = {}

    def cast_weight_old(self, src, dst, K, N, buf):
        c = self.c
        CW = 2048
        for kc in range(K // 128):
            for c0 in range(0, N, CW):
                w = min(CW, N - c0)
                i = self.cast_i % 2
                self.cast_i += 1
                st, stb, bst, bstb = self.cast_st[i]
                c.dma("pool", st[:, 0:w], src[kc * 128:(kc + 1) * 128, c0:c0 + w], writes=[bst])
                c.op("pool", lambda h: h.tensor_copy(out=stb[:, 0:w], in_=st[:, 0:w]), reads=[bst], writes=[bstb])
                c.dma("pool", dst[kc * 128:(kc + 1) * 128, c0:c0 + w], stb[:, 0:w], reads=[bstb], writes=[buf])

    def rmsnorm_tile(self, hx, bhx, gcol, xn, bxn, ncols, tmp):
        c = self.c
        sq, bsq, rt, brt = tmp
        KH = sq.shape[1]
        for c0 in range(0, ncols, 512):
            ps, bps = self.ps_next()
            for k0 in range(0, KC, KH):
                c.op("act", lambda h: h.activation(out=sq[:, :, :], in_=hx[:, k0:k0 + KH, c0:c0 + 512], func=AF.Square),
                     reads=[bhx], writes=[bsq])
                for kk in range(KH):
                    kc = k0 + kk
                    c.op("pe", lambda h: h.matmul(ps[:, :], lhsT=self.ones_f[:, :], rhs=sq[:, kk, :],
                                                  start=(kc == 0), stop=(kc == KC - 1)),
                         reads=[bsq, self.bconst], writes=[bps])
            c.op("act", lambda h: h.activation(out=rt[:, :], in_=ps[:, :], func=AF.Sqrt, scale=1.0 / D,
                                               bias=self.eps_t[:, 0:1]),
                 reads=[bps, self.bconst], writes=[brt])
            c.op("dve", lambda h: h.reciprocal(out=rt[:, :], in_=rt[:, :]), reads=[brt], writes=[brt])
            for kc in range(KC):
                c.op("dve", lambda h: h.scalar_tensor_tensor(out=xn[:, kc, c0:c0 + 512], in0=hx[:, kc, c0:c0 + 512],
                                                             scalar=self.pvec[:, gcol + kc:gcol + kc + 1],
                                                             in1=rt[:, :], op0=ALU.mult, op1=ALU.mult),
                     reads=[bhx, brt, self.bconst], writes=[bxn])

    def run_hook(self):
        h = getattr(self, "hook", None)
        if h is not None:
            self.hook = None
            E, pe = self.c.eng["pool"], self.c.eng["pe"]
            if pe.cnt > 0 and E.seen.get("pe", 0) < pe.cnt:
                E.h.wait_ge(pe.sem, pe.cnt)
                E.seen["pe"] = pe.cnt
            h()

    def ps_next(self, pool=None):
        if pool is None:
            i = self.ps_i % len(self.ps)
            self.ps_i += 1
            return self.ps[i], self.bps[i]
        lst = self.ps_pools[pool]
        i = lst[self.ps_pi.get(pool, 0) % len(lst)]
        self.ps_pi[pool] = self.ps_pi.get(pool, 0) + 1
        return self.ps[i], self.bps[i]

    def ffn_phase(self, layer):
        c = self.c
        nc = self.nc
        TS = 1024
        hT3 = self.hT.rearrange("(kc p) t -> p kc t", p=128)
        wgu = self.w_gu[layer]
        wd = self.w_d[layer]
        bw = self.bw_ffn[layer]
        with contextlib.ExitStack() as st:
            hx2 = [self.sb(st, "f_hx%d" % i, [128, KC, TS], F32) for i in range(2)]
            sq = self.sb(st, "f_sq", [128, 1, 512], F32)
            rt = self.sb(st, "f_rt", [128, 512], F32)
            xn2 = [self.sb(st, "f_xn%d" % i, [128, KC, TS], BF16) for i in range(2)]
            hf = self.sb(st, "f_hf", [128, DFF // 128, TS], BF16)
            wg = [self.sb(st, "f_wg%d" % i, [128, KC, 512], BF16) for i in range(2)]
            wu = [self.sb(st, "f_wu%d" % i, [128, KC, 512], BF16) for i in range(2)]
            wdn = [self.sb(st, "f_wd%d" % i, [128, DFF // 128, 256], BF16) for i in range(2)]
            sg = [self.sb(st, "f_sg%d" % i, [128, 512], BF16) for i in range(2)]
            bsq, brt = Buf(), Buf()
            bhx2, bxn2 = [Buf(), Buf()], [Buf(), Buf()]
            bhf = [Buf() for _ in range(DFF // 128)]
            bwg = [Buf(), Buf()]
            bwdn = [Buf(), Buf()]
            bsg = [Buf(), Buf()]
            wgu3 = wgu.rearrange("(kc p) n -> p kc n", p=128)
            wd3 = wd.rearrange("(kc p) n -> p kc n", p=128)
            NG = DFF // 512
            groups = [(g0, min(512, DFF - g0)) for g0 in range(0, DFF, 512)]
            def prep(ti_):
                t0_ = ti_ * TS
                u_ = ti_ % 2
                c.dma("sp", hx2[u_][:, :, :], hT3[:, :, t0_:t0_ + TS], reads=[self.bh[ti_]], writes=[bhx2[u_]])
                self.rmsnorm_tile(hx2[u_], bhx2[u_], self.col_nffn + layer * KC, xn2[u_], bxn2[u_], TS, (sq, bsq, rt, brt))

            prep(0)
            for t0 in range(0, S, TS):
                bh = self.bh[t0 // TS]
                u = (t0 // TS) % 2
                hx, bhx, xn, bxn = hx2[u], bhx2[u], xn2[u], bxn2[u]
                for gi, (g0, gw) in enumerate(groups):
                    b = gi % 2
                    c.dma("sp", wg[b][:, :, 0:gw], wgu3[:, :, g0:g0 + gw], reads=[bw], writes=[bwg[b]])
                    c.dma("sp", wu[b][:, :, 0:gw], wgu3[:, :, DFF + g0:DFF + g0 + gw], reads=[bw], writes=[bwg[b]])
                    for j in range(gw // 128):
                        fc = (g0 // 128) + j
                        for ct in range(TS // 512):
                            cs = slice(ct * 512, (ct + 1) * 512)
                            pg, bpg = self.ps_next()
                            pu, bpu = self.ps_next()
                            for kc in range(KC):
                                c.op("pe", lambda h: h.matmul(pg[:, :], lhsT=wg[b][:, kc, j * 128:(j + 1) * 128],
                                                              rhs=xn[:, kc, cs], start=(kc == 0), stop=(kc == KC - 1)),
                                     reads=[bwg[b], bxn], writes=[bpg])
                            for kc in range(KC):
                                c.op("pe", lambda h: h.matmul(pu[:, :], lhsT=wu[b][:, kc, j * 128:(j + 1) * 128],
                                                              rhs=xn[:, kc, cs], start=(kc == 0), stop=(kc == KC - 1)),
                                     reads=[bwg[b], bxn], writes=[bpu])
                            si = self.sg_i % 2
                            self.sg_i += 1
                            c.op("act", lambda h: h.activation(out=sg[si][:, :], in_=pg[:, :], func=AF.Silu),
                                 reads=[bpg], writes=[bsg[si]])
                            c.op("dve", lambda h: h.tensor_tensor(out=hf[:, fc, cs], in0=pu[:, :], in1=sg[si][:, :],
                                                                  op=ALU.mult),
                                 reads=[bpu, bsg[si]], writes=[bhf[fc]])
                if t0 + TS < S:
                    prep(t0 // TS + 1)
                self.run_hook()
                for dg in range(D // 256):
                    b = dg % 2
                    c.dma("sp", wdn[b][:, :, :], wd3[:, :, dg * 256:(dg + 1) * 256], reads=[bw], writes=[bwdn[b]])
                    for j in range(2):
                        dc = dg * 2 + j
                        for ct in range(TS // 512):
                            cs = slice(ct * 512, (ct + 1) * 512)
                            py, bpy = self.ps_next()
                            nk = DFF // 128
                            for kc in range(nk):
                                c.op("pe", lambda h: h.matmul(py[:, :], lhsT=wdn[b][:, kc, j * 128:(j + 1) * 128],
                                                              rhs=hf[:, kc, cs], start=(kc == 0), stop=(kc == nk - 1)),
                                     reads=[bwdn[b], bhf[kc]], writes=[bpy])
                            c.op("dve", lambda h: h.tensor_tensor(out=hx[:, dc, cs], in0=py[:, :], in1=hx[:, dc, cs],
                                                                  op=ALU.add),
                                 reads=[bpy, bhx], writes=[bhx])
                c.dma("sp", hT3[:, :, t0:t0 + TS], hx[:, :, :], reads=[bhx], writes=[bh])
            c.barrier()

    def load_cast(self, st, dst, bdst, src, shape_cols, tag):
        c = self.c
        P = dst.shape[0]
        stg = self.sb(st, "lc_" + tag, [P, 2048], F32)
        bs = Buf()
        for c0 in range(0, shape_cols, 2048):
            w = min(2048, shape_cols - c0)
            c.dma("sp", stg[:, 0:w], src[:, c0:c0 + w], writes=[bs])
            c.op("dve", lambda h: h.tensor_copy(out=dst[:, c0:c0 + w], in_=stg[:, 0:w]), reads=[bs], writes=[bdst])

    def range_reduce_sincos(self, st, x, bx, n, tag, want_cos=True):
        c = self.c
        TWO_PI = 2.0 * np.pi
        ki = self.sb(st, tag + "_ki", [128, n], I32)
        kf = self.sb(st, tag + "_kf", [128, n], F32)
        sn = self.sb(st, tag + "_sn", [128, n], F32)
        cs = self.sb(st, tag + "_cs", [128, n], F32) if want_cos else None
        bki, bkf, bsn, bcs = Buf(), Buf(), Buf(), Buf()
        self._rr(x, bx, n, ki, bki, kf, bkf, sn, bsn, cs, bcs)
        return sn, bsn, cs, bcs

    def _rr(self, x, bx, n, ki, bki, kf, bkf, sn, bsn, cs, bcs):
        c = self.c
        TWO_PI = 2.0 * np.pi
        c.op("act", lambda h: h.activation(out=ki[:, 0:n], in_=x[:, 0:n], func=AF.Identity, scale=1.0 / TWO_PI),
             reads=[bx], writes=[bki])
        c.op("dve", lambda h: h.scalar_tensor_tensor(out=sn[:, 0:n], in0=ki[:, 0:n], scalar=-TWO_PI, in1=x[:, 0:n],
                                                     op0=ALU.mult, op1=ALU.add), reads=[bki, bx], writes=[bsn])
        c.op("dve", lambda h: h.tensor_scalar(out=sn[:, 0:n], in0=sn[:, 0:n], scalar1=-PI_LO, scalar2=PI_LO,
                                              op0=ALU.max, op1=ALU.min), reads=[bsn], writes=[bsn])
        if cs is not None:
            c.op("dve", lambda h: h.scalar_tensor_tensor(out=cs[:, 0:n], in0=sn[:, 0:n], scalar=-1.0, in1=sn[:, 0:n],
                                                         op0=ALU.mult, op1=ALU.max), reads=[bsn], writes=[bcs])
        c.op("act", lambda h: h.activation(out=sn[:, 0:n], in_=sn[:, 0:n], func=AF.Sin), reads=[bsn], writes=[bsn])
        if cs is not None:
            c.op("act", lambda h: h.activation(out=cs[:, 0:n], in_=cs[:, 0:n], func=AF.Sin, scale=-1.0,
                                               bias=self.halfpi_t[:, 0:1]), reads=[bcs, self.bconst], writes=[bcs])

    def qkv_phase(self, layer, w_s, wp_s, bw, bwp):
        c = self.c
        hT3 = self.hT.rearrange("(kc p) t -> p kc t", p=128)
        QT3 = self.QT.rearrange("(kc p) t -> p kc t", p=128)
        KT3 = self.KT.rearrange("(kc p) t -> p kc t", p=128)
        w3 = w_s.rearrange("(kc p) n -> p kc n", p=128)
        self.ps_pools = {"a": [0, 1, 2, 3, 4, 5, 6, 7]}
        self.ps_pi = {}
        with contextlib.ExitStack() as st:
            SF = self.sb(st, "q_SF", [128, S], F32)
            CF = self.sb(st, "q_CF", [128, S], F32)
            bSF, bCF = Buf(), Buf()
            with contextlib.ExitStack() as st2:
                posi = self.sb(st2, "q_posi", [128, S], I32)
                x = self.sb(st2, "q_x", [128, S], F32)
                bposi, bx = Buf(), Buf()
                c.dma("sp", posi[:, :], self.posb[:, :], writes=[bposi])
                c.op("dve", lambda h: h.tensor_copy(out=x[:, :], in_=posi[:, :]), reads=[bposi], writes=[bx])
                c.op("dve", lambda h: h.tensor_scalar(out=x[:, :], in0=x[:, :], scalar1=self.pvec[:, self.col_invf:self.col_invf + 1],
                                                      scalar2=None, op0=ALU.mult), reads=[bx, self.bconst], writes=[bx])
                ki = posi
                kf = self.sb(st2, "q_kf", [128, S], F32)
                self._rr(x, bx, S, ki, bposi, kf, Buf(), SF, bSF, CF, bCF)
                c.op("dve", lambda h: h.tensor_scalar(out=SF[:, :], in0=SF[:, :],
                                                      scalar1=self.pvec[:, self.col_sgnrow:self.col_sgnrow + 1],
                                                      scalar2=None, op0=ALU.mult), reads=[bSF, self.bconst], writes=[bSF])
                c.barrier()
            wq = self.sb(st, "q_w", [128, KC, 3 * D], BF16)
            pm = self.sb(st, "q_pm", [128, 128], BF16)
            ab = [self.sb(st, "q_ab%d" % i, [128, 512], BF16) for i in range(2)]
            bab = [Buf(), Buf()]
            bpmc = Buf()
            with contextlib.ExitStack() as st3:
                self.load_cast(st3, pm, bpmc, self.ropeperm, 128, "pm")
                c.barrier()
            bwq = Buf()
            for i in range(3):
                c.dma("sp", wq[:, :, i * D:(i + 1) * D], w3[:, :, i * D:(i + 1) * D], reads=[bw], writes=[bwq])
            hx = self.sb(st, "q_hx", [128, KC, 512], F32)
            sq = self.sb(st, "q_sq", [128, 4, 512], F32)
            rt = self.sb(st, "q_rt", [128, 512], F32)
            xn = self.sb(st, "q_xn", [128, KC, 512], BF16)
            qo = [self.sb(st, "q_qo%d" % i, [128, KC, 512], BF16) for i in range(2)]
            tu = [self.sb(st, "q_tu%d" % i, [128, 512], F32) for i in range(2)]
            tt_ = [self.sb(st, "q_tt%d" % i, [128, 512], F32) for i in range(2)]
            vt = [self.sb(st, "q_vt%d" % i, [128, 16, 128], BF16) for i in range(2)]
            bhx, bsq, brt, bxn = Buf(), Buf(), Buf(), Buf()
            bqo = [Buf(), Buf()]
            btu = [Buf(), Buf()]
            btt = [Buf(), Buf()]
            bvt = [Buf(), Buf()]
            for i in range(2):
                c.op("dve", lambda h: h.memset(vt[i][:, :, 64:128], 1.0), writes=[bvt[i]])
            ei = 0
            vi = 0
            for tile in range(S // 512):
                cs = slice(tile * 512, (tile + 1) * 512)
                bh = self.bh[tile // 2]
                if tile == 2:
                    self.run_hook()
                c.dma("sp", hx[:, :, :], hT3[:, :, cs], reads=[bh], writes=[bhx])
                self.rmsnorm_tile(hx, bhx, self.col_nmix + layer * KC, xn, bxn, 512, (sq, bsq, rt, brt))
                for which in range(2):
                    for oc in range(KC):
                        pa, bpa = self.ps_next("a")
                        pb, bpb = self.ps_next("a")
                        col = which * D + oc * 128
                        for kc in range(KC):
                            c.op("pe", lambda h: h.matmul(pa[:, :], lhsT=wq[:, kc, col:col + 128], rhs=xn[:, kc, :],
                                                          start=(kc == 0), stop=(kc == KC - 1)),
                                 reads=[bwq, bxn], writes=[bpa])
                        e = ei % 2
                        ei += 1
                        c.op("act", lambda h: h.activation(out=ab[e][:, :], in_=pa[:, :], func=AF.Identity),
                             reads=[bpa], writes=[bab[e]])
                        c.op("pe", lambda h: h.matmul(pb[:, :], lhsT=pm[:, :], rhs=ab[e][:, :], start=True, stop=True),
                             reads=[bab[e], bpmc], writes=[bpb])
                        c.op("dve", lambda h: h.tensor_tensor(out=tu[e][:, :], in0=pa[:, :], in1=CF[:, cs], op=ALU.mult),
                             reads=[bpa, bCF, bab[e]], writes=[btu[e]])
                        c.op("dve", lambda h: h.tensor_tensor(out=tt_[e][:, :], in0=pb[:, :], in1=SF[:, cs], op=ALU.mult),
                             reads=[bpb, bSF], writes=[btt[e]])
                        c.op("dve", lambda h: h.tensor_tensor(out=qo[which][:, oc, :], in0=tu[e][:, :], in1=tt_[e][:, :],
                                                               op=ALU.add),
                             reads=[btu[e], btt[e]], writes=[bqo[which]])
                    dst = QT3 if which == 0 else KT3
                    c.dma("sp", dst[:, :, cs], qo[which][:, :, :], reads=[bqo[which]], writes=[self.bqkv])
                for sub in range(4):
                    v = vi % 2
                    vi += 1
                    for half in range(2):
                        pv, bpv = self.ps_next("a")
                        for kc in range(KC):
                            c.op("pe", lambda h: h.matmul(pv[:, :], lhsT=xn[:, kc, sub * 128:(sub + 1) * 128],
                                                          rhs=wq[:, kc, 2 * D + half * 512:2 * D + (half + 1) * 512],
                                                          start=(kc == 0), stop=(kc == KC - 1)),
                                 reads=[bwq, bxn], writes=[bpv])
                        c.op("act", lambda h: h.activation(out=vt[v][:, half * 8:(half + 1) * 8, 0:64],
                                                           in_=pv[:, :].rearrange("p (a b) -> p a b", b=64),
                                                           func=AF.Identity),
                             reads=[bpv], writes=[bvt[v]])
                    c.dma("sp", self.V1s[:, :, tile * 4 + sub, :].rearrange("h p e -> p h e"), vt[v][:, :, :],
                          reads=[bvt[v]], writes=[self.bqkv])
            c.barrier()

    def attn_phase(self, kind):
        c = self.c
        moba = kind == "moba"
        KR = 80 if moba else 64
        self.ps_pools = {"s": [0, 1, 2, 3], "o": [4, 5], "g": [6, 7]}
        self.ps_pi = {}
        with contextlib.ExitStack() as st:
            nm = 4 if moba else 20
            mk = self.sb(st, "a_mk", [128, nm * 512], BF16)
            bmk = Buf()
            with contextlib.ExitStack() as st2:
                self.load_cast(st2, mk, bmk, (self.mmask if moba else self.dmask), nm * 512, "mk")
                c.barrier()
            kaug = [self.sb(st, "a_k%d" % i, [KR, S], BF16) for i in range(2)]
            qaug = [self.sb(st, "a_q%d" % i, [KR, S], BF16) for i in range(2)]
            v1 = [self.sb(st, "a_v%d" % i, [128, 32, 128], BF16) for i in range(2)]
            ot = [self.sb(st, "a_o%d" % i, [64, S], BF16) for i in range(2)]
            pt = [self.sb(st, "a_p%d" % i, [128, 512], BF16) for i in range(8)]
            rd = [self.sb(st, "a_rd%d" % i, [64, 512], F32) for i in range(2)]
            bk = [Buf(), Buf()]
            bq = [Buf(), Buf()]
            bv = [Buf(), Buf()]
            bot = [Buf(), Buf()]
            bpt = [Buf() for _ in range(8)]
            brd = [Buf(), Buf()]
            if moba:
                with contextlib.ExitStack() as st2:
                    oh = self.sb(st2, "a_oh", [16, S], F32)
                    boh = Buf()
                    c.dma("sp", oh[:, :], self.onehot[:, :], writes=[boh])
                    for i in range(2):
                        c.op("dve", lambda h: h.tensor_copy(out=kaug[i][64:80, :], in_=oh[:, :]), reads=[boh], writes=[bk[i]])
                    c.barrier()
                pastneg = self.sb(st, "a_pn", [128, 512], F32)
                own = self.sb(st, "a_own", [128, 512], F32)
                ident = self.sb(st, "a_id", [128, 128], F32)
                bcn = Buf()
                c.dma("sp", pastneg[:, :], self.pastneg[:, :], writes=[bcn])
                c.dma("sp", own[:, :], self.own[:, :], writes=[bcn])
                c.dma("sp", ident[:, :], self.ident[:, :], writes=[bcn])
                km = self.sb(st, "a_km", [64, 16], F32)
                kmb = self.sb(st, "a_kmb", [64, 16], BF16)
                gm = self.sb(st, "a_gm", [128, 512], F32)
                m8 = self.sb(st, "a_m8", [128, 32, 8], F32)
                thr = self.sb(st, "a_thr", [128, 32], F32)
                sel = self.sb(st, "a_sel", [128, 512], F32)
                bkm, bkmb, bgm, bm8, bthr, bsel = Buf(), Buf(), Buf(), Buf(), Buf(), Buf()
            ri = 0

            def head_prep(hd):
                b = hd % 2
                c.dma("sp", kaug[b][0:64, :], self.KT[hd * 64:(hd + 1) * 64, :], reads=[self.bqkv], writes=[bk[b]])
                c.dma("sp", qaug[b][0:64, :], self.QT[hd * 64:(hd + 1) * 64, :], reads=[self.bqkv], writes=[bq[b]])
                c.dma("sp", v1[b][:, :, :], self.V1s[hd, :, :, :], reads=[self.bqkv], writes=[bv[b]])
                if moba:
                    c.op("dve", lambda h: h.tensor_reduce(out=km[:, :], in_=kaug[b][0:64, :].rearrange("p (n k) -> p n k", k=256),
                                                          axis=mybir.AxisListType.X, op=ALU.add),
                         reads=[bk[b]], writes=[bkm])
                    c.op("dve", lambda h: h.tensor_scalar(out=kmb[:, :], in0=km[:, :], scalar1=1.0 / 256, scalar2=None,
                                                          op0=ALU.mult), reads=[bkm], writes=[bkmb])
                    pg, bpg = self.ps_next("g")
                    for qt_ in range(32):
                        c.op("pe", lambda h: h.matmul(pg[:, qt_ * 16:(qt_ + 1) * 16], lhsT=qaug[b][0:64, qt_ * 128:(qt_ + 1) * 128],
                                                      rhs=kmb[:, :], start=True, stop=True),
                             reads=[bq[b], bkmb], writes=[bpg])
                    c.op("dve", lambda h: h.tensor_tensor(out=gm[:, :], in0=pg[:, :], in1=pastneg[:, :], op=ALU.add),
                         reads=[bpg, bcn], writes=[bgm])
                    for qt_ in range(32):
                        c.op("dve", lambda h: h.max(out=m8[:, qt_, :], in_=gm[:, qt_ * 16:(qt_ + 1) * 16]),
                             reads=[bgm], writes=[bm8])
                    c.op("dve", lambda h: h.tensor_scalar(out=thr[:, :], in0=m8[:, :, 2], scalar1=-1e29, scalar2=None,
                                                          op0=ALU.max), reads=[bm8], writes=[bthr])
                    c.op("dve", lambda h: h.tensor_tensor(out=sel[:, :].rearrange("p (a b) -> p a b", b=16),
                                                          in0=gm[:, :].rearrange("p (a b) -> p a b", b=16),
                                                          in1=thr[:, :].unsqueeze(2).to_broadcast([128, 32, 16]),
                                                          op=ALU.is_ge), reads=[bgm, bthr], writes=[bsel])
                    c.op("dve", lambda h: h.tensor_tensor(out=sel[:, :], in0=sel[:, :], in1=own[:, :], op=ALU.add),
                         reads=[bsel, bcn], writes=[bsel])
                    c.op("dve", lambda h: h.tensor_scalar(out=sel[:, :], in0=sel[:, :], scalar1=-1.0, scalar2=32768.0,
                                                          op0=ALU.add, op1=ALU.mult), reads=[bsel], writes=[bsel])
                    for g4 in range(8):
                        pt_, bpt_ = self.ps_next("g")
                        for j in range(4):
                            qt_ = g4 * 4 + j
                            c.op("pe", lambda h: h.transpose(pt_[0:16, j * 128:(j + 1) * 128], sel[:, qt_ * 16:(qt_ + 1) * 16],
                                                             ident[:, :]),
                                 reads=[bsel, bcn], writes=[bpt_])
                        c.op("act", lambda h: h.activation(out=qaug[b][64:80, g4 * 512:(g4 + 1) * 512], in_=pt_[0:16, :],
                                                           func=AF.Identity), reads=[bpt_], writes=[bq[b]])

            items = []
            for hd in range(16):
                for qt in range(8):
                    k0 = 0 if moba else max(0, 4 * qt - 16)
                    kts = list(range(k0, 4 * qt + 4))
                    for idx, kt in enumerate(kts):
                        items.append((hd, qt, idx, kt, len(kts)))
            LA = 5
            state = {}

            def front(i, it):
                hd, qt, idx, kt, n = it
                b = hd % 2
                pss, bpss = self.ps_next("s")
                c.op("pe", lambda h: h.matmul(pss[:, :], lhsT=kaug[b][0:KR, kt * 128:(kt + 1) * 128],
                                              rhs=qaug[b][0:KR, qt * 512:(qt + 1) * 512], start=True, stop=True),
                     reads=[bk[b], bq[b]], writes=[bpss])
                p = i % 8
                c.op("act", lambda h: h.activation(out=pt[p][:, :], in_=pss[:, :], func=AF.Exp, scale=0.125),
                     reads=[bpss], writes=[bpt[p]])
                off = kt - 4 * qt
                m = None
                if moba:
                    if off >= 0:
                        m = off
                else:
                    m = off + 16
                if m is not None:
                    eng = "dve"
                    c.op(eng, lambda h: h.tensor_tensor(out=pt[p][:, :], in0=pt[p][:, :], in1=mk[:, m * 512:(m + 1) * 512],
                                                        op=ALU.mult), reads=[bpt[p], bmk], writes=[bpt[p]])

            def back(i, it):
                nonlocal ri
                hd, qt, idx, kt, n = it
                b = hd % 2
                p = i % 8
                if qt == 0 and idx == 0 and hd + 1 < 16:
                    head_prep(hd + 1)
                if idx == 0:
                    state["po"] = self.ps_next("o")
                po, bpo = state["po"]
                c.op("pe", lambda h: h.matmul(po[:, :], lhsT=v1[b][:, kt, :], rhs=pt[p][:, :],
                                              start=(idx == 0), stop=(idx == n - 1)),
                     reads=[bv[b], bpt[p]], writes=[bpo])
                if idx == n - 1:
                    r = ri % 2
                    ri += 1
                    c.op("dve", lambda h: h.reciprocal(out=rd[r][:, :], in_=po[64:128, :]), reads=[bpo], writes=[brd[r]])
                    c.op("dve", lambda h: h.tensor_tensor(out=ot[b][:, qt * 512:(qt + 1) * 512], in0=po[0:64, :], in1=rd[r][:, :],
                                                          op=ALU.mult), reads=[bpo, brd[r]], writes=[bot[b]])
                    if qt == 7:
                        c.dma("sp", self.OT[hd * 64:(hd + 1) * 64, :], ot[b][:, :], reads=[bot[b]], writes=[self.bot])

            head_prep(0)
            for i in range(len(items) + LA):
                if i < len(items):
                    front(i, items[i])
                if i >= LA:
                    back(i - LA, items[i - LA])
            c.barrier()

    def linres_phase(self, inT, w_s, bw, bin_):
        c = self.c
        hT3 = self.hT.rearrange("(kc p) t -> p kc t", p=128)
        in3 = inT.rearrange("(kc p) t -> p kc t", p=128)
        w3 = w_s.rearrange("(kc p) n -> p kc n", p=128)
        with contextlib.ExitStack() as st:
            w = self.sb(st, "l_w", [128, KC, D], BF16)
            bwl = Buf()
            c.dma("sp", w[:, :, :], w3[:, :, :], reads=[bw], writes=[bwl])
            a = [self.sb(st, "l_a%d" % i, [128, KC, 512], BF16) for i in range(2)]
            hx = [self.sb(st, "l_h%d" % i, [128, KC, 512], F32) for i in range(2)]
            ba = [Buf(), Buf()]
            bhx = [Buf(), Buf()]
            for tile in range(S // 512):
                cs = slice(tile * 512, (tile + 1) * 512)
                b = tile % 2
                bh = self.bh[tile // 2]
                c.dma("sp", a[b][:, :, :], in3[:, :, cs], reads=[bin_], writes=[ba[b]])
                c.dma("sp", hx[b][:, :, :], hT3[:, :, cs], reads=[bh], writes=[bhx[b]])
                for dc in range(KC):
                    ps, bps = self.ps_next()
                    for kc in range(KC):
                        c.op("pe", lambda h: h.matmul(ps[:, :], lhsT=w[:, kc, dc * 128:(dc + 1) * 128], rhs=a[b][:, kc, :],
                                                      start=(kc == 0), stop=(kc == KC - 1)),
                             reads=[bwl, ba[b]], writes=[bps])
                    c.op("dve", lambda h: h.tensor_tensor(out=hx[b][:, dc, :], in0=ps[:, :], in1=hx[b][:, dc, :], op=ALU.add),
                         reads=[bps, bhx[b]], writes=[bhx[b]])
                c.dma("sp", hT3[:, :, cs], hx[b][:, :, :], reads=[bhx[b]], writes=[bh])
            c.barrier()

    def cast_weight_qk_perm(self, src, dst, buf):
        c = self.c
        for kc in range(KC):
            for c0 in range(0, 2 * D, 512):
                i = self.cast_i % 2
                self.cast_i += 1
                st, stb, bst, bstb = self.cast_st[i]
                c.dma("pool", st[:, :], src[kc * 128:(kc + 1) * 128, c0:c0 + 512], writes=[bst])
                st3 = st[:, :].rearrange("p (a b) -> p a b", b=64)
                sb3 = stb[:, :].rearrange("p (a b) -> p a b", b=64)
                c.op("pool", lambda h: h.tensor_copy(out=stb[:, :], in_=st[:, :]), reads=[bst], writes=[bstb])
                c.op("pool", lambda h: h.tensor_copy(out=sb3[:, :, 0:8], in_=st3[:, :, 8:16]), reads=[bst], writes=[bstb])
                c.op("pool", lambda h: h.tensor_copy(out=sb3[:, :, 8:16], in_=st3[:, :, 0:8]), reads=[bst], writes=[bstb])
                c.dma("pool", dst[kc * 128:(kc + 1) * 128, c0:c0 + 512], stb[:, :], reads=[bstb], writes=[buf])

    def hgrn_prep(self, layer, w_s, bw, ebl, bebl):
        c = self.c
        hT3 = self.hT.rearrange("(kc p) t -> p kc t", p=128)
        QT3 = self.QT.rearrange("(kc p) t -> p kc t", p=128)
        KT3 = self.KT.rearrange("(kc p) t -> p kc t", p=128)
        GT3 = self.GT.rearrange("(kc p) t -> p kc t", p=128)
        w3 = w_s.rearrange("(kc p) n -> p kc n", p=128)
        self.ps_pools = {"a": [0, 1, 2, 3, 4, 5], "t": [6, 7]}
        self.ps_pi = {}
        with contextlib.ExitStack() as st:
            w = self.sb(st, "h_w", [128, KC, 4 * D], BF16)
            bww = Buf()
            for i in range(4):
                c.dma("sp", w[:, :, i * D:(i + 1) * D], w3[:, :, i * D:(i + 1) * D], reads=[bw], writes=[bww])
            ex = self.sb(st, "h_ex", [128, 4 * KC], F32)
            ssum = self.sb(st, "h_ss", [128, KC], F32)
            lb = self.sb(st, "h_lb", [128, KC], F32)
            oml = self.sb(st, "h_oml", [128, KC], F32)
            blb = Buf()
            cl = self.col_lb
            c.op("act", lambda h: h.activation(out=ex[:, :], in_=self.pvec[:, cl:cl + 4 * KC], func=AF.Exp),
                 reads=[self.bconst], writes=[blb])
            c.op("dve", lambda h: h.tensor_tensor(out=ssum[:, :], in0=ex[:, 0:KC], in1=ex[:, KC:2 * KC], op=ALU.add),
                 reads=[blb], writes=[blb])
            c.op("dve", lambda h: h.tensor_tensor(out=ssum[:, :], in0=ssum[:, :], in1=ex[:, 2 * KC:3 * KC], op=ALU.add),
                 reads=[blb], writes=[blb])
            c.op("dve", lambda h: h.tensor_tensor(out=ssum[:, :], in0=ssum[:, :], in1=ex[:, 3 * KC:4 * KC], op=ALU.add),
                 reads=[blb], writes=[blb])
            c.op("dve", lambda h: h.reciprocal(out=ssum[:, :], in_=ssum[:, :]), reads=[blb], writes=[blb])
            c.op("dve", lambda h: h.tensor_copy(out=lb[:, :], in_=ex[:, KC:2 * KC]), reads=[blb], writes=[blb])
            for l in range(2, layer + 1):
                c.op("dve", lambda h: h.tensor_tensor(out=lb[:, :], in0=lb[:, :], in1=ex[:, l * KC:(l + 1) * KC], op=ALU.add),
                     reads=[blb], writes=[blb])
            c.op("dve", lambda h: h.tensor_tensor(out=lb[:, :], in0=lb[:, :], in1=ssum[:, :], op=ALU.mult),
                 reads=[blb], writes=[blb])
            c.op("dve", lambda h: h.tensor_scalar(out=oml[:, :], in0=lb[:, :], scalar1=-1.0, scalar2=1.0, op0=ALU.mult, op1=ALU.add),
                 reads=[blb], writes=[blb])
            rmask = self.sb(st, "h_rm", [128, 512], F32)
            identb = self.sb(st, "h_idb", [128, 128], BF16)
            identf = self.sb(st, "h_idf", [128, 128], F32)
            bcn = Buf()
            c.dma("sp", rmask[:, :], self.rmask[:, :], writes=[bcn])
            c.dma("sp", identf[:, :], self.ident[:, :], writes=[bcn])
            c.op("dve", lambda h: h.tensor_copy(out=identb[:, :], in_=identf[:, :]), reads=[bcn], writes=[bcn])
            hx = self.sb(st, "h_hx", [128, KC, 512], F32)
            sq = self.sb(st, "h_sq", [128, 2, 512], F32)
            rt = self.sb(st, "h_rt", [128, 512], F32)
            xn = self.sb(st, "h_xn", [128, KC, 512], BF16)
            bhx, bsq, brt, bxn = Buf(), Buf(), Buf(), Buf()
            names = ["qs", "th", "fg", "b", "eb", "ebn", "kk"]
            TT = [{n: self.sb(st, "h_t_%s%d" % (n, i), [128, 512], F32) for n in names} for i in range(2)]
            BB_ = [{n: Buf() for n in names} for i in range(2)]
            A_ = self.sb(st, "h_A", [128, KC], F32)
            nA_ = self.sb(st, "h_nA", [128, KC], F32)
            B_ = self.sb(st, "h_B", [128, KC], F32)
            c.op("dve", lambda h: h.tensor_scalar(out=A_[:, :], in0=oml[:, :], scalar1=0.5, scalar2=None, op0=ALU.mult), reads=[blb], writes=[blb])
            c.op("dve", lambda h: h.tensor_scalar(out=nA_[:, :], in0=oml[:, :], scalar1=-0.5, scalar2=None, op0=ALU.mult), reads=[blb], writes=[blb])
            c.op("dve", lambda h: h.tensor_tensor(out=B_[:, :], in0=lb[:, :], in1=A_[:, :], op=ALU.add), reads=[blb], writes=[blb])
            qo = self.sb(st, "h_qo", [128, KC, 512], BF16)
            ko = self.sb(st, "h_ko", [128, KC, 512], BF16)
            go = self.sb(st, "h_go", [128, KC, 512], BF16)
            ktm = self.sb(st, "h_ktm", [128, 8, 4, 128], BF16)
            itm = [self.sb(st, "h_itm%d" % i, [128, 8, 128], BF16) for i in range(2)]
            bqo, bko, bgo, bktm = Buf(), Buf(), Buf(), Buf()
            bitm = [Buf(), Buf()]
            ii = 0
            for tile in range(S // 512):
                cs = slice(tile * 512, (tile + 1) * 512)
                bh = self.bh[tile // 2]
                if tile == 2:
                    self.run_hook()
                c.dma("sp", hx[:, :, :], hT3[:, :, cs], reads=[bh], writes=[bhx])
                self.rmsnorm_tile(hx, bhx, self.col_nmix + layer * KC, xn, bxn, 512, (sq, bsq, rt, brt))
                def proj(hp_):
                    PS_ = {}
                    for i, hh in enumerate((2 * hp_, 2 * hp_ + 1)):
                        PS_[i] = [self.ps_next("a") for _ in range(3)]
                        for (pp, bpp), off in zip(PS_[i], (0, D, 3 * D)):
                            col = off + hh * 128
                            for kc in range(KC):
                                c.op("pe", lambda h: h.matmul(pp[:, :], lhsT=w[:, kc, col:col + 128], rhs=xn[:, kc, :],
                                                              start=(kc == 0), stop=(kc == KC - 1)),
                                     reads=[bww, bxn], writes=[bpp])
                    return PS_

                PSn = proj(0)
                for hp in range(4):
                    hs = (2 * hp, 2 * hp + 1)
                    PS = PSn
                    for i, hh in enumerate(hs):
                        T, B = TT[i], BB_[i]
                        (pq, bpq), (pf, bpf), (pg, bpg) = PS[i]
                        c.op("act", lambda h: h.activation(out=T["qs"][:, :], in_=pq[:, :], func=AF.Silu), reads=[bpq], writes=[B["qs"]])
                        c.op("act", lambda h: h.activation(out=go[:, hh, :], in_=pg[:, :], func=AF.Silu), reads=[bpg], writes=[bgo])
                        c.op("act", lambda h: h.activation(out=T["th"][:, :], in_=pf[:, :], func=AF.Tanh, scale=0.5), reads=[bpf], writes=[B["th"]])
                    if hp + 1 < 4:
                        PSn = proj(hp + 1)
                    for i, hh in enumerate(hs):
                        T, B = TT[i], BB_[i]
                        c.op("dve", lambda h: h.tensor_scalar(out=T["fg"][:, :], in0=T["th"][:, :], scalar1=A_[:, hh:hh + 1],
                                                              scalar2=B_[:, hh:hh + 1], op0=ALU.mult, op1=ALU.add),
                             reads=[B["th"], blb], writes=[B["fg"]])
                        c.op("dve", lambda h: h.tensor_scalar(out=T["kk"][:, :], in0=T["th"][:, :], scalar1=nA_[:, hh:hh + 1],
                                                              scalar2=A_[:, hh:hh + 1], op0=ALU.mult, op1=ALU.add),
                             reads=[B["th"], blb], writes=[B["kk"]])
                    for i, hh in enumerate(hs):
                        T, B = TT[i], BB_[i]
                        c.op("act", lambda h: h.activation(out=T["fg"][:, :], in_=T["fg"][:, :], func=AF.Ln), reads=[B["fg"]], writes=[B["fg"]])
                    for i, hh in enumerate(hs):
                        T, B = TT[i], BB_[i]
                        c.op("dve", lambda h: h.tensor_tensor_scan(out=T["b"][:, :], data0=rmask[:, :], data1=T["fg"][:, :], initial=0.0,
                                                                   op0=ALU.mult, op1=ALU.add), reads=[B["fg"], bcn], writes=[B["b"]])
                    for i, hh in enumerate(hs):
                        T, B = TT[i], BB_[i]
                        c.op("act", lambda h: h.activation(out=T["eb"][:, :], in_=T["b"][:, :], func=AF.Exp), reads=[B["b"]], writes=[B["eb"]])
                        c.op("act", lambda h: h.activation(out=T["ebn"][:, :], in_=T["b"][:, :], func=AF.Exp, scale=-1.0),
                             reads=[B["b"]], writes=[B["ebn"]])
                    for i, hh in enumerate(hs):
                        T, B = TT[i], BB_[i]
                        c.op("dve", lambda h: h.tensor_tensor(out=qo[:, hh, :], in0=T["qs"][:, :], in1=T["eb"][:, :], op=ALU.mult),
                             reads=[B["qs"], B["eb"]], writes=[bqo])
                        c.op("dve", lambda h: h.tensor_tensor(out=ko[:, hh, :], in0=T["kk"][:, :], in1=T["ebn"][:, :], op=ALU.mult),
                             reads=[B["kk"], B["ebn"]], writes=[bko])
                        c.op("dve", lambda h: h.tensor_copy(out=ebl[:, hh, tile * 8:(tile + 1) * 8],
                                                            in_=T["eb"][:, :].rearrange("p (a b) -> p a b", b=64)[:, :, 63]),
                             reads=[B["eb"]], writes=[bebl])
                        ptt, bptt = self.ps_next("t")
                        ptb = ptt[:, 0:256].bitcast(BF16)
                        for sub in range(4):
                            c.op("pe", lambda h: h.transpose(ptb[:, sub * 128:(sub + 1) * 128], ko[:, hh, sub * 128:(sub + 1) * 128],
                                                             identb[:, :]), reads=[bko, bcn], writes=[bptt])
                        c.op("act", lambda h: h.activation(out=ktm[:, hh, :, :], in_=ptb.rearrange("p (a b) -> p a b", b=128),
                                                           func=AF.Identity), reads=[bptt], writes=[bktm])
                c.dma("sp", QT3[:, :, cs], qo[:, :, :], reads=[bqo], writes=[self.bqkv])
                c.dma("sp", KT3[:, :, cs], ko[:, :, :], reads=[bko], writes=[self.bqkv])
                c.dma("sp", GT3[:, :, cs], go[:, :, :], reads=[bgo], writes=[self.bqkv])
                c.dma("sp", self.V1s[0:8, :, tile * 4:(tile + 1) * 4, :].rearrange("h p s e -> p h s e"), ktm[:, :, :, :],
                      reads=[bktm], writes=[self.bqkv])
                for sub in range(4):
                    v = ii % 2
                    ii += 1
                    for half in range(2):
                        pv, bpv = self.ps_next("a")
                        for kc in range(KC):
                            c.op("pe", lambda h: h.matmul(pv[:, :], lhsT=xn[:, kc, sub * 128:(sub + 1) * 128],
                                                          rhs=w[:, kc, 2 * D + half * 512:2 * D + (half + 1) * 512],
                                                          start=(kc == 0), stop=(kc == KC - 1)),
                                 reads=[bww, bxn], writes=[bpv])
                        c.op("act", lambda h: h.activation(out=itm[v][:, half * 4:(half + 1) * 4, :],
                                                           in_=pv[:, :].rearrange("p (a b) -> p a b", b=128), func=AF.Identity),
                             reads=[bpv], writes=[bitm[v]])
                    c.dma("sp", self.V1s[8:16, :, tile * 4 + sub, :].rearrange("h p e -> p h e"), itm[v][:, :, :],
                          reads=[bitm[v]], writes=[self.bqkv])
            c.barrier()

    def hgrn_rec(self, ebl, bebl):
        c = self.c
        QT3 = self.QT.rearrange("(kc p) t -> p kc t", p=128)
        KT3 = self.KT.rearrange("(kc p) t -> p kc t", p=128)
        GT3 = self.GT.rearrange("(kc p) t -> p kc t", p=128)
        OT3 = self.OT.rearrange("(kc p) t -> p kc t", p=128)
        self.ps_pools = {"at": [0, 1], "o": [2, 3], "d": [4, 5], "n": [6, 7]}
        self.ps_pi = {}
        with contextlib.ExitStack() as st:
            tri = self.sb(st, "r_tri", [64, 64], F32)
            bcn = Buf()
            c.dma("sp", tri[:, :], self.tri[:, :], writes=[bcn])
            S32 = self.sb(st, "r_S32", [128, 8, 128], F32)
            Sbf = self.sb(st, "r_Sbf", [128, 8, 128], BF16)
            Sbf2 = [Sbf, self.sb(st, "r_Sbfb", [128, 8, 128], BF16)]
            bSbf2 = [Buf(), Buf()]
            t32 = self.sb(st, "r_t32", [128, 8, 128], F32)
            bS32, bSbf, bt32 = Buf(), Buf(), Buf()
            c.op("dve", lambda h: h.memset(S32[:, :, :], 0.0), writes=[bS32])
            c.op("dve", lambda h: h.memset(Sbf2[0][:, :, :], 0.0), writes=[bSbf2[0]])
            c.op("dve", lambda h: h.memset(Sbf2[1][:, :, :], 0.0), writes=[bSbf2[1]])
            qT = [self.sb(st, "r_q%d" % i, [128, 8, 512], BF16) for i in range(2)]
            kT = [self.sb(st, "r_k%d" % i, [128, 8, 512], BF16) for i in range(2)]
            gT = [self.sb(st, "r_g%d" % i, [128, 8, 512], BF16) for i in range(2)]
            ktm = [self.sb(st, "r_ktm%d" % i, [128, 8, 4, 128], BF16) for i in range(2)]
            itm = [self.sb(st, "r_itm%d" % i, [128, 8, 4, 128], BF16) for i in range(2)]
            bin_ = [Buf(), Buf()]
            o32 = self.sb(st, "r_o32", [128, 8, 512], F32)
            bo32 = Buf()
            at = [self.sb(st, "r_at%d" % i, [128, 8, 64], BF16) for i in range(2)]
            bat = [Buf(), Buf()]
            sq = self.sb(st, "r_sq", [128, 512], F32)
            rt = self.sb(st, "r_rt", [128, 512], F32)
            tmp = self.sb(st, "r_tmp", [128, 512], F32)
            bsq, brt, btmp = Buf(), Buf(), Buf()
            oo = [self.sb(st, "r_oo%d" % i, [128, 8, 512], BF16) for i in range(2)]
            boo = [Buf(), Buf()]
            ai = 0
            for tile in range(S // 512):
                cs = slice(tile * 512, (tile + 1) * 512)
                b = tile % 2
                c.dma("sp", qT[b][:, :, :], QT3[:, :, cs], reads=[self.bqkv], writes=[bin_[b]])
                c.dma("sp", kT[b][:, :, :], KT3[:, :, cs], reads=[self.bqkv], writes=[bin_[b]])
                c.dma("sp", gT[b][:, :, :], GT3[:, :, cs], reads=[self.bqkv], writes=[bin_[b]])
                c.dma("sp", ktm[b][:, :, :, :], self.V1s[0:8, :, tile * 4:(tile + 1) * 4, :].rearrange("h p s e -> p h s e"),
                      reads=[self.bqkv], writes=[bin_[b]])
                c.dma("sp", itm[b][:, :, :, :], self.V1s[8:16, :, tile * 4:(tile + 1) * 4, :].rearrange("h p s e -> p h s e"),
                      reads=[self.bqkv], writes=[bin_[b]])
                for ch in range(8):
                    cc = slice(ch * 64, (ch + 1) * 64)
                    prt = 64 * (ch % 2)
                    sub = ch // 2
                    gch = tile * 8 + ch
                    pdA, bpdA = self.ps[4], self.bps[4]
                    pdB, bpdB = self.ps[5], self.bps[5]
                    for hh in range(8):
                        pd, bpd = (pdA, bpdA) if hh < 4 else (pdB, bpdB)
                        hc = (hh % 4) * 128
                        c.op("pe", lambda h: h.matmul(pd[:, hc:hc + 128], lhsT=ktm[b][prt:prt + 64, hh, sub, :],
                                                      rhs=itm[b][prt:prt + 64, hh, sub, :], start=True, stop=True),
                             reads=[bin_[b]], writes=[bpd])
                    c.op("dve", lambda h: h.tensor_tensor(out=t32[:, 0:4, :], in0=pdA[:, :].rearrange("p (a b) -> p a b", b=128),
                                                          in1=S32[:, 0:4, :], op=ALU.add), reads=[bpdA, bS32], writes=[bt32])
                    c.op("dve", lambda h: h.tensor_tensor(out=t32[:, 4:8, :], in0=pdB[:, :].rearrange("p (a b) -> p a b", b=128),
                                                          in1=S32[:, 4:8, :], op=ALU.add), reads=[bpdB, bS32], writes=[bt32])
                    c.op("dve", lambda h: h.tensor_tensor(out=S32[:, :, :], in0=t32[:, :, :],
                                                          in1=ebl[:, :, gch].unsqueeze(2).to_broadcast([128, 8, 128]), op=ALU.mult),
                         reads=[bt32, bebl], writes=[bS32])
                    c.op("act", lambda h: h.activation(out=Sbf2[gch % 2][:, :, :], in_=S32[:, :, :], func=AF.Identity),
                         reads=[bS32], writes=[bSbf2[gch % 2]])
                    pat, bpat = self.ps_next("at")
                    for hh in range(8):
                        c.op("pe", lambda h: h.matmul(pat[0:64, hh * 64:(hh + 1) * 64], lhsT=kT[b][:, hh, cc], rhs=qT[b][:, hh, cc],
                                                      start=True, stop=True), reads=[bin_[b]], writes=[bpat])
                    a = ai % 2
                    ai += 1
                    c.op("dve", lambda h: h.tensor_tensor(out=at[a][prt:prt + 64, :, :],
                                                          in0=pat[0:64, :].rearrange("p (a b) -> p a b", b=64),
                                                          in1=tri[:, :].unsqueeze(1).to_broadcast([64, 8, 64]), op=ALU.mult),
                         reads=[bpat, bcn], writes=[bat[a]])
                    po, bpo = self.ps_next("o")
                    for hh in range(8):
                        c.op("pe", lambda h: h.matmul(po[:, hh * 64:(hh + 1) * 64], lhsT=Sbf2[(gch - 1) % 2][:, hh, :], rhs=qT[b][:, hh, cc],
                                                      start=True, stop=False), reads=[bSbf2[(gch - 1) % 2], bin_[b]], writes=[bpo])
                        c.op("pe", lambda h: h.matmul(po[:, hh * 64:(hh + 1) * 64], lhsT=itm[b][prt:prt + 64, hh, sub, :],
                                                      rhs=at[a][prt:prt + 64, hh, :], start=False, stop=True),
                             reads=[bin_[b], bat[a]], writes=[bpo])
                    c.op("act", lambda h: h.activation(out=o32[:, :, cc], in_=po[:, :].rearrange("p (a b) -> p a b", b=64),
                                                       func=AF.Identity), reads=[bpo], writes=[bo32])
                for hh in range(8):
                    pn, bpn = self.ps_next("n")
                    c.op("act", lambda h: h.activation(out=sq[:, :], in_=o32[:, hh, :], func=AF.Square), reads=[bo32], writes=[bsq])
                    c.op("pe", lambda h: h.matmul(pn[:, :], lhsT=self.ones_f[:, :], rhs=sq[:, :], start=True, stop=True),
                         reads=[bsq, self.bconst], writes=[bpn])
                    c.op("act", lambda h: h.activation(out=rt[:, :], in_=pn[:, :], func=AF.Sqrt, scale=1.0 / 128,
                                                       bias=self.eps_t[:, 0:1]), reads=[bpn, self.bconst], writes=[brt])
                    c.op("dve", lambda h: h.reciprocal(out=rt[:, :], in_=rt[:, :]), reads=[brt], writes=[brt])
                    c.op("dve", lambda h: h.scalar_tensor_tensor(out=tmp[:, :], in0=o32[:, hh, :],
                                                                 scalar=self.pvec[:, self.col_hnorm:self.col_hnorm + 1],
                                                                 in1=rt[:, :], op0=ALU.mult, op1=ALU.mult),
                         reads=[bo32, brt, self.bconst], writes=[btmp])
                    c.op("dve", lambda h: h.tensor_tensor(out=oo[b][:, hh, :], in0=tmp[:, :], in1=gT[b][:, hh, :], op=ALU.mult),
                         reads=[btmp, bin_[b]], writes=[boo[b]])
                c.dma("sp", OT3[:, :, cs], oo[b][:, :, :], reads=[boo[b]], writes=[self.bot])
            c.barrier()

    def s5_phase(self, layer, wglu_s, bw):
        c = self.c
        hT3 = self.s5_src.rearrange("(kc p) t -> p kc t", p=128)
        w3 = wglu_s.rearrange("(kc p) n -> p kc n", p=128)
        self.ps_pools = {"y": [0, 1, 2, 3], "e": [4, 5, 6, 7]}
        self.ps_pi = {}
        TS = 1024
        sgnA = self.pvec[:, self.col_sgnA:self.col_sgnA + 1]
        sgnB = self.pvec[:, self.col_sgnA + 1:self.col_sgnA + 2]
        neg1 = self.pvec[:, self.col_sgnA + 2:self.col_sgnA + 3]
        with contextlib.ExitStack() as st:
            Bp = self.sb(st, "s_Bp", [128, 8, 4, 128], BF16)
            Bq = self.sb(st, "s_Bq", [128, 8, 4, 128], BF16)
            Cc = self.sb(st, "s_Cc", [128, 64, 64], BF16)
            Cs = self.sb(st, "s_Cs", [128, 64, 64], BF16)
            th = self.sb(st, "s_th", [128, 64], F32)
            rho = self.sb(st, "s_rho", [128, 64], F32)
            carry = self.sb(st, "s_carry", [128, 64], F32)
            bpar, bcarry = Buf(), Buf()
            with contextlib.ExitStack() as s2:
                def t64(n):
                    return self.sb(s2, "s_" + n, [128, 64], F32)
                are, aim, ldt, dt, lr, x0, kf0, sn0, cs0, abr, abi, den, mre, fre, fim, tA, tB = [t64(n) for n in (
                    "are", "aim", "ldt", "dt", "lr", "x0", "kf0", "sn0", "cs0", "abr", "abi", "den", "mre", "fre", "fim", "tA", "tB")]
                ki0 = self.sb(s2, "s_ki0", [128, 64], I32)
                bl = Buf()
                c.dma("sp", are[:, :], self.s5p[:, 0:64], writes=[bl])
                c.dma("sp", aim[:, :], self.s5p[:, 64:128], writes=[bl])
                c.dma("sp", ldt[:, :], self.s5p[:, 128:192], writes=[bl])
                b0 = Buf()
                R, W = [bl, b0], [b0]
                c.op("act", lambda h: h.activation(out=dt[:, :], in_=ldt[:, :], func=AF.Exp), reads=R, writes=W)
                c.op("dve", lambda h: h.tensor_tensor(out=lr[:, :], in0=are[:, :], in1=dt[:, :], op=ALU.mult), reads=R, writes=W)
                c.op("dve", lambda h: h.tensor_tensor(out=th[:, :], in0=aim[:, :], in1=dt[:, :], op=ALU.mult), reads=R, writes=[b0, bpar])
                c.op("act", lambda h: h.activation(out=rho[:, :], in_=lr[:, :], func=AF.Exp), reads=R, writes=[b0, bpar])
                self._rr(th, b0, 64, ki0, b0, kf0, b0, sn0, b0, cs0, b0)
                c.op("dve", lambda h: h.tensor_tensor(out=abr[:, :], in0=rho[:, :], in1=cs0[:, :], op=ALU.mult), reads=R, writes=W)
                c.op("dve", lambda h: h.tensor_tensor(out=abi[:, :], in0=rho[:, :], in1=sn0[:, :], op=ALU.mult), reads=R, writes=W)
                c.op("dve", lambda h: h.tensor_tensor(out=den[:, :], in0=are[:, :], in1=are[:, :], op=ALU.mult), reads=R, writes=W)
                c.op("dve", lambda h: h.tensor_tensor(out=tA[:, :], in0=aim[:, :], in1=aim[:, :], op=ALU.mult), reads=R, writes=W)
                c.op("dve", lambda h: h.tensor_tensor(out=den[:, :], in0=den[:, :], in1=tA[:, :], op=ALU.add), reads=R, writes=W)
                c.op("dve", lambda h: h.reciprocal(out=den[:, :], in_=den[:, :]), reads=R, writes=W)
                c.op("dve", lambda h: h.tensor_scalar(out=mre[:, :], in0=abr[:, :], scalar1=-1.0, scalar2=None, op0=ALU.add), reads=R, writes=W)
                c.op("dve", lambda h: h.tensor_tensor(out=tA[:, :], in0=mre[:, :], in1=are[:, :], op=ALU.mult), reads=R, writes=W)
                c.op("dve", lambda h: h.tensor_tensor(out=tB[:, :], in0=abi[:, :], in1=aim[:, :], op=ALU.mult), reads=R, writes=W)
                c.op("dve", lambda h: h.tensor_tensor(out=fre[:, :], in0=tA[:, :], in1=tB[:, :], op=ALU.add), reads=R, writes=W)
                c.op("dve", lambda h: h.tensor_tensor(out=fre[:, :], in0=fre[:, :], in1=den[:, :], op=ALU.mult), reads=R, writes=W)
                c.op("dve", lambda h: h.tensor_tensor(out=tA[:, :], in0=abi[:, :], in1=are[:, :], op=ALU.mult), reads=R, writes=W)
                c.op("dve", lambda h: h.tensor_tensor(out=tB[:, :], in0=mre[:, :], in1=aim[:, :], op=ALU.mult), reads=R, writes=W)
                c.op("dve", lambda h: h.tensor_tensor(out=fim[:, :], in0=tA[:, :], in1=tB[:, :], op=ALU.subtract), reads=R, writes=W)
                c.op("dve", lambda h: h.tensor_tensor(out=fim[:, :], in0=fim[:, :], in1=den[:, :], op=ALU.mult), reads=R, writes=W)
                c.op("dve", lambda h: h.tensor_scalar(out=tA[:, :], in0=fim[:, :], scalar1=sgnA, scalar2=None, op0=ALU.mult),
                     reads=[b0, self.bconst], writes=W)
                c.op("dve", lambda h: h.tensor_scalar(out=tB[:, :], in0=fre[:, :], scalar1=sgnB, scalar2=None, op0=ALU.mult),
                     reads=[b0, self.bconst], writes=W)
                BB = self.sb(s2, "s_BB", [128, 1024], F32)
                BS = self.sb(s2, "s_BS", [128, 1024], F32)
                u1 = self.sb(s2, "s_u1", [128, 1024], F32)
                u2 = self.sb(s2, "s_u2", [128, 1024], F32)
                Z = self.sb(s2, "s_Z", [128, 8, 4, 128], F32)
                idf = self.sb(s2, "s_idf", [128, 128], F32)
                c.dma("sp", BB[:, :], self.s5B[:, 0:1024], writes=[bl])
                c.dma("sp", BS[:, :], self.s5B[:, 1024:2048], writes=[bl])
                c.dma("sp", idf[:, :], self.ident[:, :], writes=[bl])

                def bc(t):
                    return t[:, :].unsqueeze(2).to_broadcast([128, 64, 16])

                def v3(t):
                    return t[:, :].rearrange("p (g c) -> p g c", c=16)
                for (dst, ca, cb) in ((Bp, (fre, BB, tA, BS), None), (Bq, (tB, BS, fim, BB), None)):
                    fa, Xa, fb, Xb = ca
                    c.op("dve", lambda h: h.memset(Z[:, :, :, :], 0.0), reads=R, writes=W)
                    c.op("dve", lambda h: h.tensor_tensor(out=v3(u1), in0=v3(Xa), in1=bc(fa), op=ALU.mult), reads=R, writes=W)
                    c.op("dve", lambda h: h.tensor_tensor(out=v3(u2), in0=v3(Xb), in1=bc(fb), op=ALU.mult), reads=R, writes=W)
                    for par in range(4):
                        zv = Z[:, :, par, :].rearrange("p k (m q) -> p k m q", q=64)[:, :, :, 16 * par:16 * par + 16]
                        a1 = u1[:, :].rearrange("p (k m r c) -> p k m r c", k=8, m=2, r=4, c=16)[:, :, :, par, :]
                        a2 = u2[:, :].rearrange("p (k m r c) -> p k m r c", k=8, m=2, r=4, c=16)[:, :, :, par, :]
                        c.op("dve", lambda h: h.tensor_tensor(out=zv, in0=a1, in1=a2, op=ALU.add), reads=R, writes=W)
                    for kc in range(8):
                        for par in range(4):
                            pz, bpz = self.ps_next("e")
                            c.op("pe", lambda h: h.transpose(pz[:, 0:128], Z[:, kc, par, :], idf[:, :]), reads=R, writes=[bpz])
                            c.op("act", lambda h: h.activation(out=dst[:, kc, par, :], in_=pz[:, 0:128], func=AF.Identity),
                                 reads=[bpz], writes=[bpar])
                CC = self.sb(s2, "s_CC", [128, 2048], F32)
                c.dma("sp", CC[:, :], self.s5C[:, :], writes=[bl])
                c.op("dve", lambda h: h.memset(Cc[:, :, :], 0.0), writes=[bpar])
                c.op("dve", lambda h: h.memset(Cs[:, :, :], 0.0), writes=[bpar])
                for (dst, off, sg) in ((Cc, 0, sgnB), (Cs, 1024, neg1)):
                    for par in range(4):
                        dv = dst[:, :, :].rearrange("p (gm r) (q c) -> p gm r q c", r=4, q=4)[:, :, par, par, :]
                        sv = CC[:, off:off + 1024].rearrange("p (gm r c) -> p gm r c", r=4, c=16)[:, :, par, :]
                        c.op("dve", lambda h: h.tensor_scalar(out=dv, in0=sv, scalar1=sg, scalar2=None, op0=ALU.mult),
                             reads=[bl, self.bconst], writes=[bpar])
                c.op("dve", lambda h: h.memset(carry[:, :], 0.0), writes=[bcarry])
                c.barrier()
            tt = self.sb(st, "s_tt", [128, TS], F32)
            thq = self.sb(st, "s_thq", [128, 64], F32)
            btt, bthq = Buf(), Buf()
            c.dma("sp", tt[:, :], self.ttc[:, 0:TS], writes=[btt])
            wg = self.sb(st, "s_wg", [128, KC, 2 * D], BF16)
            bwg = Buf()
            for i in range(2):
                c.dma("sp", wg[:, :, i * D:(i + 1) * D], w3[:, :, i * D:(i + 1) * D], reads=[bw], writes=[bwg])
            hx = self.sb(st, "s_hx", [128, KC, 512], F32)
            sq = self.sb(st, "s_sq", [128, 2, 512], F32)
            rt = self.sb(st, "s_rt", [128, 512], F32)
            xn = self.sb(st, "s_xn", [128, KC, TS], BF16)
            z = self.sb(st, "s_z", [128, KC, TS], BF16)
            bhx, bsq, brt, bxn, bz = Buf(), Buf(), Buf(), Buf(), Buf()
            x = self.sb(st, "s_x", [128, TS], F32)
            ki = self.sb(st, "s_ki", [128, TS], I32)
            sn = [self.sb(st, "s_sn%d" % i, [128, TS], F32) for i in range(3)]
            cs_ = [self.sb(st, "s_cs%d" % i, [128, TS], F32) for i in range(3)]
            eh = [self.sb(st, "s_eh%d" % i, [128, TS], F32) for i in range(2)]
            G = [self.sb(st, "s_G%d" % i, [128, TS], F32) for i in range(2)]
            Gc = [self.sb(st, "s_Gc%d" % i, [128, TS], BF16) for i in range(2)]
            Gs = [self.sb(st, "s_Gs%d" % i, [128, TS], BF16) for i in range(2)]
            bsn, bcs = [Buf() for _ in range(3)], [Buf() for _ in range(3)]
            beh, bG, bGc, bGs = [[Buf() for _ in range(2)] for _ in range(4)]
            bx, bki = Buf(), Buf()
            t1 = [self.sb(st, "s_t1%d" % i, [128, 512], F32) for i in range(2)]
            t2 = [self.sb(st, "s_t2%d" % i, [128, 512], F32) for i in range(2)]
            bt1 = [Buf(), Buf()]
            bt2 = [Buf(), Buf()]
            yv = self.sb(st, "s_yv", [128, 512], F32)
            y2 = self.sb(st, "s_y2", [128, 512], F32)
            y3 = self.sb(st, "s_y3", [128, 512], F32)
            byv, by2, by3 = Buf(), Buf(), Buf()
            hr = [self.sb(st, "s_hr", [128, TS], F32)] * 2
            bhr = [Buf()] * 2
            ti = 0
            for q in range(S // TS):
                t0 = q * TS
                bh = self.bh[q]
                c.op("dve", lambda h: h.tensor_scalar(out=thq[:, :], in0=th[:, :], scalar1=float(t0), scalar2=None, op0=ALU.mult),
                     reads=[bpar], writes=[bthq])
                for half in range(2):
                    gate_ = self.s5_gate if (q == 0 and half == 0) else []
                    c.dma("sp", hx[:, :, :], hT3[:, :, t0 + half * 512:t0 + (half + 1) * 512], reads=[bh] + gate_, writes=[bhx])
                    self.rmsnorm_tile(hx, bhx, self.col_nmix + layer * KC, xn[:, :, half * 512:(half + 1) * 512], bxn, 512,
                                      (sq, bsq, rt, brt))
                pys = {}
                pool_eng = "dve" if (q == 0 and self.s5_spare_pool) else "pool"

                TWO_PI = 2.0 * np.pi

                def S1(g):
                    kc, gi = g // 8, g % 8
                    u3 = g % 3
                    if gi == 0:
                        pys[kc] = [self.ps_next("y") for _ in range(2)]
                    c.op("act", lambda h: h.activation(out=x[:, :], in_=tt[:, :], func=AF.Identity, scale=th[:, g:g + 1],
                                                       bias=thq[:, g:g + 1]),
                         reads=[btt, bpar, bthq], writes=[bx])
                    c.op("act", lambda h: h.activation(out=ki[:, :], in_=x[:, :], func=AF.Identity, scale=1.0 / TWO_PI),
                         reads=[bx], writes=[bki])

                def S1b(g):
                    u3 = g % 3
                    c.op("dve", lambda h: h.scalar_tensor_tensor(out=sn[u3][:, :], in0=ki[:, :], scalar=-TWO_PI, in1=x[:, :],
                                                                 op0=ALU.mult, op1=ALU.add), reads=[bki, bx], writes=[bsn[u3]])
                    c.op("dve", lambda h: h.tensor_scalar(out=sn[u3][:, :], in0=sn[u3][:, :], scalar1=-PI_LO, scalar2=PI_LO,
                                                          op0=ALU.max, op1=ALU.min), reads=[bsn[u3]], writes=[bsn[u3]])
                    c.op("act", lambda h: h.activation(out=cs_[u3][:, :], in_=sn[u3][:, :], func=AF.Abs),
                         reads=[bsn[u3]], writes=[bcs[u3]])

                def S2a(g):
                    u3 = g % 3
                    c.op("act", lambda h: h.activation(out=sn[u3][:, :], in_=sn[u3][:, :], func=AF.Sin), reads=[bsn[u3]], writes=[bsn[u3]])
                    c.op("act", lambda h: h.activation(out=cs_[u3][:, :], in_=cs_[u3][:, :], func=AF.Sin, scale=-1.0,
                                                       bias=self.halfpi_t[:, 0:1]), reads=[bcs[u3], self.bconst], writes=[bcs[u3]])

                def S2(g):
                    nonlocal ti
                    kc, gi = g // 8, g % 8
                    u3, u = g % 3, g % 2
                    m, par = gi // 4, gi % 4
                    rows = slice(64 * m, 64 * m + 64)
                    for ct in range(2):
                        cc = slice(ct * 512, (ct + 1) * 512)
                        pe_, bpe = self.ps_next("e")
                        pq_, bpq = self.ps_next("e")
                        c.op("pe", lambda h: h.matmul(pe_[:, :], lhsT=Bp[rows, kc, par, :], rhs=xn[rows, kc, cc], start=True, stop=True),
                             reads=[bpar, bxn], writes=[bpe])
                        c.op("pe", lambda h: h.matmul(pq_[:, :], lhsT=Bq[rows, kc, par, :], rhs=xn[rows, kc, cc], start=True, stop=True),
                             reads=[bpar, bxn], writes=[bpq])
                        e = ti % 2
                        ti += 1
                        c.op("dve", lambda h: h.tensor_tensor(out=t1[e][:, :], in0=pe_[:, :], in1=cs_[u3][:, cc], op=ALU.mult),
                             reads=[bpe, bcs[u3]], writes=[bt1[e]])
                        c.op("dve", lambda h: h.tensor_tensor(out=t2[e][:, :], in0=pq_[:, :], in1=sn[u3][:, cc], op=ALU.mult),
                             reads=[bpq, bsn[u3]], writes=[bt2[e]])
                        c.op(pool_eng, lambda h: h.tensor_tensor(out=eh[u][:, cc], in0=t1[e][:, :], in1=t2[e][:, :], op=ALU.add),
                             reads=[bt1[e], bt2[e]], writes=[beh[u]])

                def S3(g):
                    u3, u = g % 3, g % 2
                    c.op("dve", lambda h: h.tensor_tensor_scan(out=G[u][:, :], data0=rho[:, g:g + 1].to_broadcast([128, TS]),
                                                               data1=eh[u][:, :], initial=carry[:, g:g + 1],
                                                               op0=ALU.mult, op1=ALU.add),
                         reads=[beh[u], bpar, bcarry], writes=[bG[u]])
                    c.op("act", lambda h: h.activation(out=carry[:, g:g + 1], in_=G[u][:, TS - 1:TS], func=AF.Identity),
                         reads=[bG[u]], writes=[bcarry])
                    c.op("dve", lambda h: h.tensor_tensor(out=Gc[u][:, :], in0=G[u][:, :], in1=cs_[u3][:, :], op=ALU.mult),
                         reads=[bG[u], bcs[u3]], writes=[bGc[u]])
                    c.op(pool_eng, lambda h: h.tensor_tensor(out=Gs[u][:, :], in0=G[u][:, :], in1=sn[u3][:, :], op=ALU.mult),
                         reads=[bG[u], bsn[u3]], writes=[bGs[u]])

                def S4(g):
                    kc, gi = g // 8, g % 8
                    u = g % 2
                    m, par = gi // 4, gi % 4
                    rows = slice(64 * m, 64 * m + 64)
                    py = pys[kc]
                    for ct in range(2):
                        cc = slice(ct * 512, (ct + 1) * 512)
                        pyy, bpy = py[ct]
                        c.op("pe", lambda h: h.matmul(pyy[rows, :], lhsT=Cc[:, g, :], rhs=Gc[u][:, cc], start=(par == 0), stop=False),
                             reads=[bpar, bGc[u]], writes=[bpy])
                        c.op("pe", lambda h: h.matmul(pyy[rows, :], lhsT=Cs[:, g, :], rhs=Gs[u][:, cc], start=False, stop=(par == 3)),
                             reads=[bpar, bGs[u]], writes=[bpy])
                    if gi == 7:
                        for ct in range(2):
                            cc = slice(ct * 512, (ct + 1) * 512)
                            pyy, bpy = py[ct]
                            dcol = self.pvec[:, self.col_s5d + kc:self.col_s5d + kc + 1]
                            c.op("dve", lambda h: h.scalar_tensor_tensor(out=yv[:, :], in0=xn[:, kc, cc], scalar=dcol, in1=pyy[:, :],
                                                                         op0=ALU.mult, op1=ALU.add),
                                 reads=[bxn, bpy, self.bconst], writes=[byv])
                            c.op("act", lambda h: h.activation(out=y2[:, :], in_=yv[:, :], func=AF.Square), reads=[byv], writes=[by2])
                            c.op(pool_eng, lambda h: h.tensor_scalar(out=y2[:, :], in0=y2[:, :], scalar1=0.044715, scalar2=1.0,
                                                                   op0=ALU.mult, op1=ALU.add), reads=[by2], writes=[by2])
                            c.op(pool_eng, lambda h: h.tensor_tensor(out=y2[:, :], in0=y2[:, :], in1=yv[:, :], op=ALU.mult),
                                 reads=[by2, byv], writes=[by2])
                            c.op("act", lambda h: h.activation(out=y3[:, :], in_=y2[:, :], func=AF.Sigmoid, scale=1.5957691216),
                                 reads=[by2], writes=[by3])
                            c.op(pool_eng, lambda h: h.tensor_tensor(out=z[:, kc, cc], in0=y3[:, :], in1=yv[:, :], op=ALU.mult),
                                 reads=[by3, byv], writes=[bz])

                NG = 64
                for i in range(NG + 3):
                    if 0 <= i - 1 < NG:
                        S2a(i - 1)
                    if 0 <= i - 2 < NG:
                        S3(i - 2)
                    if i < NG:
                        S1(i)
                    if 0 <= i - 1 < NG:
                        S2(i - 1)
                    if i < NG:
                        S1b(i)
                    if 0 <= i - 3 < NG:
                        S4(i - 3)
                for oc in range(KC):
                    r = oc % 2
                    c.dma("sp", hr[r][:, :], self.s5_src[oc * 128:(oc + 1) * 128, t0:t0 + TS], reads=[bh], writes=[bhr[r]])
                    for ct in range(2):
                        cc = slice(ct * 512, (ct + 1) * 512)
                        pv, bpv = self.ps_next("e")
                        pg, bpg = self.ps_next("e")
                        for kc in range(KC):
                            c.op("pe", lambda h: h.matmul(pv[:, :], lhsT=wg[:, kc, oc * 128:(oc + 1) * 128], rhs=z[:, kc, cc],
                                                          start=(kc == 0), stop=(kc == KC - 1)), reads=[bwg, bz], writes=[bpv])
                        for kc in range(KC):
                            c.op("pe", lambda h: h.matmul(pg[:, :], lhsT=wg[:, kc, D + oc * 128:D + (oc + 1) * 128], rhs=z[:, kc, cc],
                                                          start=(kc == 0), stop=(kc == KC - 1)), reads=[bwg, bz], writes=[bpg])
                        bv = self.pvec[:, self.col_bglu + oc:self.col_bglu + oc + 1]
                        bg = self.pvec[:, self.col_bglu + 8 + oc:self.col_bglu + 8 + oc + 1]
                        c.op("act", lambda h: h.activation(out=y2[:, :], in_=pv[:, :], func=AF.Identity, bias=bv),
                             reads=[bpv, self.bconst], writes=[by2])
                        c.op("act", lambda h: h.activation(out=y3[:, :], in_=pg[:, :], func=AF.Sigmoid, bias=bg),
                             reads=[bpg, self.bconst], writes=[by3])
                        c.op("dve", lambda h: h.tensor_tensor(out=y2[:, :], in0=y2[:, :], in1=y3[:, :], op=ALU.mult),
                             reads=[by2, by3], writes=[by2])
                        c.op("dve", lambda h: h.tensor_tensor(out=hr[r][:, cc], in0=hr[r][:, cc], in1=y2[:, :], op=ALU.add),
                             reads=[by2, bhr[r]], writes=[bhr[r]])
                    c.dma("sp", self.hT[oc * 128:(oc + 1) * 128, t0:t0 + TS], hr[r][:, :], reads=[bhr[r]], writes=[bh])
            c.barrier()

    def final_phase(self, outT, do_norm):
        c = self.c
        hT3 = self.hT.rearrange("(kc p) t -> p kc t", p=128)
        oT3 = outT.rearrange("(kc p) t -> p kc t", p=128)
        with contextlib.ExitStack() as st:
            hx = [self.sb(st, "o_hx%d" % i, [128, KC, 512], F32) for i in range(2)]
            ox = [self.sb(st, "o_ox%d" % i, [128, KC, 512], F32) for i in range(2)]
            sq = self.sb(st, "o_sq", [128, KC, 512], F32)
            rt = self.sb(st, "o_rt", [128, 512], F32)
            bhx = [Buf(), Buf()]
            box = [Buf(), Buf()]
            bsq, brt, bo = Buf(), Buf(), Buf()
            for tile in range(S // 512):
                cs = slice(tile * 512, (tile + 1) * 512)
                b = tile % 2
                c.dma("sp", hx[b][:, :, :], hT3[:, :, cs], reads=[self.bh[tile // 2]], writes=[bhx[b]])
                if do_norm:
                    self.rmsnorm_tile(hx[b], bhx[b], self.col_nfin, ox[b], box[b], 512, (sq, bsq, rt, brt))
                    c.dma("sp", oT3[:, :, cs], ox[b][:, :, :], reads=[box[b]], writes=[bo])
                else:
                    c.dma("sp", oT3[:, :, cs], hx[b][:, :, :], reads=[bhx[b]], writes=[bo])
            c.barrier()

    def build(self):
        cfg = self.cfg
        nc = self.nc
        c = self.c
        es = self.es
        stages = cfg["stages"]
        mixl = [l for (k, l) in stages if k == "mix"]
        ffnl = [l for (k, l) in stages if k == "ffn"]
        xT = self.din("xT", [D, S])
        pvec = self.din("pvec", [128, cfg["npvec"]])
        self.col_nmix, self.col_nffn, self.col_nfin = 0, 4 * KC, 8 * KC
        self.col_invf, self.col_sgnrow = 9 * KC, 9 * KC + 1
        self.posb = self.din("posb", [128, S], I32)
        self.dmask = self.din("dmask", [128, 20 * 512])
        self.mmask = self.din("mmask", [128, 4 * 512])
        self.onehot = self.din("onehot", [16, S])
        self.pastneg = self.din("pastneg", [128, 512])
        self.own = self.din("own", [128, 512])
        self.ident = self.din("ident", [128, 128])
        self.ropeperm = self.din("ropeperm", [128, 128])
        self.rmask = self.din("rmask", [128, 512])
        self.tri = self.din("tri", [64, 64])
        self.col_lb, self.col_hnorm = 9 * KC + 2, 9 * KC + 2 + 4 * KC
        self.col_sgnA = self.col_hnorm + 1
        self.col_s5d = self.col_sgnA + 3
        self.col_bglu = self.col_s5d + KC
        self.s5p = self.din("s5p", [128, 192])
        self.s5B = self.din("s5B", [128, 2048])
        self.s5C = self.din("s5C", [128, 2048])
        self.ttc = self.din("ttc", [128, S])
        w_gu_in = {l: self.din("w_gu%d" % l, [D, 2 * DFF]) for l in ffnl}
        w_d_in = {l: self.din("w_d%d" % l, [DFF, D]) for l in ffnl}
        self.w_gu = {l: self.dscr("s_wgu%d" % l, [D, 2 * DFF], BF16) for l in ffnl}
        self.w_d = {l: self.dscr("s_wd%d" % l, [DFF, D], BF16) for l in ffnl}
        self.bw_ffn = {l: Buf() for l in ffnl}
        win, wsc, bwm = {}, {}, {}
        for l in mixl:
            if l in (1, 3):
                nm = "dil" if l == 1 else "moba"
                win[l] = (self.din(nm + "_qkv", [D, 3 * D]), self.din(nm + "_o", [D, D]))
                wsc[l] = (self.dscr("s_%s_qkv" % nm, [D, 3 * D], BF16), None,
                          self.dscr("s_%s_o" % nm, [D, D], BF16))
                bwm[l] = Buf()
            elif l == 0:
                win[l] = (self.din("s5_wglu", [D, 2 * D]),)
                wsc[l] = (self.dscr("s_s5_wglu", [D, 2 * D], BF16),)
                bwm[l] = Buf()
            elif l == 2:
                win[l] = (self.din("hgrn_in", [D, 4 * D]), self.din("hgrn_o", [D, D]))
                wsc[l] = (self.dscr("s_hgrn_in", [D, 4 * D], BF16), self.dscr("s_hgrn_o", [D, D], BF16))
                bwm[l] = Buf()
        outT = nc.dram_tensor("outT", [D, S], F32, kind="ExternalOutput").ap()
        self.GT = self.dscr("GT", [D, S], BF16)
        self.hT = self.dscr("hT", [D, S], F32)
        self.QT = self.dscr("QT", [D, S], BF16)
        self.KT = self.dscr("KT", [D, S], BF16)
        self.OT = self.dscr("OT", [D, S], BF16)
        self.V1s = self.dscr("V1s", [16, 128, 32, 128], BF16)
        self.bqkv, self.bot = Buf(), Buf()
        self.bh = [Buf() for _ in range(S // 1024)]
        self.pvec = self.sb(es, "pvec_sb", [128, cfg["npvec"]], F32)
        self.ones_f = self.sb(es, "ones_f", [128, 128], F32)
        self.eps_t = self.sb(es, "eps_t", [128, 1], F32)
        self.bconst = Buf()
        c.dma("sp", self.pvec[:, :], pvec[:, :], writes=[self.bconst])
        c.op("dve", lambda h: h.memset(self.ones_f[:, :], 1.0), writes=[self.bconst])
        c.op("dve", lambda h: h.memset(self.eps_t[:, :], EPS), writes=[self.bconst])
        self.halfpi_t = self.sb(es, "halfpi_t", [128, 1], F32)
        c.op("dve", lambda h: h.memset(self.halfpi_t[:, :], float(np.pi / 2)), writes=[self.bconst])
        self.ps = [es.enter_context(nc.psum_tensor("ps%d" % i, [128, 512], F32)) for i in range(8)]
        self.bps = [Buf() for _ in range(8)]
        self.ps_i = 0
        self.sg_i = 0
        self.ps_pools = {}
        self.ps_pi = {}
        self.cast_i = 0
        self.cast_st = []
        for i in range(2):
            self.cast_st.append((self.sb(es, "cst%d" % i, [128, 512], F32), self.sb(es, "cstb%d" % i, [128, 512], BF16),
                                 Buf(), Buf()))
        xT3 = xT.rearrange("(kc p) t -> p kc t", p=128)
        hT3 = self.hT.rearrange("(kc p) t -> p kc t", p=128)
        self.s5_src = self.hT
        if stages[0] == ("mix", 0):
            self.s5_src = xT
        else:
            with contextlib.ExitStack() as st:
                tmp = [self.sb(st, "cp%d" % i, [128, KC, 1024], F32) for i in range(2)]
                bt = [Buf(), Buf()]
                for i, t0 in enumerate(range(0, S, 1024)):
                    c.dma("sp", tmp[i % 2][:, :, :], xT3[:, :, t0:t0 + 1024], writes=[bt[i % 2]])
                    c.dma("sp", hT3[:, :, t0:t0 + 1024], tmp[i % 2][:, :, :], reads=[bt[i % 2]], writes=[self.bh[i]])
                c.barrier()
        cast_done = set()

        def emit_cast(si_):
            if si_ >= len(stages) or si_ in cast_done:
                return
            cast_done.add(si_)
            k, l = stages[si_]
            if k == "ffn":
                self.cast_weight(w_gu_in[l], self.w_gu[l], D, 2 * DFF, self.bw_ffn[l])
                self.cast_weight(w_d_in[l], self.w_d[l], DFF, D, self.bw_ffn[l])
            elif k == "mix" and l in (1, 3):
                self.cast_weight(win[l][0], wsc[l][0], D, 3 * D, bwm[l])
                self.cast_weight(win[l][1], wsc[l][2], D, D, bwm[l])
            elif k == "mix" and l == 0:
                self.cast_weight(win[l][0], wsc[l][0], D, 2 * D, bwm[l])
            elif k == "mix" and l == 2:
                self.cast_weight(win[l][0], wsc[l][0], D, 4 * D, bwm[l])
                self.cast_weight(win[l][1], wsc[l][1], D, D, bwm[l])
        self.hook = None
        bwp = {l: Buf() for l in mixl}
        for si, (k, l) in enumerate(stages):
            for (k2, l2) in stages[si + 1:si + 2] + (stages[0:1] if si == 0 else []):
                if False:
                    self.cast_weight_qk_perm(win[l2][0], wsc[l2][1], bwp[l2])
                    bwp[l2].done = True
            emit_cast(si)
            if si == 0:
                emit_cast(1)
            self.s5_spare_pool = False
            self.s5_gate = []
            if si == 0 and len(stages) > 1 and stages[1][0] == "ffn":
                self.s5_gate = [self.bw_ffn[stages[1][1]]]
            self.hook = lambda si=si: emit_cast(si + 1)
            if k == "ffn":
                self.ffn_phase(l)
            elif k == "mix" and l in (1, 3):
                self.qkv_phase(l, wsc[l][0], wsc[l][1], bwm[l], bwp[l])
                self.attn_phase("dil" if l == 1 else "moba")
                self.linres_phase(self.OT, wsc[l][2], bwm[l], self.bot)
            elif k == "mix" and l == 0:
                self.s5_phase(l, wsc[l][0], bwm[l])
            elif k == "mix" and l == 2:
                with contextlib.ExitStack() as st:
                    ebl = self.sb(st, "ebl", [128, 8, 64], F32)
                    bebl = Buf()
                    self.hgrn_prep(l, wsc[l][0], bwm[l], ebl, bebl)
                    self.hgrn_rec(ebl, bebl)
                self.linres_phase(self.OT, wsc[l][1], bwm[l], self.bot)
            self.run_hook()
        self.final_phase(outT, cfg.get("final_norm", True))
        c.final_wait()
        es.close()
        return nc


ROPE_THETA = 500000.0


def consts_build():
    cst = {}
    kl = np.arange(128)[:, None]
    ql = np.arange(512)[None, :]
    dm = np.zeros((128, 20, 512), np.float32)
    for mi in range(20):
        off = mi - 16
        dl = ql - kl - off * 128
        m = ((dl >= 0) & (dl <= 128)).astype(np.float32)
        m += ((dl >= 0) & (dl <= 512) & (dl % 4 == 0)).astype(np.float32)
        m += ((dl >= 0) & (dl <= 2048) & (dl % 16 == 0)).astype(np.float32)
        dm[:, mi, :] = m
    cst["dmask"] = dm.reshape(128, 20 * 512)
    mm = np.zeros((128, 4, 512), np.float32)
    for j in range(4):
        mm[:, j, :] = (j * 128 + kl <= ql).astype(np.float32)
    cst["mmask"] = mm.reshape(128, 4 * 512)
    oh = np.zeros((16, S), np.float32)
    for n in range(16):
        oh[n, n * 256:(n + 1) * 256] = 1.0
    cst["onehot"] = oh
    pn = np.zeros((128, 32, 16), np.float32)
    ow = np.zeros((128, 32, 16), np.float32)
    for qt in range(32):
        qb = qt // 2
        pn[:, qt, qb:] = -1e30
        ow[:, qt, qb] = 1.0
    cst["pastneg"] = pn.reshape(128, 512)
    cst["own"] = ow.reshape(128, 512)
    cst["ident"] = np.eye(128, dtype=np.float32)
    pmx = np.zeros((128, 128), np.float32)
    for f in range(128):
        j = f % 64
        if j < 8:
            pmx[f + 8, f] = 1.0
        elif j < 16:
            pmx[f - 8, f] = 1.0
    cst["ropeperm"] = pmx
    rm = np.ones((128, 512), np.float32)
    rm[:, 0::64] = 0.0
    cst["rmask"] = rm
    cst["ttc"] = np.ascontiguousarray(np.broadcast_to(np.arange(S, dtype=np.float32)[None, :], (128, S)))
    cst["tri"] = (np.arange(64)[:, None] <= np.arange(64)[None, :]).astype(np.float32)
    return cst


def pvec_build(inp):
    cols = []
    for l in range(4):
        cols.append(inp["norm_mix"][l].reshape(KC, 128).T)
    for l in range(4):
        cols.append(inp["norm_ffn"][l].reshape(KC, 128).T)
    cols.append(inp["norm_final"].reshape(KC, 128).T)
    f = np.arange(128) % 64
    inv = ROPE_THETA ** (-np.arange(0, 16, 2, dtype=np.float32) / 16.0)
    invf = np.where(f < 16, inv[f % 8], 0.0).astype(np.float32)
    sgn = np.where(f < 8, -1.0, np.where(f < 16, 1.0, 0.0)).astype(np.float32)
    cols.append(invf[:, None])
    cols.append(sgn[:, None])
    for l in range(4):
        cols.append(inp["hgrn_lower_bound"][l].reshape(KC, 128).T)
    cols.append(inp["hgrn_norm"][0].reshape(128, 1))
    p = np.arange(128)
    sa = np.where(p < 64, -1.0, 1.0).astype(np.float32)
    cols.append(sa[:, None])
    cols.append(-sa[:, None])
    cols.append(-np.ones((128, 1), np.float32))
    cols.append(inp["s5_d"][0].reshape(KC, 128).T)
    cols.append(inp["s5_b_glu"][0].reshape(2 * KC, 128).T)
    return np.ascontiguousarray(np.concatenate(cols, axis=1).astype(np.float32))


def make_inmaps(inp, cfg, cores, xs=None):
    pv = pvec_build(inp)
    cfg["npvec"] = pv.shape[1]
    cst = consts_build()
    stages = cfg["stages"]
    maps = []
    for b in cores:
        x = inp["x"][b] if xs is None else xs[b]
        m = {"xT": np.ascontiguousarray(x.T), "pvec": pv,
             "posb": np.ascontiguousarray(np.broadcast_to(inp["positions"][b][None, :], (128, S)).astype(np.int32))}
        m.update(cst)
        are, aim, ldt = inp["s5_a_re"][0], inp["s5_a_im"][0], inp["s5_log_dt"][0]
        m["s5p"] = np.ascontiguousarray(np.concatenate([
            np.concatenate([are.T, are.T], 0), np.concatenate([aim.T, aim.T], 0),
            np.broadcast_to(ldt[None, :], (128, 64))], axis=1).astype(np.float32))
        bre = inp["s5_b_re"][0].transpose(1, 0, 2).reshape(64, 1024)
        bim = inp["s5_b_im"][0].transpose(1, 0, 2).reshape(64, 1024)
        m["s5B"] = np.ascontiguousarray(np.concatenate([np.concatenate([bre, bim], 0), np.concatenate([bim, bre], 0)], axis=1))
        cre = inp["s5_c_re"][0].transpose(2, 0, 1).reshape(64, 1024)
        cim = inp["s5_c_im"][0].transpose(2, 0, 1).reshape(64, 1024)
        m["s5C"] = np.ascontiguousarray(np.concatenate([np.concatenate([cre, cim], 0), np.concatenate([cim, cre], 0)], axis=1))
        for (k, l) in stages:
            if k == "ffn":
                m["w_gu%d" % l] = np.ascontiguousarray(inp["ffn_w_gate_up"][l])
                m["w_d%d" % l] = np.ascontiguousarray(inp["ffn_w_down"][l])
            elif k == "mix" and l == 1:
                m["dil_qkv"] = np.ascontiguousarray(inp["dil_w_qkv"][0])
                m["dil_o"] = np.ascontiguousarray(inp["dil_w_o"][0])
            elif k == "mix" and l == 0:
                m["s5_wglu"] = np.ascontiguousarray(inp["s5_w_glu"][0])
            elif k == "mix" and l == 2:
                m["hgrn_in"] = np.ascontiguousarray(inp["hgrn_w_in"][0])
                m["hgrn_o"] = np.ascontiguousarray(inp["hgrn_w_o"][0])
            elif k == "mix" and l == 3:
                m["moba_qkv"] = np.ascontiguousarray(inp["moba_w_qkv"][0])
                m["moba_o"] = np.ascontiguousarray(inp["moba_w_o"][0])
        maps.append(m)
    return maps


FULL_STAGES = [("mix", 0), ("ffn", 0), ("mix", 1), ("ffn", 1), ("mix", 2), ("ffn", 2), ("mix", 3), ("ffn", 3)]


def kernel(**inp):
    inp = {k: np.asarray(v) for k, v in inp.items()}
    cfg = {"stages": FULL_STAGES, "final_norm": True}
    maps = make_inmaps(inp, cfg, range(4))
    prog = Prog(cfg)
    nc = prog.build()
    res = run_bass_kernel_spmd(nc, maps, core_ids=list(range(4)))
    out = np.stack([res.results[b]["outT"].T for b in range(4)], axis=0)
    return np.ascontiguousarray(out.astype(np.float32))
```

```python
import contextlib
import numpy as np
import concourse.bass as bass
import concourse.mybir as mybir
from concourse.bass_utils import run_bass_kernel_spmd

F32 = mybir.dt.float32
BF16 = mybir.dt.bfloat16
I32 = mybir.dt.int32
AF = mybir.ActivationFunctionType
ALU = mybir.AluOpType

S = 4096
D = 1024
DFF = 2816
KC = D // 128
EPS = 1e-6
SELF_SYNC = True
PI_LO = 3.1415925


class Buf:

    def __init__(self, name=""):
        self.w = None
        self.r = {}
        self.name = name


class Eng:
    def __init__(self, ctx, name, handle, self_sync):
        self.name = name
        self.h = handle
        self.sem = ctx.es.enter_context(ctx.nc.semaphore("s_" + name))
        self.cnt = 0
        self.seen = {}
        self.self_sync = self_sync


class DQ:
    def __init__(self, ctx, name, eng, k):
        self.name = name
        self.eng = eng
        self.k = k
        self.sems = [ctx.es.enter_context(ctx.nc.semaphore("q_%s%d" % (name, i))) for i in range(k)]
        self.cnts = [0] * k
        self.n = 0


class Ctx:
    def __init__(self, nc):
        self.nc = nc
        self.es = contextlib.ExitStack()
        self.eng = {}
        for name, h, ss in (("pe", nc.tensor, False), ("act", nc.scalar, SELF_SYNC), ("dve", nc.vector, SELF_SYNC),
                            ("pool", nc.gpsimd, SELF_SYNC), ("sp", nc.sync, False)):
            self.eng[name] = Eng(self, name, h, ss)
        self.dq = {"sp": DQ(self, "sp", self.eng["sp"], 8), "pool": DQ(self, "pool", self.eng["pool"], 4)}
        self.semtab = {}
        for e in self.eng.values():
            self.semtab[e.name] = e.sem
        for q in self.dq.values():
            for i, s in enumerate(q.sems):
                self.semtab[(q.name, i)] = s
        self.nwait = 0
        self.nins = 0

    def _need(self, reads, writes):
        need = {}
        for b in reads:
            if b.w is not None:
                k, v = b.w
                if need.get(k, 0) < v:
                    need[k] = v
        for b in writes:
            if b.w is not None:
                k, v = b.w
                if need.get(k, 0) < v:
                    need[k] = v
            for k, v in b.r.items():
                if need.get(k, 0) < v:
                    need[k] = v
        return need

    def _waits(self, E, need):
        for k, v in need.items():
            if k == E.name and not E.self_sync:
                continue
            if E.seen.get(k, 0) < v:
                E.h.wait_ge(self.semtab[k], v)
                E.seen[k] = v
                self.nwait += 1

    def op(self, eng, emit, reads=(), writes=()):
        E = self.eng[eng]
        self._waits(E, self._need(reads, writes))
        ins = emit(E.h)
        E.cnt += 1
        ins.then_inc(E.sem, 1)
        self.nins += 1
        for b in reads:
            b.r[E.name] = E.cnt
        for b in writes:
            b.w = (E.name, E.cnt)
            b.r = {}

    def dma(self, q, out, in_, reads=(), writes=()):
        Q = self.dq[q]
        E = Q.eng
        i = Q.n % Q.k
        need = self._need(reads, writes)
        key = (Q.name, i)
        if Q.cnts[i] > 0:
            need[key] = max(need.get(key, 0), 16 * Q.cnts[i])
        self._waits(E, need)
        E.h.dma_start(out=out, in_=in_).then_inc(Q.sems[i], 16)
        Q.cnts[i] += 1
        Q.n += 1
        self.nins += 1
        for b in reads:
            b.r[key] = 16 * Q.cnts[i]
        for b in writes:
            b.w = (key, 16 * Q.cnts[i])
            b.r = {}

    def barrier(self):
        tgt = {}
        for e in self.eng.values():
            if e.cnt:
                tgt[e.name] = e.cnt
        for q in self.dq.values():
            for i in range(q.k):
                if q.cnts[i]:
                    tgt[(q.name, i)] = 16 * q.cnts[i]
        for e in self.eng.values():
            if e.name == "pool":
                continue
            for k, v in tgt.items():
                if k == "pool" or (isinstance(k, tuple) and k[0] == "pool"):
                    continue
                if k == e.name:
                    continue
                if e.seen.get(k, 0) < v:
                    e.h.wait_ge(self.semtab[k], v)
                    e.seen[k] = v

    def final_wait(self):
        E = self.eng["sp"]
        Q = self.dq["sp"]
        for i in range(Q.k):
            if Q.cnts[i]:
                E.h.wait_ge(Q.sems[i], 16 * Q.cnts[i])


class Prog:
    def __init__(self, cfg):
        self.cfg = cfg
        nc = bass.Bass("TRN2", target_bir_lowering=False)
        self.nc = nc
        self.c = Ctx(nc)
        self.es = self.c.es
        self.ins = {}

    def din(self, name, shape, dt=F32):
        t = self.nc.dram_tensor(name, list(shape), dt, kind="ExternalInput").ap()
        self.ins[name] = t
        return t

    def dscr(self, name, shape, dt):
        return self.nc.dram_tensor(name, list(shape), dt, kind="Internal").ap()

    def sb(self, stack, name, shape, dt):
        self.sb_n = getattr(self, "sb_n", 0) + 1
        return stack.enter_context(self.nc.sbuf_tensor("%s_%d" % (name, self.sb_n), list(shape), dt))

    def cast_weight(self, src, dst, K, N, buf):
        c = self.c
        if not hasattr(buf, "cw_key"):
            sem = self.es.enter_context(self.nc.semaphore("cw%d" % len(c.semtab)))
            buf.cw_key = ("cw", len(c.semtab))
            c.semtab[buf.cw_key] = sem
            buf.cw_n = 0
        sem = c.semtab[buf.cw_key]
        step = 512
        for r0 in range(0, K, step):
            r1 = min(K, r0 + step)
            self.nc.gpsimd.dma_start(out=dst[r0:r1, :], in_=src[r0:r1, :]).then_inc(sem, 16)
            buf.cw_n += 1
        buf.w = (buf.cw_key, 16 * buf.cw_n)
        buf.r = {}

    def cast_weight_old(self, src, dst, K, N, buf):
        c = self.c
        CW = 2048
        for kc in range(K // 128):
            for c0 in range(0, N, CW):
                w = min(CW, N - c0)
                i = self.cast_i % 2
                self.cast_i += 1
                st, stb, bst, bstb = self.cast_st[i]
                c.dma("pool", st[:, 0:w], src[kc * 128:(kc + 1) * 128, c0:c0 + w], writes=[bst])
                c.op("pool", lambda h: h.tensor_copy(out=stb[:, 0:w], in_=st[:, 0:w]), reads=[bst], writes=[bstb])
                c.dma("pool", dst[kc * 128:(kc + 1) * 128, c0:c0 + w], stb[:, 0:w], reads=[bstb], writes=[buf])

    def rmsnorm_tile(self, hx, bhx, gcol, xn, bxn, ncols, tmp):
        c = self.c
        sq, bsq, rt, brt = tmp
        KH = sq.shape[1]
        for c0 in range(0, ncols, 512):
            ps, bps = self.ps_next()
            for k0 in range(0, KC, KH):
                c.op("act", lambda h: h.activation(out=sq[:, :, :], in_=hx[:, k0:k0 + KH, c0:c0 + 512], func=AF.Square),
                     reads=[bhx], writes=[bsq])
                for kk in range(KH):
                    kc = k0 + kk
                    c.op("pe", lambda h: h.matmul(ps[:, :], lhsT=self.ones_f[:, :], rhs=sq[:, kk, :],
                                                  start=(kc == 0), stop=(kc == KC - 1)),
                         reads=[bsq, self.bconst], writes=[bps])
            c.op("act", lambda h: h.activation(out=rt[:, :], in_=ps[:, :], func=AF.Sqrt, scale=1.0 / D,
                                               bias=self.eps_t[:, 0:1]),
                 reads=[bps, self.bconst], writes=[brt])
            c.op("dve", lambda h: h.reciprocal(out=rt[:, :], in_=rt[:, :]), reads=[brt], writes=[brt])
            for kc in range(KC):
                c.op("dve", lambda h: h.scalar_tensor_tensor(out=xn[:, kc, c0:c0 + 512], in0=hx[:, kc, c0:c0 + 512],
                                                             scalar=self.pvec[:, gcol + kc:gcol + kc + 1],
                                                             in1=rt[:, :], op0=ALU.mult, op1=ALU.mult),
                     reads=[bhx, brt, self.bconst], writes=[bxn])

    def run_hook(self):
        h = getattr(self, "hook", None)
        if h is not None:
            self.hook = None
            E, pe = self.c.eng["pool"], self.c.eng["pe"]
            if pe.cnt > 0 and E.seen.get("pe", 0) < pe.cnt:
                E.h.wait_ge(pe.sem, pe.cnt)
                E.seen["pe"] = pe.cnt
            h()

    def ps_next(self, pool=None):
        if pool is None:
            i = self.ps_i % len(self.ps)
            self.ps_i += 1
            return self.ps[i], self.bps[i]
        lst = self.ps_pools[pool]
        i = lst[self.ps_pi.get(pool, 0) % len(lst)]
        self.ps_pi[pool] = self.ps_pi.get(pool, 0) + 1
        return self.ps[i], self.bps[i]

    def ffn_phase(self, layer):
        c = self.c
        nc = self.nc
        TS = 1024
        hT3 = self.hT.rearrange("(kc p) t -> p kc t", p=128)
        wgu = self.w_gu[layer]
        wd = self.w_d[layer]
        bw = self.bw_ffn[layer]
        with contextlib.ExitStack() as st:
            hx2 = [self.sb(st, "f_hx%d" % i, [128, KC, TS], F32) for i in range(2)]
            sq = self.sb(st, "f_sq", [128, 1, 512], F32)
            rt = self.sb(st, "f_rt", [128, 512], F32)
            xn2 = [self.sb(st, "f_xn%d" % i, [128, KC, TS], BF16) for i in range(2)]
            hf = self.sb(st, "f_hf", [128, DFF // 128, TS], BF16)
            wg = [self.sb(st, "f_wg%d" % i, [128, KC, 512], BF16) for i in range(2)]
            wu = [self.sb(st, "f_wu%d" % i, [128, KC, 512], BF16) for i in range(2)]
            wdn = [self.sb(st, "f_wd%d" % i, [128, DFF // 128, 256], BF16) for i in range(2)]
            sg = [self.sb(st, "f_sg%d" % i, [128, 512], BF16) for i in range(2)]
            bsq, brt = Buf(), Buf()
            bhx2, bxn2 = [Buf(), Buf()], [Buf(), Buf()]
            bhf = [Buf() for _ in range(DFF // 128)]
            bwg = [Buf(), Buf()]
            bwdn = [Buf(), Buf()]
            bsg = [Buf(), Buf()]
            wgu3 = wgu.rearrange("(kc p) n -> p kc n", p=128)
            wd3 = wd.rearrange("(kc p) n -> p kc n", p=128)
            NG = DFF // 512
            groups = [(g0, min(512, DFF - g0)) for g0 in range(0, DFF, 512)]
            def prep(ti_):
                t0_ = ti_ * TS
                u_ = ti_ % 2
                c.dma("sp", hx2[u_][:, :, :], hT3[:, :, t0_:t0_ + TS], reads=[self.bh[ti_]], writes=[bhx2[u_]])
                self.rmsnorm_tile(hx2[u_], bhx2[u_], self.col_nffn + layer * KC, xn2[u_], bxn2[u_], TS, (sq, bsq, rt, brt))

            prep(0)
            for t0 in range(0, S, TS):
                bh = self.bh[t0 // TS]
                u = (t0 // TS) % 2
                hx, bhx, xn, bxn = hx2[u], bhx2[u], xn2[u], bxn2[u]
                for gi, (g0, gw) in enumerate(groups):
                    b = gi % 2
                    c.dma("sp", wg[b][:, :, 0:gw], wgu3[:, :, g0:g0 + gw], reads=[bw], writes=[bwg[b]])
                    c.dma("sp", wu[b][:, :, 0:gw], wgu3[:, :, DFF + g0:DFF + g0 + gw], reads=[bw], writes=[bwg[b]])
                    for j in range(gw // 128):
                        fc = (g0 // 128) + j
                        for ct in range(TS // 512):
                            cs = slice(ct * 512, (ct + 1) * 512)
                            pg, bpg = self.ps_next()
                            pu, bpu = self.ps_next()
                            for kc in range(KC):
                                c.op("pe", lambda h: h.matmul(pg[:, :], lhsT=wg[b][:, kc, j * 128:(j + 1) * 128],
                                                              rhs=xn[:, kc, cs], start=(kc == 0), stop=(kc == KC - 1)),
                                     reads=[bwg[b], bxn], writes=[bpg])
                            for kc in range(KC):
                                c.op("pe", lambda h: h.matmul(pu[:, :], lhsT=wu[b][:, kc, j * 128:(j + 1) * 128],
                                                              rhs=xn[:, kc, cs], start=(kc == 0), stop=(kc == KC - 1)),
                                     reads=[bwg[b], bxn], writes=[bpu])
                            si = self.sg_i % 2
                            self.sg_i += 1
                            c.op("act", lambda h: h.activation(out=sg[si][:, :], in_=pg[:, :], func=AF.Silu),
                                 reads=[bpg], writes=[bsg[si]])
                            c.op("dve", lambda h: h.tensor_tensor(out=hf[:, fc, cs], in0=pu[:, :], in1=sg[si][:, :],
                                                                  op=ALU.mult),
                                 reads=[bpu, bsg[si]], writes=[bhf[fc]])
                if t0 + TS < S:
                    prep(t0 // TS + 1)
                self.run_hook()
                for dg in range(D // 256):
                    b = dg % 2
                    c.dma("sp", wdn[b][:, :, :], wd3[:, :, dg * 256:(dg + 1) * 256], reads=[bw], writes=[bwdn[b]])
                    for j in range(2):
                        dc = dg * 2 + j
                        for ct in range(TS // 512):
                            cs = slice(ct * 512, (ct + 1) * 512)
                            py, bpy = self.ps_next()
                            nk = DFF // 128
                            for kc in range(nk):
                                c.op("pe", lambda h: h.matmul(py[:, :], lhsT=wdn[b][:, kc, j * 128:(j + 1) * 128],
                                                              rhs=hf[:, kc, cs], start=(kc == 0), stop=(kc == nk - 1)),
                                     reads=[bwdn[b], bhf[kc]], writes=[bpy])
                            c.op("dve", lambda h: h.tensor_tensor(out=hx[:, dc, cs], in0=py[:, :], in1=hx[:, dc, cs],
                                                                  op=ALU.add),
                                 reads=[bpy, bhx], writes=[bhx])
                c.dma("sp", hT3[:, :, t0:t0 + TS], hx[:, :, :], reads=[bhx], writes=[bh])
            c.barrier()

    def load_cast(self, st, dst, bdst, src, shape_cols, tag):
        c = self.c
        P = dst.shape[0]
        stg = self.sb(st, "lc_" + tag, [P, 2048], F32)
        bs = Buf()
        for c0 in range(0, shape_cols, 2048):
            w = min(2048, shape_cols - c0)
            c.dma("sp", stg[:, 0:w], src[:, c0:c0 + w], writes=[bs])
            c.op("dve", lambda h: h.tensor_copy(out=dst[:, c0:c0 + w], in_=stg[:, 0:w]), reads=[bs], writes=[bdst])

    def range_reduce_sincos(self, st, x, bx, n, tag, want_cos=True):
        c = self.c
        TWO_PI = 2.0 * np.pi
        ki = self.sb(st, tag + "_ki", [128, n], I32)
        kf = self.sb(st, tag + "_kf", [128, n], F32)
        sn = self.sb(st, tag + "_sn", [128, n], F32)
        cs = self.sb(st, tag + "_cs", [128, n], F32) if want_cos else None
        bki, bkf, bsn, bcs = Buf(), Buf(), Buf(), Buf()
        self._rr(x, bx, n, ki, bki, kf, bkf, sn, bsn, cs, bcs)
        return sn, bsn, cs, bcs

    def _rr(self, x, bx, n, ki, bki, kf, bkf, sn, bsn, cs, bcs):
        c = self.c
        TWO_PI = 2.0 * np.pi
        c.op("act", lambda h: h.activation(out=ki[:, 0:n], in_=x[:, 0:n], func=AF.Identity, scale=1.0 / TWO_PI),
             reads=[bx], writes=[bki])
        c.op("dve", lambda h: h.scalar_tensor_tensor(out=sn[:, 0:n], in0=ki[:, 0:n], scalar=-TWO_PI, in1=x[:, 0:n],
                                                     op0=ALU.mult, op1=ALU.add), reads=[bki, bx], writes=[bsn])
        c.op("dve", lambda h: h.tensor_scalar(out=sn[:, 0:n], in0=sn[:, 0:n], scalar1=-PI_LO, scalar2=PI_LO,
                                              op0=ALU.max, op1=ALU.min), reads=[bsn], writes=[bsn])
        if cs is not None:
            c.op("dve", lambda h: h.scalar_tensor_tensor(out=cs[:, 0:n], in0=sn[:, 0:n], scalar=-1.0, in1=sn[:, 0:n],
                                                         op0=ALU.mult, op1=ALU.max), reads=[bsn], writes=[bcs])
        c.op("act", lambda h: h.activation(out=sn[:, 0:n], in_=sn[:, 0:n], func=AF.Sin), reads=[bsn], writes=[bsn])
        if cs is not None:
            c.op("act", lambda h: h.activation(out=cs[:, 0:n], in_=cs[:, 0:n], func=AF.Sin, scale=-1.0,
                                               bias=self.halfpi_t[:, 0:1]), reads=[bcs, self.bconst], writes=[bcs])

    def qkv_phase(self, layer, w_s, wp_s, bw, bwp):
        c = self.c
        hT3 = self.hT.rearrange("(kc p) t -> p kc t", p=128)
        QT3 = self.QT.rearrange("(kc p) t -> p kc t", p=128)
        KT3 = self.KT.rearrange("(kc p) t -> p kc t", p=128)
        w3 = w_s.rearrange("(kc p) n -> p kc n", p=128)
        self.ps_pools = {"a": [0, 1, 2, 3, 4, 5, 6, 7]}
        self.ps_pi = {}
        with contextlib.ExitStack() as st:
            SF = self.sb(st, "q_SF", [128, S], F32)
            CF = self.sb(st, "q_CF", [128, S], F32)
            bSF, bCF = Buf(), Buf()
            with contextlib.ExitStack() as st2:
                posi = self.sb(st2, "q_posi", [128, S], I32)
                x = self.sb(st2, "q_x", [128, S], F32)
                bposi, bx = Buf(), Buf()
                c.dma("sp", posi[:, :], self.posb[:, :], writes=[bposi])
                c.op("dve", lambda h: h.tensor_copy(out=x[:, :], in_=posi[:, :]), reads=[bposi], writes=[bx])
                c.op("dve", lambda h: h.tensor_scalar(out=x[:, :], in0=x[:, :], scalar1=self.pvec[:, self.col_invf:self.col_invf + 1],
                                                      scalar2=None, op0=ALU.mult), reads=[bx, self.bconst], writes=[bx])
                ki = posi
                kf = self.sb(st2, "q_kf", [128, S], F32)
                self._rr(x, bx, S, ki, bposi, kf, Buf(), SF, bSF, CF, bCF)
                c.op("dve", lambda h: h.tensor_scalar(out=SF[:, :], in0=SF[:, :],
                                                      scalar1=self.pvec[:, self.col_sgnrow:self.col_sgnrow + 1],
                                                      scalar2=None, op0=ALU.mult), reads=[bSF, self.bconst], writes=[bSF])
                c.barrier()
            wq = self.sb(st, "q_w", [128, KC, 3 * D], BF16)
            pm = self.sb(st, "q_pm", [128, 128], BF16)
            ab = [self.sb(st, "q_ab%d" % i, [128, 512], BF16) for i in range(2)]
            bab = [Buf(), Buf()]
            bpmc = Buf()
            with contextlib.ExitStack() as st3:
                self.load_cast(st3, pm, bpmc, self.ropeperm, 128, "pm")
                c.barrier()
            bwq = Buf()
            for i in range(3):
                c.dma("sp", wq[:, :, i * D:(i + 1) * D], w3[:, :, i * D:(i + 1) * D], reads=[bw], writes=[bwq])
            hx = self.sb(st, "q_hx", [128, KC, 512], F32)
            sq = self.sb(st, "q_sq", [128, 4, 512], F32)
            rt = self.sb(st, "q_rt", [128, 512], F32)
            xn = self.sb(st, "q_xn", [128, KC, 512], BF16)
            qo = [self.sb(st, "q_qo%d" % i, [128, KC, 512], BF16) for i in range(2)]
            tu = [self.sb(st, "q_tu%d" % i, [128, 512], F32) for i in range(2)]
            tt_ = [self.sb(st, "q_tt%d" % i, [128, 512], F32) for i in range(2)]
            vt = [self.sb(st, "q_vt%d" % i, [128, 16, 128], BF16) for i in range(2)]
            bhx, bsq, brt, bxn = Buf(), Buf(), Buf(), Buf()
            bqo = [Buf(), Buf()]
            btu = [Buf(), Buf()]
            btt = [Buf(), Buf()]
            bvt = [Buf(), Buf()]
            for i in range(2):
                c.op("dve", lambda h: h.memset(vt[i][:, :, 64:128], 1.0), writes=[bvt[i]])
            ei = 0
            vi = 0
            for tile in range(S // 512):
                cs = slice(tile * 512, (tile + 1) * 512)
                bh = self.bh[tile // 2]
                c.dma("sp", hx[:, :, :], hT3[:, :, cs], reads=[bh], writes=[bhx])
                self.rmsnorm_tile(hx, bhx, self.col_nmix + layer * KC, xn, bxn, 512, (sq, bsq, rt, brt))
                for which in range(2):
                    for oc in range(KC):
                        pa, bpa = self.ps_next("a")
                        pb, bpb = self.ps_next("a")
                        col = which * D + oc * 128
                        for kc in range(KC):
                            c.op("pe", lambda h: h.matmul(pa[:, :], lhsT=wq[:, kc, col:col + 128], rhs=xn[:, kc, :],
                                                          start=(kc == 0), stop=(kc == KC - 1)),
                                 reads=[bwq, bxn], writes=[bpa])
                        e = ei % 2
                        ei += 1
                        c.op("act", lambda h: h.activation(out=ab[e][:, :], in_=pa[:, :], func=AF.Identity),
                             reads=[bpa], writes=[bab[e]])
                        c.op("pe", lambda h: h.matmul(pb[:, :], lhsT=pm[:, :], rhs=ab[e][:, :], start=True, stop=True),
                             reads=[bab[e], bpmc], writes=[bpb])
                        c.op("dve", lambda h: h.tensor_tensor(out=tu[e][:, :], in0=pa[:, :], in1=CF[:, cs], op=ALU.mult),
                             reads=[bpa, bCF, bab[e]], writes=[btu[e]])
                        c.op("dve", lambda h: h.tensor_tensor(out=tt_[e][:, :], in0=pb[:, :], in1=SF[:, cs], op=ALU.mult),
                             reads=[bpb, bSF], writes=[btt[e]])
                        c.op("dve", lambda h: h.tensor_tensor(out=qo[which][:, oc, :], in0=tu[e][:, :], in1=tt_[e][:, :],
                                                               op=ALU.add),
                             reads=[btu[e], btt[e]], writes=[bqo[which]])
                    dst = QT3 if which == 0 else KT3
                    c.dma("sp", dst[:, :, cs], qo[which][:, :, :], reads=[bqo[which]], writes=[self.bqkv])
                for sub in range(4):
                    v = vi % 2
                    vi += 1
                    for half in range(2):
                        pv, bpv = self.ps_next("a")
                        for kc in range(KC):
                            c.op("pe", lambda h: h.matmul(pv[:, :], lhsT=xn[:, kc, sub * 128:(sub + 1) * 128],
                                                          rhs=wq[:, kc, 2 * D + half * 512:2 * D + (half + 1) * 512],
                                                          start=(kc == 0), stop=(kc == KC - 1)),
                                 reads=[bwq, bxn], writes=[bpv])
                        c.op("act", lambda h: h.activation(out=vt[v][:, half * 8:(half + 1) * 8, 0:64],
                                                           in_=pv[:, :].rearrange("p (a b) -> p a b", b=64),
                                                           func=AF.Identity),
                             reads=[bpv], writes=[bvt[v]])
                    c.dma("sp", self.V1s[:, :, tile * 4 + sub, :].rearrange("h p e -> p h e"), vt[v][:, :, :],
                          reads=[bvt[v]], writes=[self.bqkv])
            c.barrier()

    def attn_phase(self, kind):
        c = self.c
        moba = kind == "moba"
        KR = 80 if moba else 64
        self.ps_pools = {"s": [0, 1, 2, 3], "o": [4, 5], "g": [6, 7]}
        self.ps_pi = {}
        with contextlib.ExitStack() as st:
            nm = 4 if moba else 20
            mk = self.sb(st, "a_mk", [128, nm * 512], BF16)
            bmk = Buf()
            with contextlib.ExitStack() as st2:
                self.load_cast(st2, mk, bmk, (self.mmask if moba else self.dmask), nm * 512, "mk")
                c.barrier()
            kaug = [self.sb(st, "a_k%d" % i, [KR, S], BF16) for i in range(2)]
            qaug = [self.sb(st, "a_q%d" % i, [KR, S], BF16) for i in range(2)]
            v1 = [self.sb(st, "a_v%d" % i, [128, 32, 128], BF16) for i in range(2)]
            ot = [self.sb(st, "a_o%d" % i, [64, S], BF16) for i in range(2)]
            pt = [self.sb(st, "a_p%d" % i, [128, 512], BF16) for i in range(8)]
            rd = [self.sb(st, "a_rd%d" % i, [64, 512], F32) for i in range(2)]
            bk = [Buf(), Buf()]
            bq = [Buf(), Buf()]
            bv = [Buf(), Buf()]
            bot = [Buf(), Buf()]
            bpt = [Buf() for _ in range(8)]
            brd = [Buf(), Buf()]
            if moba:
                with contextlib.ExitStack() as st2:
                    oh = self.sb(st2, "a_oh", [16, S], F32)
                    boh = Buf()
                    c.dma("sp", oh[:, :], self.onehot[:, :], writes=[boh])
                    for i in range(2):
                        c.op("dve", lambda h: h.tensor_copy(out=kaug[i][64:80, :], in_=oh[:, :]), reads=[boh], writes=[bk[i]])
                    c.barrier()
                pastneg = self.sb(st, "a_pn", [128, 512], F32)
                own = self.sb(st, "a_own", [128, 512], F32)
                ident = self.sb(st, "a_id", [128, 128], F32)
                bcn = Buf()
                c.dma("sp", pastneg[:, :], self.pastneg[:, :], writes=[bcn])
                c.dma("sp", own[:, :], self.own[:, :], writes=[bcn])
                c.dma("sp", ident[:, :], self.ident[:, :], writes=[bcn])
                km = self.sb(st, "a_km", [64, 16], F32)
                kmb = self.sb(st, "a_kmb", [64, 16], BF16)
                gm = self.sb(st, "a_gm", [128, 512], F32)
                m8 = self.sb(st, "a_m8", [128, 32, 8], F32)
                thr = self.sb(st, "a_thr", [128, 32], F32)
                sel = self.sb(st, "a_sel", [128, 512], F32)
                bkm, bkmb, bgm, bm8, bthr, bsel = Buf(), Buf(), Buf(), Buf(), Buf(), Buf()
            ri = 0

            def head_prep(hd):
                b = hd % 2
                c.dma("sp", kaug[b][0:64, :], self.KT[hd * 64:(hd + 1) * 64, :], reads=[self.bqkv], writes=[bk[b]])
                c.dma("sp", qaug[b][0:64, :], self.QT[hd * 64:(hd + 1) * 64, :], reads=[self.bqkv], writes=[bq[b]])
                c.dma("sp", v1[b][:, :, :], self.V1s[hd, :, :, :], reads=[self.bqkv], writes=[bv[b]])
                if moba:
                    c.op("dve", lambda h: h.tensor_reduce(out=km[:, :], in_=kaug[b][0:64, :].rearrange("p (n k) -> p n k", k=256),
                                                          axis=mybir.AxisListType.X, op=ALU.add),
                         reads=[bk[b]], writes=[bkm])
                    c.op("dve", lambda h: h.tensor_scalar(out=kmb[:, :], in0=km[:, :], scalar1=1.0 / 256, scalar2=None,
                                                          op0=ALU.mult), reads=[bkm], writes=[bkmb])
                    pg, bpg = self.ps_next("g")
                    for qt_ in range(32):
                        c.op("pe", lambda h: h.matmul(pg[:, qt_ * 16:(qt_ + 1) * 16], lhsT=qaug[b][0:64, qt_ * 128:(qt_ + 1) * 128],
                                                      rhs=kmb[:, :], start=True, stop=True),
                             reads=[bq[b], bkmb], writes=[bpg])
                    c.op("dve", lambda h: h.tensor_tensor(out=gm[:, :], in0=pg[:, :], in1=pastneg[:, :], op=ALU.add),
                         reads=[bpg, bcn], writes=[bgm])
                    for qt_ in range(32):
                        c.op("dve", lambda h: h.max(out=m8[:, qt_, :], in_=gm[:, qt_ * 16:(qt_ + 1) * 16]),
                             reads=[bgm], writes=[bm8])
                    c.op("dve", lambda h: h.tensor_scalar(out=thr[:, :], in0=m8[:, :, 2], scalar1=-1e29, scalar2=None,
                                                          op0=ALU.max), reads=[bm8], writes=[bthr])
                    c.op("dve", lambda h: h.tensor_tensor(out=sel[:, :].rearrange("p (a b) -> p a b", b=16),
                                                          in0=gm[:, :].rearrange("p (a b) -> p a b", b=16),
                                                          in1=thr[:, :].unsqueeze(2).to_broadcast([128, 32, 16]),
                                                          op=ALU.is_ge), reads=[bgm, bthr], writes=[bsel])
                    c.op("dve", lambda h: h.tensor_tensor(out=sel[:, :], in0=sel[:, :], in1=own[:, :], op=ALU.add),
                         reads=[bsel, bcn], writes=[bsel])
                    c.op("dve", lambda h: h.tensor_scalar(out=sel[:, :], in0=sel[:, :], scalar1=-1.0, scalar2=32768.0,
                                                          op0=ALU.add, op1=ALU.mult), reads=[bsel], writes=[bsel])
                    for g4 in range(8):
                        pt_, bpt_ = self.ps_next("g")
                        for j in range(4):
                            qt_ = g4 * 4 + j
                            c.op("pe", lambda h: h.transpose(pt_[0:16, j * 128:(j + 1) * 128], sel[:, qt_ * 16:(qt_ + 1) * 16],
                                                             ident[:, :]),
                                 reads=[bsel, bcn], writes=[bpt_])
                        c.op("act", lambda h: h.activation(out=qaug[b][64:80, g4 * 512:(g4 + 1) * 512], in_=pt_[0:16, :],
                                                           func=AF.Identity), reads=[bpt_], writes=[bq[b]])

            items = []
            for hd in range(16):
                for qt in range(8):
                    k0 = 0 if moba else max(0, 4 * qt - 16)
                    kts = list(range(k0, 4 * qt + 4))
                    for idx, kt in enumerate(kts):
                        items.append((hd, qt, idx, kt, len(kts)))
            LA = 5
            state = {}

            def front(i, it):
                hd, qt, idx, kt, n = it
                b = hd % 2
                pss, bpss = self.ps_next("s")
                c.op("pe", lambda h: h.matmul(pss[:, :], lhsT=kaug[b][0:KR, kt * 128:(kt + 1) * 128],
                                              rhs=qaug[b][0:KR, qt * 512:(qt + 1) * 512], start=True, stop=True),
                     reads=[bk[b], bq[b]], writes=[bpss])
                p = i % 8
                c.op("act", lambda h: h.activation(out=pt[p][:, :], in_=pss[:, :], func=AF.Exp, scale=0.125),
                     reads=[bpss], writes=[bpt[p]])
                off = kt - 4 * qt
                m = None
                if moba:
                    if off >= 0:
                        m = off
                else:
                    m = off + 16
                if m is not None:
                    eng = "dve"
                    c.op(eng, lambda h: h.tensor_tensor(out=pt[p][:, :], in0=pt[p][:, :], in1=mk[:, m * 512:(m + 1) * 512],
                                                        op=ALU.mult), reads=[bpt[p], bmk], writes=[bpt[p]])

            def back(i, it):
                nonlocal ri
                hd, qt, idx, kt, n = it
                b = hd % 2
                p = i % 8
                if qt == 0 and idx == 0 and hd + 1 < 16:
                    head_prep(hd + 1)
                if idx == 0:
                    state["po"] = self.ps_next("o")
                po, bpo = state["po"]
                c.op("pe", lambda h: h.matmul(po[:, :], lhsT=v1[b][:, kt, :], rhs=pt[p][:, :],
                                              start=(idx == 0), stop=(idx == n - 1)),
                     reads=[bv[b], bpt[p]], writes=[bpo])
                if idx == n - 1:
                    r = ri % 2
                    ri += 1
                    c.op("dve", lambda h: h.reciprocal(out=rd[r][:, :], in_=po[64:128, :]), reads=[bpo], writes=[brd[r]])
                    c.op("dve", lambda h: h.tensor_tensor(out=ot[b][:, qt * 512:(qt + 1) * 512], in0=po[0:64, :], in1=rd[r][:, :],
                                                          op=ALU.mult), reads=[bpo, brd[r]], writes=[bot[b]])
                    if qt == 7:
                        c.dma("sp", self.OT[hd * 64:(hd + 1) * 64, :], ot[b][:, :], reads=[bot[b]], writes=[self.bot])

            head_prep(0)
            self.run_hook()
            for i in range(len(items) + LA):
                if i < len(items):
                    front(i, items[i])
                if i >= LA:
                    back(i - LA, items[i - LA])
            c.barrier()

    def linres_phase(self, inT, w_s, bw, bin_):
        c = self.c
        hT3 = self.hT.rearrange("(kc p) t -> p kc t", p=128)
        in3 = inT.rearrange("(kc p) t -> p kc t", p=128)
        w3 = w_s.rearrange("(kc p) n -> p kc n", p=128)
        with contextlib.ExitStack() as st:
            w = self.sb(st, "l_w", [128, KC, D], BF16)
            bwl = Buf()
            c.dma("sp", w[:, :, :], w3[:, :, :], reads=[bw], writes=[bwl])
            a = [self.sb(st, "l_a%d" % i, [128, KC, 512], BF16) for i in range(2)]
            hx = [self.sb(st, "l_h%d" % i, [128, KC, 512], F32) for i in range(2)]
            ba = [Buf(), Buf()]
            bhx = [Buf(), Buf()]
            for tile in range(S // 512):
                cs = slice(tile * 512, (tile + 1) * 512)
                b = tile % 2
                bh = self.bh[tile // 2]
                c.dma("sp", a[b][:, :, :], in3[:, :, cs], reads=[bin_], writes=[ba[b]])
                c.dma("sp", hx[b][:, :, :], hT3[:, :, cs], reads=[bh], writes=[bhx[b]])
                for dc in range(KC):
                    ps, bps = self.ps_next()
                    for kc in range(KC):
                        c.op("pe", lambda h: h.matmul(ps[:, :], lhsT=w[:, kc, dc * 128:(dc + 1) * 128], rhs=a[b][:, kc, :],
                                                      start=(kc == 0), stop=(kc == KC - 1)),
                             reads=[bwl, ba[b]], writes=[bps])
                    c.op("dve", lambda h: h.tensor_tensor(out=hx[b][:, dc, :], in0=ps[:, :], in1=hx[b][:, dc, :], op=ALU.add),
                         reads=[bps, bhx[b]], writes=[bhx[b]])
                c.dma("sp", hT3[:, :, cs], hx[b][:, :, :], reads=[bhx[b]], writes=[bh])
            c.barrier()

    def cast_weight_qk_perm(self, src, dst, buf):
        c = self.c
        for kc in range(KC):
            for c0 in range(0, 2 * D, 512):
                i = self.cast_i % 2
                self.cast_i += 1
                st, stb, bst, bstb = self.cast_st[i]
                c.dma("pool", st[:, :], src[kc * 128:(kc + 1) * 128, c0:c0 + 512], writes=[bst])
                st3 = st[:, :].rearrange("p (a b) -> p a b", b=64)
                sb3 = stb[:, :].rearrange("p (a b) -> p a b", b=64)
                c.op("pool", lambda h: h.tensor_copy(out=stb[:, :], in_=st[:, :]), reads=[bst], writes=[bstb])
                c.op("pool", lambda h: h.tensor_copy(out=sb3[:, :, 0:8], in_=st3[:, :, 8:16]), reads=[bst], writes=[bstb])
                c.op("pool", lambda h: h.tensor_copy(out=sb3[:, :, 8:16], in_=st3[:, :, 0:8]), reads=[bst], writes=[bstb])
                c.dma("pool", dst[kc * 128:(kc + 1) * 128, c0:c0 + 512], stb[:, :], reads=[bstb], writes=[buf])

    def hgrn_prep(self, layer, w_s, bw, ebl, bebl):
        c = self.c
        hT3 = self.hT.rearrange("(kc p) t -> p kc t", p=128)
        QT3 = self.QT.rearrange("(kc p) t -> p kc t", p=128)
        KT3 = self.KT.rearrange("(kc p) t -> p kc t", p=128)
        GT3 = self.GT.rearrange("(kc p) t -> p kc t", p=128)
        w3 = w_s.rearrange("(kc p) n -> p kc n", p=128)
        self.ps_pools = {"a": [0, 1, 2, 3, 4, 5], "t": [6, 7]}
        self.ps_pi = {}
        with contextlib.ExitStack() as st:
            w = self.sb(st, "h_w", [128, KC, 4 * D], BF16)
            bww = Buf()
            for i in range(4):
                c.dma("sp", w[:, :, i * D:(i + 1) * D], w3[:, :, i * D:(i + 1) * D], reads=[bw], writes=[bww])
            ex = self.sb(st, "h_ex", [128, 4 * KC], F32)
            ssum = self.sb(st, "h_ss", [128, KC], F32)
            lb = self.sb(st, "h_lb", [128, KC], F32)
            oml = self.sb(st, "h_oml", [128, KC], F32)
            blb = Buf()
            cl = self.col_lb
            c.op("act", lambda h: h.activation(out=ex[:, :], in_=self.pvec[:, cl:cl + 4 * KC], func=AF.Exp),
                 reads=[self.bconst], writes=[blb])
            c.op("dve", lambda h: h.tensor_tensor(out=ssum[:, :], in0=ex[:, 0:KC], in1=ex[:, KC:2 * KC], op=ALU.add),
                 reads=[blb], writes=[blb])
            c.op("dve", lambda h: h.tensor_tensor(out=ssum[:, :], in0=ssum[:, :], in1=ex[:, 2 * KC:3 * KC], op=ALU.add),
                 reads=[blb], writes=[blb])
            c.op("dve", lambda h: h.tensor_tensor(out=ssum[:, :], in0=ssum[:, :], in1=ex[:, 3 * KC:4 * KC], op=ALU.add),
                 reads=[blb], writes=[blb])
            c.op("dve", lambda h: h.reciprocal(out=ssum[:, :], in_=ssum[:, :]), reads=[blb], writes=[blb])
            c.op("dve", lambda h: h.tensor_copy(out=lb[:, :], in_=ex[:, KC:2 * KC]), reads=[blb], writes=[blb])
            for l in range(2, layer + 1):
                c.op("dve", lambda h: h.tensor_tensor(out=lb[:, :], in0=lb[:, :], in1=ex[:, l * KC:(l + 1) * KC], op=ALU.add),
                     reads=[blb], writes=[blb])
            c.op("dve", lambda h: h.tensor_tensor(out=lb[:, :], in0=lb[:, :], in1=ssum[:, :], op=ALU.mult),
                 reads=[blb], writes=[blb])
            c.op("dve", lambda h: h.tensor_scalar(out=oml[:, :], in0=lb[:, :], scalar1=-1.0, scalar2=1.0, op0=ALU.mult, op1=ALU.add),
                 reads=[blb], writes=[blb])
            rmask = self.sb(st, "h_rm", [128, 512], F32)
            identb = self.sb(st, "h_idb", [128, 128], BF16)
            identf = self.sb(st, "h_idf", [128, 128], F32)
            bcn = Buf()
            c.dma("sp", rmask[:, :], self.rmask[:, :], writes=[bcn])
            c.dma("sp", identf[:, :], self.ident[:, :], writes=[bcn])
            c.op("dve", lambda h: h.tensor_copy(out=identb[:, :], in_=identf[:, :]), reads=[bcn], writes=[bcn])
            hx = self.sb(st, "h_hx", [128, KC, 512], F32)
            sq = self.sb(st, "h_sq", [128, 2, 512], F32)
            rt = self.sb(st, "h_rt", [128, 512], F32)
            xn = self.sb(st, "h_xn", [128, KC, 512], BF16)
            bhx, bsq, brt, bxn = Buf(), Buf(), Buf(), Buf()
            names = ["qs", "th", "fg", "b", "eb", "ebn", "kk"]
            TT = [{n: self.sb(st, "h_t_%s%d" % (n, i), [128, 512], F32) for n in names} for i in range(2)]
            BB_ = [{n: Buf() for n in names} for i in range(2)]
            A_ = self.sb(st, "h_A", [128, KC], F32)
            nA_ = self.sb(st, "h_nA", [128, KC], F32)
            B_ = self.sb(st, "h_B", [128, KC], F32)
            c.op("dve", lambda h: h.tensor_scalar(out=A_[:, :], in0=oml[:, :], scalar1=0.5, scalar2=None, op0=ALU.mult), reads=[blb], writes=[blb])
            c.op("dve", lambda h: h.tensor_scalar(out=nA_[:, :], in0=oml[:, :], scalar1=-0.5, scalar2=None, op0=ALU.mult), reads=[blb], writes=[blb])
            c.op("dve", lambda h: h.tensor_tensor(out=B_[:, :], in0=lb[:, :], in1=A_[:, :], op=ALU.add), reads=[blb], writes=[blb])
            qo = self.sb(st, "h_qo", [128, KC, 512], BF16)
            ko = self.sb(st, "h_ko", [128, KC, 512], BF16)
            go = self.sb(st, "h_go", [128, KC, 512], BF16)
            ktm = self.sb(st, "h_ktm", [128, 8, 4, 128], BF16)
            itm = [self.sb(st, "h_itm%d" % i, [128, 8, 128], BF16) for i in range(2)]
            bqo, bko, bgo, bktm = Buf(), Buf(), Buf(), Buf()
            bitm = [Buf(), Buf()]
            ii = 0
            for tile in range(S // 512):
                cs = slice(tile * 512, (tile + 1) * 512)
                bh = self.bh[tile // 2]
                c.dma("sp", hx[:, :, :], hT3[:, :, cs], reads=[bh], writes=[bhx])
                self.rmsnorm_tile(hx, bhx, self.col_nmix + layer * KC, xn, bxn, 512, (sq, bsq, rt, brt))
                def proj(hp_):
                    PS_ = {}
                    for i, hh in enumerate((2 * hp_, 2 * hp_ + 1)):
                        PS_[i] = [self.ps_next("a") for _ in range(3)]
                        for (pp, bpp), off in zip(PS_[i], (0, D, 3 * D)):
                            col = off + hh * 128
                            for kc in range(KC):
                                c.op("pe", lambda h: h.matmul(pp[:, :], lhsT=w[:, kc, col:col + 128], rhs=xn[:, kc, :],
                                                              start=(kc == 0), stop=(kc == KC - 1)),
                                     reads=[bww, bxn], writes=[bpp])
                    return PS_

                PSn = proj(0)
                for hp in range(4):
                    hs = (2 * hp, 2 * hp + 1)
                    PS = PSn
                    for i, hh in enumerate(hs):
                        T, B = TT[i], BB_[i]
                        (pq, bpq), (pf, bpf), (pg, bpg) = PS[i]
                        c.op("act", lambda h: h.activation(out=T["qs"][:, :], in_=pq[:, :], func=AF.Silu), reads=[bpq], writes=[B["qs"]])
                        c.op("act", lambda h: h.activation(out=go[:, hh, :], in_=pg[:, :], func=AF.Silu), reads=[bpg], writes=[bgo])
                        c.op("act", lambda h: h.activation(out=T["th"][:, :], in_=pf[:, :], func=AF.Tanh, scale=0.5), reads=[bpf], writes=[B["th"]])
                    if hp + 1 < 4:
                        PSn = proj(hp + 1)
                    for i, hh in enumerate(hs):
                        T, B = TT[i], BB_[i]
                        c.op("dve", lambda h: h.tensor_scalar(out=T["fg"][:, :], in0=T["th"][:, :], scalar1=A_[:, hh:hh + 1],
                                                              scalar2=B_[:, hh:hh + 1], op0=ALU.mult, op1=ALU.add),
                             reads=[B["th"], blb], writes=[B["fg"]])
                        c.op("dve", lambda h: h.tensor_scalar(out=T["kk"][:, :], in0=T["th"][:, :], scalar1=nA_[:, hh:hh + 1],
                                                              scalar2=A_[:, hh:hh + 1], op0=ALU.mult, op1=ALU.add),
                             reads=[B["th"], blb], writes=[B["kk"]])
                    for i, hh in enumerate(hs):
                        T, B = TT[i], BB_[i]
                        c.op("act", lambda h: h.activation(out=T["fg"][:, :], in_=T["fg"][:, :], func=AF.Ln), reads=[B["fg"]], writes=[B["fg"]])
                    for i, hh in enumerate(hs):
                        T, B = TT[i], BB_[i]
                        c.op("dve", lambda h: h.tensor_tensor_scan(out=T["b"][:, :], data0=rmask[:, :], data1=T["fg"][:, :], initial=0.0,
                                                                   op0=ALU.mult, op1=ALU.add), reads=[B["fg"], bcn], writes=[B["b"]])
                    for i, hh in enumerate(hs):
                        T, B = TT[i], BB_[i]
                        c.op("act", lambda h: h.activation(out=T["eb"][:, :], in_=T["b"][:, :], func=AF.Exp), reads=[B["b"]], writes=[B["eb"]])
                        c.op("act", lambda h: h.activation(out=T["ebn"][:, :], in_=T["b"][:, :], func=AF.Exp, scale=-1.0),
                             reads=[B["b"]], writes=[B["ebn"]])
                    for i, hh in enumerate(hs):
                        T, B = TT[i], BB_[i]
                        c.op("dve", lambda h: h.tensor_tensor(out=qo[:, hh, :], in0=T["qs"][:, :], in1=T["eb"][:, :], op=ALU.mult),
                             reads=[B["qs"], B["eb"]], writes=[bqo])
                        c.op("dve", lambda h: h.tensor_tensor(out=ko[:, hh, :], in0=T["kk"][:, :], in1=T["ebn"][:, :], op=ALU.mult),
                             reads=[B["kk"], B["ebn"]], writes=[bko])
                        c.op("dve", lambda h: h.tensor_copy(out=ebl[:, hh, tile * 8:(tile + 1) * 8],
                                                            in_=T["eb"][:, :].rearrange("p (a b) -> p a b", b=64)[:, :, 63]),
                             reads=[B["eb"]], writes=[bebl])
                        ptt, bptt = self.ps_next("t")
                        ptb = ptt[:, 0:256].bitcast(BF16)
                        for sub in range(4):
                            c.op("pe", lambda h: h.transpose(ptb[:, sub * 128:(sub + 1) * 128], ko[:, hh, sub * 128:(sub + 1) * 128],
                                                             identb[:, :]), reads=[bko, bcn], writes=[bptt])
                        c.op("act", lambda h: h.activation(out=ktm[:, hh, :, :], in_=ptb.rearrange("p (a b) -> p a b", b=128),
                                                           func=AF.Identity), reads=[bptt], writes=[bktm])
                c.dma("sp", QT3[:, :, cs], qo[:, :, :], reads=[bqo], writes=[self.bqkv])
                c.dma("sp", KT3[:, :, cs], ko[:, :, :], reads=[bko], writes=[self.bqkv])
                c.dma("sp", GT3[:, :, cs], go[:, :, :], reads=[bgo], writes=[self.bqkv])
                c.dma("sp", self.V1s[0:8, :, tile * 4:(tile + 1) * 4, :].rearrange("h p s e -> p h s e"), ktm[:, :, :, :],
                      reads=[bktm], writes=[self.bqkv])
                for sub in range(4):
                    v = ii % 2
                    ii += 1
                    for half in range(2):
                        pv, bpv = self.ps_next("a")
                        for kc in range(KC):
                            c.op("pe", lambda h: h.matmul(pv[:, :], lhsT=xn[:, kc, sub * 128:(sub + 1) * 128],
                                                          rhs=w[:, kc, 2 * D + half * 512:2 * D + (half + 1) * 512],
                                                          start=(kc == 0), stop=(kc == KC - 1)),
                                 reads=[bww, bxn], writes=[bpv])
                        c.op("act", lambda h: h.activation(out=itm[v][:, half * 4:(half + 1) * 4, :],
                                                           in_=pv[:, :].rearrange("p (a b) -> p a b", b=128), func=AF.Identity),
                             reads=[bpv], writes=[bitm[v]])
                    c.dma("sp", self.V1s[8:16, :, tile * 4 + sub, :].rearrange("h p e -> p h e"), itm[v][:, :, :],
                          reads=[bitm[v]], writes=[self.bqkv])
            c.barrier()

    def hgrn_rec(self, ebl, bebl):
        c = self.c
        QT3 = self.QT.rearrange("(kc p) t -> p kc t", p=128)
        KT3 = self.KT.rearrange("(kc p) t -> p kc t", p=128)
        GT3 = self.GT.rearrange("(kc p) t -> p kc t", p=128)
        OT3 = self.OT.rearrange("(kc p) t -> p kc t", p=128)
        self.ps_pools = {"at": [0, 1], "o": [2, 3], "d": [4, 5], "n": [6, 7]}
        self.ps_pi = {}
        with contextlib.ExitStack() as st:
            tri = self.sb(st, "r_tri", [64, 64], F32)
            bcn = Buf()
            c.dma("sp", tri[:, :], self.tri[:, :], writes=[bcn])
            S32 = self.sb(st, "r_S32", [128, 8, 128], F32)
            Sbf = self.sb(st, "r_Sbf", [128, 8, 128], BF16)
            Sbf2 = [Sbf, self.sb(st, "r_Sbfb", [128, 8, 128], BF16)]
            bSbf2 = [Buf(), Buf()]
            t32 = self.sb(st, "r_t32", [128, 8, 128], F32)
            bS32, bSbf, bt32 = Buf(), Buf(), Buf()
            c.op("dve", lambda h: h.memset(S32[:, :, :], 0.0), writes=[bS32])
            c.op("dve", lambda h: h.memset(Sbf2[0][:, :, :], 0.0), writes=[bSbf2[0]])
            c.op("dve", lambda h: h.memset(Sbf2[1][:, :, :], 0.0), writes=[bSbf2[1]])
            qT = [self.sb(st, "r_q%d" % i, [128, 8, 512], BF16) for i in range(2)]
            kT = [self.sb(st, "r_k%d" % i, [128, 8, 512], BF16) for i in range(2)]
            gT = [self.sb(st, "r_g%d" % i, [128, 8, 512], BF16) for i in range(2)]
            ktm = [self.sb(st, "r_ktm%d" % i, [128, 8, 4, 128], BF16) for i in range(2)]
            itm = [self.sb(st, "r_itm%d" % i, [128, 8, 4, 128], BF16) for i in range(2)]
            bin_ = [Buf(), Buf()]
            o32 = self.sb(st, "r_o32", [128, 8, 512], F32)
            bo32 = Buf()
            at = [self.sb(st, "r_at%d" % i, [128, 8, 64], BF16) for i in range(2)]
            bat = [Buf(), Buf()]
            sq = self.sb(st, "r_sq", [128, 512], F32)
            rt = self.sb(st, "r_rt", [128, 512], F32)
            tmp = self.sb(st, "r_tmp", [128, 512], F32)
            bsq, brt, btmp = Buf(), Buf(), Buf()
            oo = [self.sb(st, "r_oo%d" % i, [128, 8, 512], BF16) for i in range(2)]
            boo = [Buf(), Buf()]
            ai = 0
            for tile in range(S // 512):
                cs = slice(tile * 512, (tile + 1) * 512)
                b = tile % 2
                if tile == 1:
                    self.run_hook()
                c.dma("sp", qT[b][:, :, :], QT3[:, :, cs], reads=[self.bqkv], writes=[bin_[b]])
                c.dma("sp", kT[b][:, :, :], KT3[:, :, cs], reads=[self.bqkv], writes=[bin_[b]])
                c.dma("sp", gT[b][:, :, :], GT3[:, :, cs], reads=[self.bqkv], writes=[bin_[b]])
                c.dma("sp", ktm[b][:, :, :, :], self.V1s[0:8, :, tile * 4:(tile + 1) * 4, :].rearrange("h p s e -> p h s e"),
                      reads=[self.bqkv], writes=[bin_[b]])
                c.dma("sp", itm[b][:, :, :, :], self.V1s[8:16, :, tile * 4:(tile + 1) * 4, :].rearrange("h p s e -> p h s e"),
                      reads=[self.bqkv], writes=[bin_[b]])
                for ch in range(8):
                    cc = slice(ch * 64, (ch + 1) * 64)
                    prt = 64 * (ch % 2)
                    sub = ch // 2
                    gch = tile * 8 + ch
                    pdA, bpdA = self.ps[4], self.bps[4]
                    pdB, bpdB = self.ps[5], self.bps[5]
                    for hh in range(8):
                        pd, bpd = (pdA, bpdA) if hh < 4 else (pdB, bpdB)
                        hc = (hh % 4) * 128
                        c.op("pe", lambda h: h.matmul(pd[:, hc:hc + 128], lhsT=ktm[b][prt:prt + 64, hh, sub, :],
                                                      rhs=itm[b][prt:prt + 64, hh, sub, :], start=True, stop=True),
                             reads=[bin_[b]], writes=[bpd])
                    c.op("dve", lambda h: h.tensor_tensor(out=t32[:, 0:4, :], in0=pdA[:, :].rearrange("p (a b) -> p a b", b=128),
                                                          in1=S32[:, 0:4, :], op=ALU.add), reads=[bpdA, bS32], writes=[bt32])
                    c.op("dve", lambda h: h.tensor_tensor(out=t32[:, 4:8, :], in0=pdB[:, :].rearrange("p (a b) -> p a b", b=128),
                                                          in1=S32[:, 4:8, :], op=ALU.add), reads=[bpdB, bS32], writes=[bt32])
                    c.op("dve", lambda h: h.tensor_tensor(out=S32[:, :, :], in0=t32[:, :, :],
                                                          in1=ebl[:, :, gch].unsqueeze(2).to_broadcast([128, 8, 128]), op=ALU.mult),
                         reads=[bt32, bebl], writes=[bS32])
                    c.op("act", lambda h: h.activation(out=Sbf2[gch % 2][:, :, :], in_=S32[:, :, :], func=AF.Identity),
                         reads=[bS32], writes=[bSbf2[gch % 2]])
                    pat, bpat = self.ps_next("at")
                    for hh in range(8):
                        c.op("pe", lambda h: h.matmul(pat[0:64, hh * 64:(hh + 1) * 64], lhsT=kT[b][:, hh, cc], rhs=qT[b][:, hh, cc],
                                                      start=True, stop=True), reads=[bin_[b]], writes=[bpat])
                    a = ai % 2
                    ai += 1
                    c.op("dve", lambda h: h.tensor_tensor(out=at[a][prt:prt + 64, :, :],
                                                          in0=pat[0:64, :].rearrange("p (a b) -> p a b", b=64),
                                                          in1=tri[:, :].unsqueeze(1).to_broadcast([64, 8, 64]), op=ALU.mult),
                         reads=[bpat, bcn], writes=[bat[a]])
                    po, bpo = self.ps_next("o")
                    for hh in range(8):
                        c.op("pe", lambda h: h.matmul(po[:, hh * 64:(hh + 1) * 64], lhsT=Sbf2[(gch - 1) % 2][:, hh, :], rhs=qT[b][:, hh, cc],
                                                      start=True, stop=False), reads=[bSbf2[(gch - 1) % 2], bin_[b]], writes=[bpo])
                        c.op("pe", lambda h: h.matmul(po[:, hh * 64:(hh + 1) * 64], lhsT=itm[b][prt:prt + 64, hh, sub, :],
                                                      rhs=at[a][prt:prt + 64, hh, :], start=False, stop=True),
                             reads=[bin_[b], bat[a]], writes=[bpo])
                    c.op("act", lambda h: h.activation(out=o32[:, :, cc], in_=po[:, :].rearrange("p (a b) -> p a b", b=64),
                                                       func=AF.Identity), reads=[bpo], writes=[bo32])
                for hh in range(8):
                    pn, bpn = self.ps_next("n")
                    c.op("act", lambda h: h.activation(out=sq[:, :], in_=o32[:, hh, :], func=AF.Square), reads=[bo32], writes=[bsq])
                    c.op("pe", lambda h: h.matmul(pn[:, :], lhsT=self.ones_f[:, :], rhs=sq[:, :], start=True, stop=True),
                         reads=[bsq, self.bconst], writes=[bpn])
                    c.op("act", lambda h: h.activation(out=rt[:, :], in_=pn[:, :], func=AF.Sqrt, scale=1.0 / 128,
                                                       bias=self.eps_t[:, 0:1]), reads=[bpn, self.bconst], writes=[brt])
                    c.op("dve", lambda h: h.reciprocal(out=rt[:, :], in_=rt[:, :]), reads=[brt], writes=[brt])
                    c.op("dve", lambda h: h.scalar_tensor_tensor(out=tmp[:, :], in0=o32[:, hh, :],
                                                                 scalar=self.pvec[:, self.col_hnorm:self.col_hnorm + 1],
                                                                 in1=rt[:, :], op0=ALU.mult, op1=ALU.mult),
                         reads=[bo32, brt, self.bconst], writes=[btmp])
                    c.op("dve", lambda h: h.tensor_tensor(out=oo[b][:, hh, :], in0=tmp[:, :], in1=gT[b][:, hh, :], op=ALU.mult),
                         reads=[btmp, bin_[b]], writes=[boo[b]])
                c.dma("sp", OT3[:, :, cs], oo[b][:, :, :], reads=[boo[b]], writes=[self.bot])
            c.barrier()

    def s5_phase(self, layer, wglu_s, bw):
        c = self.c
        hT3 = self.s5_src.rearrange("(kc p) t -> p kc t", p=128)
        w3 = wglu_s.rearrange("(kc p) n -> p kc n", p=128)
        self.ps_pools = {"y": [0, 1, 2, 3], "e": [4, 5, 6, 7]}
        self.ps_pi = {}
        TS = 1024
        sgnA = self.pvec[:, self.col_sgnA:self.col_sgnA + 1]
        sgnB = self.pvec[:, self.col_sgnA + 1:self.col_sgnA + 2]
        neg1 = self.pvec[:, self.col_sgnA + 2:self.col_sgnA + 3]
        with contextlib.ExitStack() as st:
            Bp = self.sb(st, "s_Bp", [128, 8, 4, 128], BF16)
            Bq = self.sb(st, "s_Bq", [128, 8, 4, 128], BF16)
            Cc = self.sb(st, "s_Cc", [128, 64, 64], BF16)
            Cs = self.sb(st, "s_Cs", [128, 64, 64], BF16)
            th = self.sb(st, "s_th", [128, 64], F32)
            rho = self.sb(st, "s_rho", [128, 64], F32)
            carry = self.sb(st, "s_carry", [128, 64], F32)
            bpar, bcarry = Buf(), Buf()
            with contextlib.ExitStack() as s2:
                def t64(n):
                    return self.sb(s2, "s_" + n, [128, 64], F32)
                are, aim, ldt, dt, lr, x0, kf0, sn0, cs0, abr, abi, den, mre, fre, fim, tA, tB = [t64(n) for n in (
                    "are", "aim", "ldt", "dt", "lr", "x0", "kf0", "sn0", "cs0", "abr", "abi", "den", "mre", "fre", "fim", "tA", "tB")]
                ki0 = self.sb(s2, "s_ki0", [128, 64], I32)
                bl = Buf()
                c.dma("sp", are[:, :], self.s5p[:, 0:64], writes=[bl])
                c.dma("sp", aim[:, :], self.s5p[:, 64:128], writes=[bl])
                c.dma("sp", ldt[:, :], self.s5p[:, 128:192], writes=[bl])
                b0 = Buf()
                R, W = [bl, b0], [b0]
                c.op("act", lambda h: h.activation(out=dt[:, :], in_=ldt[:, :], func=AF.Exp), reads=R, writes=W)
                c.op("dve", lambda h: h.tensor_tensor(out=lr[:, :], in0=are[:, :], in1=dt[:, :], op=ALU.mult), reads=R, writes=W)
                c.op("dve", lambda h: h.tensor_tensor(out=th[:, :], in0=aim[:, :], in1=dt[:, :], op=ALU.mult), reads=R, writes=[b0, bpar])
                c.op("act", lambda h: h.activation(out=rho[:, :], in_=lr[:, :], func=AF.Exp), reads=R, writes=[b0, bpar])
                self._rr(th, b0, 64, ki0, b0, kf0, b0, sn0, b0, cs0, b0)
                c.op("dve", lambda h: h.tensor_tensor(out=abr[:, :], in0=rho[:, :], in1=cs0[:, :], op=ALU.mult), reads=R, writes=W)
                c.op("dve", lambda h: h.tensor_tensor(out=abi[:, :], in0=rho[:, :], in1=sn0[:, :], op=ALU.mult), reads=R, writes=W)
                c.op("dve", lambda h: h.tensor_tensor(out=den[:, :], in0=are[:, :], in1=are[:, :], op=ALU.mult), reads=R, writes=W)
                c.op("dve", lambda h: h.tensor_tensor(out=tA[:, :], in0=aim[:, :], in1=aim[:, :], op=ALU.mult), reads=R, writes=W)
                c.op("dve", lambda h: h.tensor_tensor(out=den[:, :], in0=den[:, :], in1=tA[:, :], op=ALU.add), reads=R, writes=W)
                c.op("dve", lambda h: h.reciprocal(out=den[:, :], in_=den[:, :]), reads=R, writes=W)
                c.op("dve", lambda h: h.tensor_scalar(out=mre[:, :], in0=abr[:, :], scalar1=-1.0, scalar2=None, op0=ALU.add), reads=R, writes=W)
                c.op("dve", lambda h: h.tensor_tensor(out=tA[:, :], in0=mre[:, :], in1=are[:, :], op=ALU.mult), reads=R, writes=W)
                c.op("dve", lambda h: h.tensor_tensor(out=tB[:, :], in0=abi[:, :], in1=aim[:, :], op=ALU.mult), reads=R, writes=W)
                c.op("dve", lambda h: h.tensor_tensor(out=fre[:, :], in0=tA[:, :], in1=tB[:, :], op=ALU.add), reads=R, writes=W)
                c.op("dve", lambda h: h.tensor_tensor(out=fre[:, :], in0=fre[:, :], in1=den[:, :], op=ALU.mult), reads=R, writes=W)
                c.op("dve", lambda h: h.tensor_tensor(out=tA[:, :], in0=abi[:, :], in1=are[:, :], op=ALU.mult), reads=R, writes=W)
                c.op("dve", lambda h: h.tensor_tensor(out=tB[:, :], in0=mre[:, :], in1=aim[:, :], op=ALU.mult), reads=R, writes=W)
                c.op("dve", lambda h: h.tensor_tensor(out=fim[:, :], in0=tA[:, :], in1=tB[:, :], op=ALU.subtract), reads=R, writes=W)
                c.op("dve", lambda h: h.tensor_tensor(out=fim[:, :], in0=fim[:, :], in1=den[:, :], op=ALU.mult), reads=R, writes=W)
                c.op("dve", lambda h: h.tensor_scalar(out=tA[:, :], in0=fim[:, :], scalar1=sgnA, scalar2=None, op0=ALU.mult),
                     reads=[b0, self.bconst], writes=W)
                c.op("dve", lambda h: h.tensor_scalar(out=tB[:, :], in0=fre[:, :], scalar1=sgnB, scalar2=None, op0=ALU.mult),
                     reads=[b0, self.bconst], writes=W)
                BB = self.sb(s2, "s_BB", [128, 1024], F32)
                BS = self.sb(s2, "s_BS", [128, 1024], F32)
                u1 = self.sb(s2, "s_u1", [128, 1024], F32)
                u2 = self.sb(s2, "s_u2", [128, 1024], F32)
                Z = self.sb(s2, "s_Z", [128, 8, 4, 128], F32)
                idf = self.sb(s2, "s_idf", [128, 128], F32)
                c.dma("sp", BB[:, :], self.s5B[:, 0:1024], writes=[bl])
                c.dma("sp", BS[:, :], self.s5B[:, 1024:2048], writes=[bl])
                c.dma("sp", idf[:, :], self.ident[:, :], writes=[bl])

                def bc(t):
                    return t[:, :].unsqueeze(2).to_broadcast([128, 64, 16])

                def v3(t):
                    return t[:, :].rearrange("p (g c) -> p g c", c=16)
                for (dst, ca, cb) in ((Bp, (fre, BB, tA, BS), None), (Bq, (tB, BS, fim, BB), None)):
                    fa, Xa, fb, Xb = ca
                    c.op("dve", lambda h: h.memset(Z[:, :, :, :], 0.0), reads=R, writes=W)
                    c.op("dve", lambda h: h.tensor_tensor(out=v3(u1), in0=v3(Xa), in1=bc(fa), op=ALU.mult), reads=R, writes=W)
                    c.op("dve", lambda h: h.tensor_tensor(out=v3(u2), in0=v3(Xb), in1=bc(fb), op=ALU.mult), reads=R, writes=W)
                    for par in range(4):
                        zv = Z[:, :, par, :].rearrange("p k (m q) -> p k m q", q=64)[:, :, :, 16 * par:16 * par + 16]
                        a1 = u1[:, :].rearrange("p (k m r c) -> p k m r c", k=8, m=2, r=4, c=16)[:, :, :, par, :]
                        a2 = u2[:, :].rearrange("p (k m r c) -> p k m r c", k=8, m=2, r=4, c=16)[:, :, :, par, :]
                        c.op("dve", lambda h: h.tensor_tensor(out=zv, in0=a1, in1=a2, op=ALU.add), reads=R, writes=W)
                    for kc in range(8):
                        for par in range(4):
                            pz, bpz = self.ps_next("e")
                            c.op("pe", lambda h: h.transpose(pz[:, 0:128], Z[:, kc, par, :], idf[:, :]), reads=R, writes=[bpz])
                            c.op("act", lambda h: h.activation(out=dst[:, kc, par, :], in_=pz[:, 0:128], func=AF.Identity),
                                 reads=[bpz], writes=[bpar])
                CC = self.sb(s2, "s_CC", [128, 2048], F32)
                c.dma("sp", CC[:, :], self.s5C[:, :], writes=[bl])
                c.op("dve", lambda h: h.memset(Cc[:, :, :], 0.0), writes=[bpar])
                c.op("dve", lambda h: h.memset(Cs[:, :, :], 0.0), writes=[bpar])
                for (dst, off, sg) in ((Cc, 0, sgnB), (Cs, 1024, neg1)):
                    for par in range(4):
                        dv = dst[:, :, :].rearrange("p (gm r) (q c) -> p gm r q c", r=4, q=4)[:, :, par, par, :]
                        sv = CC[:, off:off + 1024].rearrange("p (gm r c) -> p gm r c", r=4, c=16)[:, :, par, :]
                        c.op("dve", lambda h: h.tensor_scalar(out=dv, in0=sv, scalar1=sg, scalar2=None, op0=ALU.mult),
                             reads=[bl, self.bconst], writes=[bpar])
                c.op("dve", lambda h: h.memset(carry[:, :], 0.0), writes=[bcarry])
                c.barrier()
            tt = self.sb(st, "s_tt", [128, TS], F32)
            thq = self.sb(st, "s_thq", [128, 64], F32)
            btt, bthq = Buf(), Buf()
            c.dma("sp", tt[:, :], self.ttc[:, 0:TS], writes=[btt])
            wg = self.sb(st, "s_wg", [128, KC, 2 * D], BF16)
            bwg = Buf()
            for i in range(2):
                c.dma("sp", wg[:, :, i * D:(i + 1) * D], w3[:, :, i * D:(i + 1) * D], reads=[bw], writes=[bwg])
            hx = self.sb(st, "s_hx", [128, KC, 512], F32)
            sq = self.sb(st, "s_sq", [128, 2, 512], F32)
            rt = self.sb(st, "s_rt", [128, 512], F32)
            xn = self.sb(st, "s_xn", [128, KC, TS], BF16)
            z = self.sb(st, "s_z", [128, KC, TS], BF16)
            bhx, bsq, brt, bxn, bz = Buf(), Buf(), Buf(), Buf(), Buf()
            x = self.sb(st, "s_x", [128, TS], F32)
            ki = self.sb(st, "s_ki", [128, TS], I32)
            sn = [self.sb(st, "s_sn%d" % i, [128, TS], F32) for i in range(3)]
            cs_ = [self.sb(st, "s_cs%d" % i, [128, TS], F32) for i in range(3)]
            eh = [self.sb(st, "s_eh%d" % i, [128, TS], F32) for i in range(2)]
            G = [self.sb(st, "s_G%d" % i, [128, TS], F32) for i in range(2)]
            Gc = [self.sb(st, "s_Gc%d" % i, [128, TS], BF16) for i in range(2)]
            Gs = [self.sb(st, "s_Gs%d" % i, [128, TS], BF16) for i in range(2)]
            bsn, bcs = [Buf() for _ in range(3)], [Buf() for _ in range(3)]
            beh, bG, bGc, bGs = [[Buf() for _ in range(2)] for _ in range(4)]
            bx, bki = Buf(), Buf()
            t1 = [self.sb(st, "s_t1%d" % i, [128, 512], F32) for i in range(2)]
            t2 = [self.sb(st, "s_t2%d" % i, [128, 512], F32) for i in range(2)]
            bt1 = [Buf(), Buf()]
            bt2 = [Buf(), Buf()]
            yv = self.sb(st, "s_yv", [128, 512], F32)
            y2 = self.sb(st, "s_y2", [128, 512], F32)
            y3 = self.sb(st, "s_y3", [128, 512], F32)
            byv, by2, by3 = Buf(), Buf(), Buf()
            hr = [self.sb(st, "s_hr", [128, TS], F32)] * 2
            bhr = [Buf()] * 2
            ti = 0
            for q in range(S // TS):
                t0 = q * TS
                bh = self.bh[q]
                c.op("dve", lambda h: h.tensor_scalar(out=thq[:, :], in0=th[:, :], scalar1=float(t0), scalar2=None, op0=ALU.mult),
                     reads=[bpar], writes=[bthq])
                for half in range(2):
                    gate_ = self.s5_gate if (q == 0 and half == 0) else []
                    c.dma("sp", hx[:, :, :], hT3[:, :, t0 + half * 512:t0 + (half + 1) * 512], reads=[bh] + gate_, writes=[bhx])
                    self.rmsnorm_tile(hx, bhx, self.col_nmix + layer * KC, xn[:, :, half * 512:(half + 1) * 512], bxn, 512,
                                      (sq, bsq, rt, brt))
                pys = {}
                pool_eng = "dve" if (q == 0 and self.s5_spare_pool) else "pool"

                TWO_PI = 2.0 * np.pi

                def S1(g):
                    kc, gi = g // 8, g % 8
                    u3 = g % 3
                    if gi == 0:
                        pys[kc] = [self.ps_next("y") for _ in range(2)]
                    c.op("act", lambda h: h.activation(out=x[:, :], in_=tt[:, :], func=AF.Identity, scale=th[:, g:g + 1],
                                                       bias=thq[:, g:g + 1]),
                         reads=[btt, bpar, bthq], writes=[bx])
                    c.op("act", lambda h: h.activation(out=ki[:, :], in_=x[:, :], func=AF.Identity, scale=1.0 / TWO_PI),
                         reads=[bx], writes=[bki])

                def S1b(g):
                    u3 = g % 3
                    c.op("dve", lambda h: h.scalar_tensor_tensor(out=sn[u3][:, :], in0=ki[:, :], scalar=-TWO_PI, in1=x[:, :],
                                                                 op0=ALU.mult, op1=ALU.add), reads=[bki, bx], writes=[bsn[u3]])
                    c.op("dve", lambda h: h.tensor_scalar(out=sn[u3][:, :], in0=sn[u3][:, :], scalar1=-PI_LO, scalar2=PI_LO,
                                                          op0=ALU.max, op1=ALU.min), reads=[bsn[u3]], writes=[bsn[u3]])
                    c.op("act", lambda h: h.activation(out=cs_[u3][:, :], in_=sn[u3][:, :], func=AF.Abs),
                         reads=[bsn[u3]], writes=[bcs[u3]])

                def S2a(g):
                    u3 = g % 3
                    c.op("act", lambda h: h.activation(out=sn[u3][:, :], in_=sn[u3][:, :], func=AF.Sin), reads=[bsn[u3]], writes=[bsn[u3]])
                    c.op("act", lambda h: h.activation(out=cs_[u3][:, :], in_=cs_[u3][:, :], func=AF.Sin, scale=-1.0,
                                                       bias=self.halfpi_t[:, 0:1]), reads=[bcs[u3], self.bconst], writes=[bcs[u3]])

                def S2(g):
                    nonlocal ti
                    kc, gi = g // 8, g % 8
                    u3, u = g % 3, g % 2
                    m, par = gi // 4, gi % 4
                    rows = slice(64 * m, 64 * m + 64)
                    for ct in range(2):
                        cc = slice(ct * 512, (ct + 1) * 512)
                        pe_, bpe = self.ps_next("e")
                        pq_, bpq = self.ps_next("e")
                        c.op("pe", lambda h: h.matmul(pe_[:, :], lhsT=Bp[rows, kc, par, :], rhs=xn[rows, kc, cc], start=True, stop=True),
                             reads=[bpar, bxn], writes=[bpe])
                        c.op("pe", lambda h: h.matmul(pq_[:, :], lhsT=Bq[rows, kc, par, :], rhs=xn[rows, kc, cc], start=True, stop=True),
                             reads=[bpar, bxn], writes=[bpq])
                        e = ti % 2
                        ti += 1
                        c.op("dve", lambda h: h.tensor_tensor(out=t1[e][:, :], in0=pe_[:, :], in1=cs_[u3][:, cc], op=ALU.mult),
                             reads=[bpe, bcs[u3]], writes=[bt1[e]])
                        c.op("dve", lambda h: h.tensor_tensor(out=t2[e][:, :], in0=pq_[:, :], in1=sn[u3][:, cc], op=ALU.mult),
                             reads=[bpq, bsn[u3]], writes=[bt2[e]])
                        c.op(pool_eng, lambda h: h.tensor_tensor(out=eh[u][:, cc], in0=t1[e][:, :], in1=t2[e][:, :], op=ALU.add),
                             reads=[bt1[e], bt2[e]], writes=[beh[u]])

                def S3(g):
                    u3, u = g % 3, g % 2
                    c.op("dve", lambda h: h.tensor_tensor_scan(out=G[u][:, :], data0=rho[:, g:g + 1].to_broadcast([128, TS]),
                                                               data1=eh[u][:, :], initial=carry[:, g:g + 1],
                                                               op0=ALU.mult, op1=ALU.add),
                         reads=[beh[u], bpar, bcarry], writes=[bG[u]])
                    c.op("act", lambda h: h.activation(out=carry[:, g:g + 1], in_=G[u][:, TS - 1:TS], func=AF.Identity),
                         reads=[bG[u]], writes=[bcarry])
                    c.op("dve", lambda h: h.tensor_tensor(out=Gc[u][:, :], in0=G[u][:, :], in1=cs_[u3][:, :], op=ALU.mult),
                         reads=[bG[u], bcs[u3]], writes=[bGc[u]])
                    c.op(pool_eng, lambda h: h.tensor_tensor(out=Gs[u][:, :], in0=G[u][:, :], in1=sn[u3][:, :], op=ALU.mult),
                         reads=[bG[u], bsn[u3]], writes=[bGs[u]])

                def S4(g):
                    kc, gi = g // 8, g % 8
                    u = g % 2
                    m, par = gi // 4, gi % 4
                    rows = slice(64 * m, 64 * m + 64)
                    py = pys[kc]
                    for ct in range(2):
                        cc = slice(ct * 512, (ct + 1) * 512)
                        pyy, bpy = py[ct]
                        c.op("pe", lambda h: h.matmul(pyy[rows, :], lhsT=Cc[:, g, :], rhs=Gc[u][:, cc], start=(par == 0), stop=False),
                             reads=[bpar, bGc[u]], writes=[bpy])
                        c.op("pe", lambda h: h.matmul(pyy[rows, :], lhsT=Cs[:, g, :], rhs=Gs[u][:, cc], start=False, stop=(par == 3)),
                             reads=[bpar, bGs[u]], writes=[bpy])
                    if gi == 7:
                        for ct in range(2):
                            cc = slice(ct * 512, (ct + 1) * 512)
                            pyy, bpy = py[ct]
                            dcol = self.pvec[:, self.col_s5d + kc:self.col_s5d + kc + 1]
                            c.op("dve", lambda h: h.scalar_tensor_tensor(out=yv[:, :], in0=xn[:, kc, cc], scalar=dcol, in1=pyy[:, :],
                                                                         op0=ALU.mult, op1=ALU.add),
                                 reads=[bxn, bpy, self.bconst], writes=[byv])
                            c.op("act", lambda h: h.activation(out=y2[:, :], in_=yv[:, :], func=AF.Square), reads=[byv], writes=[by2])
                            c.op(pool_eng, lambda h: h.tensor_scalar(out=y2[:, :], in0=y2[:, :], scalar1=0.044715, scalar2=1.0,
                                                                   op0=ALU.mult, op1=ALU.add), reads=[by2], writes=[by2])
                            c.op(pool_eng, lambda h: h.tensor_tensor(out=y2[:, :], in0=y2[:, :], in1=yv[:, :], op=ALU.mult),
                                 reads=[by2, byv], writes=[by2])
                            c.op("act", lambda h: h.activation(out=y3[:, :], in_=y2[:, :], func=AF.Sigmoid, scale=1.5957691216),
                                 reads=[by2], writes=[by3])
                            c.op(pool_eng, lambda h: h.tensor_tensor(out=z[:, kc, cc], in0=y3[:, :], in1=yv[:, :], op=ALU.mult),
                                 reads=[by3, byv], writes=[bz])

                NG = 64
                for i in range(NG + 3):
                    if 0 <= i - 1 < NG:
                        S2a(i - 1)
                    if 0 <= i - 2 < NG:
                        S3(i - 2)
                    if i < NG:
                        S1(i)
                    if 0 <= i - 1 < NG:
                        S2(i - 1)
                    if i < NG:
                        S1b(i)
                    if 0 <= i - 3 < NG:
                        S4(i - 3)
                for oc in range(KC):
                    r = oc % 2
                    c.dma("sp", hr[r][:, :], self.s5_src[oc * 128:(oc + 1) * 128, t0:t0 + TS], reads=[bh], writes=[bhr[r]])
                    for ct in range(2):
                        cc = slice(ct * 512, (ct + 1) * 512)
                        pv, bpv = self.ps_next("e")
                        pg, bpg = self.ps_next("e")
                        for kc in range(KC):
                            c.op("pe", lambda h: h.matmul(pv[:, :], lhsT=wg[:, kc, oc * 128:(oc + 1) * 128], rhs=z[:, kc, cc],
                                                          start=(kc == 0), stop=(kc == KC - 1)), reads=[bwg, bz], writes=[bpv])
                        for kc in range(KC):
                            c.op("pe", lambda h: h.matmul(pg[:, :], lhsT=wg[:, kc, D + oc * 128:D + (oc + 1) * 128], rhs=z[:, kc, cc],
                                                          start=(kc == 0), stop=(kc == KC - 1)), reads=[bwg, bz], writes=[bpg])
                        bv = self.pvec[:, self.col_bglu + oc:self.col_bglu + oc + 1]
                        bg = self.pvec[:, self.col_bglu + 8 + oc:self.col_bglu + 8 + oc + 1]
                        c.op("act", lambda h: h.activation(out=y2[:, :], in_=pv[:, :], func=AF.Identity, bias=bv),
                             reads=[bpv, self.bconst], writes=[by2])
                        c.op("act", lambda h: h.activation(out=y3[:, :], in_=pg[:, :], func=AF.Sigmoid, bias=bg),
                             reads=[bpg, self.bconst], writes=[by3])
                        c.op("dve", lambda h: h.tensor_tensor(out=y2[:, :], in0=y2[:, :], in1=y3[:, :], op=ALU.mult),
                             reads=[by2, by3], writes=[by2])
                        c.op("dve", lambda h: h.tensor_tensor(out=hr[r][:, cc], in0=hr[r][:, cc], in1=y2[:, :], op=ALU.add),
                             reads=[by2, bhr[r]], writes=[bhr[r]])
                    c.dma("sp", self.hT[oc * 128:(oc + 1) * 128, t0:t0 + TS], hr[r][:, :], reads=[bhr[r]], writes=[bh])
            c.barrier()

    def final_phase(self, outT, do_norm):
        c = self.c
        hT3 = self.hT.rearrange("(kc p) t -> p kc t", p=128)
        oT3 = outT.rearrange("(kc p) t -> p kc t", p=128)
        with contextlib.ExitStack() as st:
            hx = [self.sb(st, "o_hx%d" % i, [128, KC, 512], F32) for i in range(2)]
            ox = [self.sb(st, "o_ox%d" % i, [128, KC, 512], F32) for i in range(2)]
            sq = self.sb(st, "o_sq", [128, KC, 512], F32)
            rt = self.sb(st, "o_rt", [128, 512], F32)
            bhx = [Buf(), Buf()]
            box = [Buf(), Buf()]
            bsq, brt, bo = Buf(), Buf(), Buf()
            for tile in range(S // 512):
                cs = slice(tile * 512, (tile + 1) * 512)
                b = tile % 2
                c.dma("sp", hx[b][:, :, :], hT3[:, :, cs], reads=[self.bh[tile // 2]], writes=[bhx[b]])
                if do_norm:
                    self.rmsnorm_tile(hx[b], bhx[b], self.col_nfin, ox[b], box[b], 512, (sq, bsq, rt, brt))
                    c.dma("sp", oT3[:, :, cs], ox[b][:, :, :], reads=[box[b]], writes=[bo])
                else:
                    c.dma("sp", oT3[:, :, cs], hx[b][:, :, :], reads=[bhx[b]], writes=[bo])
            c.barrier()

    def build(self):
        cfg = self.cfg
        nc = self.nc
        c = self.c
        es = self.es
        stages = cfg["stages"]
        mixl = [l for (k, l) in stages if k == "mix"]
        ffnl = [l for (k, l) in stages if k == "ffn"]
        xT = self.din("xT", [D, S])
        pvec = self.din("pvec", [128, cfg["npvec"]])
        self.col_nmix, self.col_nffn, self.col_nfin = 0, 4 * KC, 8 * KC
        self.col_invf, self.col_sgnrow = 9 * KC, 9 * KC + 1
        self.posb = self.din("posb", [128, S], I32)
        self.dmask = self.din("dmask", [128, 20 * 512])
        self.mmask = self.din("mmask", [128, 4 * 512])
        self.onehot = self.din("onehot", [16, S])
        self.pastneg = self.din("pastneg", [128, 512])
        self.own = self.din("own", [128, 512])
        self.ident = self.din("ident", [128, 128])
        self.ropeperm = self.din("ropeperm", [128, 128])
        self.rmask = self.din("rmask", [128, 512])
        self.tri = self.din("tri", [64, 64])
        self.col_lb, self.col_hnorm = 9 * KC + 2, 9 * KC + 2 + 4 * KC
        self.col_sgnA = self.col_hnorm + 1
        self.col_s5d = self.col_sgnA + 3
        self.col_bglu = self.col_s5d + KC
        self.s5p = self.din("s5p", [128, 192])
        self.s5B = self.din("s5B", [128, 2048])
        self.s5C = self.din("s5C", [128, 2048])
        self.ttc = self.din("ttc", [128, S])
        w_gu_in = {l: self.din("w_gu%d" % l, [D, 2 * DFF]) for l in ffnl}
        w_d_in = {l: self.din("w_d%d" % l, [DFF, D]) for l in ffnl}
        self.w_gu = {l: self.dscr("s_wgu%d" % l, [D, 2 * DFF], BF16) for l in ffnl}
        self.w_d = {l: self.dscr("s_wd%d" % l, [DFF, D], BF16) for l in ffnl}
        self.bw_ffn = {l: Buf() for l in ffnl}
        win, wsc, bwm = {}, {}, {}
        for l in mixl:
            if l in (1, 3):
                nm = "dil" if l == 1 else "moba"
                win[l] = (self.din(nm + "_qkv", [D, 3 * D]), self.din(nm + "_o", [D, D]))
                wsc[l] = (self.dscr("s_%s_qkv" % nm, [D, 3 * D], BF16), None,
                          self.dscr("s_%s_o" % nm, [D, D], BF16))
                bwm[l] = Buf()
            elif l == 0:
                win[l] = (self.din("s5_wglu", [D, 2 * D]),)
                wsc[l] = (self.dscr("s_s5_wglu", [D, 2 * D], BF16),)
                bwm[l] = Buf()
            elif l == 2:
                win[l] = (self.din("hgrn_in", [D, 4 * D]), self.din("hgrn_o", [D, D]))
                wsc[l] = (self.dscr("s_hgrn_in", [D, 4 * D], BF16), self.dscr("s_hgrn_o", [D, D], BF16))
                bwm[l] = Buf()
        outT = nc.dram_tensor("outT", [D, S], F32, kind="ExternalOutput").ap()
        self.GT = self.dscr("GT", [D, S], BF16)
        self.hT = self.dscr("hT", [D, S], F32)
        self.QT = self.dscr("QT", [D, S], BF16)
        self.KT = self.dscr("KT", [D, S], BF16)
        self.OT = self.dscr("OT", [D, S], BF16)
        self.V1s = self.dscr("V1s", [16, 128, 32, 128], BF16)
        self.bqkv, self.bot = Buf(), Buf()
        self.bh = [Buf() for _ in range(S // 1024)]
        self.pvec = self.sb(es, "pvec_sb", [128, cfg["npvec"]], F32)
        self.ones_f = self.sb(es, "ones_f", [128, 128], F32)
        self.eps_t = self.sb(es, "eps_t", [128, 1], F32)
        self.bconst = Buf()
        c.dma("sp", self.pvec[:, :], pvec[:, :], writes=[self.bconst])
        c.op("dve", lambda h: h.memset(self.ones_f[:, :], 1.0), writes=[self.bconst])
        c.op("dve", lambda h: h.memset(self.eps_t[:, :], EPS), writes=[self.bconst])
        self.halfpi_t = self.sb(es, "halfpi_t", [128, 1], F32)
        c.op("dve", lambda h: h.memset(self.halfpi_t[:, :], float(np.pi / 2)), writes=[self.bconst])
        self.ps = [es.enter_context(nc.psum_tensor("ps%d" % i, [128, 512], F32)) for i in range(8)]
        self.bps = [Buf() for _ in range(8)]
        self.ps_i = 0
        self.sg_i = 0
        self.ps_pools = {}
        self.ps_pi = {}
        self.cast_i = 0
        self.cast_st = []
        for i in range(2):
            self.cast_st.append((self.sb(es, "cst%d" % i, [128, 512], F32), self.sb(es, "cstb%d" % i, [128, 512], BF16),
                                 Buf(), Buf()))
        xT3 = xT.rearrange("(kc p) t -> p kc t", p=128)
        hT3 = self.hT.rearrange("(kc p) t -> p kc t", p=128)
        self.s5_src = self.hT
        if stages[0] == ("mix", 0):
            self.s5_src = xT
        else:
            with contextlib.ExitStack() as st:
                tmp = [self.sb(st, "cp%d" % i, [128, KC, 1024], F32) for i in range(2)]
                bt = [Buf(), Buf()]
                for i, t0 in enumerate(range(0, S, 1024)):
                    c.dma("sp", tmp[i % 2][:, :, :], xT3[:, :, t0:t0 + 1024], writes=[bt[i % 2]])
                    c.dma("sp", hT3[:, :, t0:t0 + 1024], tmp[i % 2][:, :, :], reads=[bt[i % 2]], writes=[self.bh[i]])
                c.barrier()
        cast_done = set()

        def emit_cast(si_):
            if si_ >= len(stages) or si_ in cast_done:
                return
            cast_done.add(si_)
            k, l = stages[si_]
            if k == "ffn":
                self.cast_weight(w_gu_in[l], self.w_gu[l], D, 2 * DFF, self.bw_ffn[l])
                self.cast_weight(w_d_in[l], self.w_d[l], DFF, D, self.bw_ffn[l])
            elif k == "mix" and l in (1, 3):
                self.cast_weight(win[l][0], wsc[l][0], D, 3 * D, bwm[l])
                self.cast_weight(win[l][1], wsc[l][2], D, D, bwm[l])
            elif k == "mix" and l == 0:
                self.cast_weight(win[l][0], wsc[l][0], D, 2 * D, bwm[l])
            elif k == "mix" and l == 2:
                self.cast_weight(win[l][0], wsc[l][0], D, 4 * D, bwm[l])
                self.cast_weight(win[l][1], wsc[l][1], D, D, bwm[l])
        self.hook = None
        bwp = {l: Buf() for l in mixl}
        for si, (k, l) in enumerate(stages):
            for (k2, l2) in stages[si + 1:si + 2] + (stages[0:1] if si == 0 else []):
                if False:
                    self.cast_weight_qk_perm(win[l2][0], wsc[l2][1], bwp[l2])
                    bwp[l2].done = True
            emit_cast(si)
            if si == 0:
                emit_cast(1)
            self.s5_spare_pool = False
            self.s5_gate = []
            if si == 0 and len(stages) > 1 and stages[1][0] == "ffn":
                self.s5_gate = [self.bw_ffn[stages[1][1]]]
            self.hook = lambda si=si: emit_cast(si + 1)
            if k == "ffn":
                self.ffn_phase(l)
            elif k == "mix" and l in (1, 3):
                self.qkv_phase(l, wsc[l][0], wsc[l][1], bwm[l], bwp[l])
                self.attn_phase("dil" if l == 1 else "moba")
                self.linres_phase(self.OT, wsc[l][2], bwm[l], self.bot)
            elif k == "mix" and l == 0:
                self.s5_phase(l, wsc[l][0], bwm[l])
            elif k == "mix" and l == 2:
                with contextlib.ExitStack() as st:
                    ebl = self.sb(st, "ebl", [128, 8, 64], F32)
                    bebl = Buf()
                    self.hgrn_prep(l, wsc[l][0], bwm[l], ebl, bebl)
                    self.hgrn_rec(ebl, bebl)
                self.linres_phase(self.OT, wsc[l][1], bwm[l], self.bot)
            self.run_hook()
        self.final_phase(outT, cfg.get("final_norm", True))
        c.final_wait()
        es.close()
        return nc


ROPE_THETA = 500000.0


def consts_build():
    cst = {}
    kl = np.arange(128)[:, None]
    ql = np.arange(512)[None, :]
    dm = np.zeros((128, 20, 512), np.float32)
    for mi in range(20):
        off = mi - 16
        dl = ql - kl - off * 128
        m = ((dl >= 0) & (dl <= 128)).astype(np.float32)
        m += ((dl >= 0) & (dl <= 512) & (dl % 4 == 0)).astype(np.float32)
        m += ((dl >= 0) & (dl <= 2048) & (dl % 16 == 0)).astype(np.float32)
        dm[:, mi, :] = m
    cst["dmask"] = dm.reshape(128, 20 * 512)
    mm = np.zeros((128, 4, 512), np.float32)
    for j in range(4):
        mm[:, j, :] = (j * 128 + kl <= ql).astype(np.float32)
    cst["mmask"] = mm.reshape(128, 4 * 512)
    oh = np.zeros((16, S), np.float32)
    for n in range(16):
        oh[n, n * 256:(n + 1) * 256] = 1.0
    cst["onehot"] = oh
    pn = np.zeros((128, 32, 16), np.float32)
    ow = np.zeros((128, 32, 16), np.float32)
    for qt in range(32):
        qb = qt // 2
        pn[:, qt, qb:] = -1e30
        ow[:, qt, qb] = 1.0
    cst["pastneg"] = pn.reshape(128, 512)
    cst["own"] = ow.reshape(128, 512)
    cst["ident"] = np.eye(128, dtype=np.float32)
    pmx = np.zeros((128, 128), np.float32)
    for f in range(128):
        j = f % 64
        if j < 8:
            pmx[f + 8, f] = 1.0
        elif j < 16:
            pmx[f - 8, f] = 1.0
    cst["ropeperm"] = pmx
    rm = np.ones((128, 512), np.float32)
    rm[:, 0::64] = 0.0
    cst["rmask"] = rm
    cst["ttc"] = np.ascontiguousarray(np.broadcast_to(np.arange(S, dtype=np.float32)[None, :], (128, S)))
    cst["tri"] = (np.arange(64)[:, None] <= np.arange(64)[None, :]).astype(np.float32)
    return cst


def pvec_build(inp):
    cols = []
    for l in range(4):
        cols.append(inp["norm_mix"][l].reshape(KC, 128).T)
    for l in range(4):
        cols.append(inp["norm_ffn"][l].reshape(KC, 128).T)
    cols.append(inp["norm_final"].reshape(KC, 128).T)
    f = np.arange(128) % 64
    inv = ROPE_THETA ** (-np.arange(0, 16, 2, dtype=np.float32) / 16.0)
    invf = np.where(f < 16, inv[f % 8], 0.0).astype(np.float32)
    sgn = np.where(f < 8, -1.0, np.where(f < 16, 1.0, 0.0)).astype(np.float32)
    cols.append(invf[:, None])
    cols.append(sgn[:, None])
    for l in range(4):
        cols.append(inp["hgrn_lower_bound"][l].reshape(KC, 128).T)
    cols.append(inp["hgrn_norm"][0].reshape(128, 1))
    p = np.arange(128)
    sa = np.where(p < 64, -1.0, 1.0).astype(np.float32)
    cols.append(sa[:, None])
    cols.append(-sa[:, None])
    cols.append(-np.ones((128, 1), np.float32))
    cols.append(inp["s5_d"][0].reshape(KC, 128).T)
    cols.append(inp["s5_b_glu"][0].reshape(2 * KC, 128).T)
    return np.ascontiguousarray(np.concatenate(cols, axis=1).astype(np.float32))


def make_inmaps(inp, cfg, cores, xs=None):
    pv = pvec_build(inp)
    cfg["npvec"] = pv.shape[1]
    cst = consts_build()
    stages = cfg["stages"]
    maps = []
    for b in cores:
        x = inp["x"][b] if xs is None else xs[b]
        m = {"xT": np.ascontiguousarray(x.T), "pvec": pv,
             "posb": np.ascontiguousarray(np.broadcast_to(inp["positions"][b][None, :], (128, S)).astype(np.int32))}
        m.update(cst)
        are, aim, ldt = inp["s5_a_re"][0], inp["s5_a_im"][0], inp["s5_log_dt"][0]
        m["s5p"] = np.ascontiguousarray(np.concatenate([
            np.concatenate([are.T, are.T], 0), np.concatenate([aim.T, aim.T], 0),
            np.broadcast_to(ldt[None, :], (128, 64))], axis=1).astype(np.float32))
        bre = inp["s5_b_re"][0].transpose(1, 0, 2).reshape(64, 1024)
        bim = inp["s5_b_im"][0].transpose(1, 0, 2).reshape(64, 1024)
        m["s5B"] = np.ascontiguousarray(np.concatenate([np.concatenate([bre, bim], 0), np.concatenate([bim, bre], 0)], axis=1))
        cre = inp["s5_c_re"][0].transpose(2, 0, 1).reshape(64, 1024)
        cim = inp["s5_c_im"][0].transpose(2, 0, 1).reshape(64, 1024)
        m["s5C"] = np.ascontiguousarray(np.concatenate([np.concatenate([cre, cim], 0), np.concatenate([cim, cre], 0)], axis=1))
        for (k, l) in stages:
            if k == "ffn":
                m["w_gu%d" % l] = np.ascontiguousarray(inp["ffn_w_gate_up"][l])
                m["w_d%d" % l] = np.ascontiguousarray(inp["ffn_w_down"][l])
            elif k == "mix" and l == 1:
                m["dil_qkv"] = np.ascontiguousarray(inp["dil_w_qkv"][0])
                m["dil_o"] = np.ascontiguousarray(inp["dil_w_o"][0])
            elif k == "mix" and l == 0:
                m["s5_wglu"] = np.ascontiguousarray(inp["s5_w_glu"][0])
            elif k == "mix" and l == 2:
                m["hgrn_in"] = np.ascontiguousarray(inp["hgrn_w_in"][0])
                m["hgrn_o"] = np.ascontiguousarray(inp["hgrn_w_o"][0])
            elif k == "mix" and l == 3:
                m["moba_qkv"] = np.ascontiguousarray(inp["moba_w_qkv"][0])
                m["moba_o"] = np.ascontiguousarray(inp["moba_w_o"][0])
        maps.append(m)
    return maps


FULL_STAGES = [("mix", 0), ("ffn", 0), ("mix", 1), ("ffn", 1), ("mix", 2), ("ffn", 2), ("mix", 3), ("ffn", 3)]


def kernel(**inp):
    inp = {k: np.asarray(v) for k, v in inp.items()}
    cfg = {"stages": FULL_STAGES, "final_norm": True}
    maps = make_inmaps(inp, cfg, range(4))
    prog = Prog(cfg)
    nc = prog.build()
    res = run_bass_kernel_spmd(nc, maps, core_ids=list(range(4)))
    out = np.stack([res.results[b]["outT"].T for b in range(4)], axis=0)
    return np.ascontiguousarray(out.astype(np.float32))
```
